# Optimizing a Trainium2 kernel written in Bass

```python
import jax, jax.numpy as jnp
from jax import lax
import numpy as np

D_MODEL = 1024
BATCH = 4
SEQ = 8192
DEPTH = 4

GRID_W = 64
CTX_LEN = 256
MOD_CHUNKS = 9
D_FF = 2816
EPS = 1e-6

LRU_WIDTH = 384
LRU_BLOCKS = 6
LRU_BLOCK_W = LRU_WIDTH // LRU_BLOCKS
LRU_C = 8.0
CONV_W = 4
CONV_LEFT = 2

MLA_HEADS = 6
MLA_Q_RANK = 256
MLA_KV_RANK = 128
MLA_NOPE = 64
MLA_ROPE = 32
MLA_QK_DIM = MLA_NOPE + MLA_ROPE
MLA_V = 64
ROPE_BASE = 10000.0
ATTN_BLOCK = 128

NA_HEADS = 4
NA_HEAD_DIM = 64
NA_WIDTH = NA_HEADS * NA_HEAD_DIM
NA_WIN_ROWS = 8
NA_WIN_COLS = 16
NA_QCOL_BLOCK = 16
NA_KCOL_BLOCK = 32

MIX_WIDTH = LRU_WIDTH + MLA_HEADS * MLA_V + NA_WIDTH
IN_SPLITS = (LRU_WIDTH, LRU_WIDTH, MLA_Q_RANK, MLA_KV_RANK, MLA_ROPE, NA_WIDTH, NA_WIDTH, NA_WIDTH)
IN_WIDTH = 1952

kernel_name = 'hybrid_lru_mla_natten_macaron_dit'


def rms_norm(x, g):
    xf = x.astype(jnp.float32)
    y = xf * lax.rsqrt(jnp.mean(xf * xf, axis=-1, keepdims=True) + EPS)
    return (y * g.astype(jnp.float32)).astype(x.dtype)


def modulate(x, shift, scale):
    return x * (1 + scale[:, None, :]) + shift[:, None, :]


def swiglu(x, w_in, w_out):
    gate, up = jnp.split(x @ w_in, 2, axis=-1)
    return (jax.nn.silu(gate) * up) @ w_out


def split_columns(y):
    parts, start = [], 0
    for width in IN_SPLITS:
        parts.append(y[..., start:start + width])
        start += width
    return parts


def attention(q, k, v):
    s = jnp.einsum('bhqd,bhkd->bhqk', q, k).astype(jnp.float32) * (q.shape[-1] ** -0.5)
    p = jax.nn.softmax(s, axis=-1).astype(v.dtype)
    return jnp.einsum('bhqk,bhkd->bhqd', p, v)


def blocked_attention(q, k, v):
    B, H, T, _ = q.shape
    nb = T // ATTN_BLOCK
    qb = jnp.moveaxis(q.reshape(B, H, nb, ATTN_BLOCK, q.shape[-1]), 2, 0)
    ob = lax.map(lambda qi: attention(qi, k, v), qb)
    return jnp.moveaxis(ob, 0, 2).reshape(B, H, T, v.shape[-1])


def axial_rope(T, dtype):
    t = jnp.arange(T)
    pos = jnp.stack([t // GRID_W, t % GRID_W], axis=-1).astype(jnp.float32)
    half = MLA_ROPE // 2
    inv = ROPE_BASE ** (-jnp.arange(0, half, 2, dtype=jnp.float32) / half)
    ang = pos[:, None, :, None] * inv
    return jnp.cos(ang).astype(dtype), jnp.sin(ang).astype(dtype)


def apply_rope(x, cos, sin):
    xs = x.reshape(*x.shape[:-1], 2, 2, MLA_ROPE // 4)
    x1, x2 = xs[..., 0, :], xs[..., 1, :]
    out = jnp.stack([x1 * cos - x2 * sin, x2 * cos + x1 * sin], axis=-2)
    return out.reshape(x.shape)


def depthwise_conv(x, w, b):
    T = x.shape[1]
    xp = jnp.pad(x, ((0, 0), (CONV_LEFT, CONV_W - 1 - CONV_LEFT), (0, 0)))
    y = b
    for j in range(CONV_W):
        y = y + xp[:, j:j + T] * w[j]
    return y


def rglru(x, w_a, b_a, w_x, b_x, lam, h0):
    B, T, W = x.shape
    xb = x.reshape(B, T, LRU_BLOCKS, LRU_BLOCK_W)
    r = jax.nn.sigmoid(jnp.einsum('btnc,ncd->btnd', xb, w_a).reshape(B, T, W) + b_a).astype(jnp.float32)
    i = jax.nn.sigmoid(jnp.einsum('btnc,ncd->btnd', xb, w_x).reshape(B, T, W) + b_x)
    log_a = -LRU_C * r * jax.nn.softplus(-lam.astype(jnp.float32))
    a = jnp.exp(log_a)
    u = jnp.sqrt(-jnp.expm1(2.0 * log_a)) * (i * x).astype(jnp.float32)
    u = u.at[:, 0].add(a[:, 0] * h0.astype(jnp.float32))

    def combine(left, right):
        a_l, b_l = left
        a_r, b_r = right
        return a_l * a_r, a_r * b_l + b_r

    _, h = lax.associative_scan(combine, (a, u), axis=1)
    return h.astype(x.dtype)


def bidir_rglru(u_c, u_l, w_a, b_a, w_x, b_x, lam, need_ctx):
    zeros = jnp.zeros((u_c.shape[0], u_c.shape[2]), u_c.dtype)
    hc_f = rglru(u_c, w_a[0], b_a[0], w_x[0], b_x[0], lam[0], zeros)
    hl_f = rglru(u_l, w_a[0], b_a[0], w_x[0], b_x[0], lam[0], hc_f[:, -1])
    hc_b = jnp.flip(rglru(jnp.flip(u_c, 1), w_a[1], b_a[1], w_x[1], b_x[1], lam[1], zeros), 1)
    hl_b = jnp.flip(rglru(jnp.flip(u_l, 1), w_a[1], b_a[1], w_x[1], b_x[1], lam[1], hc_b[:, 0]), 1)
    yc = hc_f + hc_b if need_ctx else None
    return yc, hl_f + hl_b


def mla_queries(cq, q_norm, w_uq, q_gain, cos=None, sin=None):
    B, T, _ = cq.shape
    q = rms_norm((rms_norm(cq, q_norm) @ w_uq).reshape(B, T, MLA_HEADS, MLA_QK_DIM), q_gain)
    if cos is not None:
        q = jnp.concatenate([q[..., :MLA_NOPE], apply_rope(q[..., MLA_NOPE:], cos, sin)], axis=-1)
    return q.transpose(0, 2, 1, 3)


def mla_keys_values(ckv, kr, kv_norm, w_ukv, k_gain, cos=None, sin=None):
    B, T, _ = ckv.shape
    kv = (rms_norm(ckv, kv_norm) @ w_ukv).reshape(B, T, MLA_HEADS, MLA_NOPE + MLA_V)
    k_nope, v = kv[..., :MLA_NOPE], kv[..., MLA_NOPE:]
    k_rope = jnp.broadcast_to(kr[:, :, None, :], (B, T, MLA_HEADS, MLA_ROPE))
    k = rms_norm(jnp.concatenate([k_nope, k_rope], axis=-1), k_gain)
    if cos is not None:
        k = jnp.concatenate([k[..., :MLA_NOPE], apply_rope(k[..., MLA_NOPE:], cos, sin)], axis=-1)
    return k.transpose(0, 2, 1, 3), v.transpose(0, 2, 1, 3)


def neighborhood_attention(q, k, v, k_ctx, v_ctx, rpb):
    B, T, H, dh = q.shape
    rows = T // GRID_W
    wr = min(NA_WIN_ROWS, rows)
    n_cb = GRID_W // NA_QCOL_BLOCK
    n_loc = wr * NA_KCOL_BLOCK
    qg = q.reshape(B, rows, GRID_W, H, dh)
    kg = k.reshape(B, rows, GRID_W, H, dh)
    vg = v.reshape(B, rows, GRID_W, H, dh)
    q_rows = jnp.arange(rows)
    row_start = jnp.clip(q_rows - wr // 2, 0, rows - wr)
    q_cols = jnp.arange(GRID_W).reshape(n_cb, NA_QCOL_BLOCK)
    win_start = jnp.clip(q_cols - NA_WIN_COLS // 2, 0, GRID_W - NA_WIN_COLS)
    kblk_start = jnp.clip(jnp.arange(n_cb) * NA_QCOL_BLOCK - NA_WIN_COLS // 2, 0, GRID_W - NA_KCOL_BLOCK)
    k_cols = kblk_start[:, None] + jnp.arange(NA_KCOL_BLOCK)
    kc = k_cols[:, None, :]
    col_in = (kc >= win_start[..., None]) & (kc < win_start[..., None] + NA_WIN_COLS)
    col_off = jnp.clip(kc - q_cols[..., None] + NA_WIN_COLS - 1, 0, 2 * NA_WIN_COLS - 2)
    mask = jnp.broadcast_to(col_in[:, :, None, :], (n_cb, NA_QCOL_BLOCK, wr, NA_KCOL_BLOCK))
    mask = mask.reshape(n_cb, NA_QCOL_BLOCK, n_loc)
    scale = dh ** -0.5

    def one_row(args):
        q_row, r, r0 = args
        k_band = lax.dynamic_slice_in_dim(kg, r0, wr, axis=1)[:, :, k_cols]
        v_band = lax.dynamic_slice_in_dim(vg, r0, wr, axis=1)[:, :, k_cols]
        k_blk = k_band.transpose(0, 4, 2, 1, 3, 5).reshape(B, H, n_cb, n_loc, dh)
        v_blk = v_band.transpose(0, 4, 2, 1, 3, 5).reshape(B, H, n_cb, n_loc, dh)
        q_blk = q_row.reshape(B, n_cb, NA_QCOL_BLOCK, H, dh).transpose(0, 3, 1, 2, 4)
        row_off = r0 + jnp.arange(wr) - r + NA_WIN_ROWS - 1
        bias = rpb[:, row_off][:, :, col_off]
        bias = bias.transpose(0, 2, 3, 1, 4).reshape(H, n_cb, NA_QCOL_BLOCK, n_loc).astype(jnp.float32)
        s_loc = jnp.einsum('bhnqd,bhnkd->bhnqk', q_blk, k_blk).astype(jnp.float32) * scale + bias
        s_loc = jnp.where(mask, s_loc, -jnp.inf)
        s_ctx = jnp.einsum('bhnqd,bhcd->bhnqc', q_blk, k_ctx).astype(jnp.float32) * scale
        p = jax.nn.softmax(jnp.concatenate([s_loc, s_ctx], axis=-1), axis=-1).astype(v.dtype)
        o = (jnp.einsum('bhnqk,bhnkd->bhnqd', p[..., :n_loc], v_blk)
             + jnp.einsum('bhnqc,bhcd->bhnqd', p[..., n_loc:], v_ctx))
        return o.transpose(0, 2, 3, 1, 4).reshape(B, GRID_W, H, dh)

    out = lax.map(one_row, (jnp.moveaxis(qg, 1, 0), q_rows, row_start))
    return jnp.moveaxis(out, 0, 1).reshape(B, T, H, dh)


def hybrid_mixer(xc, xl, w_in, w_out, conv_w, conv_b, w_a, b_a, w_x, b_x, lam,
                 q_norm, w_uq, kv_norm, w_ukv, mq_gain, mk_gain, nq_gain, nk_gain, rpb,
                 cos, sin, need_ctx):
    B, C, _ = xc.shape
    T = xl.shape[1]
    c_lx, c_lg, c_cq, c_ckv, c_kr, c_nq, c_nk, c_nv = split_columns(xc @ w_in)
    l_lx, l_lg, l_cq, l_ckv, l_kr, l_nq, l_nk, l_nv = split_columns(xl @ w_in)

    yc_lru, yl_lru = bidir_rglru(depthwise_conv(c_lx, conv_w, conv_b), depthwise_conv(l_lx, conv_w, conv_b),
                                 w_a, b_a, w_x, b_x, lam, need_ctx)
    yl_lru = yl_lru * jax.nn.gelu(l_lg)

    kc_m, vc_m = mla_keys_values(c_ckv, c_kr, kv_norm, w_ukv, mk_gain)
    kl_m, vl_m = mla_keys_values(l_ckv, l_kr, kv_norm, w_ukv, mk_gain, cos, sin)
    ql_m = mla_queries(l_cq, q_norm, w_uq, mq_gain, cos, sin)
    yl_mla = blocked_attention(ql_m, jnp.concatenate([kc_m, kl_m], axis=2), jnp.concatenate([vc_m, vl_m], axis=2))
    yl_mla = yl_mla.transpose(0, 2, 1, 3).reshape(B, T, MLA_HEADS * MLA_V)

    kc_n = rms_norm(c_nk.reshape(B, C, NA_HEADS, NA_HEAD_DIM), nk_gain).transpose(0, 2, 1, 3)
    vc_n = c_nv.reshape(B, C, NA_HEADS, NA_HEAD_DIM).transpose(0, 2, 1, 3)
    ql_n = rms_norm(l_nq.reshape(B, T, NA_HEADS, NA_HEAD_DIM), nq_gain)
    kl_n = rms_norm(l_nk.reshape(B, T, NA_HEADS, NA_HEAD_DIM), nk_gain)
    vl_n = l_nv.reshape(B, T, NA_HEADS, NA_HEAD_DIM)
    yl_na = neighborhood_attention(ql_n, kl_n, vl_n, kc_n, vc_n, rpb).reshape(B, T, NA_WIDTH)

    yl = jnp.concatenate([yl_lru, yl_mla, yl_na], axis=-1) @ w_out
    if not need_ctx:
        return None, yl
    yc_lru = yc_lru * jax.nn.gelu(c_lg)
    qc_m = mla_queries(c_cq, q_norm, w_uq, mq_gain)
    yc_mla = attention(qc_m, kc_m, vc_m).transpose(0, 2, 1, 3).reshape(B, C, MLA_HEADS * MLA_V)
    qc_n = rms_norm(c_nq.reshape(B, C, NA_HEADS, NA_HEAD_DIM), nq_gain).transpose(0, 2, 1, 3)
    yc_na = attention(qc_n, kc_n, vc_n).transpose(0, 2, 1, 3).reshape(B, C, NA_WIDTH)
    yc = jnp.concatenate([yc_lru, yc_mla, yc_na], axis=-1) @ w_out
    return yc, yl


def setup_inputs(seed: int = 0) -> dict:
    key = jax.random.key(seed)
    ks = iter(jax.random.split(key, 40))

    def normal(shape, scale):
        return jax.random.normal(next(ks), shape, jnp.float32) * scale

    def gain(shape):
        return 1.0 + normal(shape, 0.01)

    u = jax.random.uniform(next(ks), (DEPTH, 2, LRU_WIDTH), jnp.float32, minval=0.9, maxval=0.999)
    return {
        'x': normal((BATCH, SEQ, D_MODEL), 1.0),
        'c': normal((BATCH, D_MODEL), 1.0),
        'ctx': normal((BATCH, CTX_LEN, D_MODEL), 1.0),
        'c_ctx': normal((D_MODEL,), 1.0),
        'w_mod': normal((DEPTH, D_MODEL, MOD_CHUNKS * D_MODEL), 0.5 * D_MODEL ** -0.5),
        'b_mod': normal((DEPTH, MOD_CHUNKS * D_MODEL), 0.01),
        'norm_ffn1': gain((DEPTH, D_MODEL)),
        'ffn1_w_in': normal((DEPTH, D_MODEL, 2 * D_FF), D_MODEL ** -0.5),
        'ffn1_w_out': normal((DEPTH, D_FF, D_MODEL), D_FF ** -0.5),
        'norm_mix': gain((DEPTH, D_MODEL)),
        'w_in': normal((DEPTH, D_MODEL, IN_WIDTH), D_MODEL ** -0.5),
        'w_out': normal((DEPTH, MIX_WIDTH, D_MODEL), MIX_WIDTH ** -0.5),
        'lru_conv_w': normal((DEPTH, CONV_W, LRU_WIDTH), CONV_W ** -0.5),
        'lru_conv_b': normal((DEPTH, LRU_WIDTH), 0.01),
        'lru_w_a': normal((DEPTH, 2, LRU_BLOCKS, LRU_BLOCK_W, LRU_BLOCK_W), LRU_BLOCK_W ** -0.5),
        'lru_b_a': normal((DEPTH, 2, LRU_WIDTH), 0.01),
        'lru_w_x': normal((DEPTH, 2, LRU_BLOCKS, LRU_BLOCK_W, LRU_BLOCK_W), LRU_BLOCK_W ** -0.5),
        'lru_b_x': normal((DEPTH, 2, LRU_WIDTH), 0.01),
        'lru_lambda': jnp.log(u) - jnp.log1p(-u),
        'mla_q_norm': gain((DEPTH, MLA_Q_RANK)),
        'mla_w_uq': normal((DEPTH, MLA_Q_RANK, MLA_HEADS * MLA_QK_DIM), MLA_Q_RANK ** -0.5),
        'mla_kv_norm': gain((DEPTH, MLA_KV_RANK)),
        'mla_w_ukv': normal((DEPTH, MLA_KV_RANK, MLA_HEADS * (MLA_NOPE + MLA_V)), MLA_KV_RANK ** -0.5),
        'mla_q_gain': gain((DEPTH, MLA_QK_DIM)),
        'mla_k_gain': gain((DEPTH, MLA_QK_DIM)),
        'na_q_gain': gain((DEPTH, NA_HEAD_DIM)),
        'na_k_gain': gain((DEPTH, NA_HEAD_DIM)),
        'na_rpb': normal((DEPTH, NA_HEADS, 2 * NA_WIN_ROWS - 1, 2 * NA_WIN_COLS - 1), 0.5),
        'norm_ffn2': gain((DEPTH, D_MODEL)),
        'ffn2_w_in': normal((DEPTH, D_MODEL, 2 * D_FF), D_MODEL ** -0.5),
        'ffn2_w_out': normal((DEPTH, D_FF, D_MODEL), D_FF ** -0.5),
    }


def reference(x, c, ctx, c_ctx, w_mod, b_mod, norm_ffn1, ffn1_w_in, ffn1_w_out, norm_mix, w_in, w_out,
              lru_conv_w, lru_conv_b, lru_w_a, lru_b_a, lru_w_x, lru_b_x, lru_lambda,
              mla_q_norm, mla_w_uq, mla_kv_norm, mla_w_ukv, mla_q_gain, mla_k_gain,
              na_q_gain, na_k_gain, na_rpb, norm_ffn2, ffn2_w_in, ffn2_w_out):
    T = x.shape[1]
    cos, sin = axial_rope(T, x.dtype)
    h, hc = x, ctx
    for layer in range(DEPTH):
        last = layer == DEPTH - 1
        mod = jax.nn.silu(c) @ w_mod[layer] + b_mod[layer]
        mod_c = jax.nn.silu(c_ctx)[None, :] @ w_mod[layer] + b_mod[layer]
        sh1, sc1, g1, shm, scm, gm, sh2, sc2, g2 = jnp.split(mod, MOD_CHUNKS, axis=-1)
        csh1, csc1, cg1, cshm, cscm, cgm, csh2, csc2, cg2 = jnp.split(mod_c, MOD_CHUNKS, axis=-1)

        h = h + 0.5 * g1[:, None] * swiglu(modulate(rms_norm(h, norm_ffn1[layer]), sh1, sc1),
                                           ffn1_w_in[layer], ffn1_w_out[layer])
        hc = hc + 0.5 * cg1[:, None] * swiglu(modulate(rms_norm(hc, norm_ffn1[layer]), csh1, csc1),
                                              ffn1_w_in[layer], ffn1_w_out[layer])

        yc, yl = hybrid_mixer(
            modulate(rms_norm(hc, norm_mix[layer]), cshm, cscm),
            modulate(rms_norm(h, norm_mix[layer]), shm, scm),
            w_in[layer], w_out[layer], lru_conv_w[layer], lru_conv_b[layer],
            lru_w_a[layer], lru_b_a[layer], lru_w_x[layer], lru_b_x[layer], lru_lambda[layer],
            mla_q_norm[layer], mla_w_uq[layer], mla_kv_norm[layer], mla_w_ukv[layer],
            mla_q_gain[layer], mla_k_gain[layer], na_q_gain[layer], na_k_gain[layer], na_rpb[layer],
            cos, sin, not last)
        h = h + gm[:, None] * yl

        h = h + 0.5 * g2[:, None] * swiglu(modulate(rms_norm(h, norm_ffn2[layer]), sh2, sc2),
                                           ffn2_w_in[layer], ffn2_w_out[layer])
        if not last:
            hc = hc + cgm[:, None] * yc
            hc = hc + 0.5 * cg2[:, None] * swiglu(modulate(rms_norm(hc, norm_ffn2[layer]), csh2, csc2),
                                                  ffn2_w_in[layer], ffn2_w_out[layer])
    return h
```

```python
from contextlib import ExitStack
import numpy as np
import concourse.bass as bass
import concourse.mybir as mybir
from concourse.bass_utils import run_bass_kernel_spmd

F32 = mybir.dt.float32
BF16 = mybir.dt.bfloat16
AF = mybir.ActivationFunctionType
ALU = mybir.AluOpType

D = 1024
DFF = 2816
KD = 8
JF = 22
TB = 256
EPS = 1e-6
NL = 4096
NCX = 256
NT = NL + NCX
DEPTH = 4
NEG = -30000.0
NVEC = 66
WINX = 2112
MLA_SCALE = 96 ** -0.5
NA_SCALE = 0.125

ENGS = ("pe", "act", "dve", "pool", "sp")
NDMA_SLOTS = 8


class Op:
    __slots__ = ("eng", "emit", "is_dma", "pos", "deps", "signal", "ticket", "waits", "slot", "slot_val", "idx")

    def __init__(self, eng, emit, is_dma):
        self.eng = eng
        self.emit = emit
        self.is_dma = is_dma
        self.deps = []
        self.signal = False
        self.ticket = 0
        self.waits = []
        self.slot = None
        self.slot_val = 0


class Sched:
    def __init__(self):
        self.ops = []
        self.streams = {e: [] for e in ENGS}
        self.last_writer = {}
        self.readers = {}
        self.dma_count = {e: 0 for e in ENGS}
        self.slot_last = {}
        self.barrier_ops = []
        self.barrier_pending = set()

    def barrier(self):
        ops = [s[-1] for s in self.streams.values() if s]
        ops += list(self.slot_last.values())
        self.barrier_ops = ops
        self.barrier_pending = set(ENGS)

    def add(self, eng, emit, reads=(), writes=(), dma=False, cc=False):
        op = Op(eng, emit, dma or cc)
        op.idx = len(self.ops)
        op.pos = len(self.streams[eng])
        deps = set()
        if eng in self.barrier_pending:
            deps.update(self.barrier_ops)
            self.barrier_pending.discard(eng)
        for k in reads:
            w = self.last_writer.get(k)
            if w is not None:
                deps.add(w)
        for k in writes:
            w = self.last_writer.get(k)
            if w is not None:
                deps.add(w)
            rd = self.readers.get(k)
            if rd:
                deps.update(rd.values())
        if cc:
            op.slot = ("cc", 0)
            prev = self.slot_last.get(op.slot)
            op.slot_val = (prev.slot_val + 1) if prev is not None else 1
            self.slot_last[op.slot] = op
        elif dma:
            n = self.dma_count[eng]
            self.dma_count[eng] = n + 1
            op.slot = (eng, n % NDMA_SLOTS)
            prev = self.slot_last.get(op.slot)
            if prev is not None:
                deps.add(prev)
                op.slot_val = prev.slot_val + 16
            else:
                op.slot_val = 16
            self.slot_last[op.slot] = op
        deps.discard(op)
        op.deps = sorted(deps, key=lambda o: o.idx)
        for k in reads:
            rk = ("d", op.idx) if dma else eng
            self.readers.setdefault(k, {})[rk] = op
        for k in writes:
            self.last_writer[k] = op
            self.readers[k] = {}
        self.ops.append(op)
        self.streams[eng].append(op)
        return op

    def finalize(self):
        waited = {e: {} for e in ENGS}
        for op in self.ops:
            w = waited[op.eng]
            for p in op.deps:
                if p.is_dma:
                    key = ("dma", p.slot)
                    if w.get(key, 0) >= p.slot_val:
                        continue
                    w[key] = p.slot_val
                    op.waits.append(p)
                else:
                    if p.eng == op.eng:
                        if p.eng == "pe":
                            continue
                        if not op.is_dma and op.pos - p.pos > 2:
                            continue
                    key = ("eng", p.eng)
                    if w.get(key, -1) >= p.pos:
                        continue
                    w[key] = p.pos
                    p.signal = True
                    op.waits.append(p)
        for e in ENGS:
            t = 0
            for op in self.streams[e]:
                if not op.is_dma and op.signal:
                    t += 1
                    op.ticket = t

    def emit_all(self, nc, stack):
        self.finalize()
        eng_sem = {e: stack.enter_context(nc.semaphore("s_" + e)) for e in ENGS}
        dma_sem = {}
        for e in ENGS:
            for i in range(min(NDMA_SLOTS, self.dma_count[e])):
                dma_sem[(e, i)] = stack.enter_context(nc.semaphore("d_%s%d" % (e, i)))
        if ("cc", 0) in self.slot_last:
            dma_sem[("cc", 0)] = stack.enter_context(nc.semaphore("cc_sem"))
        block = stack.enter_context(nc.Block())

        def run_stream(e, engobj):
            for op in self.streams[e]:
                for p in op.waits:
                    if p.is_dma:
                        engobj.wait_ge(dma_sem[p.slot], p.slot_val)
                    else:
                        engobj.wait_ge(eng_sem[p.eng], p.ticket)
                ins = op.emit(engobj)
                if op.is_dma:
                    if op.slot[0] == "cc":
                        ins.then_inc(dma_sem[op.slot])
                    else:
                        ins.then_inc(dma_sem[op.slot], 16)
                elif op.signal:
                    ins.then_inc(eng_sem[op.eng], 1)
            for slot, last in self.slot_last.items():
                if slot[0] == e or (slot[0] == "cc" and e == "pool"):
                    engobj.wait_ge(dma_sem[slot], last.slot_val)

        if self.streams["sp"]:
            @block.sync
            def _(eng):
                run_stream("sp", eng)
        if self.streams["act"]:
            @block.scalar
            def _(eng):
                run_stream("act", eng)
        if self.streams["dve"]:
            @block.vector
            def _(eng):
                run_stream("dve", eng)
        if self.streams["pool"]:
            @block.gpsimd
            def _(eng):
                run_stream("pool", eng)
        if self.streams["pe"]:
            @block.tensor
            def _(eng):
                run_stream("pe", eng)


def mk(name, *args, **kw):
    return lambda e: getattr(e, name)(*args, **kw)


class Ctx:
    def __init__(self):
        self.nc = bass.Bass("TRN2", target_bir_lowering=False)
        self.S = Sched()
        self.stack = ExitStack()
        self.t = {}
        self.cur = self.stack
        self.uid = 0
        self.bank_rr = 0

    def sb(self, name, shape, dtype):
        self.uid += 1
        t = self.cur.enter_context(self.nc.sbuf_tensor("%s_%d" % (name, self.uid), list(shape), dtype))
        self.t[name] = t
        return t

    def ps(self, name, shape, dtype=F32):
        self.uid += 1
        t = self.cur.enter_context(self.nc.psum_tensor("%s_%d" % (name, self.uid), list(shape), dtype))
        self.t[name] = t
        return t

    def dram(self, name, shape, dtype, kind):
        return self.nc.dram_tensor(name, list(shape), dtype, kind=kind).ap()

    def scope(self):
        return _Scope(self)

    def finish(self):
        self.S.emit_all(self.nc, self.stack)
        self.stack.close()
        return self.nc


class _Scope:
    def __init__(self, C):
        self.C = C

    def __enter__(self):
        self.prev = self.C.cur
        self.st = ExitStack()
        self.C.cur = self.st
        return self

    def __exit__(self, *a):
        self.st.close()
        self.C.cur = self.prev
        self.C.S.barrier()
        return False


def setup_consts(C):
    S = C.S
    ones_b = C.sb("ones_b", [128, 128], BF16)
    ones_bd = C.sb("ones_bd", [128, 128], BF16)
    nhalf = C.sb("nhalf", [128, 512], F32)
    phalf = C.sb("phalf", [128, 512], F32)
    S.add("pool", mk("memset", ones_b[:], 1.0), writes=["ones_b"])
    S.add("pool", mk("memset", ones_bd[:], 0.0), writes=["ones_bd"])
    S.add("pool", mk("memset", ones_bd[0:64, 0:64], 1.0), writes=["ones_bd"])
    S.add("pool", mk("memset", ones_bd[64:128, 64:128], 1.0), writes=["ones_bd"])
    S.add("pool", mk("memset", nhalf[:], -0.5), writes=["nhalf"])
    S.add("pool", mk("memset", phalf[:], 0.5), writes=["phalf"])


def rstd_from_psum(C, ps_ap, M, nt, inv_n, out_ap, tmp_ap, rkeys, wkey, tmpkey):
    S = C.S
    nhalf = C.t["nhalf"]
    S.add("dve", mk("tensor_scalar", out=tmp_ap, in0=ps_ap, scalar1=inv_n, scalar2=EPS,
                                           op0=ALU.mult, op1=ALU.add),
          reads=rkeys, writes=[tmpkey])
    S.add("pool", mk("tensor_tensor", out=out_ap, in0=tmp_ap, in1=nhalf[0:M, 0:nt], op=ALU.pow),
          reads=[tmpkey, "nhalf"], writes=[wkey])


def mod_phase(C, w_mod_d, b_mod_d, vecs, scin_d, name):
    S = C.S
    GS = C.sb("GS" + name, [128, 3, KD, 2], F32)
    SH = C.sb("SH" + name, [128, 3, KD, 2], F32)
    GT = C.sb("GT" + name, [128, 3, KD, 2], F32)
    key = "mods" + name
    with C.scope():
        sc = C.sb("sc", [128, KD, 2], F32)
        scs = C.sb("scs", [128, KD, 2], F32)
        bm = C.sb("bm", [128, 72], F32)
        modT = C.sb("modT", [128, 72, 2], F32)
        wm = [C.sb("wm%d" % i, [128, KD, 1024], F32) for i in range(2)]
        mp = C.ps("modps", [128, 72, 2])
        S.add("sp", mk("dma_start", out=sc[:], in_=scin_d), writes=["sc"], dma=True)
        S.add("sp", mk("dma_start", out=bm[:], in_=b_mod_d), writes=["bm"], dma=True)
        S.add("act", mk("activation", out=scs[:], in_=sc[:], func=AF.Silu), reads=["sc"], writes=["scs"])
        wv = w_mod_d.rearrange("(k p) n -> p k n", p=128)
        for mc in range(9):
            w = wm[mc % 2]
            for k in range(KD):
                S.add("sp", mk("dma_start", out=w[:, k, :],
                                                                   in_=w_mod_d[k * 128:(k + 1) * 128, mc * 1024:(mc + 1) * 1024]),
                      writes=[("wm", mc % 2, k)], dma=True)
            for j in range(8):
                ch = mc * 8 + j
                for k in range(KD):
                    S.add("pe", mk("matmul", mp[:, ch, :], w[:, k, j * 128:(j + 1) * 128],
                                                                         scs[:, k, :], start=(k == 0), stop=(k == KD - 1)),
                          reads=[("wm", mc % 2, k), "scs"], writes=["modps"])
        for c in range(2):
            S.add("dve", mk("tensor_tensor", out=modT[:, :, c], in0=mp[:, :, c], in1=bm[:], op=ALU.add),
                  reads=["modps", "bm"], writes=["modT"])
        for w3 in range(3):
            g = vecs[:, 8 * w3:8 * w3 + 8]
            for c in range(2):
                S.add("dve", mk("scalar_tensor_tensor",
                    out=GS[:, w3, :, c], in0=modT[:, (3 * w3 + 1) * 8:(3 * w3 + 2) * 8, c], scalar=1.0, in1=g,
                    op0=ALU.add, op1=ALU.mult), reads=["modT", "vecs"], writes=[key])
                S.add("dve", mk("tensor_copy", out=SH[:, w3, :, c], in_=modT[:, (3 * w3) * 8:(3 * w3 + 1) * 8, c]),
                      reads=["modT"], writes=[key])
                gsc = 1.0 if w3 == 1 else 0.5
                S.add("dve", mk("tensor_scalar",
                    out=GT[:, w3, :, c], in0=modT[:, (3 * w3 + 2) * 8:(3 * w3 + 3) * 8, c], scalar1=gsc, scalar2=None,
                    op0=ALU.mult), reads=["modT"], writes=[key])
    mods = {}
    for w3, wn in enumerate(("ffn1", "mix", "ffn2")):
        for c, cn in enumerate(("lat", "ctx")):
            mods[(wn, cn)] = dict(gs=GS[:, w3, :, c], sh=SH[:, w3, :, c], gh=GT[:, w3, :, c], key=key)
    return mods


def ffn_phase(C, hT_d, w_in_d, w_out_d, blocks, mods, wn):
    S = C.S
    NSPL = 4
    W = 2 * DFF // NSPL
    with C.scope():
        win = C.sb("win", [128, KD, 2 * DFF], BF16)
        wout = C.sb("wout", [128, JF, D], BF16)
        hbs = [C.sb("hb%d" % i, [128, KD, TB], F32) for i in range(3)]
        xns = [C.sb("xn%d" % i, [128, KD, TB], BF16) for i in range(2)]
        sqs = [C.sb("sq%d" % i, [128, KD, TB], BF16) for i in range(2)]
        rstds = [C.sb("rstd%d" % i, [128, TB], F32) for i in range(2)]
        tmps = [C.sb("tmp%d" % i, [128, TB], F32) for i in range(2)]
        sgs = [C.sb("sg%d" % i, [128, TB], F32) for i in range(4)]
        gjs = [C.sb("gj%d" % i, [128, TB], BF16) for i in range(4)]
        acc = C.ps("acc", [128, 8, TB])
        gu = C.ps("gu", [128, 8, TB])
        ones_b = C.t["ones_b"]
        for s in (0, 2, 1, 3):
            for k in range(KD):
                S.add("pool", mk("dma_start", out=win[:, k, s * W:(s + 1) * W],
                                                              in_=w_in_d[k * 128:(k + 1) * 128, s * W:(s + 1) * W]),
                      writes=[("win", k, s)], dma=True)
        for j in range(JF):
            S.add("pool", mk("dma_start", out=wout[:, j, :], in_=w_out_d[j * 128:(j + 1) * 128, :]),
                  writes=[("wout", j)], dma=True)
        nb = len(blocks)
        hview = hT_d.rearrange("(k p) t -> p k t", p=128)
        gu_slot = [0]

        def next_gu():
            s = gu_slot[0]
            gu_slot[0] = (s + 1) % 4
            return s

        def load(b):
            t0 = blocks[b][0]
            hb = hbs[b % 3]
            S.add("sp", mk("dma_start", out=hb[:], in_=hview[:, :, t0:t0 + TB]),
                  reads=[("hT", t0)], writes=[("hb", b % 3)], dma=True)

        def norm_a(b):
            hb, sq = hbs[b % 3], sqs[b % 2]
            S.add("act", mk("activation", out=sq[:], in_=hb[:], func=AF.Square),
                  reads=[("hb", b % 3)], writes=[("sq", b % 2)])

        def norm_b(b):
            m = mods[(wn, blocks[b][1])]
            hb, sq, xn, rstd, tmp = hbs[b % 3], sqs[b % 2], xns[b % 2], rstds[b % 2], tmps[b % 2]
            s = next_gu()
            ssp = gu[:, 2 * s, :]
            for k in range(KD):
                S.add("pe", mk("matmul", ssp, ones_b[:], sq[:, k, :], start=(k == 0), stop=(k == KD - 1)),
                      reads=[("sq", b % 2), "ones_b"], writes=[("gu", s)])
            rstd_from_psum(C, ssp, 128, TB, 1.0 / D, rstd[:], tmp[:], [("gu", s)], ("rstd", b % 2), ("tmp", b % 2))
            for k in range(KD):
                S.add("dve", mk("scalar_tensor_tensor", out=tmp[:], in0=hb[:, k, :], scalar=m["gs"][:, k:k + 1],
                                                                   in1=rstd[:], op0=ALU.mult, op1=ALU.mult),
                      reads=[("hb", b % 3), ("rstd", b % 2), m["key"]], writes=[("tmp", b % 2)])
                S.add("act", mk("activation", out=xn[:, k, :], in_=tmp[:], func=AF.Identity,
                                                         bias=m["sh"][:, k:k + 1], scale=1.0),
                      reads=[("tmp", b % 2), m["key"]], writes=[("xn", b % 2, k)])

        def main(b):
            m = mods[(wn, blocks[b][1])]
            t0 = blocks[b][0]
            hb, xn = hbs[b % 3], xns[b % 2]
            xkeys = [("xn", b % 2, k) for k in range(KD)]
            for jj in range(JF + 2):
                if jj < JF:
                    j = jj
                    s = next_gu()
                    gp = gu[:, 2 * s, :]
                    up = gu[:, 2 * s + 1, :]
                    for k in range(KD):
                        S.add("pe", mk("matmul", gp, win[:, k, j * 128:(j + 1) * 128], xn[:, k, :],
                                                                       start=(k == 0), stop=(k == KD - 1)),
                              reads=[xkeys[k], ("win", k, (j * 128) // W), ("win", k, ((j + 1) * 128 - 1) // W)],
                              writes=[("gu", s)])
                    for k in range(KD):
                        c0 = DFF + j * 128
                        S.add("pe", mk("matmul", up, win[:, k, c0:c0 + 128], xn[:, k, :],
                                                                          start=False, stop=(k == KD - 1),
                                                                          skip_group_check=True),
                              reads=[xkeys[k], ("win", k, c0 // W), ("win", k, (c0 + 127) // W)],
                              writes=[("gu", s)])
                    sg, gj = sgs[j % 4], gjs[j % 4]
                    S.add("act", mk("activation", out=sg[:], in_=gp, func=AF.Silu),
                          reads=[("gu", s)], writes=[("sg", j % 4)])
                    S.add("dve", mk("tensor_tensor", out=gj[:], in0=up, in1=sg[:], op=ALU.mult),
                          reads=[("gu", s), ("sg", j % 4)], writes=[("gj", j % 4)])
                if jj >= 2:
                    j = jj - 2
                    gj = gjs[j % 4]
                    for n in range(KD):
                        f = (j == 0 and n % 2 == 0)
                        S.add("pe", mk("matmul",
                            acc[:, n, :], wout[:, j, n * 128:(n + 1) * 128], gj[:],
                            start=f, stop=(j == JF - 1), skip_group_check=True),
                              reads=[("gj", j % 4), ("wout", j)], writes=[("acc", n // 2)])
                if jj == 6 and b + 1 < nb:
                    norm_b(b + 1)
            for n in range(KD):
                S.add("dve", mk("scalar_tensor_tensor", out=hb[:, n, :], in0=acc[:, n, :],
                                                                   scalar=m["gh"][:, n:n + 1], in1=hb[:, n, :],
                                                                   op0=ALU.mult, op1=ALU.add),
                      reads=[("acc", n // 2), m["key"], ("hb", b % 3)], writes=[("hb", b % 3)])
            S.add("sp", mk("dma_start", out=hview[:, :, t0:t0 + TB], in_=hb[:]),
                  reads=[("hb", b % 3)], writes=[("hT", t0)], dma=True)

        load(0)
        if nb > 1:
            load(1)
        norm_a(0)
        norm_b(0)
        for b in range(nb):
            if b + 2 < nb:
                load(b + 2)
            if b + 1 < nb:
                norm_a(b + 1)
            main(b)


def ffn_blocks(do_ctx=True):
    return [(t0, "lat" if t0 < NL else "ctx") for t0 in range(0, NT if do_ctx else NL, TB)]


def hT_keys(t0, nt):
    return [("hT", t) for t in range(t0, t0 + nt, TB)]


class BankRR:
    def __init__(self, C, n=8):
        self.t = C.ps("bank", [128, n, 512])
        self.n = n
        self.i = 0

    def next(self):
        b = self.i
        self.i = (self.i + 1) % self.n
        return b, self.t, ("bank", b)


def m1_phase(C, hT_d, Wd, vecs, mods, ropeC_d, ropeS_d, O, exch=None):
    S = C.S
    blocks = [(t0, 512, "lat") for t0 in range(0, NL, 512)] + [(NL, 256, "ctx")]
    with C.scope():
        winx = C.sb("winx", [128, KD, WINX], BF16)
        wuq = C.sb("wuq", [128, 2, 1152], BF16)
        wukv = C.sb("wukv", [128, 768], BF16)
        for k in range(KD):
            for s2 in range(2):
                c0, c1 = s2 * 1056, (s2 + 1) * 1056
                S.add("pool", mk("dma_start", out=winx[:, k, c0:c1],
                                                                       in_=Wd["winx"][k * 128:(k + 1) * 128, c0:c1]),
                      writes=[("winx", k)], dma=True)
        for k in range(2):
            S.add("pool", mk("dma_start", out=wuq[:, k, :], in_=Wd["wuq"][k * 128:(k + 1) * 128, :]),
                  writes=["wuq"], dma=True)
        S.add("pool", mk("dma_start", out=wukv[:], in_=Wd["wukv"]), writes=["wukv"], dma=True)
        hb = [C.sb("mhb%d" % i, [128, KD, 512], F32) for i in range(2)]
        xn = C.sb("mxn", [128, KD, 512], BF16)
        sq = C.sb("msq", [128, KD, 512], BF16)
        rstd = C.sb("mrstd", [128, 512], F32)
        tmp = C.sb("mtmp", [128, 512], F32)
        rc = C.sb("rc", [128, 512], F32)
        rs = C.sb("rs", [128, 512], F32)
        st3 = [C.sb("st3_%d" % i, [128, 3, 512], F32) for i in range(2)]
        sq2 = C.sb("sq2", [128, 2, 512], BF16)
        cqn = C.sb("cqn", [128, 2, 512], BF16)
        ckvn = C.sb("ckvn", [128, 512], BF16)
        sqk = C.sb("sqk", [128, 512], BF16)
        tA = C.sb("tA", [128, 512], F32)
        t1 = [C.sb("t1_%d" % i, [128, 512], F32) for i in range(2)]
        t2 = [C.sb("t2_%d" % i, [128, 512], F32) for i in range(2)]
        rq = [C.sb("rq_%d" % i, [128, 512], F32) for i in range(2)]
        tq = [C.sb("tq_%d" % i, [128, 512], F32) for i in range(2)]
        sqh = [C.sb("sqh_%d" % i, [128, 512], BF16) for i in range(2)]
        qh = [C.sb("qh_%d" % i, [128, 512], BF16) for i in range(2)]
        kh = [C.sb("kh_%d" % i, [128, 512], BF16) for i in range(2)]
        nst = [C.sb("nst_%d" % i, [128, 2, 512], BF16) for i in range(2)]
        nva = C.sb("nva", [128, 4, 4, 128], BF16)
        va = C.sb("va", [128, 4, 6, 128], BF16)
        B = BankRR(C, 8)
        ones_b, ones_bd = C.t["ones_b"], C.t["ones_bd"]
        S.add("pool", mk("memset", nva[:], 1.0), writes=["nva"])
        S.add("pool", mk("memset", va[:], 1.0), writes=["va"])
        hview = hT_d.rearrange("(k p) t -> p k t", p=128)
        V = lambda c: vecs[:, c:c + 1]
        exch_done = set()

        def load(bi):
            t0, nt, kind = blocks[bi]
            h = hb[bi % 2]
            S.add("sp", mk("dma_start", out=h[:, :, 0:nt], in_=hview[:, :, t0:t0 + nt]),
                  reads=hT_keys(t0, nt), writes=[("mhb", bi % 2)], dma=True)

        load(0)
        for bi, (t0, nt, kind) in enumerate(blocks):
            if bi + 1 < len(blocks):
                load(bi + 1)
            m = mods[("mix", kind)]
            h = hb[bi % 2]
            hk = ("mhb", bi % 2)
            if kind == "lat":
                q4, off = t0 // 1024, t0 % 1024
                dst2 = lambda nm, q4=q4: O["L_" + nm][q4]
            else:
                q4, off = "c", 0
                dst2 = lambda nm: O["L_" + nm + "c"]
            S.add("sp", mk("dma_start", out=rc[64:96, 0:nt], in_=ropeC_d[:, t0:t0 + nt]),
                  writes=["rc"], dma=True)
            S.add("sp", mk("dma_start", out=rs[64:96, 0:nt], in_=ropeS_d[:, t0:t0 + nt]),
                  writes=["rs"], dma=True)
            S.add("act", mk("activation", out=sq[:, :, 0:nt], in_=h[:, :, 0:nt], func=AF.Square),
                  reads=[hk], writes=["msq"])
            b, bt, bk = B.next()
            for k in range(KD):
                S.add("pe", mk("matmul", bt[:, b, 0:nt], ones_b[:], sq[:, k, 0:nt],
                                                               start=(k == 0), stop=(k == KD - 1)),
                      reads=["msq", "ones_b"], writes=[bk])
            rstd_from_psum(C, bt[:, b, 0:nt], 128, nt, 1.0 / D, rstd[:, 0:nt], tmp[:, 0:nt], [bk], "mrstd", "mtmp")
            for k in range(KD):
                S.add("dve", mk("scalar_tensor_tensor",
                    out=tmp[:, 0:nt], in0=h[:, k, 0:nt], scalar=m["gs"][:, k:k + 1], in1=rstd[:, 0:nt],
                    op0=ALU.mult, op1=ALU.mult), reads=[hk, "mrstd", m["key"]], writes=["mtmp"])
                S.add("act", mk("activation", out=xn[:, k, 0:nt], in_=tmp[:, 0:nt], func=AF.Identity,
                                                                   bias=m["sh"][:, k:k + 1], scale=1.0),
                      reads=["mtmp", m["key"]], writes=[("mxn", k)])

            def proj(col0, M):
                b, bt, bk = B.next()
                for k in range(KD):
                    S.add("pe", mk("matmul", bt[0:M, b, 0:nt], winx[:, k, col0:col0 + M], xn[:, k, 0:nt],
                                                             start=(k == 0), stop=(k == KD - 1)),
                          reads=[("mxn", k), ("winx", k)], writes=[bk])
                return bt[0:M, b, 0:nt], bk

            s3 = st3[0]
            for c in range(3):
                p, pk = proj(c * 128, 128)
                S.add("act", mk("activation", out=s3[:, c, 0:nt], in_=p, func=AF.Copy),
                      reads=[pk], writes=[("st3", 0)])
            S.add("sp", mk("dma_start", out=dst2("lx").rearrange("(c p) t -> p c t", p=128)[:, :, off:off + nt],
                                                     in_=s3[:, :, 0:nt]),
                  reads=[("st3", 0)], writes=[("L", "lx", q4)], dma=True)
            s3 = st3[1]
            for c in range(3):
                p, pk = proj(384 + c * 128, 128)
                S.add("act", mk("activation", out=s3[:, c, 0:nt], in_=p, func=AF.Gelu_apprx_tanh),
                      reads=[pk], writes=[("st3", 1)])
            S.add("sp", mk("dma_start", out=O["lgel"].rearrange("(c p) t -> p c t", p=128)[:, :, t0:t0 + nt],
                                                     in_=s3[:, :, 0:nt]),
                  reads=[("st3", 1)], writes=[("lgel", t0)], dma=True)
            pcq = []
            for c in range(2):
                p, pk = proj(768 + c * 128, 128)
                pcq.append((p, pk))
                S.add("act", mk("activation", out=sq2[:, c, 0:nt], in_=p, func=AF.Square),
                      reads=[pk], writes=["sq2"])
            b, bt, bk = B.next()
            for c in range(2):
                S.add("pe", mk("matmul", bt[:, b, 0:nt], ones_b[:], sq2[:, c, 0:nt], start=(c == 0), stop=(c == 1)),
                      reads=["sq2", "ones_b"], writes=[bk])
            rstd_from_psum(C, bt[:, b, 0:nt], 128, nt, 1.0 / 256, rstd[:, 0:nt], tmp[:, 0:nt], [bk], "mrstd", "mtmp")
            for c in range(2):
                p, pk = pcq[c]
                S.add("dve", mk("scalar_tensor_tensor", out=cqn[:, c, 0:nt], in0=p, scalar=V(24 + c),
                                                                        in1=rstd[:, 0:nt], op0=ALU.mult, op1=ALU.mult),
                      reads=[pk, "mrstd", "vecs"], writes=["cqn"])
            p, pk = proj(1024, 128)
            S.add("act", mk("activation", out=sq2[:, 0, 0:nt], in_=p, func=AF.Square), reads=[pk], writes=["sq2"])
            b, bt, bk = B.next()
            S.add("pe", mk("matmul", bt[:, b, 0:nt], ones_b[:], sq2[:, 0, 0:nt], start=True, stop=True),
                  reads=["sq2", "ones_b"], writes=[bk])
            rstd_from_psum(C, bt[:, b, 0:nt], 128, nt, 1.0 / 128, rstd[:, 0:nt], tmp[:, 0:nt], [bk], "mrstd", "mtmp")
            S.add("dve", mk("scalar_tensor_tensor", out=ckvn[:, 0:nt], in0=p, scalar=V(26), in1=rstd[:, 0:nt],
                                                               op0=ALU.mult, op1=ALU.mult),
                  reads=[pk, "mrstd", "vecs"], writes=["ckvn"])
            pkr, pkrk = proj(1152, 96)
            pks, pksk = proj(1248, 96)
            S.add("act", mk("activation", out=sqk[64:96, 0:nt], in_=pkr[64:96, :], func=AF.Square),
                  reads=[pkrk], writes=["sqk_r"])
            S.add("dve", mk("scalar_tensor_tensor", out=tA[64:96, 0:nt], in0=pkr[64:96, :], scalar=vecs[64:96, 29:30],
                                                          in1=rc[64:96, 0:nt], op0=ALU.mult, op1=ALU.mult),
                  reads=[pkrk, "rc", "vecs"], writes=["tA"])
            S.add("dve", mk("scalar_tensor_tensor", out=tmp[64:96, 0:nt], in0=pks[64:96, :], scalar=vecs[64:96, 30:31],
                                                          in1=rs[64:96, 0:nt], op0=ALU.mult, op1=ALU.mult),
                  reads=[pksk, "rs", "vecs"], writes=["mtmp"])
            S.add("pool", mk("tensor_tensor", out=tA[64:96, 0:nt], in0=tA[64:96, 0:nt], in1=tmp[64:96, 0:nt], op=ALU.add),
                  reads=["tA", "mtmp"], writes=["tA"])
            for which, col0, gcol, oname in (("q", 1344, 31, "nqT"), ("k", 1600, 32, "nkT")):
                ns = nst[0 if which == "q" else 1]
                nk_ = ("nst", which)
                for c in range(2):
                    p, pk = proj(col0 + c * 128, 128)
                    S.add("act", mk("activation", out=sq2[:, 0, 0:nt], in_=p, func=AF.Square), reads=[pk], writes=["sq2"])
                    b, bt, bk = B.next()
                    S.add("pe", mk("matmul", bt[:, b, 0:nt], ones_bd[:], sq2[:, 0, 0:nt], start=True, stop=True),
                          reads=["sq2", "ones_bd"], writes=[bk])
                    rstd_from_psum(C, bt[:, b, 0:nt], 128, nt, 1.0 / 64, rstd[:, 0:nt], tmp[:, 0:nt], [bk], "mrstd", "mtmp")
                    S.add("dve", mk("scalar_tensor_tensor",
                        out=ns[:, c, 0:nt], in0=p, scalar=V(gcol), in1=rstd[:, 0:nt], op0=ALU.mult, op1=ALU.mult),
                          reads=[pk, "mrstd", "vecs"], writes=[nk_])
                odst = (O["nqT"].rearrange("(c p) t -> p c t", p=128)[:, :, t0:t0 + nt] if which == "q"
                        else dst2("nkT").rearrange("(c p) t -> p c t", p=128)[:, :, off:off + nt])
                S.add("sp", mk("dma_start", out=odst, in_=ns[:, :, 0:nt]),
                      reads=[nk_], writes=[("L", oname, q4) if which == "k" else (oname, t0)], dma=True)
            nsub = nt // 128
            for sb_ in range(nsub):
                b, bt, bk = B.next()
                for k in range(KD):
                    S.add("pe", mk("matmul", bt[:, b, 0:256], xn[:, k, sb_ * 128:(sb_ + 1) * 128],
                                                                           winx[:, k, 1856:2112], start=(k == 0), stop=(k == KD - 1)),
                          reads=[("mxn", k), ("winx", k)], writes=[bk])
                pv = bt[:, b, 0:256].rearrange("p (h d) -> p h d", h=4)
                S.add("act", mk("activation", out=nva[:, sb_, 0:4:2, 0:64], in_=pv[:, 0:4:2, :], func=AF.Copy),
                      reads=[bk], writes=["nva"])
                S.add("dve", mk("tensor_copy", out=nva[:, sb_, 1:4:2, 64:128], in_=pv[:, 1:4:2, :]),
                      reads=[bk], writes=["nva"])
            S.add("sp", mk("dma_start",
                out=dst2("nvA")[off:off + nt, :].rearrange("(s p) (h d) -> p s h d", p=128, h=4), in_=nva[:, 0:nsub]),
                  reads=["nva"], writes=[("L", "nvA", q4)], dma=True)
            for sb_ in range(nsub):
                b, bt, bk = B.next()
                S.add("pe", mk("matmul", bt[:, b, 0:384], ckvn[:, sb_ * 128:(sb_ + 1) * 128],
                                                                  wukv[:, 384:768], start=True, stop=True),
                      reads=["ckvn", "wukv"], writes=[bk])
                pv = bt[:, b, 0:384].rearrange("p (h d) -> p h d", h=6)
                S.add("act", mk("activation", out=va[:, sb_, 0:6:2, 0:64], in_=pv[:, 0:6:2, :], func=AF.Copy),
                      reads=[bk], writes=["va"])
                S.add("dve", mk("tensor_copy", out=va[:, sb_, 1:6:2, 64:128], in_=pv[:, 1:6:2, :]),
                      reads=[bk], writes=["va"])
            S.add("sp", mk("dma_start",
                out=dst2("vA")[off:off + nt, :].rearrange("(s p) (h d) -> p s h d", p=128, h=6), in_=va[:, 0:nsub]),
                  reads=["va"], writes=[("L", "vA", q4)], dma=True)
            for hd in range(6):
                i2 = hd % 2
                b, bt, bk = B.next()
                pkn = bt[0:64, b, 0:nt]
                S.add("pe", mk("matmul", pkn, wukv[:, hd * 64:(hd + 1) * 64], ckvn[:, 0:nt], start=True, stop=True),
                      reads=["ckvn", "wukv"], writes=[bk])
                S.add("act", mk("activation", out=sqk[0:64, 0:nt], in_=pkn, func=AF.Square),
                      reads=[bk], writes=["sqk_n"])
                b2, bt2, bk2 = B.next()
                pss = bt2[0:96, b2, 0:nt]
                S.add("pe", mk("matmul", pss, ones_b[0:96, 0:96], sqk[0:96, 0:nt], start=True, stop=True),
                      reads=["sqk_n", "sqk_r", "ones_b"], writes=[bk2])
                r_, t_ = rq[i2], tq[i2]
                rstd_from_psum(C, pss, 96, nt, 1.0 / 96, r_[0:96, 0:nt], t_[0:96, 0:nt], [bk2], ("rq", i2), ("tq", i2))
                khh = kh[i2]
                S.add("dve", mk("scalar_tensor_tensor",
                    out=khh[0:64, 0:nt], in0=pkn, scalar=vecs[0:64, 29:30], in1=r_[0:64, 0:nt], op0=ALU.mult, op1=ALU.mult),
                      reads=[bk, ("rq", i2), "vecs"], writes=[("kh", i2)])
                S.add("pool", mk("tensor_tensor", out=khh[64:96, 0:nt], in0=tA[64:96, 0:nt],
                                                                        in1=r_[64:96, 0:nt], op=ALU.mult),
                      reads=["tA", ("rq", i2)], writes=[("kh", i2)])
                S.add("sp", mk("dma_start", out=dst2("kT")[hd * 96:(hd + 1) * 96, off:off + nt], in_=khh[0:96, 0:nt]),
                      reads=[("kh", i2)], writes=[("L", "kT", q4)], dma=True)
            for hd in range(6):
                i2 = hd % 2
                b, bt, bk = B.next()
                pq = bt[0:96, b, 0:nt]
                for k in range(2):
                    S.add("pe", mk("matmul", pq, wuq[:, k, hd * 192:hd * 192 + 96], cqn[:, k, 0:nt],
                                                                      start=(k == 0), stop=(k == 1)),
                          reads=["cqn", "wuq"], writes=[bk])
                b3, bt3, bk3 = B.next()
                pw = bt3[0:96, b3, 0:nt]
                for k in range(2):
                    S.add("pe", mk("matmul", pw, wuq[:, k, hd * 192 + 96:hd * 192 + 192], cqn[:, k, 0:nt],
                                                                      start=(k == 0), stop=(k == 1)),
                          reads=["cqn", "wuq"], writes=[bk3])
                sh_ = sqh[i2]
                S.add("act", mk("activation", out=sh_[0:96, 0:nt], in_=pq, func=AF.Square),
                      reads=[bk], writes=[("sqh", i2)])
                b2, bt2, bk2 = B.next()
                pss = bt2[0:96, b2, 0:nt]
                S.add("pe", mk("matmul", pss, ones_b[0:96, 0:96], sh_[0:96, 0:nt], start=True, stop=True),
                      reads=[("sqh", i2), "ones_b"], writes=[bk2])
                r_, t_ = rq[i2], tq[i2]
                rstd_from_psum(C, pss, 96, nt, 1.0 / 96, r_[0:96, 0:nt], t_[0:96, 0:nt], [bk2], ("rq", i2), ("tq", i2))
                qhh, a1, a2 = qh[i2], t1[i2], t2[i2]
                S.add("dve", mk("scalar_tensor_tensor",
                    out=qhh[0:64, 0:nt], in0=pq[0:64, :], scalar=vecs[0:64, 27:28], in1=r_[0:64, 0:nt], op0=ALU.mult, op1=ALU.mult),
                      reads=[bk, ("rq", i2), "vecs"], writes=[("qh", i2)])
                S.add("dve", mk("scalar_tensor_tensor",
                    out=a1[64:96, 0:nt], in0=pq[64:96, :], scalar=vecs[64:96, 27:28], in1=rc[64:96, 0:nt], op0=ALU.mult, op1=ALU.mult),
                      reads=[bk, "rc", "vecs"], writes=[("t1", i2)])
                S.add("dve", mk("scalar_tensor_tensor",
                    out=a2[64:96, 0:nt], in0=pw[64:96, :], scalar=vecs[64:96, 28:29], in1=rs[64:96, 0:nt], op0=ALU.mult, op1=ALU.mult),
                      reads=[bk3, "rs", "vecs"], writes=[("t2", i2)])
                S.add("pool", mk("tensor_tensor", out=a1[64:96, 0:nt], in0=a1[64:96, 0:nt], in1=a2[64:96, 0:nt], op=ALU.add),
                      reads=[("t1", i2), ("t2", i2)], writes=[("t1", i2)])
                S.add("pool", mk("tensor_tensor", out=qhh[64:96, 0:nt], in0=a1[64:96, 0:nt],
                                                                              in1=r_[64:96, 0:nt], op=ALU.mult),
                      reads=[("t1", i2), ("rq", i2)], writes=[("qh", i2)])
                S.add("sp", mk("dma_start", out=O["qT"][hd, :, t0:t0 + nt], in_=qhh[0:96, 0:nt]),
                      reads=[("qh", i2)], writes=[("qT", hd, t0)], dma=True)
            if exch is not None:
                for qq in range(4):
                    if bi == min(2 * qq + 2, len(blocks) - 1) or (bi == len(blocks) - 1 and 2 * qq + 2 > bi):
                        if qq not in exch_done:
                            exch_done.add(qq)
                            exch(qq)


def lru_phase(C, I, vecs, lruw_d, halfmask_d, yT_d):
    S = C.S
    XW = 2 + NCX + 3 + 2 * NL + 2
    CT0 = 2
    LT0 = 2 + NCX + 3
    SEG = 512
    with C.scope():
        lw = C.sb("lw", [128, 1536], BF16)
        S.add("pool", mk("dma_start", out=lw[:], in_=lruw_d), writes=["lw"], dma=True)
        hm = C.sb("hm", [128, 2], F32)
        S.add("sp", mk("dma_start", out=hm[:], in_=halfmask_d), writes=["hm"], dma=True)
        par = C.sb("lpar", [128, 18], F32)
        e1 = C.sb("le1", [128, 6], F32)
        one_t = C.sb("one_t", [128, 1], F32)
        S.add("pool", mk("memset", one_t[:], 1.0), writes=["one_t"])
        S.add("act", mk("activation", out=e1[:], in_=vecs[:, 60:66], func=AF.Exp, scale=-1.0), reads=["vecs"], writes=["le1"])
        S.add("act", mk("activation", out=e1[:], in_=e1[:], func=AF.Ln, bias=one_t[:], scale=1.0), reads=["le1", "one_t"], writes=["le1"])
        S.add("dve", mk("tensor_scalar", out=par[:, 0:6], in0=e1[:], scalar1=-4.0, scalar2=None, op0=ALU.mult),
              reads=["le1"], writes=["lpar"])
        S.add("dve", mk("tensor_scalar", out=par[:, 6:18], in0=vecs[:, 48:60], scalar1=0.5, scalar2=None, op0=ALU.mult),
              reads=["vecs"], writes=["lpar"])
        xc = C.sb("xc", [128, XW], F32)
        xcb = C.sb("xcb", [128, XW], BF16)
        hsum = C.sb("hsum", [128, NT], F32)
        lg = C.sb("lg", [128, NT], F32)
        NB = 2
        tr = [C.sb("tr%d" % i, [128, SEG], F32) for i in range(NB)]
        ti = [C.sb("ti%d" % i, [128, SEG], F32) for i in range(NB)]
        aa = [C.sb("aa%d" % i, [128, SEG], F32) for i in range(NB)]
        a2 = [C.sb("a2%d" % i, [128, SEG], F32) for i in range(NB)]
        uu = [C.sb("uu%d" % i, [128, SEG], F32) for i in range(NB)]
        hh = [C.sb("hh%d" % i, [128, SEG], F32) for i in range(NB)]
        yb = C.sb("yb", [128, NT], BF16)
        stt = C.sb("lstate", [128, 1], F32)
        B = BankRR(C, 4)
        phalf = C.t["phalf"]
        for c in range(3):
            with C.scope():
                stg = C.sb("stg", [128, 2, NL], F32)
                xf = C.sb("xf", [128, XW], F32)
                S.add("pool", mk("memset", xf[:], 0.0), writes=["xf"])
                for hf in range(2):
                    for q4 in range(4):
                        S.add("sp", mk("dma_start", out=stg[:, hf, q4 * 1024:(q4 + 1) * 1024],
                                       in_=I["G_lx"][q4, hf, c * 128:(c + 1) * 128, :]),
                              reads=[("G", "lx", q4)], writes=[("stg", hf)], dma=True)
                S.add("sp", mk("dma_start", out=xf[:, CT0:CT0 + NCX], in_=I["L_lxc"][c * 128:(c + 1) * 128, :]),
                      reads=[("L", "lx", "c"), "xf"], writes=["xf"], dma=True)
                lat = xf[:, LT0:LT0 + 2 * NL].rearrange("p (r h c) -> p r h c", h=2, c=32)
                for hf in range(2):
                    eng = "act" if hf == 0 else "dve"
                    src = stg[:, hf, :].rearrange("p (r c) -> p r c", c=32)
                    if eng == "act":
                        S.add("act", mk("activation", out=lat[:, :, hf, :], in_=src, func=AF.Copy),
                              reads=[("stg", hf), "xf"], writes=["xf"])
                    else:
                        S.add("dve", mk("tensor_copy", out=lat[:, :, hf, :], in_=src),
                              reads=[("stg", hf), "xf"], writes=["xf"])
                n = XW - 3
                S.add("dve", mk("tensor_scalar", out=xc[:, 2:2 + n], in0=xf[:, 0:n], scalar1=vecs[:, 36 + 4 * c:37 + 4 * c],
                                                              scalar2=vecs[:, 33 + c:34 + c], op0=ALU.mult, op1=ALU.add),
                      reads=["xf", "vecs"], writes=["xc"])
                for j in range(1, 4):
                    S.add("dve", mk("scalar_tensor_tensor", out=xc[:, 2:2 + n], in0=xf[:, j:j + n],
                                                                              scalar=vecs[:, 36 + 4 * c + j:37 + 4 * c + j],
                                                                              in1=xc[:, 2:2 + n], op0=ALU.mult, op1=ALU.add),
                          reads=["xf", "vecs", "xc"], writes=["xc"])
                S.add("act", mk("activation", out=xcb[:, 2:2 + n], in_=xc[:, 2:2 + n], func=AF.Copy), reads=["xc"], writes=["xcb"])
            S.add("sp", mk("dma_start", out=lg[:], in_=I["lgel"][c * 128:(c + 1) * 128, :]),
                  reads=[("lgel", t) for t in list(range(0, NL, 512)) + [NL]], writes=["lg"], dma=True)
            segs = [(CT0, NCX, "ctx", 0)] + [(LT0 + i * SEG, SEG, "lat", i) for i in range(2 * NL // SEG)]
            for d in range(2):
                order = segs if d == 0 else [segs[0]] + segs[:0:-1]
                pidx = d * 3 + c
                hc = par[:, pidx:pidx + 1]
                hba = par[:, 6 + pidx:7 + pidx]
                hbx = par[:, 12 + pidx:13 + pidx]
                wa = lw[:, (0 * 6 + pidx) * 128:(0 * 6 + pidx + 1) * 128]
                wx = lw[:, (1 * 6 + pidx) * 128:(1 * 6 + pidx + 1) * 128]
                first = True
                for si, (x0, n, kind, li) in enumerate(order):
                    ib = si % NB
                    r_, i_, a_, q_, u_, h_ = tr[ib], ti[ib], aa[ib], a2[ib], uu[ib], hh[ib]
                    for p0 in range(0, n, 512):
                        pn = min(512, n - p0)
                        b, bt, bk = B.next()
                        S.add("pe", mk("matmul",
                            bt[:, b, 0:pn], wa, xcb[:, x0 + p0:x0 + p0 + pn], start=True, stop=True),
                              reads=["xcb", "lw"], writes=[bk])
                        S.add("act", mk("activation",
                            out=r_[:, p0:p0 + pn], in_=bt[:, b, 0:pn], func=AF.Tanh, bias=hba, scale=0.5),
                              reads=[bk, "lpar"], writes=[("tr", ib)])
                        b, bt, bk = B.next()
                        S.add("pe", mk("matmul",
                            bt[:, b, 0:pn], wx, xcb[:, x0 + p0:x0 + p0 + pn], start=True, stop=True),
                              reads=["xcb", "lw"], writes=[bk])
                        S.add("act", mk("activation",
                            out=i_[:, p0:p0 + pn], in_=bt[:, b, 0:pn], func=AF.Tanh, bias=hbx, scale=0.5),
                              reads=[bk, "lpar"], writes=[("ti", ib)])
                    S.add("act", mk("activation", out=a_[:, 0:n], in_=r_[:, 0:n], func=AF.Exp,
                                                                                 bias=hc, scale=hc),
                          reads=[("tr", ib), "lpar"], writes=[("aa", ib)])
                    S.add("pool", mk("tensor_tensor", out=q_[:, 0:n], in0=a_[:, 0:n], in1=a_[:, 0:n], op=ALU.mult),
                          reads=[("aa", ib)], writes=[("a2", ib)])
                    S.add("dve", mk("tensor_scalar", out=q_[:, 0:n], in0=q_[:, 0:n], scalar1=-0.25, scalar2=0.25,
                                                                      op0=ALU.mult, op1=ALU.add),
                          reads=[("a2", ib)], writes=[("a2", ib)])
                    S.add("pool", mk("tensor_tensor", out=q_[:, 0:n], in0=q_[:, 0:n], in1=phalf[:, 0:n], op=ALU.pow),
                          reads=[("a2", ib), "phalf"], writes=[("a2", ib)])
                    S.add("dve", mk("scalar_tensor_tensor",
                        out=u_[:, 0:n], in0=i_[:, 0:n], scalar=1.0, in1=xc[:, x0:x0 + n], op0=ALU.add, op1=ALU.mult),
                          reads=[("ti", ib), "xc"], writes=[("uu", ib)])
                    S.add("pool", mk("tensor_tensor", out=u_[:, 0:n], in0=u_[:, 0:n], in1=q_[:, 0:n], op=ALU.mult),
                          reads=[("uu", ib), ("a2", ib)], writes=[("uu", ib)])
                    init = 0.0 if first else stt[:, 0:1]
                    if d == 0:
                        S.add("dve", mk("tensor_tensor_scan",
                            out=h_[:, 0:n], data0=a_[:, 0:n], data1=u_[:, 0:n], initial=init, op0=ALU.mult, op1=ALU.add),
                              reads=[("aa", ib), ("uu", ib), "lstate"], writes=[("hh", ib)])
                        S.add("dve", mk("tensor_copy", out=stt[:, 0:1], in_=h_[:, n - 1:n]),
                              reads=[("hh", ib)], writes=["lstate"])
                    else:
                        S.add("dve", mk("tensor_tensor_scan",
                            out=h_[:, 0:n][:, ::-1], data0=a_[:, 0:n][:, ::-1], data1=u_[:, 0:n][:, ::-1],
                            initial=init, op0=ALU.mult, op1=ALU.add),
                              reads=[("aa", ib), ("uu", ib), "lstate"], writes=[("hh", ib)])
                        S.add("dve", mk("tensor_copy", out=stt[:, 0:1], in_=h_[:, 0:1]),
                              reads=[("hh", ib)], writes=["lstate"])
                    first = False
                    if kind == "ctx":
                        if d == 0:
                            S.add("act", mk("activation", out=hsum[:, NL:NT], in_=h_[:, 0:NCX], func=AF.Copy),
                                  reads=[("hh", ib)], writes=[("hsum", "c")])
                        else:
                            S.add("pool", mk("tensor_tensor", out=hsum[:, NL:NT], in0=hsum[:, NL:NT], in1=h_[:, 0:NCX], op=ALU.add),
                                  reads=[("hh", ib), ("hsum", "c")], writes=[("hsum", "c")])
                    else:
                        rows = SEG // 64
                        hv = h_[:, 0:SEG].rearrange("p (r h c) -> p r h c", h=2, c=32)
                        ov = hsum[:, li * (SEG // 2):(li + 1) * (SEG // 2)].rearrange("p (r c) -> p r c", c=32)
                        hk_ = ("hsum", li)
                        if d == 0:
                            S.add("dve", mk("tensor_scalar", out=ov, in0=hv[:, :, 0, :], scalar1=hm[:, 0:1], scalar2=None,
                                                                                op0=ALU.mult),
                                  reads=[("hh", ib), "hm"], writes=[hk_])
                        else:
                            S.add("dve", mk("scalar_tensor_tensor", out=ov, in0=hv[:, :, 0, :], scalar=hm[:, 0:1], in1=ov,
                                                                                       op0=ALU.mult, op1=ALU.add),
                                  reads=[("hh", ib), "hm", hk_], writes=[hk_])
                        S.add("dve", mk("scalar_tensor_tensor", out=ov, in0=hv[:, :, 1, :], scalar=hm[:, 1:2], in1=ov,
                                                                                   op0=ALU.mult, op1=ALU.add),
                              reads=[("hh", ib), "hm", hk_], writes=[hk_])
            hkeys = [("hsum", "c")] + [("hsum", i) for i in range(2 * NL // SEG)]
            S.add("pool", mk("tensor_tensor", out=yb[:], in0=hsum[:], in1=lg[:], op=ALU.mult),
                  reads=hkeys + ["lg"], writes=["yb"])
            S.add("sp", mk("dma_start", out=yT_d[c * 128:(c + 1) * 128, :], in_=yb[:]),
                  reads=["yb"], writes=[("yT", c)], dma=True)


def mla_phase(C, I, yT_d, do_ctx=True):
    S = C.S
    NK = NCX + 2 * NL
    NJ = NK // 128
    with C.scope():
        kts = [C.sb("kt%d" % i, [128, NK], BF16) for i in range(2)]
        vas = [C.sb("vas%d" % i, [128, NJ, 128], BF16) for i in range(2)]
        qts = [C.sb("qt%d" % i, [128, NT], BF16) for i in range(2)]
        pts = [C.sb("pt%d" % i, [128, 512], BF16) for i in range(4)]
        osb = [C.sb("osb%d" % i, [128, 512], F32) for i in range(2)]
        rcp = [C.sb("rcp%d" % i, [128, 512], F32) for i in range(2)]
        ysb = [C.sb("ysb%d" % i, [128, 512], BF16) for i in range(2)]
        sps = C.ps("sps", [128, 4, 512])
        ops = C.ps("ops", [128, 2, 512])
        kq_all = [(nm, t) for nm in ("kT0", "kT1") for t in range(0, NL, 512)]

        def loadh(hd):
            i2 = hd % 2
            kt, va, qt = kts[i2], vas[i2], qts[i2]
            S.add("sp", mk("dma_start", out=kt[0:96, 0:NCX], in_=I["L_kTc"][hd * 96:(hd + 1) * 96, :]),
                  reads=[("L", "kT", "c")], writes=[("kt", i2, "c")], dma=True)
            S.add("sp", mk("dma_start", out=va[:, 0:2, :],
                           in_=I["L_vAc"][:, hd * 128:(hd + 1) * 128].rearrange("(j p) d -> p j d", p=128)),
                  reads=[("L", "vA", "c")], writes=[("vas", i2, "c")], dma=True)
            for hf in range(2):
                for q4 in range(4):
                    k0 = NCX + hf * NL + q4 * 1024
                    S.add("sp", mk("dma_start", out=kt[0:96, k0:k0 + 1024], in_=I["G_kT"][q4, hf, hd * 96:(hd + 1) * 96, :]),
                          reads=[("G", "kT", q4)], writes=[("kt", i2, hf, q4)], dma=True)
                    j0 = 2 + hf * 32 + q4 * 8
                    S.add("sp", mk("dma_start", out=va[:, j0:j0 + 8, :],
                                   in_=I["G_vA"][q4, hf, :, hd * 128:(hd + 1) * 128].rearrange("(j p) d -> p j d", p=128)),
                          reads=[("G", "vA", q4)], writes=[("vas", i2, hf, q4)], dma=True)
            S.add("sp", mk("dma_start", out=qt[0:96, :], in_=I["qT"][hd, :, :]),
                  reads=[("qT", hd, t) for t in list(range(0, NL, 512)) + [NL]], writes=[("qts", i2)], dma=True)

        def jpart(j):
            return ("c",) if j < 2 else ((j - 2) // 32, ((j - 2) % 32) // 8)

        cnt = [0, 0]
        loadh(0)
        for hd in range(6):
            if hd + 1 < 6:
                loadh(hd + 1)
            i2 = hd % 2
            kt, va, qt = kts[i2], vas[i2], qts[i2]
            qblocks = [(q0, 512, 0, NJ) for q0 in range(0, NL, 512)] + ([(NL, 256, 0, 2)] if do_ctx else [])
            for (q0, nq, j0, j1) in qblocks:
                ob = cnt[1] % 2
                cnt[1] += 1
                oacc = ops[:, ob, 0:nq]
                pend = []
                js = list(range(j0, j1))
                for idx in range(len(js) + 2):
                    if idx < len(js):
                        j = js[idx]
                        sb_ = cnt[0] % 4
                        cnt[0] += 1
                        sp_ = sps[:, sb_, 0:nq]
                        S.add("pe", mk("matmul",
                            sp_, kt[0:96, j * 128:(j + 1) * 128], qt[0:96, q0:q0 + nq], start=True, stop=True),
                              reads=[("kt", i2) + jpart(j), ("qts", i2)], writes=[("sps", sb_)])
                        pt = pts[sb_]
                        S.add("act", mk("activation", out=pt[:, 0:nq], in_=sp_, func=AF.Exp, scale=MLA_SCALE),
                              reads=[("sps", sb_)], writes=[("pt", sb_)])
                        pend.append((j, sb_))
                    if idx >= 2:
                        j, sb_ = pend[idx - 2]
                        pt = pts[sb_]
                        S.add("pe", mk("matmul",
                            oacc, va[:, j, :], pt[:, 0:nq], start=(idx == 2), stop=(idx == len(js) + 1)),
                              reads=[("pt", sb_), ("vas", i2) + jpart(j)], writes=[("ops", ob)])
                o_, r_, y_ = osb[ob], rcp[ob], ysb[ob]
                lo, hi = (0, 64) if hd % 2 == 0 else (64, 128)
                slo, shi = (64, 128) if hd % 2 == 0 else (0, 64)
                S.add("dve", mk("reciprocal", out=r_[slo:shi, 0:nq], in_=oacc[slo:shi, :]),
                      reads=[("ops", ob)], writes=[("rcp", ob)])
                S.add("act", mk("activation", out=o_[lo:hi, 0:nq], in_=oacc[lo:hi, :], func=AF.Copy),
                      reads=[("ops", ob)], writes=[("osb", ob)])
                S.add("dve", mk("tensor_copy", out=r_[lo:hi, 0:nq], in_=r_[slo:shi, 0:nq]),
                      reads=[("rcp", ob)], writes=[("rcp", ob)])
                S.add("pool", mk("tensor_tensor",
                    out=y_[lo:hi, 0:nq], in0=o_[lo:hi, 0:nq], in1=r_[lo:hi, 0:nq], op=ALU.mult),
                      reads=[("osb", ob), ("rcp", ob)], writes=[("ysb", ob)])
                row0 = 384 + hd * 64
                S.add("sp", mk("dma_start",
                    out=yT_d[row0:row0 + 64, q0:q0 + nq], in_=y_[lo:hi, 0:nq]),
                      reads=[("ysb", ob)], writes=[("yT", "m", hd, q0)], dma=True)


def na_pair_plan():
    cfg = {}
    plan = []
    r0f = lambda r: min(max(r - 4, 0), 120)
    for r in range(0, 128, 2):
        lo, hi = r0f(r), r0f(r + 1) + 8
        off0, off1 = r0f(r) - r, r0f(r + 1) - r
        chunks = []
        for ci in range(lo // 2, (hi - 1) // 2 + 1):
            key = (2 * ci - r, off0, off1)
            if key not in cfg:
                cfg[key] = len(cfg)
            chunks.append((ci, cfg[key]))
        plan.append(chunks)
    return plan, cfg


NA_PLAN, NA_CFG = na_pair_plan()
NTAB = len(NA_CFG)


def na_phase(C, I, natab_d, yT_d, do_ctx=True):
    S = C.S
    with C.scope():
        nk = C.sb("nk", [128, 2, 64, 2, 64], BF16)
        nkc = C.sb("nkc", [128, 2, NCX], BF16)
        nv = C.sb("nv", [128, 64, 4, 128], BF16)
        nvc = C.sb("nvc", [128, 2, 4, 128], BF16)
        nq = C.sb("nq", [128, 2, NT], BF16)
        tab = C.sb("natab", [128, 4, NTAB, 64], F32)
        ssb = [C.sb("nssb%d" % i, [128, 5, 64], F32) for i in range(2)]
        ptl = [C.sb("nptl%d" % i, [128, 5, 64], BF16) for i in range(2)]
        ptc = [C.sb("nptc%d" % i, [128, 2, 64], BF16) for i in range(2)]
        rcp = [C.sb("nrcp%d" % i, [128, 4, 64], F32) for i in range(2)]
        ysb = [C.sb("nysb%d" % i, [128, 2, 64], BF16) for i in range(2)]
        sps = C.ps("nsps", [128, 4, 512])
        ops = C.ps("nops", [128, 2, 512])
        S.add("sp", mk("dma_start", out=tab[:], in_=natab_d), writes=["natab"], dma=True)
        with C.scope():
            nks = C.sb("nks", [128, 2, 2, NL], BF16)
            for ck in range(2):
                for hf in range(2):
                    for q4 in range(4):
                        S.add("sp", mk("dma_start", out=nks[:, ck, hf, q4 * 1024:(q4 + 1) * 1024],
                                       in_=I["G_nkT"][q4, hf, ck * 128:(ck + 1) * 128, :]),
                              reads=[("G", "nkT", q4)], writes=[("nks", ck, hf)], dma=True)
                    src = nks[:, ck, hf, :].rearrange("p (ci t) -> p ci t", t=64)
                    if hf == 0:
                        S.add("act", mk("activation", out=nk[:, ck, :, hf, :], in_=src, func=AF.Copy),
                              reads=[("nks", ck, hf)], writes=["nk"])
                    else:
                        S.add("dve", mk("tensor_copy", out=nk[:, ck, :, hf, :], in_=src),
                              reads=[("nks", ck, hf)], writes=["nk"])
        for ck in range(2):
            S.add("sp", mk("dma_start", out=nkc[:, ck, :], in_=I["L_nkTc"][ck * 128:(ck + 1) * 128, :]),
                  reads=[("L", "nkT", "c")], writes=["nkc"], dma=True)
            S.add("sp", mk("dma_start", out=nq[:, ck, :], in_=I["nqT"][ck * 128:(ck + 1) * 128, :]),
                  reads=[("nqT", t) for t in list(range(0, NL, 512)) + [NL]], writes=["nq"], dma=True)
        for hf in range(2):
            for q4 in range(4):
                S.add("sp", mk("dma_start", out=nv[hf * 64:(hf + 1) * 64, q4 * 16:(q4 + 1) * 16],
                               in_=I["G_nvA"][q4, hf].rearrange("(ci q) (h d) -> q ci h d", q=64, h=4)),
                      reads=[("G", "nvA", q4)], writes=["nv"], dma=True)
        S.add("sp", mk("dma_start", out=nvc[:], in_=I["L_nvAc"].rearrange("(j p) (h d) -> p j h d", p=128, h=4)),
              reads=[("L", "nvA", "c")], writes=["nvc"], dma=True)
        cnt = [0, 0]

        def attend(q0, nqn, hd, loc, it):
            ck, pl = hd // 2, (hd % 2) * 64
            sb_ = cnt[0] % 4
            cnt[0] += 1
            nl = len(loc)
            sp_ = sps[:, sb_, 0:(nl + 2) * nqn].rearrange("p (j q) -> p j q", q=nqn)
            qap = nq[pl:pl + 64, ck, q0:q0 + nqn]
            for jl, (ci, ti_) in enumerate(loc):
                S.add("pe", mk("matmul", sp_[:, jl, :], nk[pl:pl + 64, ck, ci].rearrange("p h t -> p (h t)"), qap,
                                                                      start=True, stop=True),
                      reads=["nk", "nq"], writes=[("nsps", sb_)])
            for jc in range(2):
                S.add("pe", mk("matmul", sp_[:, nl + jc, :], nkc[pl:pl + 64, ck, jc * 128:(jc + 1) * 128], qap,
                                                               start=True, stop=True),
                      reads=["nkc", "nq"], writes=[("nsps", sb_)])
            i2 = it % 2
            if nl:
                s_, p_ = ssb[i2], ptl[i2]
                for jl, (ci, ti_) in enumerate(loc):
                    S.add("dve", mk("scalar_tensor_tensor",
                        out=s_[:, jl, :], in0=sp_[:, jl, :], scalar=NA_SCALE, in1=tab[:, hd, ti_, :], op0=ALU.mult, op1=ALU.add),
                          reads=[("nsps", sb_), "natab"], writes=[("nssb", i2)])
                S.add("act", mk("activation", out=p_[:, 0:nl, :], in_=s_[:, 0:nl, :], func=AF.Exp),
                      reads=[("nssb", i2)], writes=[("nptl", i2)])
            pc_ = ptc[i2]
            S.add("act", mk("activation", out=pc_[:, :, 0:nqn], in_=sp_[:, nl:nl + 2, :], func=AF.Exp, scale=NA_SCALE),
                  reads=[("nsps", sb_)], writes=[("nptc", i2)])
            return (loc, nl, i2)

        for pi, chunks in enumerate(NA_PLAN):
            q0 = pi * 64
            ob = cnt[1] % 2
            cnt[1] += 1
            ov = ops[:, ob, 0:256].rearrange("p (h q) -> p h q", q=64)
            for hd in range(4):
                loc, nl, i2 = attend(q0, 64, hd, chunks, pi * 4 + hd)
                p_, pc_ = ptl[i2], ptc[i2]
                nmm = nl + 2
                for jl, (ci, ti_) in enumerate(loc):
                    S.add("pe", mk("matmul", ov[:, hd, :], nv[:, ci, hd, :], p_[:, jl, :],
                                                                                     start=(jl == 0), stop=False),
                          reads=[("nptl", i2), "nv"], writes=[("nops", ob)])
                for jc in range(2):
                    S.add("pe", mk("matmul", ov[:, hd, :], nvc[:, jc, hd, :], pc_[:, jc, 0:64],
                                                                                start=False, stop=(jc == 1)),
                          reads=[("nptc", i2), "nvc"], writes=[("nops", ob)])
            r_, y_ = rcp[ob], ysb[ob]
            S.add("dve", mk("reciprocal", out=r_[64:128, 0:4:2, :], in_=ov[64:128, 0:4:2, :]),
                  reads=[("nops", ob)], writes=[("nrcp", ob)])
            S.add("dve", mk("reciprocal", out=r_[0:64, 1:4:2, :], in_=ov[0:64, 1:4:2, :]),
                  reads=[("nops", ob)], writes=[("nrcp", ob)])
            S.add("dve", mk("tensor_copy", out=r_[0:64, 0:4:2, :], in_=r_[64:128, 0:4:2, :]),
                  reads=[("nrcp", ob)], writes=[("nrcp", ob)])
            S.add("dve", mk("tensor_copy", out=r_[64:128, 1:4:2, :], in_=r_[0:64, 1:4:2, :]),
                  reads=[("nrcp", ob)], writes=[("nrcp", ob)])
            S.add("dve", mk("tensor_tensor", out=y_[0:64, :, :], in0=ov[0:64, 0:4:2, :], in1=r_[0:64, 0:4:2, :], op=ALU.mult),
                  reads=[("nops", ob), ("nrcp", ob)], writes=[("nysb", ob)])
            S.add("dve", mk("tensor_tensor", out=y_[64:128, :, :], in0=ov[64:128, 1:4:2, :], in1=r_[64:128, 1:4:2, :], op=ALU.mult),
                  reads=[("nops", ob), ("nrcp", ob)], writes=[("nysb", ob)])
            S.add("sp", mk("dma_start", out=yT_d[768:1024, q0:q0 + 64].rearrange("(c p) q -> p c q", p=128), in_=y_[:, :, :]),
                  reads=[("nysb", ob)], writes=[("yT", "n", q0)], dma=True)
        for qi in range(NCX // 64 if do_ctx else 0):
            q0 = NL + qi * 64
            ob = cnt[1] % 2
            cnt[1] += 1
            ov = ops[:, ob, 0:256].rearrange("p (h q) -> p h q", q=64)
            for hd in range(4):
                loc, nl, i2 = attend(q0, 64, hd, [], 1000 + qi * 4 + hd)
                pc_ = ptc[i2]
                for jc in range(2):
                    S.add("pe", mk("matmul", ov[:, hd, :], nvc[:, jc, hd, :], pc_[:, jc, 0:64],
                                                                                start=(jc == 0), stop=(jc == 1)),
                          reads=[("nptc", i2), "nvc"], writes=[("nops", ob)])
            r_, y_ = rcp[ob], ysb[ob]
            S.add("dve", mk("reciprocal", out=r_[64:128, 0:4:2, :], in_=ov[64:128, 0:4:2, :]),
                  reads=[("nops", ob)], writes=[("nrcp", ob)])
            S.add("dve", mk("reciprocal", out=r_[0:64, 1:4:2, :], in_=ov[0:64, 1:4:2, :]),
                  reads=[("nops", ob)], writes=[("nrcp", ob)])
            S.add("dve", mk("tensor_copy", out=r_[0:64, 0:4:2, :], in_=r_[64:128, 0:4:2, :]),
                  reads=[("nrcp", ob)], writes=[("nrcp", ob)])
            S.add("dve", mk("tensor_copy", out=r_[64:128, 1:4:2, :], in_=r_[0:64, 1:4:2, :]),
                  reads=[("nrcp", ob)], writes=[("nrcp", ob)])
            S.add("dve", mk("tensor_tensor", out=y_[0:64, :, :], in0=ov[0:64, 0:4:2, :], in1=r_[0:64, 0:4:2, :], op=ALU.mult),
                  reads=[("nops", ob), ("nrcp", ob)], writes=[("nysb", ob)])
            S.add("dve", mk("tensor_tensor", out=y_[64:128, :, :], in0=ov[64:128, 1:4:2, :], in1=r_[64:128, 1:4:2, :], op=ALU.mult),
                  reads=[("nops", ob), ("nrcp", ob)], writes=[("nysb", ob)])
            S.add("sp", mk("dma_start", out=yT_d[768:1024, q0:q0 + 64].rearrange("(c p) q -> p c q", p=128), in_=y_[:, :, :]),
                  reads=[("nysb", ob)], writes=[("yT", "n", q0)], dma=True)


def outproj_phase(C, hT_d, yT_d, wo_d, mods, do_ctx=True):
    S = C.S
    blocks = [(t0, 512, "lat") for t0 in range(0, NL, 512)] + ([(NL, 256, "ctx")] if do_ctx else [])
    with C.scope():
        wo = C.sb("wo", [128, KD, D], BF16)
        for k in range(KD):
            S.add("pool", mk("dma_start", out=wo[:, k, :], in_=wo_d[k * 128:(k + 1) * 128, :]), writes=[("wo", k)], dma=True)
        yb = [C.sb("oyb%d" % i, [128, KD, 512], BF16) for i in range(2)]
        hb = [C.sb("ohb%d" % i, [128, KD, 512], F32) for i in range(2)]
        B = BankRR(C, 8)
        hview = hT_d.rearrange("(k p) t -> p k t", p=128)
        yview = yT_d.rearrange("(k p) t -> p k t", p=128)
        for bi, (t0, nt, kind) in enumerate(blocks):
            i2 = bi % 2
            m = mods[("mix", kind)]
            y_, h_ = yb[i2], hb[i2]
            S.add("sp", mk("dma_start", out=y_[:, :, 0:nt], in_=yview[:, :, t0:t0 + nt]),
                  reads=["yT_all"], writes=[("oyb", i2)], dma=True)
            S.add("sp", mk("dma_start", out=h_[:, :, 0:nt], in_=hview[:, :, t0:t0 + nt]),
                  reads=hT_keys(t0, nt), writes=[("ohb", i2)], dma=True)
            for n in range(KD):
                b, bt, bk = B.next()
                for k in range(KD):
                    S.add("pe", mk("matmul", bt[:, b, 0:nt], wo[:, k, n * 128:(n + 1) * 128], y_[:, k, 0:nt],
                                                                                     start=(k == 0), stop=(k == KD - 1)),
                          reads=[("oyb", i2), ("wo", k)], writes=[bk])
                S.add("dve", mk("scalar_tensor_tensor",
                    out=h_[:, n, 0:nt], in0=bt[:, b, 0:nt], scalar=m["gh"][:, n:n + 1], in1=h_[:, n, 0:nt], op0=ALU.mult, op1=ALU.add),
                      reads=[bk, ("ohb", i2), m["key"]], writes=[("ohb", i2)])
            S.add("sp", mk("dma_start", out=hview[:, :, t0:t0 + nt], in_=h_[:, :, 0:nt]),
                  reads=[("ohb", i2)], writes=hT_keys(t0, nt), dma=True)


def mark_yT_done(C):
    S = C.S
    keys = [k for k in list(S.last_writer.keys()) if isinstance(k, tuple) and k[0] == "yT"]
    S.add("sp", mk("nop", ), reads=keys, writes=["yT_all"])


PAIRS = [[0, 1], [2, 3], [4, 5], [6, 7]]
XCH = (("lx", 384, 1024, F32), ("kT", 576, 1024, BF16), ("vA", 1024, 768, BF16),
       ("nkT", 256, 1024, BF16), ("nvA", 1024, 512, BF16))
XCHC = dict(lx=(384, NCX), kT=(576, NCX), vA=(NCX, 768), nkT=(256, NCX), nvA=(NCX, 512))


def build_fused(nlayers=DEPTH, dbg=False):
    C = Ctx()
    S = C.S
    nl = nlayers
    hT_in = C.dram("hT_in", [D, NT], F32, "ExternalInput")
    scin = C.dram("scin", [128, KD, 2], F32, "ExternalInput")
    ropeC = C.dram("ropeC", [32, NT], F32, "ExternalInput")
    ropeS = C.dram("ropeS", [32, NT], F32, "ExternalInput")
    halfmask = C.dram("halfmask", [128, 2], F32, "ExternalInput")
    natab = C.dram("natab", [nl, 128, 4, NTAB, 64], F32, "ExternalInput")
    w_mod = C.dram("w_mod", [nl, D, 9 * D], F32, "ExternalInput")
    b_mod = C.dram("b_mod", [nl, 128, 72], F32, "ExternalInput")
    vecs_d = C.dram("vecs", [nl, 128, NVEC], F32, "ExternalInput")
    f1_in = C.dram("f1_in", [nl, D, 2 * DFF], F32, "ExternalInput")
    f1_out = C.dram("f1_out", [nl, DFF, D], F32, "ExternalInput")
    f2_in = C.dram("f2_in", [nl, D, 2 * DFF], F32, "ExternalInput")
    f2_out = C.dram("f2_out", [nl, DFF, D], F32, "ExternalInput")
    winx = C.dram("winx", [nl, D, WINX], F32, "ExternalInput")
    wuq = C.dram("wuq", [nl, 256, 1152], F32, "ExternalInput")
    wukv = C.dram("wukv", [nl, 128, 768], F32, "ExternalInput")
    lruw = C.dram("lruw", [nl, 128, 1536], F32, "ExternalInput")
    wo = C.dram("wo", [nl, D, D], F32, "ExternalInput")
    hT = C.dram("hT", [D, NT], F32, "ExternalOutput")
    yT = C.dram("yT", [D, NT], BF16, "ExternalOutput" if dbg else "Internal")
    O = dict(qT=C.dram("qT", [6, 96, NT], BF16, "Internal"), nqT=C.dram("nqT", [256, NT], BF16, "Internal"),
             lgel=C.dram("lgel", [384, NT], F32, "Internal"))
    for nm, r, c, dt in XCH:
        O["L_" + nm] = C.dram("L_" + nm, [4, r, c], dt, "Internal")
        O["L_" + nm + "c"] = C.dram("L_" + nm + "c", list(XCHC[nm]), dt, "Internal")
        O["G_" + nm] = C.dram("G_" + nm, [4, 2, r, c], dt, "Internal")
    setup_consts(C)
    vt = C.sb("vecs", [128, nl, NVEC], F32)
    S.add("sp", mk("dma_start", out=vt[:], in_=vecs_d.rearrange("l p v -> p l v")), writes=["vecs"], dma=True)
    for t0 in range(0, NT, TB):
        S.add("sp", mk("dma_start", out=hT[:, t0:t0 + TB], in_=hT_in[:, t0:t0 + TB]), writes=[("hT", t0)], dma=True)
    mods = [mod_phase(C, w_mod[l], b_mod[l], vt[:, l, :], scin, "L%d" % l) for l in range(nl)]

    def exch(qq):
        for nm, r, c, dt in XCH:
            S.add("pool", mk("collective_compute", "AllGather", ALU.bypass, replica_groups=PAIRS,
                             ins=[O["L_" + nm][qq]], outs=[O["G_" + nm][qq].rearrange("r a b -> (r a) b")]),
                  reads=[("L", nm, qq)], writes=[("G", nm, qq)], cc=True)

    for l in range(nl):
        last = (l == DEPTH - 1)
        vl = vt[:, l, :]
        ffn_phase(C, hT, f1_in[l], f1_out[l], ffn_blocks(), mods[l], "ffn1")
        m1_phase(C, hT, dict(winx=winx[l], wuq=wuq[l], wukv=wukv[l]), vl, mods[l], ropeC, ropeS, O, exch)
        lru_phase(C, O, vl, lruw[l], halfmask, yT)
        mla_phase(C, O, yT, do_ctx=not last)
        na_phase(C, O, natab[l], yT, do_ctx=not last)
        mark_yT_done(C)
        outproj_phase(C, hT, yT, wo[l], mods[l], do_ctx=not last)
        ffn_phase(C, hT, f2_in[l], f2_out[l], ffn_blocks(do_ctx=not last), mods[l], "ffn2")
    return C.finish()


def rope_swap_index():
    i = np.arange(32)
    axis, half, f = i // 16, (i // 8) % 2, i % 8
    return axis * 16 + (1 - half) * 8 + f


def rope_tables(s):
    i = np.arange(NL)
    row = (i // 32).astype(np.float32)
    col = (32 * s + i % 32).astype(np.float32)
    inv = (np.float32(10000.0) ** (-np.arange(0, 16, 2, dtype=np.float32) / np.float32(16))).astype(np.float32)
    Cc = np.ones((32, NT), np.float32)
    Ss = np.zeros((32, NT), np.float32)
    for d in range(32):
        axis, half, f = d // 16, (d // 8) % 2, d % 8
        pos = row if axis == 0 else col
        ang = (pos * inv[f]).astype(np.float32)
        Cc[d, :NL] = np.cos(ang)
        Ss[d, :NL] = (-np.sin(ang)) if half == 0 else np.sin(ang)
    return Cc, Ss


def fm(v, k=None):
    v = np.asarray(v, np.float32)
    return np.ascontiguousarray(v.reshape(-1, 128).T)


def build_na_tables(rpb_l, s):
    rpb_l = np.asarray(rpb_l, np.float32)
    c = 32 * s + np.arange(32)
    kc = np.arange(64)
    w0 = np.clip(c - 8, 0, 48)
    col_in = (kc[:, None] >= w0[None, :]) & (kc[:, None] < w0[None, :] + 16)
    col_off = np.clip(kc[:, None] - c[None, :] + 15, 0, 30)
    G = rpb_l[:, :, col_off]
    G = np.where(col_in[None, None], G, np.float32(NEG)).astype(np.float32)
    T = np.full((128, 4, NTAB, 64), NEG, np.float32)
    for (k0mr, off0, off1), ti in NA_CFG.items():
        for hf in range(2):
            for e in range(2):
                for eq in range(2):
                    rel = k0mr + e
                    off = (off0, off1)[eq]
                    if not (off <= rel < off + 8):
                        continue
                    di = rel - eq + 7
                    p0 = hf * 64 + e * 32
                    T[p0:p0 + 32, :, ti, eq * 32:(eq + 1) * 32] = np.transpose(G[:, di, hf * 32:(hf + 1) * 32, :], (1, 0, 2))
    return T


def layer_shared(inp, l):
    sw = rope_swap_index()
    w_in = np.asarray(inp["w_in"][l], np.float32)
    winx = np.concatenate([w_in[:, 0:1152], w_in[:, 1088:1184], w_in[:, 1088:1152], w_in[:, 1152 + sw],
                           w_in[:, 1184:1952]], axis=1)
    assert winx.shape[1] == WINX
    wuq0 = np.asarray(inp["mla_w_uq"][l], np.float32).reshape(256, 6, 96)
    wuq_sw = np.concatenate([wuq0[:, :, 0:64], wuq0[:, :, 64 + sw]], axis=2)
    wuq = np.stack([wuq0, wuq_sw], axis=2).reshape(256, 1152)
    wukv0 = np.asarray(inp["mla_w_ukv"][l], np.float32).reshape(128, 6, 128)
    wukv = np.concatenate([wukv0[:, :, 0:64].reshape(128, 384), wukv0[:, :, 64:128].reshape(128, 384)], axis=1)
    vecs = np.zeros((128, NVEC), np.float32)
    vecs[:, 0:8] = fm(inp["norm_ffn1"][l])
    vecs[:, 8:16] = fm(inp["norm_mix"][l])
    vecs[:, 16:24] = fm(inp["norm_ffn2"][l])
    vecs[:, 24:26] = fm(inp["mla_q_norm"][l])
    vecs[:, 26] = np.asarray(inp["mla_kv_norm"][l], np.float32)
    gq = np.asarray(inp["mla_q_gain"][l], np.float32)
    gk = np.asarray(inp["mla_k_gain"][l], np.float32)
    vecs[0:96, 27] = gq
    vecs[64:96, 28] = gq[64 + sw]
    vecs[0:96, 29] = gk
    vecs[64:96, 30] = gk[64 + sw]
    vecs[:, 31] = np.tile(np.asarray(inp["na_q_gain"][l], np.float32), 2)
    vecs[:, 32] = np.tile(np.asarray(inp["na_k_gain"][l], np.float32), 2)
    vecs[:, 33:36] = fm(inp["lru_conv_b"][l])
    cw = np.asarray(inp["lru_conv_w"][l], np.float32)
    for c in range(3):
        for j in range(4):
            vecs[:, 36 + 4 * c + j] = cw[j, c * 128:(c + 1) * 128]
    for d in range(2):
        vecs[:, 48 + d * 3:51 + d * 3] = fm(inp["lru_b_a"][l][d])
        vecs[:, 54 + d * 3:57 + d * 3] = fm(inp["lru_b_x"][l][d])
        vecs[:, 60 + d * 3:63 + d * 3] = fm(inp["lru_lambda"][l][d])
    lruw = np.zeros((128, 2, 2, 3, 128), np.float32)
    for g, nm in enumerate(("lru_w_a", "lru_w_x")):
        w = np.asarray(inp[nm][l], np.float32)
        for d in range(2):
            for c in range(3):
                for i in range(2):
                    lruw[64 * i:64 * i + 64, g, d, c, 64 * i:64 * i + 64] = w[d, 2 * c + i]
    b_mod = fm(inp["b_mod"][l])
    return dict(winx=np.ascontiguousarray(winx), wuq=np.ascontiguousarray(wuq), wukv=np.ascontiguousarray(wukv),
                vecs=vecs, lruw=np.ascontiguousarray(lruw.reshape(128, 1536)), b_mod=b_mod,
                w_mod=np.asarray(inp["w_mod"][l], np.float32))


_PROGS = {}


def host_inputs(inp, nlayers=DEPTH, ncore=8):
    x = np.asarray(inp["x"], np.float32)
    ctx = np.asarray(inp["ctx"], np.float32)
    c = np.asarray(inp["c"], np.float32)
    c_ctx = np.asarray(inp["c_ctx"], np.float32)
    Ls = [layer_shared(inp, l) for l in range(nlayers)]
    shared = dict(
        w_mod=np.ascontiguousarray(np.asarray(inp["w_mod"], np.float32)[:nlayers]),
        b_mod=np.stack([L["b_mod"] for L in Ls]), vecs=np.stack([L["vecs"] for L in Ls]),
        f1_in=np.ascontiguousarray(np.asarray(inp["ffn1_w_in"], np.float32)[:nlayers]),
        f1_out=np.ascontiguousarray(np.asarray(inp["ffn1_w_out"], np.float32)[:nlayers]),
        f2_in=np.ascontiguousarray(np.asarray(inp["ffn2_w_in"], np.float32)[:nlayers]),
        f2_out=np.ascontiguousarray(np.asarray(inp["ffn2_w_out"], np.float32)[:nlayers]),
        winx=np.stack([L["winx"] for L in Ls]), wuq=np.stack([L["wuq"] for L in Ls]),
        wukv=np.stack([L["wukv"] for L in Ls]), lruw=np.stack([L["lruw"] for L in Ls]),
        wo=np.ascontiguousarray(np.asarray(inp["w_out"], np.float32)[:nlayers]))
    natabs = [np.stack([build_na_tables(inp["na_rpb"][l], s) for l in range(nlayers)]) for s in range(2)]
    ropes = [rope_tables(s) for s in range(2)]
    maps = []
    for core in range(ncore):
        b, s = core // 2, core % 2
        xl = x[b].reshape(128, 64, D)[:, 32 * s:32 * s + 32, :].reshape(NL, D)
        hm = np.zeros((128, 2), np.float32)
        hm[:, s] = 1.0
        m = dict(shared)
        m.update(hT_in=np.ascontiguousarray(np.concatenate([xl, ctx[b]], axis=0).T),
                 scin=np.ascontiguousarray(np.stack([fm(c[b]), fm(c_ctx)], axis=2)),
                 ropeC=ropes[s][0], ropeS=ropes[s][1], halfmask=hm, natab=natabs[s])
        maps.append(m)
    return maps


def kernel(**inp):
    ncore = 8
    if "fused" not in _PROGS:
        _PROGS["fused"] = build_fused()
    maps = host_inputs(inp)
    res = run_bass_kernel_spmd(_PROGS["fused"], maps, core_ids=list(range(ncore))).results
    out = np.zeros((4, 128, 64, D), np.float32)
    for core in range(ncore):
        b, s = core // 2, core % 2
        out[b, :, 32 * s:32 * s + 32, :] = res[core]["hT"][:, :NL].T.reshape(128, 32, D)
    return out.reshape(4, 8192, D)
```

```python
from contextlib import ExitStack
import numpy as np
import concourse.bass as bass
import concourse.mybir as mybir
from concourse.bass_utils import run_bass_kernel_spmd

F32 = mybir.dt.float32
BF16 = mybir.dt.bfloat16
AF = mybir.ActivationFunctionType
ALU = mybir.AluOpType

D = 1024
DFF = 2816
KD = 8
JF = 22
TB = 256
EPS = 1e-6
NL = 4096
NCX = 256
NT = NL + NCX
DEPTH = 4
NEG = -30000.0
NVEC = 66
WINX = 2112
MLA_SCALE = 96 ** -0.5
NA_SCALE = 0.125

ENGS = ("pe", "act", "dve", "pool", "sp")
NDMA_SLOTS = 8


class Op:
    __slots__ = ("eng", "emit", "is_dma", "pos", "deps", "signal", "ticket", "waits", "slot", "slot_val", "idx")

    def __init__(self, eng, emit, is_dma):
        self.eng = eng
        self.emit = emit
        self.is_dma = is_dma
        self.deps = []
        self.signal = False
        self.ticket = 0
        self.waits = []
        self.slot = None
        self.slot_val = 0


class Sched:
    def __init__(self):
        self.ops = []
        self.streams = {e: [] for e in ENGS}
        self.last_writer = {}
        self.readers = {}
        self.dma_count = {e: 0 for e in ENGS}
        self.slot_last = {}
        self.barrier_ops = []
        self.barrier_pending = set()
        self.marks = []

    def mark(self, name):
        self.marks.append((name, {e: len(v) for e, v in self.streams.items()}))

    def barrier(self):
        ops = [s[-1] for s in self.streams.values() if s]
        ops += list(self.slot_last.values())
        self.barrier_ops = ops
        self.barrier_pending = set(ENGS)

    def add(self, eng, emit, reads=(), writes=(), dma=False, cc=False):
        op = Op(eng, emit, dma or cc)
        op.idx = len(self.ops)
        op.pos = len(self.streams[eng])
        deps = set()
        if eng in self.barrier_pending:
            deps.update(self.barrier_ops)
            self.barrier_pending.discard(eng)
        for k in reads:
            w = self.last_writer.get(k)
            if w is not None:
                deps.add(w)
        for k in writes:
            w = self.last_writer.get(k)
            if w is not None:
                deps.add(w)
            rd = self.readers.get(k)
            if rd:
                deps.update(rd.values())
        if cc:
            op.slot = ("cc", 0)
            prev = self.slot_last.get(op.slot)
            op.slot_val = (prev.slot_val + 1) if prev is not None else 1
            self.slot_last[op.slot] = op
        elif dma:
            n = self.dma_count[eng]
            self.dma_count[eng] = n + 1
            op.slot = (eng, n % NDMA_SLOTS)
            prev = self.slot_last.get(op.slot)
            if prev is not None:
                deps.add(prev)
                op.slot_val = prev.slot_val + 16
            else:
                op.slot_val = 16
            self.slot_last[op.slot] = op
        deps.discard(op)
        op.deps = sorted(deps, key=lambda o: o.idx)
        for k in reads:
            rk = ("d", op.idx) if dma else eng
            self.readers.setdefault(k, {})[rk] = op
        for k in writes:
            self.last_writer[k] = op
            self.readers[k] = {}
        self.ops.append(op)
        self.streams[eng].append(op)
        return op

    def finalize(self):
        waited = {e: {} for e in ENGS}
        for op in self.ops:
            w = waited[op.eng]
            for p in op.deps:
                if p.is_dma:
                    key = ("dma", p.slot)
                    if w.get(key, 0) >= p.slot_val:
                        continue
                    w[key] = p.slot_val
                    op.waits.append(p)
                else:
                    if p.eng == op.eng:
                        if p.eng == "pe":
                            continue
                        if not op.is_dma and op.pos - p.pos > 2:
                            continue
                    key = ("eng", p.eng)
                    if w.get(key, -1) >= p.pos:
                        continue
                    w[key] = p.pos
                    p.signal = True
                    op.waits.append(p)
        for e in ENGS:
            t = 0
            for op in self.streams[e]:
                if not op.is_dma and op.signal:
                    t += 1
                    op.ticket = t

    def emit_all(self, nc, stack):
        self.finalize()
        eng_sem = {e: stack.enter_context(nc.semaphore("s_" + e)) for e in ENGS}
        dma_sem = {}
        for e in ENGS:
            for i in range(min(NDMA_SLOTS, self.dma_count[e])):
                dma_sem[(e, i)] = stack.enter_context(nc.semaphore("d_%s%d" % (e, i)))
        if ("cc", 0) in self.slot_last:
            dma_sem[("cc", 0)] = stack.enter_context(nc.semaphore("cc_sem"))
        block = stack.enter_context(nc.Block())

        def run_stream(e, engobj):
            for op in self.streams[e]:
                for p in op.waits:
                    if p.is_dma:
                        engobj.wait_ge(dma_sem[p.slot], p.slot_val)
                    else:
                        engobj.wait_ge(eng_sem[p.eng], p.ticket)
                ins = op.emit(engobj)
                if op.is_dma:
                    if op.slot[0] == "cc":
                        ins.then_inc(dma_sem[op.slot])
                    else:
                        ins.then_inc(dma_sem[op.slot], 16)
                elif op.signal:
                    ins.then_inc(eng_sem[op.eng], 1)
            for slot, last in self.slot_last.items():
                if slot[0] == e or (slot[0] == "cc" and e == "pool"):
                    engobj.wait_ge(dma_sem[slot], last.slot_val)

        if self.streams["sp"]:
            @block.sync
            def _(eng):
                run_stream("sp", eng)
        if self.streams["act"]:
            @block.scalar
            def _(eng):
                run_stream("act", eng)
        if self.streams["dve"]:
            @block.vector
            def _(eng):
                run_stream("dve", eng)
        if self.streams["pool"]:
            @block.gpsimd
            def _(eng):
                run_stream("pool", eng)
        if self.streams["pe"]:
            @block.tensor
            def _(eng):
                run_stream("pe", eng)


def mk(name, *args, **kw):
    return lambda e: getattr(e, name)(*args, **kw)


class Ctx:
    def __init__(self):
        self.nc = bass.Bass("TRN2", target_bir_lowering=False)
        self.S = Sched()
        self.stack = ExitStack()
        self.t = {}
        self.cur = self.stack
        self.uid = 0
        self.bank_rr = 0

    def sb(self, name, shape, dtype):
        self.uid += 1
        t = self.cur.enter_context(self.nc.sbuf_tensor("%s_%d" % (name, self.uid), list(shape), dtype))
        self.t[name] = t
        return t

    def ps(self, name, shape, dtype=F32):
        self.uid += 1
        t = self.cur.enter_context(self.nc.psum_tensor("%s_%d" % (name, self.uid), list(shape), dtype))
        self.t[name] = t
        return t

    def dram(self, name, shape, dtype, kind):
        return self.nc.dram_tensor(name, list(shape), dtype, kind=kind).ap()

    def scope(self):
        return _Scope(self)

    def finish(self):
        self.S.emit_all(self.nc, self.stack)
        self.stack.close()
        return self.nc


class _Scope:
    def __init__(self, C):
        self.C = C

    def __enter__(self):
        self.prev = self.C.cur
        self.st = ExitStack()
        self.C.cur = self.st
        return self

    def __exit__(self, *a):
        self.st.close()
        self.C.cur = self.prev
        self.C.S.barrier()
        return False


def setup_consts(C):
    S = C.S
    ones_b = C.sb("ones_b", [128, 128], BF16)
    ones_bd = C.sb("ones_bd", [128, 128], BF16)
    nhalf = C.sb("nhalf", [128, 512], F32)
    phalf = C.sb("phalf", [128, 512], F32)
    S.add("pool", mk("memset", ones_b[:], 1.0), writes=["ones_b"])
    S.add("pool", mk("memset", ones_bd[:], 0.0), writes=["ones_bd"])
    S.add("pool", mk("memset", ones_bd[0:64, 0:64], 1.0), writes=["ones_bd"])
    S.add("pool", mk("memset", ones_bd[64:128, 64:128], 1.0), writes=["ones_bd"])
    S.add("pool", mk("memset", nhalf[:], -0.5), writes=["nhalf"])
    S.add("pool", mk("memset", phalf[:], 0.5), writes=["phalf"])
    eps_t = C.sb("eps_t", [128, 1], F32)
    S.add("pool", mk("memset", eps_t[:], EPS), writes=["eps_t"])


class WLoader:
    def __init__(self, C, width, n=3):
        self.C = C
        self.st = [C.sb("wstg%d" % i, [128, width], F32) for i in range(n)]
        self.cnt = 0

    def load(self, dst_ap, src_ap, ncols, wkey, nparts=128):
        S = self.C.S
        i = self.cnt % len(self.st)
        self.cnt += 1
        st = self.st[i]
        S.add("act", mk("dma_start", out=st[0:nparts, 0:ncols], in_=src_ap), writes=[("wstg", i)], dma=True)
        S.add("dve", mk("tensor_copy", out=dst_ap, in_=st[0:nparts, 0:ncols]), reads=[("wstg", i)], writes=[wkey])


def rstd_from_psum(C, ps_ap, M, nt, inv_n, out_ap, tmp_ap, rkeys, wkey, tmpkey):
    S = C.S
    eps_t = C.t["eps_t"]
    S.add("act", mk("activation", out=tmp_ap, in_=ps_ap, func=AF.Sqrt, bias=eps_t[0:M, :], scale=inv_n),
          reads=rkeys + ["eps_t"], writes=[tmpkey])
    S.add("dve", mk("reciprocal", out=out_ap, in_=tmp_ap),
          reads=[tmpkey], writes=[wkey])


def mod_phase(C, w_mod_d, b_mod_d, vecs, scin_d, name):
    S = C.S
    GS = C.sb("GS" + name, [128, 3, KD, 2], F32)
    SH = C.sb("SH" + name, [128, 3, KD, 2], F32)
    GT = C.sb("GT" + name, [128, 3, KD, 2], F32)
    key = "mods" + name
    with C.scope():
        sc = C.sb("sc", [128, KD, 2], F32)
        scs = C.sb("scs", [128, KD, 2], F32)
        bm = C.sb("bm", [128, 72], F32)
        modT = C.sb("modT", [128, 72, 2], F32)
        wm = [C.sb("wm%d" % i, [128, KD, 1024], F32) for i in range(2)]
        mp = C.ps("modps", [128, 72, 2])
        S.add("sp", mk("dma_start", out=sc[:], in_=scin_d), writes=["sc"], dma=True)
        S.add("sp", mk("dma_start", out=bm[:], in_=b_mod_d), writes=["bm"], dma=True)
        S.add("act", mk("activation", out=scs[:], in_=sc[:], func=AF.Silu), reads=["sc"], writes=["scs"])
        wv = w_mod_d.rearrange("(k p) n -> p k n", p=128)
        for mc in range(9):
            w = wm[mc % 2]
            for k in range(KD):
                S.add("sp", mk("dma_start", out=w[:, k, :],
                                                                   in_=w_mod_d[k * 128:(k + 1) * 128, mc * 1024:(mc + 1) * 1024]),
                      writes=[("wm", mc % 2, k)], dma=True)
            for j in range(8):
                ch = mc * 8 + j
                for k in range(KD):
                    S.add("pe", mk("matmul", mp[:, ch, :], w[:, k, j * 128:(j + 1) * 128],
                                                                         scs[:, k, :], start=(k == 0), stop=(k == KD - 1)),
                          reads=[("wm", mc % 2, k), "scs"], writes=["modps"])
        for c in range(2):
            S.add("dve", mk("tensor_tensor", out=modT[:, :, c], in0=mp[:, :, c], in1=bm[:], op=ALU.add),
                  reads=["modps", "bm"], writes=["modT"])
        for w3 in range(3):
            g = vecs[:, 8 * w3:8 * w3 + 8]
            for c in range(2):
                S.add("dve", mk("scalar_tensor_tensor",
                    out=GS[:, w3, :, c], in0=modT[:, (3 * w3 + 1) * 8:(3 * w3 + 2) * 8, c], scalar=1.0, in1=g,
                    op0=ALU.add, op1=ALU.mult), reads=["modT", "vecs"], writes=[key])
                S.add("dve", mk("tensor_copy", out=SH[:, w3, :, c], in_=modT[:, (3 * w3) * 8:(3 * w3 + 1) * 8, c]),
                      reads=["modT"], writes=[key])
                gsc = 1.0 if w3 == 1 else 0.5
                S.add("dve", mk("tensor_scalar",
                    out=GT[:, w3, :, c], in0=modT[:, (3 * w3 + 2) * 8:(3 * w3 + 3) * 8, c], scalar1=gsc, scalar2=None,
                    op0=ALU.mult), reads=["modT"], writes=[key])
    mods = {}
    for w3, wn in enumerate(("ffn1", "mix", "ffn2")):
        for c, cn in enumerate(("lat", "ctx")):
            mods[(wn, cn)] = dict(gs=GS[:, w3, :, c], sh=SH[:, w3, :, c], gh=GT[:, w3, :, c], key=key)
    return mods


def ffn_phase(C, hT_d, w_in_d, w_out_d, blocks, mods, wn):
    S = C.S
    NSPL = 4
    W = 2 * DFF // NSPL
    with C.scope():
        win = C.sb("win", [128, KD, 2 * DFF], BF16)
        wout = C.sb("wout", [128, JF, D], BF16)
        hbs = [C.sb("hb%d" % i, [128, KD, TB], F32) for i in range(3)]
        xns = [C.sb("xn%d" % i, [128, KD, TB], BF16) for i in range(2)]
        sqs = [C.sb("sq%d" % i, [128, KD, TB], BF16) for i in range(2)]
        rstds = [C.sb("rstd%d" % i, [128, TB], F32) for i in range(2)]
        tmps = [C.sb("tmp%d" % i, [128, TB], F32) for i in range(2)]
        sgs = [C.sb("sg%d" % i, [128, TB], F32) for i in range(4)]
        gjs = [C.sb("gj%d" % i, [128, TB], BF16) for i in range(4)]
        acc = C.ps("acc", [128, 8, TB])
        gu = C.ps("gu", [128, 8, TB])
        ones_b = C.t["ones_b"]
        WL = WLoader(C, W)
        for s in (0, 2, 1, 3):
            for k in range(KD):
                WL.load(win[:, k, s * W:(s + 1) * W], w_in_d[k * 128:(k + 1) * 128, s * W:(s + 1) * W], W, ("win", k, s))
        for j in range(JF):
            WL.load(wout[:, j, :], w_out_d[j * 128:(j + 1) * 128, :], D, ("wout", j))
        nb = len(blocks)
        hview = hT_d.rearrange("(k p) t -> p k t", p=128)
        gu_slot = [0]

        def next_gu():
            s = gu_slot[0]
            gu_slot[0] = (s + 1) % 4
            return s

        def load(b):
            t0 = blocks[b][0]
            hb = hbs[b % 3]
            S.add("sp", mk("dma_start", out=hb[:], in_=hview[:, :, t0:t0 + TB]),
                  reads=[("hT", t0)], writes=[("hb", b % 3)], dma=True)

        def norm_a(b):
            hb, sq = hbs[b % 3], sqs[b % 2]
            S.add("act", mk("activation", out=sq[:], in_=hb[:], func=AF.Square),
                  reads=[("hb", b % 3)], writes=[("sq", b % 2)])

        def norm_b(b):
            m = mods[(wn, blocks[b][1])]
            hb, sq, xn, rstd, tmp = hbs[b % 3], sqs[b % 2], xns[b % 2], rstds[b % 2], tmps[b % 2]
            s = next_gu()
            ssp = gu[:, 2 * s, :]
            for k in range(KD):
                S.add("pe", mk("matmul", ssp, ones_b[:], sq[:, k, :], start=(k == 0), stop=(k == KD - 1)),
                      reads=[("sq", b % 2), "ones_b"], writes=[("gu", s)])
            rstd_from_psum(C, ssp, 128, TB, 1.0 / D, rstd[:], tmp[:], [("gu", s)], ("rstd", b % 2), ("tmp", b % 2))
            for k in range(KD):
                S.add("dve", mk("scalar_tensor_tensor", out=tmp[:], in0=hb[:, k, :], scalar=m["gs"][:, k:k + 1],
                                                                   in1=rstd[:], op0=ALU.mult, op1=ALU.mult),
                      reads=[("hb", b % 3), ("rstd", b % 2), m["key"]], writes=[("tmp", b % 2)])
                S.add("act", mk("activation", out=xn[:, k, :], in_=tmp[:], func=AF.Identity,
                                                         bias=m["sh"][:, k:k + 1], scale=1.0),
                      reads=[("tmp", b % 2), m["key"]], writes=[("xn", b % 2, k)])

        def main(b):
            m = mods[(wn, blocks[b][1])]
            t0 = blocks[b][0]
            hb, xn = hbs[b % 3], xns[b % 2]
            xkeys = [("xn", b % 2, k) for k in range(KD)]
            for jj in range(JF + 2):
                if jj < JF:
                    j = jj
                    s = next_gu()
                    gp = gu[:, 2 * s, :]
                    up = gu[:, 2 * s + 1, :]
                    for k in range(KD):
                        S.add("pe", mk("matmul", gp, win[:, k, j * 128:(j + 1) * 128], xn[:, k, :],
                                                                       start=(k == 0), stop=(k == KD - 1)),
                              reads=[xkeys[k], ("win", k, (j * 128) // W), ("win", k, ((j + 1) * 128 - 1) // W)],
                              writes=[("gu", s)])
                    for k in range(KD):
                        c0 = DFF + j * 128
                        S.add("pe", mk("matmul", up, win[:, k, c0:c0 + 128], xn[:, k, :],
                                                                          start=False, stop=(k == KD - 1),
                                                                          skip_group_check=True),
                              reads=[xkeys[k], ("win", k, c0 // W), ("win", k, (c0 + 127) // W)],
                              writes=[("gu", s)])
                    sg, gj = sgs[j % 4], gjs[j % 4]
                    S.add("act", mk("activation", out=sg[:], in_=gp, func=AF.Silu),
                          reads=[("gu", s)], writes=[("sg", j % 4)])
                    S.add("dve", mk("tensor_tensor", out=gj[:], in0=up, in1=sg[:], op=ALU.mult),
                          reads=[("gu", s), ("sg", j % 4)], writes=[("gj", j % 4)])
                if jj >= 2:
                    j = jj - 2
                    gj = gjs[j % 4]
                    for n in range(KD):
                        f = (j == 0 and n % 2 == 0)
                        S.add("pe", mk("matmul",
                            acc[:, n, :], wout[:, j, n * 128:(n + 1) * 128], gj[:],
                            start=f, stop=(j == JF - 1), skip_group_check=True),
                              reads=[("gj", j % 4), ("wout", j)], writes=[("acc", n // 2)])
                if jj == 6 and b + 1 < nb:
                    norm_b(b + 1)
            for n in range(KD):
                S.add("dve", mk("scalar_tensor_tensor", out=hb[:, n, :], in0=acc[:, n, :],
                                                                   scalar=m["gh"][:, n:n + 1], in1=hb[:, n, :],
                                                                   op0=ALU.mult, op1=ALU.add),
                      reads=[("acc", n // 2), m["key"], ("hb", b % 3)], writes=[("hb", b % 3)])
            S.add("sp", mk("dma_start", out=hview[:, :, t0:t0 + TB], in_=hb[:]),
                  reads=[("hb", b % 3)], writes=[("hT", t0)], dma=True)

        load(0)
        if nb > 1:
            load(1)
        norm_a(0)
        norm_b(0)
        for b in range(nb):
            if b + 2 < nb:
                load(b + 2)
            if b + 1 < nb:
                norm_a(b + 1)
            main(b)


def ffn_blocks(do_ctx=True):
    return [(t0, "lat" if t0 < NL else "ctx") for t0 in range(0, NT if do_ctx else NL, TB)]


def hT_keys(t0, nt):
    return [("hT", t) for t in range(t0, t0 + nt, TB)]


class BankRR:
    def __init__(self, C, n=8):
        self.t = C.ps("bank", [128, n, 512])
        self.n = n
        self.i = 0

    def next(self):
        b = self.i
        self.i = (self.i + 1) % self.n
        return b, self.t, ("bank", b)


def m1_phase(C, hT_d, Wd, vecs, mods, ropeC_d, ropeS_d, O, exch=None):
    S = C.S
    blocks = [(t0, 512, "lat") for t0 in range(0, NL, 512)] + [(NL, 256, "ctx")]
    with C.scope():
        winx = C.sb("winx", [128, KD, WINX], BF16)
        wuq = C.sb("wuq", [128, 2, 1152], BF16)
        wukv = C.sb("wukv", [128, 768], BF16)
        WL = WLoader(C, 1152)
        for k in range(KD):
            for s2 in range(2):
                c0, c1 = s2 * 1056, (s2 + 1) * 1056
                WL.load(winx[:, k, c0:c1], Wd["winx"][k * 128:(k + 1) * 128, c0:c1], 1056, ("winx", k))
        for k in range(2):
            WL.load(wuq[:, k, :], Wd["wuq"][k * 128:(k + 1) * 128, :], 1152, "wuq")
        WL.load(wukv[:], Wd["wukv"], 768, "wukv")
        hb = [C.sb("mhb%d" % i, [128, KD, 512], F32) for i in range(2)]
        xn = C.sb("mxn", [128, KD, 512], BF16)
        sq = C.sb("msq", [128, KD, 512], BF16)
        rstd = C.sb("mrstd", [128, 512], F32)
        tmp = C.sb("mtmp", [128, 512], F32)
        rc = C.sb("rc", [128, 512], F32)
        rs = C.sb("rs", [128, 512], F32)
        st3 = [C.sb("st3_%d" % i, [128, 3, 512], F32) for i in range(2)]
        sq2 = C.sb("sq2", [128, 2, 512], BF16)
        cqn = C.sb("cqn", [128, 2, 512], BF16)
        ckvn = C.sb("ckvn", [128, 512], BF16)
        sqk = C.sb("sqk", [128, 512], BF16)
        tA = C.sb("tA", [128, 512], F32)
        t1 = [C.sb("t1_%d" % i, [128, 512], F32) for i in range(2)]
        t2 = [C.sb("t2_%d" % i, [128, 512], F32) for i in range(2)]
        rq = [C.sb("rq_%d" % i, [128, 512], F32) for i in range(2)]
        tq = [C.sb("tq_%d" % i, [128, 512], F32) for i in range(2)]
        sqh = [C.sb("sqh_%d" % i, [128, 512], BF16) for i in range(2)]
        qh = [C.sb("qh_%d" % i, [128, 512], BF16) for i in range(2)]
        kh = [C.sb("kh_%d" % i, [128, 512], BF16) for i in range(2)]
        nst = [C.sb("nst_%d" % i, [128, 2, 512], BF16) for i in range(2)]
        nva = C.sb("nva", [128, 4, 4, 128], BF16)
        va = C.sb("va", [128, 4, 6, 128], BF16)
        B = BankRR(C, 8)
        ones_b, ones_bd = C.t["ones_b"], C.t["ones_bd"]
        S.add("pool", mk("memset", nva[:], 1.0), writes=["nva"])
        S.add("pool", mk("memset", va[:], 1.0), writes=["va"])
        hview = hT_d.rearrange("(k p) t -> p k t", p=128)
        V = lambda c: vecs[:, c:c + 1]
        exch_done = set()

        def load(bi):
            t0, nt, kind = blocks[bi]
            h = hb[bi % 2]
            S.add("sp", mk("dma_start", out=h[:, :, 0:nt], in_=hview[:, :, t0:t0 + nt]),
                  reads=hT_keys(t0, nt), writes=[("mhb", bi % 2)], dma=True)

        load(0)
        for bi, (t0, nt, kind) in enumerate(blocks):
            if bi + 1 < len(blocks):
                load(bi + 1)
            m = mods[("mix", kind)]
            h = hb[bi % 2]
            hk = ("mhb", bi % 2)
            if kind == "lat":
                q4, off = t0 // 1024, t0 % 1024
                dst2 = lambda nm, q4=q4: O["L_" + nm][q4]
            else:
                q4, off = "c", 0
                dst2 = lambda nm: O["L_" + nm + "c"]
            S.add("sp", mk("dma_start", out=rc[64:96, 0:nt], in_=ropeC_d[:, t0:t0 + nt]),
                  writes=["rc"], dma=True)
            S.add("sp", mk("dma_start", out=rs[64:96, 0:nt], in_=ropeS_d[:, t0:t0 + nt]),
                  writes=["rs"], dma=True)
            S.add("act", mk("activation", out=sq[:, :, 0:nt], in_=h[:, :, 0:nt], func=AF.Square),
                  reads=[hk], writes=["msq"])
            b, bt, bk = B.next()
            for k in range(KD):
                S.add("pe", mk("matmul", bt[:, b, 0:nt], ones_b[:], sq[:, k, 0:nt],
                                                               start=(k == 0), stop=(k == KD - 1)),
                      reads=["msq", "ones_b"], writes=[bk])
            rstd_from_psum(C, bt[:, b, 0:nt], 128, nt, 1.0 / D, rstd[:, 0:nt], tmp[:, 0:nt], [bk], "mrstd", "mtmp")
            for k in range(KD):
                S.add("dve", mk("scalar_tensor_tensor",
                    out=tmp[:, 0:nt], in0=h[:, k, 0:nt], scalar=m["gs"][:, k:k + 1], in1=rstd[:, 0:nt],
                    op0=ALU.mult, op1=ALU.mult), reads=[hk, "mrstd", m["key"]], writes=["mtmp"])
                S.add("act", mk("activation", out=xn[:, k, 0:nt], in_=tmp[:, 0:nt], func=AF.Identity,
                                                                   bias=m["sh"][:, k:k + 1], scale=1.0),
                      reads=["mtmp", m["key"]], writes=[("mxn", k)])

            def proj(col0, M):
                b, bt, bk = B.next()
                for k in range(KD):
                    S.add("pe", mk("matmul", bt[0:M, b, 0:nt], winx[:, k, col0:col0 + M], xn[:, k, 0:nt],
                                                             start=(k == 0), stop=(k == KD - 1)),
                          reads=[("mxn", k), ("winx", k)], writes=[bk])
                return bt[0:M, b, 0:nt], bk

            s3 = st3[0]
            for c in range(3):
                p, pk = proj(c * 128, 128)
                S.add("act", mk("activation", out=s3[:, c, 0:nt], in_=p, func=AF.Copy),
                      reads=[pk], writes=[("st3", 0)])
            S.add("sp", mk("dma_start", out=dst2("lx").rearrange("(c p) t -> p c t", p=128)[:, :, off:off + nt],
                                                     in_=s3[:, :, 0:nt]),
                  reads=[("st3", 0)], writes=[("L", "lx", q4)], dma=True)
            s3 = st3[1]
            for c in range(3):
                p, pk = proj(384 + c * 128, 128)
                S.add("act", mk("activation", out=s3[:, c, 0:nt], in_=p, func=AF.Gelu_apprx_tanh),
                      reads=[pk], writes=[("st3", 1)])
            S.add("sp", mk("dma_start", out=O["lgel"].rearrange("(c p) t -> p c t", p=128)[:, :, t0:t0 + nt],
                                                     in_=s3[:, :, 0:nt]),
                  reads=[("st3", 1)], writes=[("lgel", t0)], dma=True)
            pcq = []
            for c in range(2):
                p, pk = proj(768 + c * 128, 128)
                pcq.append((p, pk))
                S.add("act", mk("activation", out=sq2[:, c, 0:nt], in_=p, func=AF.Square),
                      reads=[pk], writes=["sq2"])
            b, bt, bk = B.next()
            for c in range(2):
                S.add("pe", mk("matmul", bt[:, b, 0:nt], ones_b[:], sq2[:, c, 0:nt], start=(c == 0), stop=(c == 1)),
                      reads=["sq2", "ones_b"], writes=[bk])
            rstd_from_psum(C, bt[:, b, 0:nt], 128, nt, 1.0 / 256, rstd[:, 0:nt], tmp[:, 0:nt], [bk], "mrstd", "mtmp")
            for c in range(2):
                p, pk = pcq[c]
                S.add("dve", mk("scalar_tensor_tensor", out=cqn[:, c, 0:nt], in0=p, scalar=V(24 + c),
                                                                        in1=rstd[:, 0:nt], op0=ALU.mult, op1=ALU.mult),
                      reads=[pk, "mrstd", "vecs"], writes=["cqn"])
            p, pk = proj(1024, 128)
            S.add("act", mk("activation", out=sq2[:, 0, 0:nt], in_=p, func=AF.Square), reads=[pk], writes=["sq2"])
            b, bt, bk = B.next()
            S.add("pe", mk("matmul", bt[:, b, 0:nt], ones_b[:], sq2[:, 0, 0:nt], start=True, stop=True),
                  reads=["sq2", "ones_b"], writes=[bk])
            rstd_from_psum(C, bt[:, b, 0:nt], 128, nt, 1.0 / 128, rstd[:, 0:nt], tmp[:, 0:nt], [bk], "mrstd", "mtmp")
            S.add("dve", mk("scalar_tensor_tensor", out=ckvn[:, 0:nt], in0=p, scalar=V(26), in1=rstd[:, 0:nt],
                                                               op0=ALU.mult, op1=ALU.mult),
                  reads=[pk, "mrstd", "vecs"], writes=["ckvn"])
            pkr, pkrk = proj(1152, 96)
            pks, pksk = proj(1248, 96)
            S.add("act", mk("activation", out=sqk[64:96, 0:nt], in_=pkr[64:96, :], func=AF.Square),
                  reads=[pkrk], writes=["sqk_r"])
            S.add("dve", mk("scalar_tensor_tensor", out=tA[64:96, 0:nt], in0=pkr[64:96, :], scalar=vecs[64:96, 29:30],
                                                          in1=rc[64:96, 0:nt], op0=ALU.mult, op1=ALU.mult),
                  reads=[pkrk, "rc", "vecs"], writes=["tA"])
            S.add("dve", mk("scalar_tensor_tensor", out=tmp[64:96, 0:nt], in0=pks[64:96, :], scalar=vecs[64:96, 30:31],
                                                          in1=rs[64:96, 0:nt], op0=ALU.mult, op1=ALU.mult),
                  reads=[pksk, "rs", "vecs"], writes=["mtmp"])
            S.add("dve", mk("tensor_tensor", out=tA[64:96, 0:nt], in0=tA[64:96, 0:nt], in1=tmp[64:96, 0:nt], op=ALU.add),
                  reads=["tA", "mtmp"], writes=["tA"])
            for which, col0, gcol, oname in (("q", 1344, 31, "nqT"), ("k", 1600, 32, "nkT")):
                ns = nst[0 if which == "q" else 1]
                nk_ = ("nst", which)
                for c in range(2):
                    p, pk = proj(col0 + c * 128, 128)
                    S.add("act", mk("activation", out=sq2[:, 0, 0:nt], in_=p, func=AF.Square), reads=[pk], writes=["sq2"])
                    b, bt, bk = B.next()
                    S.add("pe", mk("matmul", bt[:, b, 0:nt], ones_bd[:], sq2[:, 0, 0:nt], start=True, stop=True),
                          reads=["sq2", "ones_bd"], writes=[bk])
                    rstd_from_psum(C, bt[:, b, 0:nt], 128, nt, 1.0 / 64, rstd[:, 0:nt], tmp[:, 0:nt], [bk], "mrstd", "mtmp")
                    S.add("dve", mk("scalar_tensor_tensor",
                        out=ns[:, c, 0:nt], in0=p, scalar=V(gcol), in1=rstd[:, 0:nt], op0=ALU.mult, op1=ALU.mult),
                          reads=[pk, "mrstd", "vecs"], writes=[nk_])
                odst = (O["nqT"].rearrange("(c p) t -> p c t", p=128)[:, :, t0:t0 + nt] if which == "q"
                        else dst2("nkT").rearrange("(c p) t -> p c t", p=128)[:, :, off:off + nt])
                S.add("sp", mk("dma_start", out=odst, in_=ns[:, :, 0:nt]),
                      reads=[nk_], writes=[("L", oname, q4) if which == "k" else (oname, t0)], dma=True)
            nsub = nt // 128
            for sb_ in range(nsub):
                b, bt, bk = B.next()
                for k in range(KD):
                    S.add("pe", mk("matmul", bt[:, b, 0:256], xn[:, k, sb_ * 128:(sb_ + 1) * 128],
                                                                           winx[:, k, 1856:2112], start=(k == 0), stop=(k == KD - 1)),
                          reads=[("mxn", k), ("winx", k)], writes=[bk])
                pv = bt[:, b, 0:256].rearrange("p (h d) -> p h d", h=4)
                S.add("act", mk("activation", out=nva[:, sb_, 0:4:2, 0:64], in_=pv[:, 0:4:2, :], func=AF.Copy),
                      reads=[bk], writes=["nva"])
                S.add("dve", mk("tensor_copy", out=nva[:, sb_, 1:4:2, 64:128], in_=pv[:, 1:4:2, :]),
                      reads=[bk], writes=["nva"])
            S.add("sp", mk("dma_start",
                out=dst2("nvA")[off:off + nt, :].rearrange("(s p) (h d) -> p s h d", p=128, h=4), in_=nva[:, 0:nsub]),
                  reads=["nva"], writes=[("L", "nvA", q4)], dma=True)
            for sb_ in range(nsub):
                b, bt, bk = B.next()
                S.add("pe", mk("matmul", bt[:, b, 0:384], ckvn[:, sb_ * 128:(sb_ + 1) * 128],
                                                                  wukv[:, 384:768], start=True, stop=True),
                      reads=["ckvn", "wukv"], writes=[bk])
                pv = bt[:, b, 0:384].rearrange("p (h d) -> p h d", h=6)
                S.add("act", mk("activation", out=va[:, sb_, 0:6:2, 0:64], in_=pv[:, 0:6:2, :], func=AF.Copy),
                      reads=[bk], writes=["va"])
                S.add("dve", mk("tensor_copy", out=va[:, sb_, 1:6:2, 64:128], in_=pv[:, 1:6:2, :]),
                      reads=[bk], writes=["va"])
            S.add("sp", mk("dma_start",
                out=dst2("vA")[off:off + nt, :].rearrange("(s p) (h d) -> p s h d", p=128, h=6), in_=va[:, 0:nsub]),
                  reads=["va"], writes=[("L", "vA", q4)], dma=True)
            for hd in range(6):
                i2 = hd % 2
                b, bt, bk = B.next()
                pkn = bt[0:64, b, 0:nt]
                S.add("pe", mk("matmul", pkn, wukv[:, hd * 64:(hd + 1) * 64], ckvn[:, 0:nt], start=True, stop=True),
                      reads=["ckvn", "wukv"], writes=[bk])
                S.add("act", mk("activation", out=sqk[0:64, 0:nt], in_=pkn, func=AF.Square),
                      reads=[bk], writes=["sqk_n"])
                b2, bt2, bk2 = B.next()
                pss = bt2[0:96, b2, 0:nt]
                S.add("pe", mk("matmul", pss, ones_b[0:96, 0:96], sqk[0:96, 0:nt], start=True, stop=True),
                      reads=["sqk_n", "sqk_r", "ones_b"], writes=[bk2])
                r_, t_ = rq[i2], tq[i2]
                rstd_from_psum(C, pss, 96, nt, 1.0 / 96, r_[0:96, 0:nt], t_[0:96, 0:nt], [bk2], ("rq", i2), ("tq", i2))
                khh = kh[i2]
                S.add("dve", mk("scalar_tensor_tensor",
                    out=khh[0:64, 0:nt], in0=pkn, scalar=vecs[0:64, 29:30], in1=r_[0:64, 0:nt], op0=ALU.mult, op1=ALU.mult),
                      reads=[bk, ("rq", i2), "vecs"], writes=[("kh", i2)])
                S.add("dve", mk("tensor_tensor", out=khh[64:96, 0:nt], in0=tA[64:96, 0:nt],
                                                                        in1=r_[64:96, 0:nt], op=ALU.mult),
                      reads=["tA", ("rq", i2)], writes=[("kh", i2)])
                S.add("sp", mk("dma_start", out=dst2("kT")[hd * 96:(hd + 1) * 96, off:off + nt], in_=khh[0:96, 0:nt]),
                      reads=[("kh", i2)], writes=[("L", "kT", q4)], dma=True)
            for hd in range(6):
                i2 = hd % 2
                b, bt, bk = B.next()
                pq = bt[0:96, b, 0:nt]
                for k in range(2):
                    S.add("pe", mk("matmul", pq, wuq[:, k, hd * 192:hd * 192 + 96], cqn[:, k, 0:nt],
                                                                      start=(k == 0), stop=(k == 1)),
                          reads=["cqn", "wuq"], writes=[bk])
                b3, bt3, bk3 = B.next()
                pw = bt3[0:96, b3, 0:nt]
                for k in range(2):
                    S.add("pe", mk("matmul", pw, wuq[:, k, hd * 192 + 96:hd * 192 + 192], cqn[:, k, 0:nt],
                                                                      start=(k == 0), stop=(k == 1)),
                          reads=["cqn", "wuq"], writes=[bk3])
                sh_ = sqh[i2]
                S.add("act", mk("activation", out=sh_[0:96, 0:nt], in_=pq, func=AF.Square),
                      reads=[bk], writes=[("sqh", i2)])
                b2, bt2, bk2 = B.next()
                pss = bt2[0:96, b2, 0:nt]
                S.add("pe", mk("matmul", pss, ones_b[0:96, 0:96], sh_[0:96, 0:nt], start=True, stop=True),
                      reads=[("sqh", i2), "ones_b"], writes=[bk2])
                r_, t_ = rq[i2], tq[i2]
                rstd_from_psum(C, pss, 96, nt, 1.0 / 96, r_[0:96, 0:nt], t_[0:96, 0:nt], [bk2], ("rq", i2), ("tq", i2))
                qhh, a1, a2 = qh[i2], t1[i2], t2[i2]
                S.add("dve", mk("scalar_tensor_tensor",
                    out=qhh[0:64, 0:nt], in0=pq[0:64, :], scalar=vecs[0:64, 27:28], in1=r_[0:64, 0:nt], op0=ALU.mult, op1=ALU.mult),
                      reads=[bk, ("rq", i2), "vecs"], writes=[("qh", i2)])
                S.add("dve", mk("scalar_tensor_tensor",
                    out=a1[64:96, 0:nt], in0=pq[64:96, :], scalar=vecs[64:96, 27:28], in1=rc[64:96, 0:nt], op0=ALU.mult, op1=ALU.mult),
                      reads=[bk, "rc", "vecs"], writes=[("t1", i2)])
                S.add("dve", mk("scalar_tensor_tensor",
                    out=a2[64:96, 0:nt], in0=pw[64:96, :], scalar=vecs[64:96, 28:29], in1=rs[64:96, 0:nt], op0=ALU.mult, op1=ALU.mult),
                      reads=[bk3, "rs", "vecs"], writes=[("t2", i2)])
                S.add("dve", mk("tensor_tensor", out=a1[64:96, 0:nt], in0=a1[64:96, 0:nt], in1=a2[64:96, 0:nt], op=ALU.add),
                      reads=[("t1", i2), ("t2", i2)], writes=[("t1", i2)])
                S.add("dve", mk("tensor_tensor", out=qhh[64:96, 0:nt], in0=a1[64:96, 0:nt],
                                                                              in1=r_[64:96, 0:nt], op=ALU.mult),
                      reads=[("t1", i2), ("rq", i2)], writes=[("qh", i2)])
                S.add("sp", mk("dma_start", out=O["qT"][hd, :, t0:t0 + nt], in_=qhh[0:96, 0:nt]),
                      reads=[("qh", i2)], writes=[("qT", hd, t0)], dma=True)
            if exch is not None:
                for qq in range(4):
                    if bi == min(2 * qq + 2, len(blocks) - 1) or (bi == len(blocks) - 1 and 2 * qq + 2 > bi):
                        if qq not in exch_done:
                            exch_done.add(qq)
                            exch(qq)


def lru_phase(C, I, vecs, lruw_d, halfmask_d, yT_d):
    S = C.S
    XW = 2 + NCX + 3 + 2 * NL + 2
    CT0 = 2
    LT0 = 2 + NCX + 3
    SEG = 512
    with C.scope():
        lw = C.sb("lw", [128, 1536], BF16)
        with C.scope():
            WL = WLoader(C, 1536, n=1)
            WL.load(lw[:], lruw_d, 1536, "lw")
        hm = C.sb("hm", [128, 2], F32)
        S.add("sp", mk("dma_start", out=hm[:], in_=halfmask_d), writes=["hm"], dma=True)
        par = C.sb("lpar", [128, 18], F32)
        e1 = C.sb("le1", [128, 6], F32)
        one_t = C.sb("one_t", [128, 1], F32)
        S.add("pool", mk("memset", one_t[:], 1.0), writes=["one_t"])
        S.add("act", mk("activation", out=e1[:], in_=vecs[:, 60:66], func=AF.Exp, scale=-1.0), reads=["vecs"], writes=["le1"])
        S.add("act", mk("activation", out=e1[:], in_=e1[:], func=AF.Ln, bias=one_t[:], scale=1.0), reads=["le1", "one_t"], writes=["le1"])
        S.add("dve", mk("tensor_scalar", out=par[:, 0:6], in0=e1[:], scalar1=-4.0, scalar2=None, op0=ALU.mult),
              reads=["le1"], writes=["lpar"])
        S.add("dve", mk("tensor_scalar", out=par[:, 6:18], in0=vecs[:, 48:60], scalar1=0.5, scalar2=None, op0=ALU.mult),
              reads=["vecs"], writes=["lpar"])
        xc = C.sb("xc", [128, XW], F32)
        xcb = C.sb("xcb", [128, XW], BF16)
        hsum = C.sb("hsum", [128, NT], F32)
        lg = C.sb("lg", [128, NT], F32)
        NB = 2
        tr = [C.sb("tr%d" % i, [128, SEG], F32) for i in range(NB)]
        ti = [C.sb("ti%d" % i, [128, SEG], F32) for i in range(NB)]
        aa = [C.sb("aa%d" % i, [128, SEG], F32) for i in range(NB)]
        a2 = [C.sb("a2%d" % i, [128, SEG], F32) for i in range(NB)]
        uu = [C.sb("uu%d" % i, [128, SEG], F32) for i in range(NB)]
        hh = [C.sb("hh%d" % i, [128, SEG], F32) for i in range(NB)]
        yb = C.sb("yb", [128, NT], BF16)
        stt = C.sb("lstate", [128, 1], F32)
        B = BankRR(C, 4)
        phalf = C.t["phalf"]
        for c in range(3):
            with C.scope():
                stg = C.sb("stg", [128, 2, NL], F32)
                xf = C.sb("xf", [128, XW], F32)
                S.add("dve", mk("memset", xf[:], 0.0), writes=["xf"])
                for hf in range(2):
                    for q4 in range(4):
                        S.add("sp", mk("dma_start", out=stg[:, hf, q4 * 1024:(q4 + 1) * 1024],
                                       in_=I["G_lx"][q4, hf, c * 128:(c + 1) * 128, :]),
                              reads=[("G", "lx", q4)], writes=[("stg", hf)], dma=True)
                S.add("sp", mk("dma_start", out=xf[:, CT0:CT0 + NCX], in_=I["L_lxc"][c * 128:(c + 1) * 128, :]),
                      reads=[("L", "lx", "c"), "xf"], writes=["xf"], dma=True)
                lat = xf[:, LT0:LT0 + 2 * NL].rearrange("p (r h c) -> p r h c", h=2, c=32)
                for hf in range(2):
                    eng = "act" if hf == 0 else "dve"
                    src = stg[:, hf, :].rearrange("p (r c) -> p r c", c=32)
                    if eng == "act":
                        S.add("act", mk("activation", out=lat[:, :, hf, :], in_=src, func=AF.Copy),
                              reads=[("stg", hf), "xf"], writes=["xf"])
                    else:
                        S.add("dve", mk("tensor_copy", out=lat[:, :, hf, :], in_=src),
                              reads=[("stg", hf), "xf"], writes=["xf"])
                n = XW - 3
                S.add("dve", mk("tensor_scalar", out=xc[:, 2:2 + n], in0=xf[:, 0:n], scalar1=vecs[:, 36 + 4 * c:37 + 4 * c],
                                                              scalar2=vecs[:, 33 + c:34 + c], op0=ALU.mult, op1=ALU.add),
                      reads=["xf", "vecs"], writes=["xc"])
                for j in range(1, 4):
                    S.add("dve", mk("scalar_tensor_tensor", out=xc[:, 2:2 + n], in0=xf[:, j:j + n],
                                                                              scalar=vecs[:, 36 + 4 * c + j:37 + 4 * c + j],
                                                                              in1=xc[:, 2:2 + n], op0=ALU.mult, op1=ALU.add),
                          reads=["xf", "vecs", "xc"], writes=["xc"])
                S.add("act", mk("activation", out=xcb[:, 2:2 + n], in_=xc[:, 2:2 + n], func=AF.Copy), reads=["xc"], writes=["xcb"])
            S.add("sp", mk("dma_start", out=lg[:], in_=I["lgel"][c * 128:(c + 1) * 128, :]),
                  reads=[("lgel", t) for t in list(range(0, NL, 512)) + [NL]], writes=["lg"], dma=True)
            segs = [(CT0, NCX, "ctx", 0)] + [(LT0 + i * SEG, SEG, "lat", i) for i in range(2 * NL // SEG)]
            for d in range(2):
                order = segs if d == 0 else [segs[0]] + segs[:0:-1]
                pidx = d * 3 + c
                hc = par[:, pidx:pidx + 1]
                hba = par[:, 6 + pidx:7 + pidx]
                hbx = par[:, 12 + pidx:13 + pidx]
                wa = lw[:, (0 * 6 + pidx) * 128:(0 * 6 + pidx + 1) * 128]
                wx = lw[:, (1 * 6 + pidx) * 128:(1 * 6 + pidx + 1) * 128]
                first = True
                for si, (x0, n, kind, li) in enumerate(order):
                    ib = si % NB
                    r_, i_, a_, q_, u_, h_ = tr[ib], ti[ib], aa[ib], a2[ib], uu[ib], hh[ib]
                    for p0 in range(0, n, 512):
                        pn = min(512, n - p0)
                        b, bt, bk = B.next()
                        S.add("pe", mk("matmul",
                            bt[:, b, 0:pn], wa, xcb[:, x0 + p0:x0 + p0 + pn], start=True, stop=True),
                              reads=["xcb", "lw"], writes=[bk])
                        S.add("act", mk("activation",
                            out=r_[:, p0:p0 + pn], in_=bt[:, b, 0:pn], func=AF.Tanh, bias=hba, scale=0.5),
                              reads=[bk, "lpar"], writes=[("tr", ib)])
                        b, bt, bk = B.next()
                        S.add("pe", mk("matmul",
                            bt[:, b, 0:pn], wx, xcb[:, x0 + p0:x0 + p0 + pn], start=True, stop=True),
                              reads=["xcb", "lw"], writes=[bk])
                        S.add("act", mk("activation",
                            out=i_[:, p0:p0 + pn], in_=bt[:, b, 0:pn], func=AF.Tanh, bias=hbx, scale=0.5),
                              reads=[bk, "lpar"], writes=[("ti", ib)])
                    S.add("act", mk("activation", out=a_[:, 0:n], in_=r_[:, 0:n], func=AF.Exp,
                                                                                 bias=hc, scale=hc),
                          reads=[("tr", ib), "lpar"], writes=[("aa", ib)])
                    S.add("dve", mk("tensor_tensor", out=q_[:, 0:n], in0=a_[:, 0:n], in1=a_[:, 0:n], op=ALU.mult),
                          reads=[("aa", ib)], writes=[("a2", ib)])
                    S.add("dve", mk("tensor_scalar", out=q_[:, 0:n], in0=q_[:, 0:n], scalar1=-0.25, scalar2=0.25,
                                                                      op0=ALU.mult, op1=ALU.add),
                          reads=[("a2", ib)], writes=[("a2", ib)])
                    S.add("act", mk("activation", out=q_[:, 0:n], in_=q_[:, 0:n], func=AF.Sqrt),
                          reads=[("a2", ib)], writes=[("a2", ib)])
                    S.add("dve", mk("scalar_tensor_tensor",
                        out=u_[:, 0:n], in0=i_[:, 0:n], scalar=1.0, in1=xc[:, x0:x0 + n], op0=ALU.add, op1=ALU.mult),
                          reads=[("ti", ib), "xc"], writes=[("uu", ib)])
                    S.add("dve", mk("tensor_tensor", out=u_[:, 0:n], in0=u_[:, 0:n], in1=q_[:, 0:n], op=ALU.mult),
                          reads=[("uu", ib), ("a2", ib)], writes=[("uu", ib)])
                    init = 0.0 if first else stt[:, 0:1]
                    if d == 0:
                        S.add("dve", mk("tensor_tensor_scan",
                            out=h_[:, 0:n], data0=a_[:, 0:n], data1=u_[:, 0:n], initial=init, op0=ALU.mult, op1=ALU.add),
                              reads=[("aa", ib), ("uu", ib), "lstate"], writes=[("hh", ib)])
                        S.add("dve", mk("tensor_copy", out=stt[:, 0:1], in_=h_[:, n - 1:n]),
                              reads=[("hh", ib)], writes=["lstate"])
                    else:
                        S.add("dve", mk("tensor_tensor_scan",
                            out=h_[:, 0:n][:, ::-1], data0=a_[:, 0:n][:, ::-1], data1=u_[:, 0:n][:, ::-1],
                            initial=init, op0=ALU.mult, op1=ALU.add),
                              reads=[("aa", ib), ("uu", ib), "lstate"], writes=[("hh", ib)])
                        S.add("dve", mk("tensor_copy", out=stt[:, 0:1], in_=h_[:, 0:1]),
                              reads=[("hh", ib)], writes=["lstate"])
                    first = False
                    if kind == "ctx":
                        if d == 0:
                            S.add("act", mk("activation", out=hsum[:, NL:NT], in_=h_[:, 0:NCX], func=AF.Copy),
                                  reads=[("hh", ib)], writes=[("hsum", "c")])
                        else:
                            S.add("dve", mk("tensor_tensor", out=hsum[:, NL:NT], in0=hsum[:, NL:NT], in1=h_[:, 0:NCX], op=ALU.add),
                                  reads=[("hh", ib), ("hsum", "c")], writes=[("hsum", "c")])
                    else:
                        rows = SEG // 64
                        hv = h_[:, 0:SEG].rearrange("p (r h c) -> p r h c", h=2, c=32)
                        ov = hsum[:, li * (SEG // 2):(li + 1) * (SEG // 2)].rearrange("p (r c) -> p r c", c=32)
                        hk_ = ("hsum", li)
                        if d == 0:
                            S.add("dve", mk("tensor_scalar", out=ov, in0=hv[:, :, 0, :], scalar1=hm[:, 0:1], scalar2=None,
                                                                                op0=ALU.mult),
                                  reads=[("hh", ib), "hm"], writes=[hk_])
                        else:
                            S.add("dve", mk("scalar_tensor_tensor", out=ov, in0=hv[:, :, 0, :], scalar=hm[:, 0:1], in1=ov,
                                                                                       op0=ALU.mult, op1=ALU.add),
                                  reads=[("hh", ib), "hm", hk_], writes=[hk_])
                        S.add("dve", mk("scalar_tensor_tensor", out=ov, in0=hv[:, :, 1, :], scalar=hm[:, 1:2], in1=ov,
                                                                                   op0=ALU.mult, op1=ALU.add),
                              reads=[("hh", ib), "hm", hk_], writes=[hk_])
            hkeys = [("hsum", "c")] + [("hsum", i) for i in range(2 * NL // SEG)]
            S.add("dve", mk("tensor_tensor", out=yb[:], in0=hsum[:], in1=lg[:], op=ALU.mult),
                  reads=hkeys + ["lg"], writes=["yb"])
            S.add("sp", mk("dma_start", out=yT_d[c * 128:(c + 1) * 128, :], in_=yb[:]),
                  reads=["yb"], writes=[("yT", c)], dma=True)


def mla_phase(C, I, yT_d, do_ctx=True):
    S = C.S
    NK = NCX + 2 * NL
    NJ = NK // 128
    with C.scope():
        kts = [C.sb("kt%d" % i, [128, NK], BF16) for i in range(2)]
        vas = [C.sb("vas%d" % i, [128, NJ, 128], BF16) for i in range(2)]
        qts = [C.sb("qt%d" % i, [128, NT], BF16) for i in range(2)]
        pts = [C.sb("pt%d" % i, [128, 512], BF16) for i in range(4)]
        osb = [C.sb("osb%d" % i, [128, 512], F32) for i in range(2)]
        rcp = [C.sb("rcp%d" % i, [128, 512], F32) for i in range(2)]
        ysb = [C.sb("ysb%d" % i, [128, 512], BF16) for i in range(2)]
        sps = C.ps("sps", [128, 4, 512])
        ops = C.ps("ops", [128, 2, 512])
        kq_all = [(nm, t) for nm in ("kT0", "kT1") for t in range(0, NL, 512)]

        def loadh(hd):
            i2 = hd % 2
            kt, va, qt = kts[i2], vas[i2], qts[i2]
            S.add("sp", mk("dma_start", out=kt[0:96, 0:NCX], in_=I["L_kTc"][hd * 96:(hd + 1) * 96, :]),
                  reads=[("L", "kT", "c")], writes=[("kt", i2, "c")], dma=True)
            S.add("sp", mk("dma_start", out=va[:, 0:2, :],
                           in_=I["L_vAc"][:, hd * 128:(hd + 1) * 128].rearrange("(j p) d -> p j d", p=128)),
                  reads=[("L", "vA", "c")], writes=[("vas", i2, "c")], dma=True)
            for hf in range(2):
                for q4 in range(4):
                    k0 = NCX + hf * NL + q4 * 1024
                    S.add("sp", mk("dma_start", out=kt[0:96, k0:k0 + 1024], in_=I["G_kT"][q4, hf, hd * 96:(hd + 1) * 96, :]),
                          reads=[("G", "kT", q4)], writes=[("kt", i2, hf, q4)], dma=True)
                    j0 = 2 + hf * 32 + q4 * 8
                    S.add("sp", mk("dma_start", out=va[:, j0:j0 + 8, :],
                                   in_=I["G_vA"][q4, hf, :, hd * 128:(hd + 1) * 128].rearrange("(j p) d -> p j d", p=128)),
                          reads=[("G", "vA", q4)], writes=[("vas", i2, hf, q4)], dma=True)
            S.add("sp", mk("dma_start", out=qt[0:96, :], in_=I["qT"][hd, :, :]),
                  reads=[("qT", hd, t) for t in list(range(0, NL, 512)) + [NL]], writes=[("qts", i2)], dma=True)

        def jpart(j):
            return ("c",) if j < 2 else ((j - 2) // 32, ((j - 2) % 32) // 8)

        cnt = [0, 0]
        loadh(0)
        for hd in range(6):
            if hd + 1 < 6:
                loadh(hd + 1)
            i2 = hd % 2
            kt, va, qt = kts[i2], vas[i2], qts[i2]
            qblocks = [(q0, 512, 0, NJ) for q0 in range(0, NL, 512)] + ([(NL, 256, 0, 2)] if do_ctx else [])
            for (q0, nq, j0, j1) in qblocks:
                ob = cnt[1] % 2
                cnt[1] += 1
                oacc = ops[:, ob, 0:nq]
                pend = []
                js = list(range(j0, j1))
                for idx in range(len(js) + 2):
                    if idx < len(js):
                        j = js[idx]
                        sb_ = cnt[0] % 4
                        cnt[0] += 1
                        sp_ = sps[:, sb_, 0:nq]
                        S.add("pe", mk("matmul",
                            sp_, kt[0:96, j * 128:(j + 1) * 128], qt[0:96, q0:q0 + nq], start=True, stop=True),
                              reads=[("kt", i2) + jpart(j), ("qts", i2)], writes=[("sps", sb_)])
                        pt = pts[sb_]
                        S.add("act", mk("activation", out=pt[:, 0:nq], in_=sp_, func=AF.Exp, scale=MLA_SCALE),
                              reads=[("sps", sb_)], writes=[("pt", sb_)])
                        pend.append((j, sb_))
                    if idx >= 2:
                        j, sb_ = pend[idx - 2]
                        pt = pts[sb_]
                        S.add("pe", mk("matmul",
                            oacc, va[:, j, :], pt[:, 0:nq], start=(idx == 2), stop=(idx == len(js) + 1)),
                              reads=[("pt", sb_), ("vas", i2) + jpart(j)], writes=[("ops", ob)])
                o_, r_, y_ = osb[ob], rcp[ob], ysb[ob]
                lo, hi = (0, 64) if hd % 2 == 0 else (64, 128)
                slo, shi = (64, 128) if hd % 2 == 0 else (0, 64)
                S.add("dve", mk("reciprocal", out=r_[slo:shi, 0:nq], in_=oacc[slo:shi, :]),
                      reads=[("ops", ob)], writes=[("rcp", ob)])
                S.add("act", mk("activation", out=o_[lo:hi, 0:nq], in_=oacc[lo:hi, :], func=AF.Copy),
                      reads=[("ops", ob)], writes=[("osb", ob)])
                S.add("dve", mk("tensor_copy", out=r_[lo:hi, 0:nq], in_=r_[slo:shi, 0:nq]),
                      reads=[("rcp", ob)], writes=[("rcp", ob)])
                S.add("dve", mk("tensor_tensor",
                    out=y_[lo:hi, 0:nq], in0=o_[lo:hi, 0:nq], in1=r_[lo:hi, 0:nq], op=ALU.mult),
                      reads=[("osb", ob), ("rcp", ob)], writes=[("ysb", ob)])
                row0 = 384 + hd * 64
                S.add("sp", mk("dma_start",
                    out=yT_d[row0:row0 + 64, q0:q0 + nq], in_=y_[lo:hi, 0:nq]),
                      reads=[("ysb", ob)], writes=[("yT", "m", hd, q0)], dma=True)


def na_pair_plan():
    cfg = {}
    plan = []
    r0f = lambda r: min(max(r - 4, 0), 120)
    for r in range(0, 128, 2):
        lo, hi = r0f(r), r0f(r + 1) + 8
        off0, off1 = r0f(r) - r, r0f(r + 1) - r
        chunks = []
        for ci in range(lo // 2, (hi - 1) // 2 + 1):
            key = (2 * ci - r, off0, off1)
            if key not in cfg:
                cfg[key] = len(cfg)
            chunks.append((ci, cfg[key]))
        plan.append(chunks)
    return plan, cfg


NA_PLAN, NA_CFG = na_pair_plan()
NTAB = len(NA_CFG)


def na_phase(C, I, natab_d, yT_d, do_ctx=True):
    S = C.S
    with C.scope():
        nk = C.sb("nk", [128, 2, 64, 2, 64], BF16)
        nkc = C.sb("nkc", [128, 2, NCX], BF16)
        nv = C.sb("nv", [128, 64, 4, 128], BF16)
        nvc = C.sb("nvc", [128, 2, 4, 128], BF16)
        nq = C.sb("nq", [128, 2, NT], BF16)
        tab = C.sb("natab", [128, 4, NTAB, 64], F32)
        ssb = [C.sb("nssb%d" % i, [128, 5, 64], F32) for i in range(2)]
        ptl = [C.sb("nptl%d" % i, [128, 5, 64], BF16) for i in range(2)]
        ptc = [C.sb("nptc%d" % i, [128, 2, 64], BF16) for i in range(2)]
        rcp = [C.sb("nrcp%d" % i, [128, 4, 64], F32) for i in range(2)]
        ysb = [C.sb("nysb%d" % i, [128, 2, 64], BF16) for i in range(2)]
        sps = C.ps("nsps", [128, 4, 512])
        ops = C.ps("nops", [128, 2, 512])
        S.add("sp", mk("dma_start", out=tab[:], in_=natab_d), writes=["natab"], dma=True)
        with C.scope():
            nks = C.sb("nks", [128, 2, 2, NL], BF16)
            for ck in range(2):
                for hf in range(2):
                    for q4 in range(4):
                        S.add("sp", mk("dma_start", out=nks[:, ck, hf, q4 * 1024:(q4 + 1) * 1024],
                                       in_=I["G_nkT"][q4, hf, ck * 128:(ck + 1) * 128, :]),
                              reads=[("G", "nkT", q4)], writes=[("nks", ck, hf)], dma=True)
                    src = nks[:, ck, hf, :].rearrange("p (ci t) -> p ci t", t=64)
                    if hf == 0:
                        S.add("act", mk("activation", out=nk[:, ck, :, hf, :], in_=src, func=AF.Copy),
                              reads=[("nks", ck, hf)], writes=["nk"])
                    else:
                        S.add("dve", mk("tensor_copy", out=nk[:, ck, :, hf, :], in_=src),
                              reads=[("nks", ck, hf)], writes=["nk"])
        for ck in range(2):
            S.add("sp", mk("dma_start", out=nkc[:, ck, :], in_=I["L_nkTc"][ck * 128:(ck + 1) * 128, :]),
                  reads=[("L", "nkT", "c")], writes=["nkc"], dma=True)
            S.add("sp", mk("dma_start", out=nq[:, ck, :], in_=I["nqT"][ck * 128:(ck + 1) * 128, :]),
                  reads=[("nqT", t) for t in list(range(0, NL, 512)) + [NL]], writes=["nq"], dma=True)
        for hf in range(2):
            for q4 in range(4):
                S.add("sp", mk("dma_start", out=nv[hf * 64:(hf + 1) * 64, q4 * 16:(q4 + 1) * 16],
                               in_=I["G_nvA"][q4, hf].rearrange("(ci q) (h d) -> q ci h d", q=64, h=4)),
                      reads=[("G", "nvA", q4)], writes=["nv"], dma=True)
        S.add("sp", mk("dma_start", out=nvc[:], in_=I["L_nvAc"].rearrange("(j p) (h d) -> p j h d", p=128, h=4)),
              reads=[("L", "nvA", "c")], writes=["nvc"], dma=True)
        cnt = [0, 0]

        def attend(q0, nqn, hd, loc, it):
            ck, pl = hd // 2, (hd % 2) * 64
            sb_ = cnt[0] % 4
            cnt[0] += 1
            nl = len(loc)
            sp_ = sps[:, sb_, 0:(nl + 2) * nqn].rearrange("p (j q) -> p j q", q=nqn)
            qap = nq[pl:pl + 64, ck, q0:q0 + nqn]
            for jl, (ci, ti_) in enumerate(loc):
                S.add("pe", mk("matmul", sp_[:, jl, :], nk[pl:pl + 64, ck, ci].rearrange("p h t -> p (h t)"), qap,
                                                                      start=True, stop=True),
                      reads=["nk", "nq"], writes=[("nsps", sb_)])
            for jc in range(2):
                S.add("pe", mk("matmul", sp_[:, nl + jc, :], nkc[pl:pl + 64, ck, jc * 128:(jc + 1) * 128], qap,
                                                               start=True, stop=True),
                      reads=["nkc", "nq"], writes=[("nsps", sb_)])
            i2 = it % 2
            if nl:
                s_, p_ = ssb[i2], ptl[i2]
                for jl, (ci, ti_) in enumerate(loc):
                    S.add("dve", mk("scalar_tensor_tensor",
                        out=s_[:, jl, :], in0=sp_[:, jl, :], scalar=NA_SCALE, in1=tab[:, hd, ti_, :], op0=ALU.mult, op1=ALU.add),
                          reads=[("nsps", sb_), "natab"], writes=[("nssb", i2)])
                S.add("act", mk("activation", out=p_[:, 0:nl, :], in_=s_[:, 0:nl, :], func=AF.Exp),
                      reads=[("nssb", i2)], writes=[("nptl", i2)])
            pc_ = ptc[i2]
            S.add("act", mk("activation", out=pc_[:, :, 0:nqn], in_=sp_[:, nl:nl + 2, :], func=AF.Exp, scale=NA_SCALE),
                  reads=[("nsps", sb_)], writes=[("nptc", i2)])
            return (loc, nl, i2)

        for pi, chunks in enumerate(NA_PLAN):
            q0 = pi * 64
            ob = cnt[1] % 2
            cnt[1] += 1
            ov = ops[:, ob, 0:256].rearrange("p (h q) -> p h q", q=64)
            for hd in range(4):
                loc, nl, i2 = attend(q0, 64, hd, chunks, pi * 4 + hd)
                p_, pc_ = ptl[i2], ptc[i2]
                nmm = nl + 2
                for jl, (ci, ti_) in enumerate(loc):
                    S.add("pe", mk("matmul", ov[:, hd, :], nv[:, ci, hd, :], p_[:, jl, :],
                                                                                     start=(jl == 0), stop=False),
                          reads=[("nptl", i2), "nv"], writes=[("nops", ob)])
                for jc in range(2):
                    S.add("pe", mk("matmul", ov[:, hd, :], nvc[:, jc, hd, :], pc_[:, jc, 0:64],
                                                                                start=False, stop=(jc == 1)),
                          reads=[("nptc", i2), "nvc"], writes=[("nops", ob)])
            r_, y_ = rcp[ob], ysb[ob]
            S.add("dve", mk("reciprocal", out=r_[64:128, 0:4:2, :], in_=ov[64:128, 0:4:2, :]),
                  reads=[("nops", ob)], writes=[("nrcp", ob)])
            S.add("dve", mk("reciprocal", out=r_[0:64, 1:4:2, :], in_=ov[0:64, 1:4:2, :]),
                  reads=[("nops", ob)], writes=[("nrcp", ob)])
            S.add("dve", mk("tensor_copy", out=r_[0:64, 0:4:2, :], in_=r_[64:128, 0:4:2, :]),
                  reads=[("nrcp", ob)], writes=[("nrcp", ob)])
            S.add("dve", mk("tensor_copy", out=r_[64:128, 1:4:2, :], in_=r_[0:64, 1:4:2, :]),
                  reads=[("nrcp", ob)], writes=[("nrcp", ob)])
            S.add("dve", mk("tensor_tensor", out=y_[0:64, :, :], in0=ov[0:64, 0:4:2, :], in1=r_[0:64, 0:4:2, :], op=ALU.mult),
                  reads=[("nops", ob), ("nrcp", ob)], writes=[("nysb", ob)])
            S.add("dve", mk("tensor_tensor", out=y_[64:128, :, :], in0=ov[64:128, 1:4:2, :], in1=r_[64:128, 1:4:2, :], op=ALU.mult),
                  reads=[("nops", ob), ("nrcp", ob)], writes=[("nysb", ob)])
            S.add("sp", mk("dma_start", out=yT_d[768:1024, q0:q0 + 64].rearrange("(c p) q -> p c q", p=128), in_=y_[:, :, :]),
                  reads=[("nysb", ob)], writes=[("yT", "n", q0)], dma=True)
        for qi in range(NCX // 64 if do_ctx else 0):
            q0 = NL + qi * 64
            ob = cnt[1] % 2
            cnt[1] += 1
            ov = ops[:, ob, 0:256].rearrange("p (h q) -> p h q", q=64)
            for hd in range(4):
                loc, nl, i2 = attend(q0, 64, hd, [], 1000 + qi * 4 + hd)
                pc_ = ptc[i2]
                for jc in range(2):
                    S.add("pe", mk("matmul", ov[:, hd, :], nvc[:, jc, hd, :], pc_[:, jc, 0:64],
                                                                                start=(jc == 0), stop=(jc == 1)),
                          reads=[("nptc", i2), "nvc"], writes=[("nops", ob)])
            r_, y_ = rcp[ob], ysb[ob]
            S.add("dve", mk("reciprocal", out=r_[64:128, 0:4:2, :], in_=ov[64:128, 0:4:2, :]),
                  reads=[("nops", ob)], writes=[("nrcp", ob)])
            S.add("dve", mk("reciprocal", out=r_[0:64, 1:4:2, :], in_=ov[0:64, 1:4:2, :]),
                  reads=[("nops", ob)], writes=[("nrcp", ob)])
            S.add("dve", mk("tensor_copy", out=r_[0:64, 0:4:2, :], in_=r_[64:128, 0:4:2, :]),
                  reads=[("nrcp", ob)], writes=[("nrcp", ob)])
            S.add("dve", mk("tensor_copy", out=r_[64:128, 1:4:2, :], in_=r_[0:64, 1:4:2, :]),
                  reads=[("nrcp", ob)], writes=[("nrcp", ob)])
            S.add("dve", mk("tensor_tensor", out=y_[0:64, :, :], in0=ov[0:64, 0:4:2, :], in1=r_[0:64, 0:4:2, :], op=ALU.mult),
                  reads=[("nops", ob), ("nrcp", ob)], writes=[("nysb", ob)])
            S.add("dve", mk("tensor_tensor", out=y_[64:128, :, :], in0=ov[64:128, 1:4:2, :], in1=r_[64:128, 1:4:2, :], op=ALU.mult),
                  reads=[("nops", ob), ("nrcp", ob)], writes=[("nysb", ob)])
            S.add("sp", mk("dma_start", out=yT_d[768:1024, q0:q0 + 64].rearrange("(c p) q -> p c q", p=128), in_=y_[:, :, :]),
                  reads=[("nysb", ob)], writes=[("yT", "n", q0)], dma=True)


def outproj_phase(C, hT_d, yT_d, wo_d, mods, do_ctx=True):
    S = C.S
    blocks = [(t0, 512, "lat") for t0 in range(0, NL, 512)] + ([(NL, 256, "ctx")] if do_ctx else [])
    with C.scope():
        wo = C.sb("wo", [128, KD, D], BF16)
        WL = WLoader(C, D)
        for k in range(KD):
            WL.load(wo[:, k, :], wo_d[k * 128:(k + 1) * 128, :], D, ("wo", k))
        yb = [C.sb("oyb%d" % i, [128, KD, 512], BF16) for i in range(2)]
        hb = [C.sb("ohb%d" % i, [128, KD, 512], F32) for i in range(2)]
        B = BankRR(C, 8)
        hview = hT_d.rearrange("(k p) t -> p k t", p=128)
        yview = yT_d.rearrange("(k p) t -> p k t", p=128)
        for bi, (t0, nt, kind) in enumerate(blocks):
            i2 = bi % 2
            m = mods[("mix", kind)]
            y_, h_ = yb[i2], hb[i2]
            S.add("sp", mk("dma_start", out=y_[:, :, 0:nt], in_=yview[:, :, t0:t0 + nt]),
                  reads=["yT_all"], writes=[("oyb", i2)], dma=True)
            S.add("sp", mk("dma_start", out=h_[:, :, 0:nt], in_=hview[:, :, t0:t0 + nt]),
                  reads=hT_keys(t0, nt), writes=[("ohb", i2)], dma=True)
            for n in range(KD):
                b, bt, bk = B.next()
                for k in range(KD):
                    S.add("pe", mk("matmul", bt[:, b, 0:nt], wo[:, k, n * 128:(n + 1) * 128], y_[:, k, 0:nt],
                                                                                     start=(k == 0), stop=(k == KD - 1)),
                          reads=[("oyb", i2), ("wo", k)], writes=[bk])
                S.add("dve", mk("scalar_tensor_tensor",
                    out=h_[:, n, 0:nt], in0=bt[:, b, 0:nt], scalar=m["gh"][:, n:n + 1], in1=h_[:, n, 0:nt], op0=ALU.mult, op1=ALU.add),
                      reads=[bk, ("ohb", i2), m["key"]], writes=[("ohb", i2)])
            S.add("sp", mk("dma_start", out=hview[:, :, t0:t0 + nt], in_=h_[:, :, 0:nt]),
                  reads=[("ohb", i2)], writes=hT_keys(t0, nt), dma=True)


def mark_yT_done(C):
    S = C.S
    keys = [k for k in list(S.last_writer.keys()) if isinstance(k, tuple) and k[0] == "yT"]
    S.add("sp", mk("nop", ), reads=keys, writes=["yT_all"])


PAIRS = [[0, 1], [2, 3], [4, 5], [6, 7]]
XCH = (("lx", 384, 1024, F32), ("kT", 576, 1024, BF16), ("vA", 1024, 768, BF16),
       ("nkT", 256, 1024, BF16), ("nvA", 1024, 512, BF16))
XCHC = dict(lx=(384, NCX), kT=(576, NCX), vA=(NCX, 768), nkT=(256, NCX), nvA=(NCX, 512))


def build_fused(nlayers=DEPTH, dbg=False):
    C = Ctx()
    S = C.S
    nl = nlayers
    hT_in = C.dram("hT_in", [D, NT], F32, "ExternalInput")
    scin = C.dram("scin", [128, KD, 2], F32, "ExternalInput")
    ropeC = C.dram("ropeC", [32, NT], F32, "ExternalInput")
    ropeS = C.dram("ropeS", [32, NT], F32, "ExternalInput")
    halfmask = C.dram("halfmask", [128, 2], F32, "ExternalInput")
    natab = C.dram("natab", [nl, 128, 4, NTAB, 64], F32, "ExternalInput")
    w_mod = C.dram("w_mod", [nl, D, 9 * D], F32, "ExternalInput")
    b_mod = C.dram("b_mod", [nl, 128, 72], F32, "ExternalInput")
    vecs_d = C.dram("vecs", [nl, 128, NVEC], F32, "ExternalInput")
    f1_in = C.dram("f1_in", [nl, D, 2 * DFF], F32, "ExternalInput")
    f1_out = C.dram("f1_out", [nl, DFF, D], F32, "ExternalInput")
    f2_in = C.dram("f2_in", [nl, D, 2 * DFF], F32, "ExternalInput")
    f2_out = C.dram("f2_out", [nl, DFF, D], F32, "ExternalInput")
    winx = C.dram("winx", [nl, D, WINX], F32, "ExternalInput")
    wuq = C.dram("wuq", [nl, 256, 1152], F32, "ExternalInput")
    wukv = C.dram("wukv", [nl, 128, 768], F32, "ExternalInput")
    lruw = C.dram("lruw", [nl, 128, 1536], F32, "ExternalInput")
    wo = C.dram("wo", [nl, D, D], F32, "ExternalInput")
    hT = C.dram("hT", [D, NT], F32, "ExternalOutput")
    yT = C.dram("yT", [D, NT], BF16, "ExternalOutput" if dbg else "Internal")
    O = dict(qT=C.dram("qT", [6, 96, NT], BF16, "Internal"), nqT=C.dram("nqT", [256, NT], BF16, "Internal"),
             lgel=C.dram("lgel", [384, NT], F32, "Internal"))
    for nm, r, c, dt in XCH:
        O["L_" + nm] = C.dram("L_" + nm, [4, r, c], dt, "Internal")
        O["L_" + nm + "c"] = C.dram("L_" + nm + "c", list(XCHC[nm]), dt, "Internal")
        O["G_" + nm] = C.dram("G_" + nm, [4, 2, r, c], dt, "Internal")
    setup_consts(C)
    vt = C.sb("vecs", [128, nl, NVEC], F32)
    S.add("sp", mk("dma_start", out=vt[:], in_=vecs_d.rearrange("l p v -> p l v")), writes=["vecs"], dma=True)
    for t0 in range(0, NT, TB):
        S.add("sp", mk("dma_start", out=hT[:, t0:t0 + TB], in_=hT_in[:, t0:t0 + TB]), writes=[("hT", t0)], dma=True)
    S.mark("mods")
    mods = [mod_phase(C, w_mod[l], b_mod[l], vt[:, l, :], scin, "L%d" % l) for l in range(nl)]

    def exch(qq):
        for nm, r, c, dt in XCH:
            S.add("pool", mk("collective_compute", "AllGather", ALU.bypass, replica_groups=PAIRS,
                             ins=[O["L_" + nm][qq]], outs=[O["G_" + nm][qq].rearrange("r a b -> (r a) b")]),
                  reads=[("L", nm, qq)], writes=[("G", nm, qq)], cc=True)

    for l in range(nl):
        last = (l == DEPTH - 1)
        vl = vt[:, l, :]
        S.mark("ffn1_%d" % l)
        ffn_phase(C, hT, f1_in[l], f1_out[l], ffn_blocks(), mods[l], "ffn1")
        S.mark("m1_%d" % l)
        m1_phase(C, hT, dict(winx=winx[l], wuq=wuq[l], wukv=wukv[l]), vl, mods[l], ropeC, ropeS, O, exch)
        S.mark("lru_%d" % l)
        lru_phase(C, O, vl, lruw[l], halfmask, yT)
        S.mark("mla_%d" % l)
        mla_phase(C, O, yT, do_ctx=not last)
        S.mark("na_%d" % l)
        na_phase(C, O, natab[l], yT, do_ctx=not last)
        mark_yT_done(C)
        S.mark("outp_%d" % l)
        outproj_phase(C, hT, yT, wo[l], mods[l], do_ctx=not last)
        S.mark("ffn2_%d" % l)
        ffn_phase(C, hT, f2_in[l], f2_out[l], ffn_blocks(do_ctx=not last), mods[l], "ffn2")
    S.mark("end")
    C.marks = S.marks
    nc = C.finish()
    nc._marks = S.marks if hasattr(nc, "__dict__") else None
    return nc


def rope_swap_index():
    i = np.arange(32)
    axis, half, f = i // 16, (i // 8) % 2, i % 8
    return axis * 16 + (1 - half) * 8 + f


def rope_tables(s):
    i = np.arange(NL)
    row = (i // 32).astype(np.float32)
    col = (32 * s + i % 32).astype(np.float32)
    inv = (np.float32(10000.0) ** (-np.arange(0, 16, 2, dtype=np.float32) / np.float32(16))).astype(np.float32)
    Cc = np.ones((32, NT), np.float32)
    Ss = np.zeros((32, NT), np.float32)
    for d in range(32):
        axis, half, f = d // 16, (d // 8) % 2, d % 8
        pos = row if axis == 0 else col
        ang = (pos * inv[f]).astype(np.float32)
        Cc[d, :NL] = np.cos(ang)
        Ss[d, :NL] = (-np.sin(ang)) if half == 0 else np.sin(ang)
    return Cc, Ss


def fm(v, k=None):
    v = np.asarray(v, np.float32)
    return np.ascontiguousarray(v.reshape(-1, 128).T)


def build_na_tables(rpb_l, s):
    rpb_l = np.asarray(rpb_l, np.float32)
    c = 32 * s + np.arange(32)
    kc = np.arange(64)
    w0 = np.clip(c - 8, 0, 48)
    col_in = (kc[:, None] >= w0[None, :]) & (kc[:, None] < w0[None, :] + 16)
    col_off = np.clip(kc[:, None] - c[None, :] + 15, 0, 30)
    G = rpb_l[:, :, col_off]
    G = np.where(col_in[None, None], G, np.float32(NEG)).astype(np.float32)
    T = np.full((128, 4, NTAB, 64), NEG, np.float32)
    for (k0mr, off0, off1), ti in NA_CFG.items():
        for hf in range(2):
            for e in range(2):
                for eq in range(2):
                    rel = k0mr + e
                    off = (off0, off1)[eq]
                    if not (off <= rel < off + 8):
                        continue
                    di = rel - eq + 7
                    p0 = hf * 64 + e * 32
                    T[p0:p0 + 32, :, ti, eq * 32:(eq + 1) * 32] = np.transpose(G[:, di, hf * 32:(hf + 1) * 32, :], (1, 0, 2))
    return T


def layer_shared(inp, l):
    sw = rope_swap_index()
    w_in = np.asarray(inp["w_in"][l], np.float32)
    winx = np.concatenate([w_in[:, 0:1152], w_in[:, 1088:1184], w_in[:, 1088:1152], w_in[:, 1152 + sw],
                           w_in[:, 1184:1952]], axis=1)
    assert winx.shape[1] == WINX
    wuq0 = np.asarray(inp["mla_w_uq"][l], np.float32).reshape(256, 6, 96)
    wuq_sw = np.concatenate([wuq0[:, :, 0:64], wuq0[:, :, 64 + sw]], axis=2)
    wuq = np.stack([wuq0, wuq_sw], axis=2).reshape(256, 1152)
    wukv0 = np.asarray(inp["mla_w_ukv"][l], np.float32).reshape(128, 6, 128)
    wukv = np.concatenate([wukv0[:, :, 0:64].reshape(128, 384), wukv0[:, :, 64:128].reshape(128, 384)], axis=1)
    vecs = np.zeros((128, NVEC), np.float32)
    vecs[:, 0:8] = fm(inp["norm_ffn1"][l])
    vecs[:, 8:16] = fm(inp["norm_mix"][l])
    vecs[:, 16:24] = fm(inp["norm_ffn2"][l])
    vecs[:, 24:26] = fm(inp["mla_q_norm"][l])
    vecs[:, 26] = np.asarray(inp["mla_kv_norm"][l], np.float32)
    gq = np.asarray(inp["mla_q_gain"][l], np.float32)
    gk = np.asarray(inp["mla_k_gain"][l], np.float32)
    vecs[0:96, 27] = gq
    vecs[64:96, 28] = gq[64 + sw]
    vecs[0:96, 29] = gk
    vecs[64:96, 30] = gk[64 + sw]
    vecs[:, 31] = np.tile(np.asarray(inp["na_q_gain"][l], np.float32), 2)
    vecs[:, 32] = np.tile(np.asarray(inp["na_k_gain"][l], np.float32), 2)
    vecs[:, 33:36] = fm(inp["lru_conv_b"][l])
    cw = np.asarray(inp["lru_conv_w"][l], np.float32)
    for c in range(3):
        for j in range(4):
            vecs[:, 36 + 4 * c + j] = cw[j, c * 128:(c + 1) * 128]
    for d in range(2):
        vecs[:, 48 + d * 3:51 + d * 3] = fm(inp["lru_b_a"][l][d])
        vecs[:, 54 + d * 3:57 + d * 3] = fm(inp["lru_b_x"][l][d])
        vecs[:, 60 + d * 3:63 + d * 3] = fm(inp["lru_lambda"][l][d])
    lruw = np.zeros((128, 2, 2, 3, 128), np.float32)
    for g, nm in enumerate(("lru_w_a", "lru_w_x")):
        w = np.asarray(inp[nm][l], np.float32)
        for d in range(2):
            for c in range(3):
                for i in range(2):
                    lruw[64 * i:64 * i + 64, g, d, c, 64 * i:64 * i + 64] = w[d, 2 * c + i]
    b_mod = fm(inp["b_mod"][l])
    return dict(winx=np.ascontiguousarray(winx), wuq=np.ascontiguousarray(wuq), wukv=np.ascontiguousarray(wukv),
                vecs=vecs, lruw=np.ascontiguousarray(lruw.reshape(128, 1536)), b_mod=b_mod,
                w_mod=np.asarray(inp["w_mod"][l], np.float32))


_PROGS = {}


def host_inputs(inp, nlayers=DEPTH, ncore=8):
    x = np.asarray(inp["x"], np.float32)
    ctx = np.asarray(inp["ctx"], np.float32)
    c = np.asarray(inp["c"], np.float32)
    c_ctx = np.asarray(inp["c_ctx"], np.float32)
    Ls = [layer_shared(inp, l) for l in range(nlayers)]
    shared = dict(
        w_mod=np.ascontiguousarray(np.asarray(inp["w_mod"], np.float32)[:nlayers]),
        b_mod=np.stack([L["b_mod"] for L in Ls]), vecs=np.stack([L["vecs"] for L in Ls]),
        f1_in=np.ascontiguousarray(np.asarray(inp["ffn1_w_in"], np.float32)[:nlayers]),
        f1_out=np.ascontiguousarray(np.asarray(inp["ffn1_w_out"], np.float32)[:nlayers]),
        f2_in=np.ascontiguousarray(np.asarray(inp["ffn2_w_in"], np.float32)[:nlayers]),
        f2_out=np.ascontiguousarray(np.asarray(inp["ffn2_w_out"], np.float32)[:nlayers]),
        winx=np.stack([L["winx"] for L in Ls]), wuq=np.stack([L["wuq"] for L in Ls]),
        wukv=np.stack([L["wukv"] for L in Ls]), lruw=np.stack([L["lruw"] for L in Ls]),
        wo=np.ascontiguousarray(np.asarray(inp["w_out"], np.float32)[:nlayers]))
    natabs = [np.stack([build_na_tables(inp["na_rpb"][l], s) for l in range(nlayers)]) for s in range(2)]
    ropes = [rope_tables(s) for s in range(2)]
    maps = []
    for core in range(ncore):
        b, s = core // 2, core % 2
        xl = x[b].reshape(128, 64, D)[:, 32 * s:32 * s + 32, :].reshape(NL, D)
        hm = np.zeros((128, 2), np.float32)
        hm[:, s] = 1.0
        m = dict(shared)
        m.update(hT_in=np.ascontiguousarray(np.concatenate([xl, ctx[b]], axis=0).T),
                 scin=np.ascontiguousarray(np.stack([fm(c[b]), fm(c_ctx)], axis=2)),
                 ropeC=ropes[s][0], ropeS=ropes[s][1], halfmask=hm, natab=natabs[s])
        maps.append(m)
    return maps


def kernel(**inp):
    ncore = 8
    if "fused" not in _PROGS:
        _PROGS["fused"] = build_fused()
    maps = host_inputs(inp)
    res = run_bass_kernel_spmd(_PROGS["fused"], maps, core_ids=list(range(ncore))).results
    out = np.zeros((4, 128, 64, D), np.float32)
    for core in range(ncore):
        b, s = core // 2, core % 2
        out[b, :, 32 * s:32 * s + 32, :] = res[core]["hT"][:, :NL].T.reshape(128, 32, D)
    return out.reshape(4, 8192, D)
```

```python
from contextlib import ExitStack
import numpy as np
import concourse.bass as bass
import concourse.mybir as mybir
from concourse.bass_utils import run_bass_kernel_spmd

F32 = mybir.dt.float32
BF16 = mybir.dt.bfloat16
AF = mybir.ActivationFunctionType
ALU = mybir.AluOpType

D = 1024
DFF = 2816
KD = 8
JF = 22
TB = 256
EPS = 1e-6
NL = 4096
NCX = 256
NT = NL + NCX
DEPTH = 4
NEG = -30000.0
NVEC = 66
WINX = 2112
MLA_SCALE = 96 ** -0.5
NA_SCALE = 0.125

ENGS = ("pe", "act", "dve", "pool", "sp")
NDMA_SLOTS = 8


class Op:
    __slots__ = ("eng", "emit", "is_dma", "pos", "deps", "signal", "ticket", "waits", "slot", "slot_val", "idx")

    def __init__(self, eng, emit, is_dma):
        self.eng = eng
        self.emit = emit
        self.is_dma = is_dma
        self.deps = []
        self.signal = False
        self.ticket = 0
        self.waits = []
        self.slot = None
        self.slot_val = 0


class Sched:
    def __init__(self):
        self.ops = []
        self.streams = {e: [] for e in ENGS}
        self.last_writer = {}
        self.readers = {}
        self.dma_count = {e: 0 for e in ENGS}
        self.slot_last = {}
        self.barrier_ops = []
        self.barrier_pending = set()
        self.marks = []

    def mark(self, name):
        self.marks.append((name, {e: len(v) for e, v in self.streams.items()}))

    def barrier(self):
        ops = [s[-1] for s in self.streams.values() if s]
        ops += list(self.slot_last.values())
        self.barrier_ops = ops
        self.barrier_pending = set(ENGS)

    def add(self, eng, emit, reads=(), writes=(), dma=False, cc=False):
        op = Op(eng, emit, dma or cc)
        op.idx = len(self.ops)
        op.pos = len(self.streams[eng])
        deps = set()
        if eng in self.barrier_pending:
            deps.update(self.barrier_ops)
            self.barrier_pending.discard(eng)
        for k in reads:
            w = self.last_writer.get(k)
            if w is not None:
                deps.add(w)
        for k in writes:
            w = self.last_writer.get(k)
            if w is not None:
                deps.add(w)
            rd = self.readers.get(k)
            if rd:
                deps.update(rd.values())
        if cc:
            op.slot = ("cc", 0)
            prev = self.slot_last.get(op.slot)
            op.slot_val = (prev.slot_val + 1) if prev is not None else 1
            self.slot_last[op.slot] = op
        elif dma:
            n = self.dma_count[eng]
            self.dma_count[eng] = n + 1
            op.slot = (eng, n % NDMA_SLOTS)
            prev = self.slot_last.get(op.slot)
            if prev is not None:
                deps.add(prev)
                op.slot_val = prev.slot_val + 16
            else:
                op.slot_val = 16
            self.slot_last[op.slot] = op
        deps.discard(op)
        op.deps = sorted(deps, key=lambda o: o.idx)
        for k in reads:
            rk = ("d", op.idx) if dma else eng
            self.readers.setdefault(k, {})[rk] = op
        for k in writes:
            self.last_writer[k] = op
            self.readers[k] = {}
        self.ops.append(op)
        self.streams[eng].append(op)
        return op

    def finalize(self):
        waited = {e: {} for e in ENGS}
        for op in self.ops:
            w = waited[op.eng]
            for p in op.deps:
                if p.is_dma:
                    key = ("dma", p.slot)
                    if w.get(key, 0) >= p.slot_val:
                        continue
                    w[key] = p.slot_val
                    op.waits.append(p)
                else:
                    if p.eng == op.eng:
                        if p.eng == "pe":
                            continue
                        if not op.is_dma and op.pos - p.pos > 2:
                            continue
                    key = ("eng", p.eng)
                    if w.get(key, -1) >= p.pos:
                        continue
                    w[key] = p.pos
                    p.signal = True
                    op.waits.append(p)
        for e in ENGS:
            t = 0
            for op in self.streams[e]:
                if not op.is_dma and op.signal:
                    t += 1
                    op.ticket = t

    def emit_all(self, nc, stack):
        self.finalize()
        eng_sem = {e: stack.enter_context(nc.semaphore("s_" + e)) for e in ENGS}
        dma_sem = {}
        for e in ENGS:
            for i in range(min(NDMA_SLOTS, self.dma_count[e])):
                dma_sem[(e, i)] = stack.enter_context(nc.semaphore("d_%s%d" % (e, i)))
        if ("cc", 0) in self.slot_last:
            dma_sem[("cc", 0)] = stack.enter_context(nc.semaphore("cc_sem"))
        block = stack.enter_context(nc.Block())

        def run_stream(e, engobj):
            for op in self.streams[e]:
                for p in op.waits:
                    if p.is_dma:
                        engobj.wait_ge(dma_sem[p.slot], p.slot_val)
                    else:
                        engobj.wait_ge(eng_sem[p.eng], p.ticket)
                ins = op.emit(engobj)
                if op.is_dma:
                    if op.slot[0] == "cc":
                        ins.then_inc(dma_sem[op.slot])
                    else:
                        ins.then_inc(dma_sem[op.slot], 16)
                elif op.signal:
                    ins.then_inc(eng_sem[op.eng], 1)
            for slot, last in self.slot_last.items():
                if slot[0] == e or (slot[0] == "cc" and e == "pool"):
                    engobj.wait_ge(dma_sem[slot], last.slot_val)

        if self.streams["sp"]:
            @block.sync
            def _(eng):
                run_stream("sp", eng)
        if self.streams["act"]:
            @block.scalar
            def _(eng):
                run_stream("act", eng)
        if self.streams["dve"]:
            @block.vector
            def _(eng):
                run_stream("dve", eng)
        if self.streams["pool"]:
            @block.gpsimd
            def _(eng):
                run_stream("pool", eng)
        if self.streams["pe"]:
            @block.tensor
            def _(eng):
                run_stream("pe", eng)


def mk(name, *args, **kw):
    return lambda e: getattr(e, name)(*args, **kw)


class Ctx:
    def __init__(self):
        self.nc = bass.Bass("TRN2", target_bir_lowering=False)
        self.S = Sched()
        self.stack = ExitStack()
        self.t = {}
        self.cur = self.stack
        self.uid = 0
        self.bank_rr = 0

    def sb(self, name, shape, dtype):
        self.uid += 1
        t = self.cur.enter_context(self.nc.sbuf_tensor("%s_%d" % (name, self.uid), list(shape), dtype))
        self.t[name] = t
        return t

    def ps(self, name, shape, dtype=F32):
        self.uid += 1
        t = self.cur.enter_context(self.nc.psum_tensor("%s_%d" % (name, self.uid), list(shape), dtype))
        self.t[name] = t
        return t

    def dram(self, name, shape, dtype, kind):
        return self.nc.dram_tensor(name, list(shape), dtype, kind=kind).ap()

    def scope(self):
        return _Scope(self)

    def finish(self):
        self.S.emit_all(self.nc, self.stack)
        self.stack.close()
        return self.nc


class _Scope:
    def __init__(self, C):
        self.C = C

    def __enter__(self):
        self.prev = self.C.cur
        self.st = ExitStack()
        self.C.cur = self.st
        return self

    def __exit__(self, *a):
        self.st.close()
        self.C.cur = self.prev
        self.C.S.barrier()
        return False


def setup_consts(C):
    S = C.S
    ones_b = C.sb("ones_b", [128, 128], BF16)
    ones_bd = C.sb("ones_bd", [128, 128], BF16)
    nhalf = C.sb("nhalf", [128, 512], F32)
    phalf = C.sb("phalf", [128, 512], F32)
    S.add("pool", mk("memset", ones_b[:], 1.0), writes=["ones_b"])
    S.add("pool", mk("memset", ones_bd[:], 0.0), writes=["ones_bd"])
    S.add("pool", mk("memset", ones_bd[0:64, 0:64], 1.0), writes=["ones_bd"])
    S.add("pool", mk("memset", ones_bd[64:128, 64:128], 1.0), writes=["ones_bd"])
    S.add("pool", mk("memset", nhalf[:], -0.5), writes=["nhalf"])
    S.add("pool", mk("memset", phalf[:], 0.5), writes=["phalf"])
    eps_t = C.sb("eps_t", [128, 1], F32)
    S.add("pool", mk("memset", eps_t[:], EPS), writes=["eps_t"])


class WLoader:
    def __init__(self, C, width, n=3):
        self.C = C
        self.st = [C.sb("wstg%d" % i, [128, width], F32) for i in range(n)]
        self.cnt = 0

    def load(self, dst_ap, src_ap, ncols, wkey, nparts=128):
        S = self.C.S
        i = self.cnt % len(self.st)
        self.cnt += 1
        st = self.st[i]
        S.add("act", mk("dma_start", out=st[0:nparts, 0:ncols], in_=src_ap), writes=[("wstg", i)], dma=True)
        S.add("dve", mk("tensor_copy", out=dst_ap, in_=st[0:nparts, 0:ncols]), reads=[("wstg", i)], writes=[wkey])


def rstd_from_psum(C, ps_ap, M, nt, inv_n, out_ap, tmp_ap, rkeys, wkey, tmpkey):
    S = C.S
    eps_t = C.t["eps_t"]
    S.add("act", mk("activation", out=tmp_ap, in_=ps_ap, func=AF.Sqrt, bias=eps_t[0:M, :], scale=inv_n),
          reads=rkeys + ["eps_t"], writes=[tmpkey])
    S.add("dve", mk("reciprocal", out=out_ap, in_=tmp_ap),
          reads=[tmpkey], writes=[wkey])


def mod_phase(C, w_mod_d, b_mod_d, vecs, scin_d, name):
    S = C.S
    GS = C.sb("GS" + name, [128, 3, KD, 2], F32)
    SH = C.sb("SH" + name, [128, 3, KD, 2], F32)
    GT = C.sb("GT" + name, [128, 3, KD, 2], F32)
    key = "mods" + name
    with C.scope():
        sc = C.sb("sc", [128, KD, 2], F32)
        scs = C.sb("scs", [128, KD, 2], F32)
        bm = C.sb("bm", [128, 72], F32)
        modT = C.sb("modT", [128, 72, 2], F32)
        wm = [C.sb("wm%d" % i, [128, KD, 1024], F32) for i in range(2)]
        mp = C.ps("modps", [128, 72, 2])
        S.add("sp", mk("dma_start", out=sc[:], in_=scin_d), writes=["sc"], dma=True)
        S.add("sp", mk("dma_start", out=bm[:], in_=b_mod_d), writes=["bm"], dma=True)
        S.add("act", mk("activation", out=scs[:], in_=sc[:], func=AF.Silu), reads=["sc"], writes=["scs"])
        wv = w_mod_d.rearrange("(k p) n -> p k n", p=128)
        for mc in range(9):
            w = wm[mc % 2]
            for k in range(KD):
                S.add("sp", mk("dma_start", out=w[:, k, :],
                                                                   in_=w_mod_d[k * 128:(k + 1) * 128, mc * 1024:(mc + 1) * 1024]),
                      writes=[("wm", mc % 2, k)], dma=True)
            for j in range(8):
                ch = mc * 8 + j
                for k in range(KD):
                    S.add("pe", mk("matmul", mp[:, ch, :], w[:, k, j * 128:(j + 1) * 128],
                                                                         scs[:, k, :], start=(k == 0), stop=(k == KD - 1)),
                          reads=[("wm", mc % 2, k), "scs"], writes=["modps"])
        for c in range(2):
            S.add("dve", mk("tensor_tensor", out=modT[:, :, c], in0=mp[:, :, c], in1=bm[:], op=ALU.add),
                  reads=["modps", "bm"], writes=["modT"])
        for w3 in range(3):
            g = vecs[:, 8 * w3:8 * w3 + 8]
            for c in range(2):
                S.add("dve", mk("scalar_tensor_tensor",
                    out=GS[:, w3, :, c], in0=modT[:, (3 * w3 + 1) * 8:(3 * w3 + 2) * 8, c], scalar=1.0, in1=g,
                    op0=ALU.add, op1=ALU.mult), reads=["modT", "vecs"], writes=[key])
                S.add("dve", mk("tensor_copy", out=SH[:, w3, :, c], in_=modT[:, (3 * w3) * 8:(3 * w3 + 1) * 8, c]),
                      reads=["modT"], writes=[key])
                gsc = 1.0 if w3 == 1 else 0.5
                S.add("dve", mk("tensor_scalar",
                    out=GT[:, w3, :, c], in0=modT[:, (3 * w3 + 2) * 8:(3 * w3 + 3) * 8, c], scalar1=gsc, scalar2=None,
                    op0=ALU.mult), reads=["modT"], writes=[key])
    mods = {}
    for w3, wn in enumerate(("ffn1", "mix", "ffn2")):
        for c, cn in enumerate(("lat", "ctx")):
            mods[(wn, cn)] = dict(gs=GS[:, w3, :, c], sh=SH[:, w3, :, c], gh=GT[:, w3, :, c], key=key)
    return mods


def ffn_phase(C, hT_d, w_in_d, w_out_d, blocks, mods, wn):
    S = C.S
    NSPL = 4
    W = 2 * DFF // NSPL
    with C.scope():
        win = C.sb("win", [128, KD, 2 * DFF], BF16)
        wout = C.sb("wout", [128, JF, D], BF16)
        hbs = [C.sb("hb%d" % i, [128, KD, TB], F32) for i in range(3)]
        xns = [C.sb("xn%d" % i, [128, KD, TB], BF16) for i in range(2)]
        sqs = [C.sb("sq%d" % i, [128, KD, TB], BF16) for i in range(2)]
        rstds = [C.sb("rstd%d" % i, [128, TB], F32) for i in range(2)]
        tmps = [C.sb("tmp%d" % i, [128, TB], F32) for i in range(2)]
        sgs = [C.sb("sg%d" % i, [128, TB], F32) for i in range(4)]
        gjs = [C.sb("gj%d" % i, [128, TB], BF16) for i in range(4)]
        acc = C.ps("acc", [128, 8, TB])
        gu = C.ps("gu", [128, 8, TB])
        ones_b = C.t["ones_b"]
        WL = WLoader(C, W)
        for s in (0, 2, 1, 3):
            for k in range(KD):
                WL.load(win[:, k, s * W:(s + 1) * W], w_in_d[k * 128:(k + 1) * 128, s * W:(s + 1) * W], W, ("win", k, s))
        for j in range(JF):
            WL.load(wout[:, j, :], w_out_d[j * 128:(j + 1) * 128, :], D, ("wout", j))
        nb = len(blocks)
        hview = hT_d.rearrange("(k p) t -> p k t", p=128)
        gu_slot = [0]

        def next_gu():
            s = gu_slot[0]
            gu_slot[0] = (s + 1) % 4
            return s

        def load(b):
            t0 = blocks[b][0]
            hb = hbs[b % 3]
            S.add("sp", mk("dma_start", out=hb[:], in_=hview[:, :, t0:t0 + TB]),
                  reads=[("hT", t0)], writes=[("hb", b % 3)], dma=True)

        def norm_a(b):
            hb, sq = hbs[b % 3], sqs[b % 2]
            S.add("act", mk("activation", out=sq[:], in_=hb[:], func=AF.Square),
                  reads=[("hb", b % 3)], writes=[("sq", b % 2)])

        def norm_b(b):
            m = mods[(wn, blocks[b][1])]
            hb, sq, xn, rstd, tmp = hbs[b % 3], sqs[b % 2], xns[b % 2], rstds[b % 2], tmps[b % 2]
            s = next_gu()
            ssp = gu[:, 2 * s, :]
            for k in range(KD):
                S.add("pe", mk("matmul", ssp, ones_b[:], sq[:, k, :], start=(k == 0), stop=(k == KD - 1)),
                      reads=[("sq", b % 2), "ones_b"], writes=[("gu", s)])
            rstd_from_psum(C, ssp, 128, TB, 1.0 / D, rstd[:], tmp[:], [("gu", s)], ("rstd", b % 2), ("tmp", b % 2))
            for k in range(KD):
                S.add("dve", mk("scalar_tensor_tensor", out=tmp[:], in0=hb[:, k, :], scalar=m["gs"][:, k:k + 1],
                                                                   in1=rstd[:], op0=ALU.mult, op1=ALU.mult),
                      reads=[("hb", b % 3), ("rstd", b % 2), m["key"]], writes=[("tmp", b % 2)])
                S.add("act", mk("activation", out=xn[:, k, :], in_=tmp[:], func=AF.Identity,
                                                         bias=m["sh"][:, k:k + 1], scale=1.0),
                      reads=[("tmp", b % 2), m["key"]], writes=[("xn", b % 2, k)])

        def main(b):
            m = mods[(wn, blocks[b][1])]
            t0 = blocks[b][0]
            hb, xn = hbs[b % 3], xns[b % 2]
            xkeys = [("xn", b % 2, k) for k in range(KD)]
            for jj in range(JF + 2):
                if jj < JF:
                    j = jj
                    s = next_gu()
                    gp = gu[:, 2 * s, :]
                    up = gu[:, 2 * s + 1, :]
                    for k in range(KD):
                        S.add("pe", mk("matmul", gp, win[:, k, j * 128:(j + 1) * 128], xn[:, k, :],
                                                                       start=(k == 0), stop=(k == KD - 1)),
                              reads=[xkeys[k], ("win", k, (j * 128) // W), ("win", k, ((j + 1) * 128 - 1) // W)],
                              writes=[("gu", s)])
                    for k in range(KD):
                        c0 = DFF + j * 128
                        S.add("pe", mk("matmul", up, win[:, k, c0:c0 + 128], xn[:, k, :],
                                                                          start=False, stop=(k == KD - 1),
                                                                          skip_group_check=True),
                              reads=[xkeys[k], ("win", k, c0 // W), ("win", k, (c0 + 127) // W)],
                              writes=[("gu", s)])
                    sg, gj = sgs[j % 4], gjs[j % 4]
                    S.add("act", mk("activation", out=sg[:], in_=gp, func=AF.Silu),
                          reads=[("gu", s)], writes=[("sg", j % 4)])
                    S.add("dve", mk("tensor_tensor", out=gj[:], in0=up, in1=sg[:], op=ALU.mult),
                          reads=[("gu", s), ("sg", j % 4)], writes=[("gj", j % 4)])
                if jj >= 2:
                    j = jj - 2
                    gj = gjs[j % 4]
                    for n in range(KD):
                        f = (j == 0 and n % 2 == 0)
                        S.add("pe", mk("matmul",
                            acc[:, n, :], wout[:, j, n * 128:(n + 1) * 128], gj[:],
                            start=f, stop=(j == JF - 1), skip_group_check=True),
                              reads=[("gj", j % 4), ("wout", j)], writes=[("acc", n // 2)])
                if jj == 6 and b + 1 < nb:
                    norm_b(b + 1)
            for n in range(KD):
                S.add("dve", mk("scalar_tensor_tensor", out=hb[:, n, :], in0=acc[:, n, :],
                                                                   scalar=m["gh"][:, n:n + 1], in1=hb[:, n, :],
                                                                   op0=ALU.mult, op1=ALU.add),
                      reads=[("acc", n // 2), m["key"], ("hb", b % 3)], writes=[("hb", b % 3)])
            S.add("sp", mk("dma_start", out=hview[:, :, t0:t0 + TB], in_=hb[:]),
                  reads=[("hb", b % 3)], writes=[("hT", t0)], dma=True)

        load(0)
        if nb > 1:
            load(1)
        norm_a(0)
        norm_b(0)
        for b in range(nb):
            if b + 2 < nb:
                load(b + 2)
            if b + 1 < nb:
                norm_a(b + 1)
            main(b)


def ffn_blocks(do_ctx=True):
    return [(t0, "lat" if t0 < NL else "ctx") for t0 in range(0, NT if do_ctx else NL, TB)]


def hT_keys(t0, nt):
    return [("hT", t) for t in range(t0, t0 + nt, TB)]


class BankRR:
    def __init__(self, C, n=8):
        self.t = C.ps("bank", [128, n, 512])
        self.n = n
        self.i = 0

    def next(self):
        b = self.i
        self.i = (self.i + 1) % self.n
        return b, self.t, ("bank", b)


def m1_phase(C, hT_d, Wd, vecs, mods, ropeC_d, ropeS_d, O, exch=None):
    S = C.S
    blocks = [(t0, 512, "lat") for t0 in range(0, NL, 512)] + [(NL, 256, "ctx")]
    with C.scope():
        winx = C.sb("winx", [128, KD, WINX], BF16)
        wuq = C.sb("wuq", [128, 2, 1152], BF16)
        wukv = C.sb("wukv", [128, 768], BF16)
        WL = WLoader(C, 1152)
        for k in range(KD):
            for s2 in range(2):
                c0, c1 = s2 * 1056, (s2 + 1) * 1056
                WL.load(winx[:, k, c0:c1], Wd["winx"][k * 128:(k + 1) * 128, c0:c1], 1056, ("winx", k))
        for k in range(2):
            WL.load(wuq[:, k, :], Wd["wuq"][k * 128:(k + 1) * 128, :], 1152, "wuq")
        WL.load(wukv[:], Wd["wukv"], 768, "wukv")
        hb = [C.sb("mhb%d" % i, [128, KD, 512], F32) for i in range(2)]
        xn = C.sb("mxn", [128, KD, 512], BF16)
        sq = C.sb("msq", [128, KD, 512], BF16)
        rstd = C.sb("mrstd", [128, 512], F32)
        tmp = C.sb("mtmp", [128, 512], F32)
        rc = C.sb("rc", [128, 512], F32)
        rs = C.sb("rs", [128, 512], F32)
        st3 = [C.sb("st3_%d" % i, [128, 3, 512], F32) for i in range(2)]
        sq2 = C.sb("sq2", [128, 2, 512], BF16)
        cqn = C.sb("cqn", [128, 2, 512], BF16)
        ckvn = C.sb("ckvn", [128, 512], BF16)
        sqk = C.sb("sqk", [128, 512], BF16)
        tA = C.sb("tA", [128, 512], F32)
        t1 = [C.sb("t1_%d" % i, [128, 512], F32) for i in range(2)]
        t2 = [C.sb("t2_%d" % i, [128, 512], F32) for i in range(2)]
        rq = [C.sb("rq_%d" % i, [128, 512], F32) for i in range(2)]
        tq = [C.sb("tq_%d" % i, [128, 512], F32) for i in range(2)]
        sqh = [C.sb("sqh_%d" % i, [128, 512], BF16) for i in range(2)]
        qh = [C.sb("qh_%d" % i, [128, 512], BF16) for i in range(2)]
        kh = [C.sb("kh_%d" % i, [128, 512], BF16) for i in range(2)]
        nst = [C.sb("nst_%d" % i, [128, 2, 512], BF16) for i in range(2)]
        nva = C.sb("nva", [128, 4, 4, 128], BF16)
        va = C.sb("va", [128, 4, 6, 128], BF16)
        B = BankRR(C, 8)
        ones_b, ones_bd = C.t["ones_b"], C.t["ones_bd"]
        S.add("pool", mk("memset", nva[:], 1.0), writes=["nva"])
        S.add("pool", mk("memset", va[:], 1.0), writes=["va"])
        hview = hT_d.rearrange("(k p) t -> p k t", p=128)
        V = lambda c: vecs[:, c:c + 1]
        exch_done = set()

        def load(bi):
            t0, nt, kind = blocks[bi]
            h = hb[bi % 2]
            S.add("sp", mk("dma_start", out=h[:, :, 0:nt], in_=hview[:, :, t0:t0 + nt]),
                  reads=hT_keys(t0, nt), writes=[("mhb", bi % 2)], dma=True)

        load(0)
        for bi, (t0, nt, kind) in enumerate(blocks):
            if bi + 1 < len(blocks):
                load(bi + 1)
            m = mods[("mix", kind)]
            h = hb[bi % 2]
            hk = ("mhb", bi % 2)
            if kind == "lat":
                q4, off = t0 // 1024, t0 % 1024
                dst2 = lambda nm, q4=q4: O["L_" + nm][q4]
            else:
                q4, off = "c", 0
                dst2 = lambda nm: O["L_" + nm + "c"]
            S.add("sp", mk("dma_start", out=rc[64:96, 0:nt], in_=ropeC_d[:, t0:t0 + nt]),
                  writes=["rc"], dma=True)
            S.add("sp", mk("dma_start", out=rs[64:96, 0:nt], in_=ropeS_d[:, t0:t0 + nt]),
                  writes=["rs"], dma=True)
            S.add("act", mk("activation", out=sq[:, :, 0:nt], in_=h[:, :, 0:nt], func=AF.Square),
                  reads=[hk], writes=["msq"])
            b, bt, bk = B.next()
            for k in range(KD):
                S.add("pe", mk("matmul", bt[:, b, 0:nt], ones_b[:], sq[:, k, 0:nt],
                                                               start=(k == 0), stop=(k == KD - 1)),
                      reads=["msq", "ones_b"], writes=[bk])
            rstd_from_psum(C, bt[:, b, 0:nt], 128, nt, 1.0 / D, rstd[:, 0:nt], tmp[:, 0:nt], [bk], "mrstd", "mtmp")
            for k in range(KD):
                S.add("dve", mk("scalar_tensor_tensor",
                    out=tmp[:, 0:nt], in0=h[:, k, 0:nt], scalar=m["gs"][:, k:k + 1], in1=rstd[:, 0:nt],
                    op0=ALU.mult, op1=ALU.mult), reads=[hk, "mrstd", m["key"]], writes=["mtmp"])
                S.add("act", mk("activation", out=xn[:, k, 0:nt], in_=tmp[:, 0:nt], func=AF.Identity,
                                                                   bias=m["sh"][:, k:k + 1], scale=1.0),
                      reads=["mtmp", m["key"]], writes=[("mxn", k)])

            def proj(col0, M):
                b, bt, bk = B.next()
                for k in range(KD):
                    S.add("pe", mk("matmul", bt[0:M, b, 0:nt], winx[:, k, col0:col0 + M], xn[:, k, 0:nt],
                                                             start=(k == 0), stop=(k == KD - 1)),
                          reads=[("mxn", k), ("winx", k)], writes=[bk])
                return bt[0:M, b, 0:nt], bk

            s3 = st3[0]
            for c in range(3):
                p, pk = proj(c * 128, 128)
                S.add("act", mk("activation", out=s3[:, c, 0:nt], in_=p, func=AF.Copy),
                      reads=[pk], writes=[("st3", 0)])
            S.add("sp", mk("dma_start", out=dst2("lx").rearrange("(c p) t -> p c t", p=128)[:, :, off:off + nt],
                                                     in_=s3[:, :, 0:nt]),
                  reads=[("st3", 0)], writes=[("L", "lx", q4)], dma=True)
            s3 = st3[1]
            for c in range(3):
                p, pk = proj(384 + c * 128, 128)
                S.add("act", mk("activation", out=s3[:, c, 0:nt], in_=p, func=AF.Gelu_apprx_tanh),
                      reads=[pk], writes=[("st3", 1)])
            S.add("sp", mk("dma_start", out=O["lgel"].rearrange("(c p) t -> p c t", p=128)[:, :, t0:t0 + nt],
                                                     in_=s3[:, :, 0:nt]),
                  reads=[("st3", 1)], writes=[("lgel", t0)], dma=True)
            pcq = []
            for c in range(2):
                p, pk = proj(768 + c * 128, 128)
                pcq.append((p, pk))
                S.add("act", mk("activation", out=sq2[:, c, 0:nt], in_=p, func=AF.Square),
                      reads=[pk], writes=["sq2"])
            b, bt, bk = B.next()
            for c in range(2):
                S.add("pe", mk("matmul", bt[:, b, 0:nt], ones_b[:], sq2[:, c, 0:nt], start=(c == 0), stop=(c == 1)),
                      reads=["sq2", "ones_b"], writes=[bk])
            rstd_from_psum(C, bt[:, b, 0:nt], 128, nt, 1.0 / 256, rstd[:, 0:nt], tmp[:, 0:nt], [bk], "mrstd", "mtmp")
            for c in range(2):
                p, pk = pcq[c]
                S.add("dve", mk("scalar_tensor_tensor", out=cqn[:, c, 0:nt], in0=p, scalar=V(24 + c),
                                                                        in1=rstd[:, 0:nt], op0=ALU.mult, op1=ALU.mult),
                      reads=[pk, "mrstd", "vecs"], writes=["cqn"])
            p, pk = proj(1024, 128)
            S.add("act", mk("activation", out=sq2[:, 0, 0:nt], in_=p, func=AF.Square), reads=[pk], writes=["sq2"])
            b, bt, bk = B.next()
            S.add("pe", mk("matmul", bt[:, b, 0:nt], ones_b[:], sq2[:, 0, 0:nt], start=True, stop=True),
                  reads=["sq2", "ones_b"], writes=[bk])
            rstd_from_psum(C, bt[:, b, 0:nt], 128, nt, 1.0 / 128, rstd[:, 0:nt], tmp[:, 0:nt], [bk], "mrstd", "mtmp")
            S.add("dve", mk("scalar_tensor_tensor", out=ckvn[:, 0:nt], in0=p, scalar=V(26), in1=rstd[:, 0:nt],
                                                               op0=ALU.mult, op1=ALU.mult),
                  reads=[pk, "mrstd", "vecs"], writes=["ckvn"])
            pkr, pkrk = proj(1152, 96)
            pks, pksk = proj(1248, 96)
            S.add("act", mk("activation", out=sqk[64:96, 0:nt], in_=pkr[64:96, :], func=AF.Square),
                  reads=[pkrk], writes=["sqk_r"])
            S.add("dve", mk("scalar_tensor_tensor", out=tA[64:96, 0:nt], in0=pkr[64:96, :], scalar=vecs[64:96, 29:30],
                                                          in1=rc[64:96, 0:nt], op0=ALU.mult, op1=ALU.mult),
                  reads=[pkrk, "rc", "vecs"], writes=["tA"])
            S.add("dve", mk("scalar_tensor_tensor", out=tmp[64:96, 0:nt], in0=pks[64:96, :], scalar=vecs[64:96, 30:31],
                                                          in1=rs[64:96, 0:nt], op0=ALU.mult, op1=ALU.mult),
                  reads=[pksk, "rs", "vecs"], writes=["mtmp"])
            S.add("dve", mk("tensor_tensor", out=tA[64:96, 0:nt], in0=tA[64:96, 0:nt], in1=tmp[64:96, 0:nt], op=ALU.add),
                  reads=["tA", "mtmp"], writes=["tA"])
            for which, col0, gcol, oname in (("q", 1344, 31, "nqT"), ("k", 1600, 32, "nkT")):
                ns = nst[0 if which == "q" else 1]
                nk_ = ("nst", which)
                for c in range(2):
                    p, pk = proj(col0 + c * 128, 128)
                    S.add("act", mk("activation", out=sq2[:, 0, 0:nt], in_=p, func=AF.Square), reads=[pk], writes=["sq2"])
                    b, bt, bk = B.next()
                    S.add("pe", mk("matmul", bt[:, b, 0:nt], ones_bd[:], sq2[:, 0, 0:nt], start=True, stop=True),
                          reads=["sq2", "ones_bd"], writes=[bk])
                    rstd_from_psum(C, bt[:, b, 0:nt], 128, nt, 1.0 / 64, rstd[:, 0:nt], tmp[:, 0:nt], [bk], "mrstd", "mtmp")
                    S.add("dve", mk("scalar_tensor_tensor",
                        out=ns[:, c, 0:nt], in0=p, scalar=V(gcol), in1=rstd[:, 0:nt], op0=ALU.mult, op1=ALU.mult),
                          reads=[pk, "mrstd", "vecs"], writes=[nk_])
                odst = (O["nqT"].rearrange("(c p) t -> p c t", p=128)[:, :, t0:t0 + nt] if which == "q"
                        else dst2("nkT").rearrange("(c p) t -> p c t", p=128)[:, :, off:off + nt])
                S.add("sp", mk("dma_start", out=odst, in_=ns[:, :, 0:nt]),
                      reads=[nk_], writes=[("L", oname, q4) if which == "k" else (oname, t0)], dma=True)
            nsub = nt // 128
            for sb_ in range(nsub):
                b, bt, bk = B.next()
                for k in range(KD):
                    S.add("pe", mk("matmul", bt[:, b, 0:256], xn[:, k, sb_ * 128:(sb_ + 1) * 128],
                                                                           winx[:, k, 1856:2112], start=(k == 0), stop=(k == KD - 1)),
                          reads=[("mxn", k), ("winx", k)], writes=[bk])
                pv = bt[:, b, 0:256].rearrange("p (h d) -> p h d", h=4)
                S.add("act", mk("activation", out=nva[:, sb_, 0:4:2, 0:64], in_=pv[:, 0:4:2, :], func=AF.Copy),
                      reads=[bk], writes=["nva"])
                S.add("dve", mk("tensor_copy", out=nva[:, sb_, 1:4:2, 64:128], in_=pv[:, 1:4:2, :]),
                      reads=[bk], writes=["nva"])
            S.add("sp", mk("dma_start",
                out=dst2("nvA")[off:off + nt, :].rearrange("(s p) (h d) -> p s h d", p=128, h=4), in_=nva[:, 0:nsub]),
                  reads=["nva"], writes=[("L", "nvA", q4)], dma=True)
            for sb_ in range(nsub):
                b, bt, bk = B.next()
                S.add("pe", mk("matmul", bt[:, b, 0:384], ckvn[:, sb_ * 128:(sb_ + 1) * 128],
                                                                  wukv[:, 384:768], start=True, stop=True),
                      reads=["ckvn", "wukv"], writes=[bk])
                pv = bt[:, b, 0:384].rearrange("p (h d) -> p h d", h=6)
                S.add("act", mk("activation", out=va[:, sb_, 0:6:2, 0:64], in_=pv[:, 0:6:2, :], func=AF.Copy),
                      reads=[bk], writes=["va"])
                S.add("dve", mk("tensor_copy", out=va[:, sb_, 1:6:2, 64:128], in_=pv[:, 1:6:2, :]),
                      reads=[bk], writes=["va"])
            S.add("sp", mk("dma_start",
                out=dst2("vA")[off:off + nt, :].rearrange("(s p) (h d) -> p s h d", p=128, h=6), in_=va[:, 0:nsub]),
                  reads=["va"], writes=[("L", "vA", q4)], dma=True)
            for hd in range(6):
                i2 = hd % 2
                b, bt, bk = B.next()
                pkn = bt[0:64, b, 0:nt]
                S.add("pe", mk("matmul", pkn, wukv[:, hd * 64:(hd + 1) * 64], ckvn[:, 0:nt], start=True, stop=True),
                      reads=["ckvn", "wukv"], writes=[bk])
                S.add("act", mk("activation", out=sqk[0:64, 0:nt], in_=pkn, func=AF.Square),
                      reads=[bk], writes=["sqk_n"])
                b2, bt2, bk2 = B.next()
                pss = bt2[0:96, b2, 0:nt]
                S.add("pe", mk("matmul", pss, ones_b[0:96, 0:96], sqk[0:96, 0:nt], start=True, stop=True),
                      reads=["sqk_n", "sqk_r", "ones_b"], writes=[bk2])
                r_, t_ = rq[i2], tq[i2]
                rstd_from_psum(C, pss, 96, nt, 1.0 / 96, r_[0:96, 0:nt], t_[0:96, 0:nt], [bk2], ("rq", i2), ("tq", i2))
                khh = kh[i2]
                S.add("dve", mk("scalar_tensor_tensor",
                    out=khh[0:64, 0:nt], in0=pkn, scalar=vecs[0:64, 29:30], in1=r_[0:64, 0:nt], op0=ALU.mult, op1=ALU.mult),
                      reads=[bk, ("rq", i2), "vecs"], writes=[("kh", i2)])
                S.add("dve", mk("tensor_tensor", out=khh[64:96, 0:nt], in0=tA[64:96, 0:nt],
                                                                        in1=r_[64:96, 0:nt], op=ALU.mult),
                      reads=["tA", ("rq", i2)], writes=[("kh", i2)])
                S.add("sp", mk("dma_start", out=dst2("kT")[hd * 96:(hd + 1) * 96, off:off + nt], in_=khh[0:96, 0:nt]),
                      reads=[("kh", i2)], writes=[("L", "kT", q4)], dma=True)
            for hd in range(6):
                i2 = hd % 2
                b, bt, bk = B.next()
                pq = bt[0:96, b, 0:nt]
                for k in range(2):
                    S.add("pe", mk("matmul", pq, wuq[:, k, hd * 192:hd * 192 + 96], cqn[:, k, 0:nt],
                                                                      start=(k == 0), stop=(k == 1)),
                          reads=["cqn", "wuq"], writes=[bk])
                b3, bt3, bk3 = B.next()
                pw = bt3[0:96, b3, 0:nt]
                for k in range(2):
                    S.add("pe", mk("matmul", pw, wuq[:, k, hd * 192 + 96:hd * 192 + 192], cqn[:, k, 0:nt],
                                                                      start=(k == 0), stop=(k == 1)),
                          reads=["cqn", "wuq"], writes=[bk3])
                sh_ = sqh[i2]
                S.add("act", mk("activation", out=sh_[0:96, 0:nt], in_=pq, func=AF.Square),
                      reads=[bk], writes=[("sqh", i2)])
                b2, bt2, bk2 = B.next()
                pss = bt2[0:96, b2, 0:nt]
                S.add("pe", mk("matmul", pss, ones_b[0:96, 0:96], sh_[0:96, 0:nt], start=True, stop=True),
                      reads=[("sqh", i2), "ones_b"], writes=[bk2])
                r_, t_ = rq[i2], tq[i2]
                rstd_from_psum(C, pss, 96, nt, 1.0 / 96, r_[0:96, 0:nt], t_[0:96, 0:nt], [bk2], ("rq", i2), ("tq", i2))
                qhh, a1, a2 = qh[i2], t1[i2], t2[i2]
                S.add("dve", mk("scalar_tensor_tensor",
                    out=qhh[0:64, 0:nt], in0=pq[0:64, :], scalar=vecs[0:64, 27:28], in1=r_[0:64, 0:nt], op0=ALU.mult, op1=ALU.mult),
                      reads=[bk, ("rq", i2), "vecs"], writes=[("qh", i2)])
                S.add("dve", mk("scalar_tensor_tensor",
                    out=a1[64:96, 0:nt], in0=pq[64:96, :], scalar=vecs[64:96, 27:28], in1=rc[64:96, 0:nt], op0=ALU.mult, op1=ALU.mult),
                      reads=[bk, "rc", "vecs"], writes=[("t1", i2)])
                S.add("dve", mk("scalar_tensor_tensor",
                    out=a2[64:96, 0:nt], in0=pw[64:96, :], scalar=vecs[64:96, 28:29], in1=rs[64:96, 0:nt], op0=ALU.mult, op1=ALU.mult),
                      reads=[bk3, "rs", "vecs"], writes=[("t2", i2)])
                S.add("dve", mk("tensor_tensor", out=a1[64:96, 0:nt], in0=a1[64:96, 0:nt], in1=a2[64:96, 0:nt], op=ALU.add),
                      reads=[("t1", i2), ("t2", i2)], writes=[("t1", i2)])
                S.add("dve", mk("tensor_tensor", out=qhh[64:96, 0:nt], in0=a1[64:96, 0:nt],
                                                                              in1=r_[64:96, 0:nt], op=ALU.mult),
                      reads=[("t1", i2), ("rq", i2)], writes=[("qh", i2)])
                S.add("sp", mk("dma_start", out=O["qT"][hd, :, t0:t0 + nt], in_=qhh[0:96, 0:nt]),
                      reads=[("qh", i2)], writes=[("qT", hd, t0)], dma=True)
            if exch is not None:
                for qq in range(4):
                    if bi == min(2 * qq + 2, len(blocks) - 1) or (bi == len(blocks) - 1 and 2 * qq + 2 > bi):
                        if qq not in exch_done:
                            exch_done.add(qq)
                            exch(qq)


def lru_phase(C, I, vecs, lruw_d, halfmask_d, yT_d):
    S = C.S
    XW = 2 + NCX + 3 + 2 * NL + 2
    CT0 = 2
    LT0 = 2 + NCX + 3
    SEG = 512
    with C.scope():
        lw = C.sb("lw", [128, 1536], BF16)
        with C.scope():
            WL = WLoader(C, 1536, n=1)
            WL.load(lw[:], lruw_d, 1536, "lw")
        hm = C.sb("hm", [128, 2], F32)
        S.add("sp", mk("dma_start", out=hm[:], in_=halfmask_d), writes=["hm"], dma=True)
        par = C.sb("lpar", [128, 18], F32)
        e1 = C.sb("le1", [128, 6], F32)
        one_t = C.sb("one_t", [128, 1], F32)
        S.add("pool", mk("memset", one_t[:], 1.0), writes=["one_t"])
        S.add("act", mk("activation", out=e1[:], in_=vecs[:, 60:66], func=AF.Exp, scale=-1.0), reads=["vecs"], writes=["le1"])
        S.add("act", mk("activation", out=e1[:], in_=e1[:], func=AF.Ln, bias=one_t[:], scale=1.0), reads=["le1", "one_t"], writes=["le1"])
        S.add("dve", mk("tensor_scalar", out=par[:, 0:6], in0=e1[:], scalar1=-4.0, scalar2=None, op0=ALU.mult),
              reads=["le1"], writes=["lpar"])
        S.add("dve", mk("tensor_scalar", out=par[:, 6:18], in0=vecs[:, 48:60], scalar1=0.5, scalar2=None, op0=ALU.mult),
              reads=["vecs"], writes=["lpar"])
        xc = C.sb("xc", [128, XW], F32)
        xcb = C.sb("xcb", [128, XW], BF16)
        hsum = C.sb("hsum", [128, NT], F32)
        lg = C.sb("lg", [128, NT], F32)
        NB = 2
        tr = [C.sb("tr%d" % i, [128, SEG], F32) for i in range(NB)]
        ti = [C.sb("ti%d" % i, [128, SEG], F32) for i in range(NB)]
        aa = [C.sb("aa%d" % i, [128, SEG], F32) for i in range(NB)]
        a2 = [C.sb("a2%d" % i, [128, SEG], F32) for i in range(NB)]
        uu = [C.sb("uu%d" % i, [128, SEG], F32) for i in range(NB)]
        hh = [C.sb("hh%d" % i, [128, SEG], F32) for i in range(NB)]
        yb = C.sb("yb", [128, NT], BF16)
        stt = C.sb("lstate", [128, 1], F32)
        B = BankRR(C, 4)
        phalf = C.t["phalf"]
        for c in range(3):
            with C.scope():
                stg = C.sb("stg", [128, 2, NL], F32)
                xf = C.sb("xf", [128, XW], F32)
                S.add("dve", mk("memset", xf[:], 0.0), writes=["xf"])
                for hf in range(2):
                    for q4 in range(4):
                        S.add("sp", mk("dma_start", out=stg[:, hf, q4 * 1024:(q4 + 1) * 1024],
                                       in_=I["G_lx"][q4, hf, c * 128:(c + 1) * 128, :]),
                              reads=[("G", "lx", q4)], writes=[("stg", hf)], dma=True)
                S.add("sp", mk("dma_start", out=xf[:, CT0:CT0 + NCX], in_=I["L_lxc"][c * 128:(c + 1) * 128, :]),
                      reads=[("L", "lx", "c"), "xf"], writes=["xf"], dma=True)
                lat = xf[:, LT0:LT0 + 2 * NL].rearrange("p (r h c) -> p r h c", h=2, c=32)
                for hf in range(2):
                    eng = "act" if hf == 0 else "dve"
                    src = stg[:, hf, :].rearrange("p (r c) -> p r c", c=32)
                    if eng == "act":
                        S.add("act", mk("activation", out=lat[:, :, hf, :], in_=src, func=AF.Copy),
                              reads=[("stg", hf), "xf"], writes=["xf"])
                    else:
                        S.add("dve", mk("tensor_copy", out=lat[:, :, hf, :], in_=src),
                              reads=[("stg", hf), "xf"], writes=["xf"])
                n = XW - 3
                S.add("dve", mk("tensor_scalar", out=xc[:, 2:2 + n], in0=xf[:, 0:n], scalar1=vecs[:, 36 + 4 * c:37 + 4 * c],
                                                              scalar2=vecs[:, 33 + c:34 + c], op0=ALU.mult, op1=ALU.add),
                      reads=["xf", "vecs"], writes=["xc"])
                for j in range(1, 4):
                    S.add("dve", mk("scalar_tensor_tensor", out=xc[:, 2:2 + n], in0=xf[:, j:j + n],
                                                                              scalar=vecs[:, 36 + 4 * c + j:37 + 4 * c + j],
                                                                              in1=xc[:, 2:2 + n], op0=ALU.mult, op1=ALU.add),
                          reads=["xf", "vecs", "xc"], writes=["xc"])
                S.add("act", mk("activation", out=xcb[:, 2:2 + n], in_=xc[:, 2:2 + n], func=AF.Copy), reads=["xc"], writes=["xcb"])
            S.add("sp", mk("dma_start", out=lg[:], in_=I["lgel"][c * 128:(c + 1) * 128, :]),
                  reads=[("lgel", t) for t in list(range(0, NL, 512)) + [NL]], writes=["lg"], dma=True)
            segs = [(CT0, NCX, "ctx", 0)] + [(LT0 + i * SEG, SEG, "lat", i) for i in range(2 * NL // SEG)]
            for d in range(2):
                order = segs if d == 0 else [segs[0]] + segs[:0:-1]
                pidx = d * 3 + c
                hc = par[:, pidx:pidx + 1]
                hba = par[:, 6 + pidx:7 + pidx]
                hbx = par[:, 12 + pidx:13 + pidx]
                wa = lw[:, (0 * 6 + pidx) * 128:(0 * 6 + pidx + 1) * 128]
                wx = lw[:, (1 * 6 + pidx) * 128:(1 * 6 + pidx + 1) * 128]
                first = True
                for si, (x0, n, kind, li) in enumerate(order):
                    ib = si % NB
                    r_, i_, a_, q_, u_, h_ = tr[ib], ti[ib], aa[ib], a2[ib], uu[ib], hh[ib]
                    for p0 in range(0, n, 512):
                        pn = min(512, n - p0)
                        b, bt, bk = B.next()
                        S.add("pe", mk("matmul",
                            bt[:, b, 0:pn], wa, xcb[:, x0 + p0:x0 + p0 + pn], start=True, stop=True),
                              reads=["xcb", "lw"], writes=[bk])
                        S.add("act", mk("activation",
                            out=r_[:, p0:p0 + pn], in_=bt[:, b, 0:pn], func=AF.Tanh, bias=hba, scale=0.5),
                              reads=[bk, "lpar"], writes=[("tr", ib)])
                        b, bt, bk = B.next()
                        S.add("pe", mk("matmul",
                            bt[:, b, 0:pn], wx, xcb[:, x0 + p0:x0 + p0 + pn], start=True, stop=True),
                              reads=["xcb", "lw"], writes=[bk])
                        S.add("act", mk("activation",
                            out=i_[:, p0:p0 + pn], in_=bt[:, b, 0:pn], func=AF.Tanh, bias=hbx, scale=0.5),
                              reads=[bk, "lpar"], writes=[("ti", ib)])
                    S.add("act", mk("activation", out=a_[:, 0:n], in_=r_[:, 0:n], func=AF.Exp,
                                                                                 bias=hc, scale=hc),
                          reads=[("tr", ib), "lpar"], writes=[("aa", ib)])
                    S.add("dve", mk("tensor_tensor", out=q_[:, 0:n], in0=a_[:, 0:n], in1=a_[:, 0:n], op=ALU.mult),
                          reads=[("aa", ib)], writes=[("a2", ib)])
                    S.add("dve", mk("tensor_scalar", out=q_[:, 0:n], in0=q_[:, 0:n], scalar1=-0.25, scalar2=0.25,
                                                                      op0=ALU.mult, op1=ALU.add),
                          reads=[("a2", ib)], writes=[("a2", ib)])
                    S.add("act", mk("activation", out=q_[:, 0:n], in_=q_[:, 0:n], func=AF.Sqrt),
                          reads=[("a2", ib)], writes=[("a2", ib)])
                    S.add("dve", mk("scalar_tensor_tensor",
                        out=u_[:, 0:n], in0=i_[:, 0:n], scalar=1.0, in1=xc[:, x0:x0 + n], op0=ALU.add, op1=ALU.mult),
                          reads=[("ti", ib), "xc"], writes=[("uu", ib)])
                    S.add("dve", mk("tensor_tensor", out=u_[:, 0:n], in0=u_[:, 0:n], in1=q_[:, 0:n], op=ALU.mult),
                          reads=[("uu", ib), ("a2", ib)], writes=[("uu", ib)])
                    init = 0.0 if first else stt[:, 0:1]
                    if d == 0:
                        S.add("dve", mk("tensor_tensor_scan",
                            out=h_[:, 0:n], data0=a_[:, 0:n], data1=u_[:, 0:n], initial=init, op0=ALU.mult, op1=ALU.add),
                              reads=[("aa", ib), ("uu", ib), "lstate"], writes=[("hh", ib)])
                        S.add("dve", mk("tensor_copy", out=stt[:, 0:1], in_=h_[:, n - 1:n]),
                              reads=[("hh", ib)], writes=["lstate"])
                    else:
                        S.add("dve", mk("tensor_tensor_scan",
                            out=h_[:, 0:n][:, ::-1], data0=a_[:, 0:n][:, ::-1], data1=u_[:, 0:n][:, ::-1],
                            initial=init, op0=ALU.mult, op1=ALU.add),
                              reads=[("aa", ib), ("uu", ib), "lstate"], writes=[("hh", ib)])
                        S.add("dve", mk("tensor_copy", out=stt[:, 0:1], in_=h_[:, 0:1]),
                              reads=[("hh", ib)], writes=["lstate"])
                    first = False
                    if kind == "ctx":
                        if d == 0:
                            S.add("act", mk("activation", out=hsum[:, NL:NT], in_=h_[:, 0:NCX], func=AF.Copy),
                                  reads=[("hh", ib)], writes=[("hsum", "c")])
                        else:
                            S.add("dve", mk("tensor_tensor", out=hsum[:, NL:NT], in0=hsum[:, NL:NT], in1=h_[:, 0:NCX], op=ALU.add),
                                  reads=[("hh", ib), ("hsum", "c")], writes=[("hsum", "c")])
                    else:
                        rows = SEG // 64
                        hv = h_[:, 0:SEG].rearrange("p (r h c) -> p r h c", h=2, c=32)
                        ov = hsum[:, li * (SEG // 2):(li + 1) * (SEG // 2)].rearrange("p (r c) -> p r c", c=32)
                        hk_ = ("hsum", li)
                        if d == 0:
                            S.add("dve", mk("tensor_scalar", out=ov, in0=hv[:, :, 0, :], scalar1=hm[:, 0:1], scalar2=None,
                                                                                op0=ALU.mult),
                                  reads=[("hh", ib), "hm"], writes=[hk_])
                        else:
                            S.add("dve", mk("scalar_tensor_tensor", out=ov, in0=hv[:, :, 0, :], scalar=hm[:, 0:1], in1=ov,
                                                                                       op0=ALU.mult, op1=ALU.add),
                                  reads=[("hh", ib), "hm", hk_], writes=[hk_])
                        S.add("dve", mk("scalar_tensor_tensor", out=ov, in0=hv[:, :, 1, :], scalar=hm[:, 1:2], in1=ov,
                                                                                   op0=ALU.mult, op1=ALU.add),
                              reads=[("hh", ib), "hm", hk_], writes=[hk_])
            hkeys = [("hsum", "c")] + [("hsum", i) for i in range(2 * NL // SEG)]
            S.add("dve", mk("tensor_tensor", out=yb[:], in0=hsum[:], in1=lg[:], op=ALU.mult),
                  reads=hkeys + ["lg"], writes=["yb"])
            S.add("sp", mk("dma_start", out=yT_d[c * 128:(c + 1) * 128, :], in_=yb[:]),
                  reads=["yb"], writes=[("yT", c)], dma=True)


def mla_phase(C, I, yT_d, do_ctx=True):
    S = C.S
    NK = NCX + 2 * NL
    NJ = NK // 128
    with C.scope():
        kts = [C.sb("kt%d" % i, [128, NK], BF16) for i in range(2)]
        vas = [C.sb("vas%d" % i, [128, NJ, 128], BF16) for i in range(2)]
        qts = [C.sb("qt%d" % i, [128, NT], BF16) for i in range(2)]
        pts = [C.sb("pt%d" % i, [128, 2, 512], BF16) for i in range(3)]
        osb = [C.sb("osb%d" % i, [128, 512], F32) for i in range(2)]
        rcp = [C.sb("rcp%d" % i, [128, 512], F32) for i in range(2)]
        ysb = [C.sb("ysb%d" % i, [128, 512], BF16) for i in range(2)]
        sps = C.ps("sps", [128, 6, 512])
        ops = C.ps("ops", [128, 2, 512])
        kq_all = [(nm, t) for nm in ("kT0", "kT1") for t in range(0, NL, 512)]

        def loadh(hd):
            i2 = hd % 2
            kt, va, qt = kts[i2], vas[i2], qts[i2]
            S.add("sp", mk("dma_start", out=kt[0:96, 0:NCX], in_=I["L_kTc"][hd * 96:(hd + 1) * 96, :]),
                  reads=[("L", "kT", "c")], writes=[("kt", i2, "c")], dma=True)
            S.add("sp", mk("dma_start", out=va[:, 0:2, :],
                           in_=I["L_vAc"][:, hd * 128:(hd + 1) * 128].rearrange("(j p) d -> p j d", p=128)),
                  reads=[("L", "vA", "c")], writes=[("vas", i2, "c")], dma=True)
            for hf in range(2):
                for q4 in range(4):
                    k0 = NCX + hf * NL + q4 * 1024
                    S.add("sp", mk("dma_start", out=kt[0:96, k0:k0 + 1024], in_=I["G_kT"][q4, hf, hd * 96:(hd + 1) * 96, :]),
                          reads=[("G", "kT", q4)], writes=[("kt", i2, hf, q4)], dma=True)
                    j0 = 2 + hf * 32 + q4 * 8
                    S.add("sp", mk("dma_start", out=va[:, j0:j0 + 8, :],
                                   in_=I["G_vA"][q4, hf, :, hd * 128:(hd + 1) * 128].rearrange("(j p) d -> p j d", p=128)),
                          reads=[("G", "vA", q4)], writes=[("vas", i2, hf, q4)], dma=True)
            S.add("sp", mk("dma_start", out=qt[0:96, :], in_=I["qT"][hd, :, :]),
                  reads=[("qT", hd, t) for t in list(range(0, NL, 512)) + [NL]], writes=[("qts", i2)], dma=True)

        def jpart(j):
            return ("c",) if j < 2 else ((j - 2) // 32, ((j - 2) % 32) // 8)

        cnt = [0, 0]
        loadh(0)
        for hd in range(6):
            if hd + 1 < 6:
                loadh(hd + 1)
            i2 = hd % 2
            kt, va, qt = kts[i2], vas[i2], qts[i2]
            qblocks = [(q0, 512, 0, NJ) for q0 in range(0, NL, 512)] + ([(NL, 256, 0, 2)] if do_ctx else [])
            for (q0, nq, j0, j1) in qblocks:
                ob = cnt[1] % 2
                cnt[1] += 1
                oacc = ops[:, ob, 0:nq]
                pend = []
                prs = list(range(j0, j1, 2))
                for idx in range(len(prs) + 1):
                    if idx < len(prs):
                        j = prs[idx]
                        sb_ = cnt[0] % 3
                        cnt[0] += 1
                        sp2 = sps[:, 2 * sb_:2 * sb_ + 2, 0:nq]
                        for u in range(2):
                            S.add("pe", mk("matmul", sp2[:, u, :], kt[0:96, (j + u) * 128:(j + u + 1) * 128], qt[0:96, q0:q0 + nq],
                                           start=True, stop=True),
                                  reads=[("kt", i2) + jpart(j), ("qts", i2)], writes=[("sps", sb_)])
                        pt = pts[sb_]
                        S.add("act", mk("activation", out=pt[:, :, 0:nq], in_=sp2, func=AF.Exp, scale=MLA_SCALE),
                              reads=[("sps", sb_)], writes=[("pt", sb_)])
                        pend.append((j, sb_))
                    if idx >= 1:
                        j, sb_ = pend[idx - 1]
                        pt = pts[sb_]
                        for u in range(2):
                            S.add("pe", mk("matmul", oacc, va[:, j + u, :], pt[:, u, 0:nq],
                                           start=(idx == 1 and u == 0), stop=(idx == len(prs) and u == 1)),
                                  reads=[("pt", sb_), ("vas", i2) + jpart(j)], writes=[("ops", ob)])
                o_, r_, y_ = osb[ob], rcp[ob], ysb[ob]
                lo, hi = (0, 64) if hd % 2 == 0 else (64, 128)
                slo, shi = (64, 128) if hd % 2 == 0 else (0, 64)
                S.add("dve", mk("reciprocal", out=r_[slo:shi, 0:nq], in_=oacc[slo:shi, :]),
                      reads=[("ops", ob)], writes=[("rcp", ob)])
                S.add("act", mk("activation", out=o_[lo:hi, 0:nq], in_=oacc[lo:hi, :], func=AF.Copy),
                      reads=[("ops", ob)], writes=[("osb", ob)])
                S.add("dve", mk("tensor_copy", out=r_[lo:hi, 0:nq], in_=r_[slo:shi, 0:nq]),
                      reads=[("rcp", ob)], writes=[("rcp", ob)])
                S.add("dve", mk("tensor_tensor",
                    out=y_[lo:hi, 0:nq], in0=o_[lo:hi, 0:nq], in1=r_[lo:hi, 0:nq], op=ALU.mult),
                      reads=[("osb", ob), ("rcp", ob)], writes=[("ysb", ob)])
                row0 = 384 + hd * 64
                S.add("sp", mk("dma_start",
                    out=yT_d[row0:row0 + 64, q0:q0 + nq], in_=y_[lo:hi, 0:nq]),
                      reads=[("ysb", ob)], writes=[("yT", "m", hd, q0)], dma=True)


def na_pair_plan():
    cfg = {}
    plan = []
    r0f = lambda r: min(max(r - 4, 0), 120)
    for r in range(0, 128, 2):
        lo, hi = r0f(r), r0f(r + 1) + 8
        off0, off1 = r0f(r) - r, r0f(r + 1) - r
        chunks = []
        for ci in range(lo // 2, (hi - 1) // 2 + 1):
            key = (2 * ci - r, off0, off1)
            if key not in cfg:
                cfg[key] = len(cfg)
            chunks.append((ci, cfg[key]))
        plan.append(chunks)
    return plan, cfg


NA_PLAN, NA_CFG = na_pair_plan()
NTAB = len(NA_CFG)


def na_phase(C, I, natab_d, yT_d, do_ctx=True):
    S = C.S
    with C.scope():
        nk = C.sb("nk", [128, 2, 64, 2, 64], BF16)
        nkc = C.sb("nkc", [128, 2, NCX], BF16)
        nv = C.sb("nv", [128, 64, 4, 128], BF16)
        nvc = C.sb("nvc", [128, 2, 4, 128], BF16)
        nq = C.sb("nq", [128, 2, NT], BF16)
        tab = C.sb("natab", [128, 4, NTAB, 64], F32)
        ssb = [C.sb("nssb%d" % i, [128, 5, 64], F32) for i in range(8)]
        ptl = [C.sb("nptl%d" % i, [128, 5, 64], BF16) for i in range(8)]
        ptc = [C.sb("nptc%d" % i, [128, 2, 64], BF16) for i in range(8)]
        rcp = [C.sb("nrcp%d" % i, [128, 4, 64], F32) for i in range(2)]
        ysb = [C.sb("nysb%d" % i, [128, 2, 64], BF16) for i in range(2)]
        sps = C.ps("nsps", [128, 6, 512])
        ops = C.ps("nops", [128, 2, 512])
        S.add("sp", mk("dma_start", out=tab[:], in_=natab_d), writes=["natab"], dma=True)
        with C.scope():
            nks = C.sb("nks", [128, 2, 2, NL], BF16)
            for ck in range(2):
                for hf in range(2):
                    for q4 in range(4):
                        S.add("sp", mk("dma_start", out=nks[:, ck, hf, q4 * 1024:(q4 + 1) * 1024],
                                       in_=I["G_nkT"][q4, hf, ck * 128:(ck + 1) * 128, :]),
                              reads=[("G", "nkT", q4)], writes=[("nks", ck, hf)], dma=True)
                    src = nks[:, ck, hf, :].rearrange("p (ci t) -> p ci t", t=64)
                    if hf == 0:
                        S.add("act", mk("activation", out=nk[:, ck, :, hf, :], in_=src, func=AF.Copy),
                              reads=[("nks", ck, hf)], writes=["nk"])
                    else:
                        S.add("dve", mk("tensor_copy", out=nk[:, ck, :, hf, :], in_=src),
                              reads=[("nks", ck, hf)], writes=["nk"])
        for ck in range(2):
            S.add("sp", mk("dma_start", out=nkc[:, ck, :], in_=I["L_nkTc"][ck * 128:(ck + 1) * 128, :]),
                  reads=[("L", "nkT", "c")], writes=["nkc"], dma=True)
            S.add("sp", mk("dma_start", out=nq[:, ck, :], in_=I["nqT"][ck * 128:(ck + 1) * 128, :]),
                  reads=[("nqT", t) for t in list(range(0, NL, 512)) + [NL]], writes=["nq"], dma=True)
        for hf in range(2):
            for q4 in range(4):
                S.add("sp", mk("dma_start", out=nv[hf * 64:(hf + 1) * 64, q4 * 16:(q4 + 1) * 16],
                               in_=I["G_nvA"][q4, hf].rearrange("(ci q) (h d) -> q ci h d", q=64, h=4)),
                      reads=[("G", "nvA", q4)], writes=["nv"], dma=True)
        S.add("sp", mk("dma_start", out=nvc[:], in_=I["L_nvAc"].rearrange("(j p) (h d) -> p j h d", p=128, h=4)),
              reads=[("L", "nvA", "c")], writes=["nvc"], dma=True)
        cnt = [0, 0]
        NSB = 6

        def attend(q0, hd, loc, bset):
            ck, pl = hd // 2, (hd % 2) * 64
            sb_ = cnt[0] % NSB
            cnt[0] += 1
            nl = len(loc)
            sp_ = sps[:, sb_, 0:(nl + 2) * 64].rearrange("p (j q) -> p j q", q=64)
            qap = nq[pl:pl + 64, ck, q0:q0 + 64]
            for jl, (ci, ti_) in enumerate(loc):
                S.add("pe", mk("matmul", sp_[:, jl, :], nk[pl:pl + 64, ck, ci].rearrange("p h t -> p (h t)"), qap,
                               start=True, stop=True),
                      reads=["nk", "nq"], writes=[("nsps", sb_)])
            for jc in range(2):
                S.add("pe", mk("matmul", sp_[:, nl + jc, :], nkc[pl:pl + 64, ck, jc * 128:(jc + 1) * 128], qap,
                               start=True, stop=True),
                      reads=["nkc", "nq"], writes=[("nsps", sb_)])
            bi_ = bset * 4 + hd
            if nl:
                s_, p_ = ssb[bi_], ptl[bi_]
                ti0 = loc[0][1]
                assert [t for _, t in loc] == list(range(ti0, ti0 + nl))
                S.add("dve", mk("scalar_tensor_tensor", out=s_[:, 0:nl, :], in0=sp_[:, 0:nl, :], scalar=NA_SCALE,
                                in1=tab[:, hd, ti0:ti0 + nl, :], op0=ALU.mult, op1=ALU.add),
                      reads=[("nsps", sb_), "natab"], writes=[("nssb", bi_)])
                S.add("act", mk("activation", out=p_[:, 0:nl, :], in_=s_[:, 0:nl, :], func=AF.Exp),
                      reads=[("nssb", bi_)], writes=[("nptl", bi_)])
            pc_ = ptc[bi_]
            S.add("act", mk("activation", out=pc_[:, :, :], in_=sp_[:, nl:nl + 2, :], func=AF.Exp, scale=NA_SCALE),
                  reads=[("nsps", sb_)], writes=[("nptc", bi_)])

        def pv_and_store(q0, loc, bset, ob):
            ov = ops[:, ob, 0:256].rearrange("p (h q) -> p h q", q=64)
            nl = len(loc)
            for hd in range(4):
                bi_ = bset * 4 + hd
                p_, pc_ = ptl[bi_], ptc[bi_]
                for jl, (ci, ti_) in enumerate(loc):
                    S.add("pe", mk("matmul", ov[:, hd, :], nv[:, ci, hd, :], p_[:, jl, :], start=(jl == 0), stop=False),
                          reads=[("nptl", bi_), "nv"], writes=[("nops", ob)])
                for jc in range(2):
                    S.add("pe", mk("matmul", ov[:, hd, :], nvc[:, jc, hd, :], pc_[:, jc, :],
                                   start=(nl == 0 and jc == 0), stop=(jc == 1)),
                          reads=[("nptc", bi_), "nvc"], writes=[("nops", ob)])
            r_, y_ = rcp[ob], ysb[ob]
            S.add("dve", mk("reciprocal", out=r_[64:128, 0:4:2, :], in_=ov[64:128, 0:4:2, :]),
                  reads=[("nops", ob)], writes=[("nrcp", ob)])
            S.add("dve", mk("reciprocal", out=r_[0:64, 1:4:2, :], in_=ov[0:64, 1:4:2, :]),
                  reads=[("nops", ob)], writes=[("nrcp", ob)])
            S.add("dve", mk("tensor_copy", out=r_[0:64, 0:4:2, :], in_=r_[64:128, 0:4:2, :]),
                  reads=[("nrcp", ob)], writes=[("nrcp", ob)])
            S.add("dve", mk("tensor_copy", out=r_[64:128, 1:4:2, :], in_=r_[0:64, 1:4:2, :]),
                  reads=[("nrcp", ob)], writes=[("nrcp", ob)])
            S.add("dve", mk("tensor_tensor", out=y_[0:64, :, :], in0=ov[0:64, 0:4:2, :], in1=r_[0:64, 0:4:2, :], op=ALU.mult),
                  reads=[("nops", ob), ("nrcp", ob)], writes=[("nysb", ob)])
            S.add("dve", mk("tensor_tensor", out=y_[64:128, :, :], in0=ov[64:128, 1:4:2, :], in1=r_[64:128, 1:4:2, :], op=ALU.mult),
                  reads=[("nops", ob), ("nrcp", ob)], writes=[("nysb", ob)])
            S.add("sp", mk("dma_start", out=yT_d[768:1024, q0:q0 + 64].rearrange("(c p) q -> p c q", p=128), in_=y_[:, :, :]),
                  reads=[("nysb", ob)], writes=[("yT", "n", q0)], dma=True)

        items = [(pi * 64, chunks) for pi, chunks in enumerate(NA_PLAN)]
        if do_ctx:
            items += [(NL + qi * 64, []) for qi in range(NCX // 64)]
        for hd in range(4):
            attend(items[0][0], hd, items[0][1], 0)
        for ii, (q0, loc) in enumerate(items):
            if ii + 1 < len(items):
                for hd in range(4):
                    attend(items[ii + 1][0], hd, items[ii + 1][1], (ii + 1) % 2)
            pv_and_store(q0, loc, ii % 2, ii % 2)


def outproj_phase(C, hT_d, yT_d, wo_d, mods, do_ctx=True):
    S = C.S
    blocks = [(t0, 512, "lat") for t0 in range(0, NL, 512)] + ([(NL, 256, "ctx")] if do_ctx else [])
    with C.scope():
        wo = C.sb("wo", [128, KD, D], BF16)
        WL = WLoader(C, D)
        for k in range(KD):
            WL.load(wo[:, k, :], wo_d[k * 128:(k + 1) * 128, :], D, ("wo", k))
        yb = [C.sb("oyb%d" % i, [128, KD, 512], BF16) for i in range(2)]
        hb = [C.sb("ohb%d" % i, [128, KD, 512], F32) for i in range(2)]
        B = BankRR(C, 8)
        hview = hT_d.rearrange("(k p) t -> p k t", p=128)
        yview = yT_d.rearrange("(k p) t -> p k t", p=128)
        for bi, (t0, nt, kind) in enumerate(blocks):
            i2 = bi % 2
            m = mods[("mix", kind)]
            y_, h_ = yb[i2], hb[i2]
            S.add("sp", mk("dma_start", out=y_[:, :, 0:nt], in_=yview[:, :, t0:t0 + nt]),
                  reads=["yT_all"], writes=[("oyb", i2)], dma=True)
            S.add("sp", mk("dma_start", out=h_[:, :, 0:nt], in_=hview[:, :, t0:t0 + nt]),
                  reads=hT_keys(t0, nt), writes=[("ohb", i2)], dma=True)
            for n in range(KD):
                b, bt, bk = B.next()
                for k in range(KD):
                    S.add("pe", mk("matmul", bt[:, b, 0:nt], wo[:, k, n * 128:(n + 1) * 128], y_[:, k, 0:nt],
                                                                                     start=(k == 0), stop=(k == KD - 1)),
                          reads=[("oyb", i2), ("wo", k)], writes=[bk])
                S.add("dve", mk("scalar_tensor_tensor",
                    out=h_[:, n, 0:nt], in0=bt[:, b, 0:nt], scalar=m["gh"][:, n:n + 1], in1=h_[:, n, 0:nt], op0=ALU.mult, op1=ALU.add),
                      reads=[bk, ("ohb", i2), m["key"]], writes=[("ohb", i2)])
            S.add("sp", mk("dma_start", out=hview[:, :, t0:t0 + nt], in_=h_[:, :, 0:nt]),
                  reads=[("ohb", i2)], writes=hT_keys(t0, nt), dma=True)


def mark_yT_done(C):
    S = C.S
    keys = [k for k in list(S.last_writer.keys()) if isinstance(k, tuple) and k[0] == "yT"]
    S.add("sp", mk("nop", ), reads=keys, writes=["yT_all"])


PAIRS = [[0, 1], [2, 3], [4, 5], [6, 7]]
XCH = (("lx", 384, 1024, F32), ("kT", 576, 1024, BF16), ("vA", 1024, 768, BF16),
       ("nkT", 256, 1024, BF16), ("nvA", 1024, 512, BF16))
XCHC = dict(lx=(384, NCX), kT=(576, NCX), vA=(NCX, 768), nkT=(256, NCX), nvA=(NCX, 512))


def build_fused(nlayers=DEPTH, dbg=False):
    C = Ctx()
    S = C.S
    nl = nlayers
    hT_in = C.dram("hT_in", [D, NT], F32, "ExternalInput")
    scin = C.dram("scin", [128, KD, 2], F32, "ExternalInput")
    ropeC = C.dram("ropeC", [32, NT], F32, "ExternalInput")
    ropeS = C.dram("ropeS", [32, NT], F32, "ExternalInput")
    halfmask = C.dram("halfmask", [128, 2], F32, "ExternalInput")
    natab = C.dram("natab", [nl, 128, 4, NTAB, 64], F32, "ExternalInput")
    w_mod = C.dram("w_mod", [nl, D, 9 * D], F32, "ExternalInput")
    b_mod = C.dram("b_mod", [nl, 128, 72], F32, "ExternalInput")
    vecs_d = C.dram("vecs", [nl, 128, NVEC], F32, "ExternalInput")
    f1_in = C.dram("f1_in", [nl, D, 2 * DFF], F32, "ExternalInput")
    f1_out = C.dram("f1_out", [nl, DFF, D], F32, "ExternalInput")
    f2_in = C.dram("f2_in", [nl, D, 2 * DFF], F32, "ExternalInput")
    f2_out = C.dram("f2_out", [nl, DFF, D], F32, "ExternalInput")
    winx = C.dram("winx", [nl, D, WINX], F32, "ExternalInput")
    wuq = C.dram("wuq", [nl, 256, 1152], F32, "ExternalInput")
    wukv = C.dram("wukv", [nl, 128, 768], F32, "ExternalInput")
    lruw = C.dram("lruw", [nl, 128, 1536], F32, "ExternalInput")
    wo = C.dram("wo", [nl, D, D], F32, "ExternalInput")
    hT = C.dram("hT", [D, NT], F32, "ExternalOutput")
    yT = C.dram("yT", [D, NT], BF16, "ExternalOutput" if dbg else "Internal")
    O = dict(qT=C.dram("qT", [6, 96, NT], BF16, "Internal"), nqT=C.dram("nqT", [256, NT], BF16, "Internal"),
             lgel=C.dram("lgel", [384, NT], F32, "Internal"))
    for nm, r, c, dt in XCH:
        O["L_" + nm] = C.dram("L_" + nm, [4, r, c], dt, "Internal")
        O["L_" + nm + "c"] = C.dram("L_" + nm + "c", list(XCHC[nm]), dt, "Internal")
        O["G_" + nm] = C.dram("G_" + nm, [4, 2, r, c], dt, "Internal")
    setup_consts(C)
    vt = C.sb("vecs", [128, nl, NVEC], F32)
    S.add("sp", mk("dma_start", out=vt[:], in_=vecs_d.rearrange("l p v -> p l v")), writes=["vecs"], dma=True)
    for t0 in range(0, NT, TB):
        S.add("sp", mk("dma_start", out=hT[:, t0:t0 + TB], in_=hT_in[:, t0:t0 + TB]), writes=[("hT", t0)], dma=True)
    S.mark("mods")
    mods = [mod_phase(C, w_mod[l], b_mod[l], vt[:, l, :], scin, "L%d" % l) for l in range(nl)]

    def exch(qq):
        for nm, r, c, dt in XCH:
            S.add("pool", mk("collective_compute", "AllGather", ALU.bypass, replica_groups=PAIRS,
                             ins=[O["L_" + nm][qq]], outs=[O["G_" + nm][qq].rearrange("r a b -> (r a) b")]),
                  reads=[("L", nm, qq)], writes=[("G", nm, qq)], cc=True)

    for l in range(nl):
        last = (l == DEPTH - 1)
        vl = vt[:, l, :]
        S.mark("ffn1_%d" % l)
        ffn_phase(C, hT, f1_in[l], f1_out[l], ffn_blocks(), mods[l], "ffn1")
        S.mark("m1_%d" % l)
        m1_phase(C, hT, dict(winx=winx[l], wuq=wuq[l], wukv=wukv[l]), vl, mods[l], ropeC, ropeS, O, exch)
        S.mark("lru_%d" % l)
        lru_phase(C, O, vl, lruw[l], halfmask, yT)
        S.mark("mla_%d" % l)
        mla_phase(C, O, yT, do_ctx=not last)
        S.mark("na_%d" % l)
        na_phase(C, O, natab[l], yT, do_ctx=not last)
        mark_yT_done(C)
        S.mark("outp_%d" % l)
        outproj_phase(C, hT, yT, wo[l], mods[l], do_ctx=not last)
        S.mark("ffn2_%d" % l)
        ffn_phase(C, hT, f2_in[l], f2_out[l], ffn_blocks(do_ctx=not last), mods[l], "ffn2")
    S.mark("end")
    C.marks = S.marks
    nc = C.finish()
    nc._marks = S.marks if hasattr(nc, "__dict__") else None
    return nc


def rope_swap_index():
    i = np.arange(32)
    axis, half, f = i // 16, (i // 8) % 2, i % 8
    return axis * 16 + (1 - half) * 8 + f


def rope_tables(s):
    i = np.arange(NL)
    row = (i // 32).astype(np.float32)
    col = (32 * s + i % 32).astype(np.float32)
    inv = (np.float32(10000.0) ** (-np.arange(0, 16, 2, dtype=np.float32) / np.float32(16))).astype(np.float32)
    Cc = np.ones((32, NT), np.float32)
    Ss = np.zeros((32, NT), np.float32)
    for d in range(32):
        axis, half, f = d // 16, (d // 8) % 2, d % 8
        pos = row if axis == 0 else col
        ang = (pos * inv[f]).astype(np.float32)
        Cc[d, :NL] = np.cos(ang)
        Ss[d, :NL] = (-np.sin(ang)) if half == 0 else np.sin(ang)
    return Cc, Ss


def fm(v, k=None):
    v = np.asarray(v, np.float32)
    return np.ascontiguousarray(v.reshape(-1, 128).T)


def build_na_tables(rpb_l, s):
    rpb_l = np.asarray(rpb_l, np.float32)
    c = 32 * s + np.arange(32)
    kc = np.arange(64)
    w0 = np.clip(c - 8, 0, 48)
    col_in = (kc[:, None] >= w0[None, :]) & (kc[:, None] < w0[None, :] + 16)
    col_off = np.clip(kc[:, None] - c[None, :] + 15, 0, 30)
    G = rpb_l[:, :, col_off]
    G = np.where(col_in[None, None], G, np.float32(NEG)).astype(np.float32)
    T = np.full((128, 4, NTAB, 64), NEG, np.float32)
    for (k0mr, off0, off1), ti in NA_CFG.items():
        for hf in range(2):
            for e in range(2):
                for eq in range(2):
                    rel = k0mr + e
                    off = (off0, off1)[eq]
                    if not (off <= rel < off + 8):
                        continue
                    di = rel - eq + 7
                    p0 = hf * 64 + e * 32
                    T[p0:p0 + 32, :, ti, eq * 32:(eq + 1) * 32] = np.transpose(G[:, di, hf * 32:(hf + 1) * 32, :], (1, 0, 2))
    return T


def layer_shared(inp, l):
    sw = rope_swap_index()
    w_in = np.asarray(inp["w_in"][l], np.float32)
    winx = np.concatenate([w_in[:, 0:1152], w_in[:, 1088:1184], w_in[:, 1088:1152], w_in[:, 1152 + sw],
                           w_in[:, 1184:1952]], axis=1)
    assert winx.shape[1] == WINX
    wuq0 = np.asarray(inp["mla_w_uq"][l], np.float32).reshape(256, 6, 96)
    wuq_sw = np.concatenate([wuq0[:, :, 0:64], wuq0[:, :, 64 + sw]], axis=2)
    wuq = np.stack([wuq0, wuq_sw], axis=2).reshape(256, 1152)
    wukv0 = np.asarray(inp["mla_w_ukv"][l], np.float32).reshape(128, 6, 128)
    wukv = np.concatenate([wukv0[:, :, 0:64].reshape(128, 384), wukv0[:, :, 64:128].reshape(128, 384)], axis=1)
    vecs = np.zeros((128, NVEC), np.float32)
    vecs[:, 0:8] = fm(inp["norm_ffn1"][l])
    vecs[:, 8:16] = fm(inp["norm_mix"][l])
    vecs[:, 16:24] = fm(inp["norm_ffn2"][l])
    vecs[:, 24:26] = fm(inp["mla_q_norm"][l])
    vecs[:, 26] = np.asarray(inp["mla_kv_norm"][l], np.float32)
    gq = np.asarray(inp["mla_q_gain"][l], np.float32)
    gk = np.asarray(inp["mla_k_gain"][l], np.float32)
    vecs[0:96, 27] = gq
    vecs[64:96, 28] = gq[64 + sw]
    vecs[0:96, 29] = gk
    vecs[64:96, 30] = gk[64 + sw]
    vecs[:, 31] = np.tile(np.asarray(inp["na_q_gain"][l], np.float32), 2)
    vecs[:, 32] = np.tile(np.asarray(inp["na_k_gain"][l], np.float32), 2)
    vecs[:, 33:36] = fm(inp["lru_conv_b"][l])
    cw = np.asarray(inp["lru_conv_w"][l], np.float32)
    for c in range(3):
        for j in range(4):
            vecs[:, 36 + 4 * c + j] = cw[j, c * 128:(c + 1) * 128]
    for d in range(2):
        vecs[:, 48 + d * 3:51 + d * 3] = fm(inp["lru_b_a"][l][d])
        vecs[:, 54 + d * 3:57 + d * 3] = fm(inp["lru_b_x"][l][d])
        vecs[:, 60 + d * 3:63 + d * 3] = fm(inp["lru_lambda"][l][d])
    lruw = np.zeros((128, 2, 2, 3, 128), np.float32)
    for g, nm in enumerate(("lru_w_a", "lru_w_x")):
        w = np.asarray(inp[nm][l], np.float32)
        for d in range(2):
            for c in range(3):
                for i in range(2):
                    lruw[64 * i:64 * i + 64, g, d, c, 64 * i:64 * i + 64] = w[d, 2 * c + i]
    b_mod = fm(inp["b_mod"][l])
    return dict(winx=np.ascontiguousarray(winx), wuq=np.ascontiguousarray(wuq), wukv=np.ascontiguousarray(wukv),
                vecs=vecs, lruw=np.ascontiguousarray(lruw.reshape(128, 1536)), b_mod=b_mod,
                w_mod=np.asarray(inp["w_mod"][l], np.float32))


_PROGS = {}


def host_inputs(inp, nlayers=DEPTH, ncore=8):
    x = np.asarray(inp["x"], np.float32)
    ctx = np.asarray(inp["ctx"], np.float32)
    c = np.asarray(inp["c"], np.float32)
    c_ctx = np.asarray(inp["c_ctx"], np.float32)
    Ls = [layer_shared(inp, l) for l in range(nlayers)]
    shared = dict(
        w_mod=np.ascontiguousarray(np.asarray(inp["w_mod"], np.float32)[:nlayers]),
        b_mod=np.stack([L["b_mod"] for L in Ls]), vecs=np.stack([L["vecs"] for L in Ls]),
        f1_in=np.ascontiguousarray(np.asarray(inp["ffn1_w_in"], np.float32)[:nlayers]),
        f1_out=np.ascontiguousarray(np.asarray(inp["ffn1_w_out"], np.float32)[:nlayers]),
        f2_in=np.ascontiguousarray(np.asarray(inp["ffn2_w_in"], np.float32)[:nlayers]),
        f2_out=np.ascontiguousarray(np.asarray(inp["ffn2_w_out"], np.float32)[:nlayers]),
        winx=np.stack([L["winx"] for L in Ls]), wuq=np.stack([L["wuq"] for L in Ls]),
        wukv=np.stack([L["wukv"] for L in Ls]), lruw=np.stack([L["lruw"] for L in Ls]),
        wo=np.ascontiguousarray(np.asarray(inp["w_out"], np.float32)[:nlayers]))
    natabs = [np.stack([build_na_tables(inp["na_rpb"][l], s) for l in range(nlayers)]) for s in range(2)]
    ropes = [rope_tables(s) for s in range(2)]
    maps = []
    for core in range(ncore):
        b, s = core // 2, core % 2
        xl = x[b].reshape(128, 64, D)[:, 32 * s:32 * s + 32, :].reshape(NL, D)
        hm = np.zeros((128, 2), np.float32)
        hm[:, s] = 1.0
        m = dict(shared)
        m.update(hT_in=np.ascontiguousarray(np.concatenate([xl, ctx[b]], axis=0).T),
                 scin=np.ascontiguousarray(np.stack([fm(c[b]), fm(c_ctx)], axis=2)),
                 ropeC=ropes[s][0], ropeS=ropes[s][1], halfmask=hm, natab=natabs[s])
        maps.append(m)
    return maps


def kernel(**inp):
    ncore = 8
    if "fused" not in _PROGS:
        _PROGS["fused"] = build_fused()
    maps = host_inputs(inp)
    res = run_bass_kernel_spmd(_PROGS["fused"], maps, core_ids=list(range(ncore))).results
    out = np.zeros((4, 128, 64, D), np.float32)
    for core in range(ncore):
        b, s = core // 2, core % 2
        out[b, :, 32 * s:32 * s + 32, :] = res[core]["hT"][:, :NL].T.reshape(128, 32, D)
    return out.reshape(4, 8192, D)
```

```python
from contextlib import ExitStack
import numpy as np
import concourse.bass as bass
import concourse.mybir as mybir
from concourse.bass_utils import run_bass_kernel_spmd

F32 = mybir.dt.float32
BF16 = mybir.dt.bfloat16
AF = mybir.ActivationFunctionType
ALU = mybir.AluOpType

D = 1024
DFF = 2816
KD = 8
JF = 22
TB = 256
EPS = 1e-6
NL = 4096
NCX = 256
NT = NL + NCX
DEPTH = 4
NEG = -30000.0
NVEC = 66
WINX = 2112
MLA_SCALE = 96 ** -0.5
NA_SCALE = 0.125
USE_LNEXP = False

ENGS = ("pe", "act", "dve", "pool", "sp")
NDMA_SLOTS = 8


class Op:
    __slots__ = ("eng", "emit", "is_dma", "pos", "deps", "signal", "ticket", "waits", "slot", "slot_val", "idx")

    def __init__(self, eng, emit, is_dma):
        self.eng = eng
        self.emit = emit
        self.is_dma = is_dma
        self.deps = []
        self.signal = False
        self.ticket = 0
        self.waits = []
        self.slot = None
        self.slot_val = 0


class Sched:
    def __init__(self):
        self.ops = []
        self.streams = {e: [] for e in ENGS}
        self.last_writer = {}
        self.readers = {}
        self.dma_count = {e: 0 for e in ENGS}
        self.slot_last = {}
        self.barrier_ops = []
        self.barrier_pending = set()
        self.marks = []

    def mark(self, name):
        self.marks.append((name, {e: len(v) for e, v in self.streams.items()}))

    def barrier(self):
        ops = [s[-1] for s in self.streams.values() if s]
        ops += list(self.slot_last.values())
        self.barrier_ops = ops
        self.barrier_pending = set(ENGS)

    def add(self, eng, emit, reads=(), writes=(), dma=False, cc=False):
        op = Op(eng, emit, dma or cc)
        op.idx = len(self.ops)
        op.pos = len(self.streams[eng])
        deps = set()
        if eng in self.barrier_pending:
            deps.update(self.barrier_ops)
            self.barrier_pending.discard(eng)
        for k in reads:
            w = self.last_writer.get(k)
            if w is not None:
                deps.add(w)
        for k in writes:
            w = self.last_writer.get(k)
            if w is not None:
                deps.add(w)
            rd = self.readers.get(k)
            if rd:
                deps.update(rd.values())
        if cc:
            op.slot = ("cc", 0)
            prev = self.slot_last.get(op.slot)
            op.slot_val = (prev.slot_val + 1) if prev is not None else 1
            self.slot_last[op.slot] = op
        elif dma:
            n = self.dma_count[eng]
            self.dma_count[eng] = n + 1
            op.slot = (eng, n % NDMA_SLOTS)
            prev = self.slot_last.get(op.slot)
            if prev is not None:
                deps.add(prev)
                op.slot_val = prev.slot_val + 16
            else:
                op.slot_val = 16
            self.slot_last[op.slot] = op
        deps.discard(op)
        op.deps = sorted(deps, key=lambda o: o.idx)
        for k in reads:
            rk = ("d", op.idx) if dma else eng
            self.readers.setdefault(k, {})[rk] = op
        for k in writes:
            self.last_writer[k] = op
            self.readers[k] = {}
        self.ops.append(op)
        self.streams[eng].append(op)
        return op

    def finalize(self):
        waited = {e: {} for e in ENGS}
        for op in self.ops:
            w = waited[op.eng]
            for p in op.deps:
                if p.is_dma:
                    key = ("dma", p.slot)
                    if w.get(key, 0) >= p.slot_val:
                        continue
                    w[key] = p.slot_val
                    op.waits.append(p)
                else:
                    if p.eng == op.eng:
                        if p.eng == "pe":
                            continue
                        if not op.is_dma and op.pos - p.pos > 2:
                            continue
                    key = ("eng", p.eng)
                    if w.get(key, -1) >= p.pos:
                        continue
                    w[key] = p.pos
                    p.signal = True
                    op.waits.append(p)
        for e in ENGS:
            t = 0
            for op in self.streams[e]:
                if not op.is_dma and op.signal:
                    t += 1
                    op.ticket = t

    def emit_all(self, nc, stack):
        self.finalize()
        eng_sem = {e: stack.enter_context(nc.semaphore("s_" + e)) for e in ENGS}
        dma_sem = {}
        for e in ENGS:
            for i in range(min(NDMA_SLOTS, self.dma_count[e])):
                dma_sem[(e, i)] = stack.enter_context(nc.semaphore("d_%s%d" % (e, i)))
        if ("cc", 0) in self.slot_last:
            dma_sem[("cc", 0)] = stack.enter_context(nc.semaphore("cc_sem"))
        block = stack.enter_context(nc.Block())

        def run_stream(e, engobj):
            for op in self.streams[e]:
                for p in op.waits:
                    if p.is_dma:
                        engobj.wait_ge(dma_sem[p.slot], p.slot_val)
                    else:
                        engobj.wait_ge(eng_sem[p.eng], p.ticket)
                ins = op.emit(engobj)
                if op.is_dma:
                    if op.slot[0] == "cc":
                        ins.then_inc(dma_sem[op.slot])
                    else:
                        ins.then_inc(dma_sem[op.slot], 16)
                elif op.signal:
                    ins.then_inc(eng_sem[op.eng], 1)
            for slot, last in self.slot_last.items():
                if slot[0] == e or (slot[0] == "cc" and e == "pool"):
                    engobj.wait_ge(dma_sem[slot], last.slot_val)

        if self.streams["sp"]:
            @block.sync
            def _(eng):
                run_stream("sp", eng)
        if self.streams["act"]:
            @block.scalar
            def _(eng):
                run_stream("act", eng)
        if self.streams["dve"]:
            @block.vector
            def _(eng):
                run_stream("dve", eng)
        if self.streams["pool"]:
            @block.gpsimd
            def _(eng):
                run_stream("pool", eng)
        if self.streams["pe"]:
            @block.tensor
            def _(eng):
                run_stream("pe", eng)


def mk(name, *args, **kw):
    return lambda e: getattr(e, name)(*args, **kw)


class Ctx:
    def __init__(self):
        self.nc = bass.Bass("TRN2", target_bir_lowering=False)
        self.S = Sched()
        self.stack = ExitStack()
        self.t = {}
        self.cur = self.stack
        self.uid = 0
        self.bank_rr = 0

    def sb(self, name, shape, dtype):
        self.uid += 1
        t = self.cur.enter_context(self.nc.sbuf_tensor("%s_%d" % (name, self.uid), list(shape), dtype))
        self.t[name] = t
        return t

    def ps(self, name, shape, dtype=F32):
        self.uid += 1
        t = self.cur.enter_context(self.nc.psum_tensor("%s_%d" % (name, self.uid), list(shape), dtype))
        self.t[name] = t
        return t

    def dram(self, name, shape, dtype, kind):
        return self.nc.dram_tensor(name, list(shape), dtype, kind=kind).ap()

    def scope(self):
        return _Scope(self)

    def finish(self):
        self.S.emit_all(self.nc, self.stack)
        self.stack.close()
        return self.nc


class _Scope:
    def __init__(self, C):
        self.C = C

    def __enter__(self):
        self.prev = self.C.cur
        self.st = ExitStack()
        self.C.cur = self.st
        return self

    def __exit__(self, *a):
        self.st.close()
        self.C.cur = self.prev
        self.C.S.barrier()
        return False


def setup_consts(C):
    S = C.S
    ones_b = C.sb("ones_b", [128, 128], BF16)
    ones_bd = C.sb("ones_bd", [128, 128], BF16)
    nhalf = C.sb("nhalf", [128, 512], F32)
    phalf = C.sb("phalf", [128, 512], F32)
    S.add("pool", mk("memset", ones_b[:], 1.0), writes=["ones_b"])
    S.add("pool", mk("memset", ones_bd[:], 0.0), writes=["ones_bd"])
    S.add("pool", mk("memset", ones_bd[0:64, 0:64], 1.0), writes=["ones_bd"])
    S.add("pool", mk("memset", ones_bd[64:128, 64:128], 1.0), writes=["ones_bd"])
    S.add("pool", mk("memset", nhalf[:], -0.5), writes=["nhalf"])
    S.add("pool", mk("memset", phalf[:], 0.5), writes=["phalf"])
    eps_t = C.sb("eps_t", [128, 1], F32)
    S.add("pool", mk("memset", eps_t[:], EPS), writes=["eps_t"])


class WLoader:
    def __init__(self, C, width, n=3):
        self.C = C
        self.st = [C.sb("wstg%d" % i, [128, width], F32) for i in range(n)]
        self.cnt = 0

    def load(self, dst_ap, src_ap, ncols, wkey, nparts=128):
        S = self.C.S
        i = self.cnt % len(self.st)
        self.cnt += 1
        st = self.st[i]
        S.add("act", mk("dma_start", out=st[0:nparts, 0:ncols], in_=src_ap), writes=[("wstg", i)], dma=True)
        S.add("dve", mk("tensor_copy", out=dst_ap, in_=st[0:nparts, 0:ncols]), reads=[("wstg", i)], writes=[wkey])


def rstd_from_psum(C, ps_ap, M, nt, inv_n, out_ap, tmp_ap, rkeys, wkey, tmpkey, lnexp=False):
    S = C.S
    eps_t = C.t["eps_t"]
    if lnexp and USE_LNEXP:
        S.add("act", mk("activation", out=tmp_ap, in_=ps_ap, func=AF.Ln, bias=eps_t[0:M, :], scale=inv_n),
              reads=rkeys + ["eps_t"], writes=[tmpkey])
        S.add("act", mk("activation", out=out_ap, in_=tmp_ap, func=AF.Exp, scale=-0.5),
              reads=[tmpkey], writes=[wkey])
        return
    S.add("act", mk("activation", out=tmp_ap, in_=ps_ap, func=AF.Sqrt, bias=eps_t[0:M, :], scale=inv_n),
          reads=rkeys + ["eps_t"], writes=[tmpkey])
    S.add("dve", mk("reciprocal", out=out_ap, in_=tmp_ap),
          reads=[tmpkey], writes=[wkey])


def mod_phase(C, w_mod_d, b_mod_d, vecs, scin_d, name):
    S = C.S
    GS = C.sb("GS" + name, [128, 3, KD, 2], F32)
    SH = C.sb("SH" + name, [128, 3, KD, 2], F32)
    GT = C.sb("GT" + name, [128, 3, KD, 2], F32)
    key = "mods" + name
    with C.scope():
        sc = C.sb("sc", [128, KD, 2], F32)
        scs = C.sb("scs", [128, KD, 2], F32)
        bm = C.sb("bm", [128, 72], F32)
        modT = C.sb("modT", [128, 72, 2], F32)
        wm = [C.sb("wm%d" % i, [128, KD, 1024], F32) for i in range(2)]
        mp = C.ps("modps", [128, 72, 2])
        S.add("sp", mk("dma_start", out=sc[:], in_=scin_d), writes=["sc"], dma=True)
        S.add("sp", mk("dma_start", out=bm[:], in_=b_mod_d), writes=["bm"], dma=True)
        S.add("act", mk("activation", out=scs[:], in_=sc[:], func=AF.Silu), reads=["sc"], writes=["scs"])
        wv = w_mod_d.rearrange("(k p) n -> p k n", p=128)
        for mc in range(9):
            w = wm[mc % 2]
            for k in range(KD):
                S.add("sp", mk("dma_start", out=w[:, k, :],
                                                                   in_=w_mod_d[k * 128:(k + 1) * 128, mc * 1024:(mc + 1) * 1024]),
                      writes=[("wm", mc % 2, k)], dma=True)
            for j in range(8):
                ch = mc * 8 + j
                for k in range(KD):
                    S.add("pe", mk("matmul", mp[:, ch, :], w[:, k, j * 128:(j + 1) * 128],
                                                                         scs[:, k, :], start=(k == 0), stop=(k == KD - 1)),
                          reads=[("wm", mc % 2, k), "scs"], writes=["modps"])
        for c in range(2):
            S.add("dve", mk("tensor_tensor", out=modT[:, :, c], in0=mp[:, :, c], in1=bm[:], op=ALU.add),
                  reads=["modps", "bm"], writes=["modT"])
        for w3 in range(3):
            g = vecs[:, 8 * w3:8 * w3 + 8]
            for c in range(2):
                S.add("dve", mk("scalar_tensor_tensor",
                    out=GS[:, w3, :, c], in0=modT[:, (3 * w3 + 1) * 8:(3 * w3 + 2) * 8, c], scalar=1.0, in1=g,
                    op0=ALU.add, op1=ALU.mult), reads=["modT", "vecs"], writes=[key])
                S.add("dve", mk("tensor_copy", out=SH[:, w3, :, c], in_=modT[:, (3 * w3) * 8:(3 * w3 + 1) * 8, c]),
                      reads=["modT"], writes=[key])
                gsc = 1.0 if w3 == 1 else 0.5
                S.add("dve", mk("tensor_scalar",
                    out=GT[:, w3, :, c], in0=modT[:, (3 * w3 + 2) * 8:(3 * w3 + 3) * 8, c], scalar1=gsc, scalar2=None,
                    op0=ALU.mult), reads=["modT"], writes=[key])
    mods = {}
    for w3, wn in enumerate(("ffn1", "mix", "ffn2")):
        for c, cn in enumerate(("lat", "ctx")):
            mods[(wn, cn)] = dict(gs=GS[:, w3, :, c], sh=SH[:, w3, :, c], gh=GT[:, w3, :, c], key=key)
    return mods


def ffn_phase(C, hT_d, w_in_d, w_out_d, blocks, mods, wn):
    S = C.S
    NSPL = 4
    W = 2 * DFF // NSPL
    with C.scope():
        win = C.sb("win", [128, KD, 2 * DFF], BF16)
        wout = C.sb("wout", [128, JF, D], BF16)
        hbs = [C.sb("hb%d" % i, [128, KD, TB], F32) for i in range(3)]
        xns = [C.sb("xn%d" % i, [128, KD, TB], BF16) for i in range(2)]
        sqs = [C.sb("sq%d" % i, [128, KD, TB], BF16) for i in range(2)]
        rstds = [C.sb("rstd%d" % i, [128, TB], F32) for i in range(2)]
        tmps = [C.sb("tmp%d" % i, [128, TB], F32) for i in range(2)]
        sgs = [C.sb("sg%d" % i, [128, TB], F32) for i in range(4)]
        gjs = [C.sb("gj%d" % i, [128, TB], BF16) for i in range(4)]
        acc = C.ps("acc", [128, 8, TB])
        gu = C.ps("gu", [128, 8, TB])
        ones_b = C.t["ones_b"]
        WL = WLoader(C, W)
        for s in (0, 2):
            for k in range(KD):
                WL.load(win[:, k, s * W:(s + 1) * W], w_in_d[k * 128:(k + 1) * 128, s * W:(s + 1) * W], W, ("win", k, s))
        for j in range(0, 11):
            WL.load(wout[:, j, :], w_out_d[j * 128:(j + 1) * 128, :], D, ("wout", j))
        for s in (1, 3):
            for k in range(KD):
                WL.load(win[:, k, s * W:(s + 1) * W], w_in_d[k * 128:(k + 1) * 128, s * W:(s + 1) * W], W, ("win", k, s))
        for j in range(11, JF):
            WL.load(wout[:, j, :], w_out_d[j * 128:(j + 1) * 128, :], D, ("wout", j))
        nb = len(blocks)
        hview = hT_d.rearrange("(k p) t -> p k t", p=128)
        gu_slot = [0]

        def next_gu():
            s = gu_slot[0]
            gu_slot[0] = (s + 1) % 4
            return s

        def load(b):
            t0 = blocks[b][0]
            hb = hbs[b % 3]
            S.add("sp", mk("dma_start", out=hb[:], in_=hview[:, :, t0:t0 + TB]),
                  reads=[("hT", t0)], writes=[("hb", b % 3)], dma=True)

        def norm_a(b):
            hb, sq = hbs[b % 3], sqs[b % 2]
            S.add("act", mk("activation", out=sq[:], in_=hb[:], func=AF.Square),
                  reads=[("hb", b % 3)], writes=[("sq", b % 2)])

        def norm_b(b):
            m = mods[(wn, blocks[b][1])]
            hb, sq, xn, rstd, tmp = hbs[b % 3], sqs[b % 2], xns[b % 2], rstds[b % 2], tmps[b % 2]
            s = next_gu()
            ssp = gu[:, 2 * s, :]
            for k in range(KD):
                S.add("pe", mk("matmul", ssp, ones_b[:], sq[:, k, :], start=(k == 0), stop=(k == KD - 1)),
                      reads=[("sq", b % 2), "ones_b"], writes=[("gu", s)])
            rstd_from_psum(C, ssp, 128, TB, 1.0 / D, rstd[:], tmp[:], [("gu", s)], ("rstd", b % 2), ("tmp", b % 2))
            for k in range(KD):
                S.add("dve", mk("scalar_tensor_tensor", out=tmp[:], in0=hb[:, k, :], scalar=m["gs"][:, k:k + 1],
                                                                   in1=rstd[:], op0=ALU.mult, op1=ALU.mult),
                      reads=[("hb", b % 3), ("rstd", b % 2), m["key"]], writes=[("tmp", b % 2)])
                S.add("act", mk("activation", out=xn[:, k, :], in_=tmp[:], func=AF.Identity,
                                                         bias=m["sh"][:, k:k + 1], scale=1.0),
                      reads=[("tmp", b % 2), m["key"]], writes=[("xn", b % 2, k)])

        def main(b):
            m = mods[(wn, blocks[b][1])]
            t0 = blocks[b][0]
            hb, xn = hbs[b % 3], xns[b % 2]
            xkeys = [("xn", b % 2, k) for k in range(KD)]
            for jj in range(JF + 2):
                if jj < JF:
                    j = jj
                    s = next_gu()
                    gp = gu[:, 2 * s, :]
                    up = gu[:, 2 * s + 1, :]
                    for k in range(KD):
                        S.add("pe", mk("matmul", gp, win[:, k, j * 128:(j + 1) * 128], xn[:, k, :],
                                                                       start=(k == 0), stop=(k == KD - 1)),
                              reads=[xkeys[k], ("win", k, (j * 128) // W), ("win", k, ((j + 1) * 128 - 1) // W)],
                              writes=[("gu", s)])
                    for k in range(KD):
                        c0 = DFF + j * 128
                        S.add("pe", mk("matmul", up, win[:, k, c0:c0 + 128], xn[:, k, :],
                                                                          start=False, stop=(k == KD - 1),
                                                                          skip_group_check=True),
                              reads=[xkeys[k], ("win", k, c0 // W), ("win", k, (c0 + 127) // W)],
                              writes=[("gu", s)])
                    sg, gj = sgs[j % 4], gjs[j % 4]
                    S.add("act", mk("activation", out=sg[:], in_=gp, func=AF.Silu),
                          reads=[("gu", s)], writes=[("sg", j % 4)])
                    S.add("dve", mk("tensor_tensor", out=gj[:], in0=up, in1=sg[:], op=ALU.mult),
                          reads=[("gu", s), ("sg", j % 4)], writes=[("gj", j % 4)])
                if jj >= 2:
                    j = jj - 2
                    gj = gjs[j % 4]
                    for n in range(KD):
                        f = (j == 0 and n % 2 == 0)
                        S.add("pe", mk("matmul",
                            acc[:, n, :], wout[:, j, n * 128:(n + 1) * 128], gj[:],
                            start=f, stop=(j == JF - 1), skip_group_check=True),
                              reads=[("gj", j % 4), ("wout", j)], writes=[("acc", n // 2)])
                if jj == 6 and b + 1 < nb:
                    norm_b(b + 1)
            for n in range(KD):
                S.add("dve", mk("scalar_tensor_tensor", out=hb[:, n, :], in0=acc[:, n, :],
                                                                   scalar=m["gh"][:, n:n + 1], in1=hb[:, n, :],
                                                                   op0=ALU.mult, op1=ALU.add),
                      reads=[("acc", n // 2), m["key"], ("hb", b % 3)], writes=[("hb", b % 3)])
            S.add("sp", mk("dma_start", out=hview[:, :, t0:t0 + TB], in_=hb[:]),
                  reads=[("hb", b % 3)], writes=[("hT", t0)], dma=True)

        load(0)
        if nb > 1:
            load(1)
        norm_a(0)
        norm_b(0)
        for b in range(nb):
            if b + 2 < nb:
                load(b + 2)
            if b + 1 < nb:
                norm_a(b + 1)
            main(b)


def ffn_blocks(do_ctx=True):
    return [(t0, "lat" if t0 < NL else "ctx") for t0 in range(0, NT if do_ctx else NL, TB)]


def hT_keys(t0, nt):
    return [("hT", t) for t in range(t0, t0 + nt, TB)]


class BankRR:
    def __init__(self, C, n=8):
        self.t = C.ps("bank", [128, n, 512])
        self.n = n
        self.i = 0

    def next(self):
        b = self.i
        self.i = (self.i + 1) % self.n
        return b, self.t, ("bank", b)


def m1_phase(C, hT_d, Wd, vecs, mods, ropeC_d, ropeS_d, O, exch=None):
    S = C.S
    blocks = [(t0, 512, "lat") for t0 in range(0, NL, 512)] + [(NL, 256, "ctx")]
    with C.scope():
        winx = C.sb("winx", [128, KD, WINX], BF16)
        wuq = C.sb("wuq", [128, 2, 1152], BF16)
        wukv = C.sb("wukv", [128, 768], BF16)
        WL = WLoader(C, 1152)
        for k in range(KD):
            for s2 in range(2):
                c0, c1 = s2 * 1056, (s2 + 1) * 1056
                WL.load(winx[:, k, c0:c1], Wd["winx"][k * 128:(k + 1) * 128, c0:c1], 1056, ("winx", k))
        for k in range(2):
            WL.load(wuq[:, k, :], Wd["wuq"][k * 128:(k + 1) * 128, :], 1152, "wuq")
        WL.load(wukv[:], Wd["wukv"], 768, "wukv")
        hb = [C.sb("mhb%d" % i, [128, KD, 512], F32) for i in range(2)]
        xn = C.sb("mxn", [128, KD, 512], BF16)
        sq = C.sb("msq", [128, KD, 512], BF16)
        rstd = C.sb("mrstd", [128, 512], F32)
        tmp = C.sb("mtmp", [128, 512], F32)
        rc = C.sb("rc", [128, 512], F32)
        rs = C.sb("rs", [128, 512], F32)
        st3 = [C.sb("st3_%d" % i, [128, 3, 512], F32) for i in range(2)]
        sq2 = C.sb("sq2", [128, 2, 512], BF16)
        cqn = C.sb("cqn", [128, 2, 512], BF16)
        ckvn = C.sb("ckvn", [128, 512], BF16)
        sqk = C.sb("sqk", [128, 512], BF16)
        tA = C.sb("tA", [128, 512], F32)
        t1 = [C.sb("t1_%d" % i, [128, 512], F32) for i in range(2)]
        t2 = [C.sb("t2_%d" % i, [128, 512], F32) for i in range(2)]
        rq = [C.sb("rq_%d" % i, [128, 512], F32) for i in range(2)]
        tq = [C.sb("tq_%d" % i, [128, 512], F32) for i in range(2)]
        sqh = [C.sb("sqh_%d" % i, [128, 512], BF16) for i in range(2)]
        qh = [C.sb("qh_%d" % i, [128, 512], BF16) for i in range(2)]
        kh = [C.sb("kh_%d" % i, [128, 512], BF16) for i in range(2)]
        nst = [C.sb("nst_%d" % i, [128, 2, 512], BF16) for i in range(2)]
        nva = C.sb("nva", [128, 4, 4, 128], BF16)
        va = C.sb("va", [128, 4, 6, 128], BF16)
        B = BankRR(C, 8)
        ones_b, ones_bd = C.t["ones_b"], C.t["ones_bd"]
        S.add("pool", mk("memset", nva[:], 1.0), writes=["nva"])
        S.add("pool", mk("memset", va[:], 1.0), writes=["va"])
        hview = hT_d.rearrange("(k p) t -> p k t", p=128)
        V = lambda c: vecs[:, c:c + 1]
        exch_done = set()

        def load(bi):
            t0, nt, kind = blocks[bi]
            h = hb[bi % 2]
            S.add("sp", mk("dma_start", out=h[:, :, 0:nt], in_=hview[:, :, t0:t0 + nt]),
                  reads=hT_keys(t0, nt), writes=[("mhb", bi % 2)], dma=True)

        load(0)
        for bi, (t0, nt, kind) in enumerate(blocks):
            if bi + 1 < len(blocks):
                load(bi + 1)
            m = mods[("mix", kind)]
            h = hb[bi % 2]
            hk = ("mhb", bi % 2)
            if kind == "lat":
                q4, off = t0 // 1024, t0 % 1024
                dst2 = lambda nm, q4=q4: O["L_" + nm][q4]
            else:
                q4, off = "c", 0
                dst2 = lambda nm: O["L_" + nm + "c"]
            S.add("sp", mk("dma_start", out=rc[64:96, 0:nt], in_=ropeC_d[:, t0:t0 + nt]),
                  writes=["rc"], dma=True)
            S.add("sp", mk("dma_start", out=rs[64:96, 0:nt], in_=ropeS_d[:, t0:t0 + nt]),
                  writes=["rs"], dma=True)
            S.add("act", mk("activation", out=sq[:, :, 0:nt], in_=h[:, :, 0:nt], func=AF.Square),
                  reads=[hk], writes=["msq"])
            b, bt, bk = B.next()
            for k in range(KD):
                S.add("pe", mk("matmul", bt[:, b, 0:nt], ones_b[:], sq[:, k, 0:nt],
                                                               start=(k == 0), stop=(k == KD - 1)),
                      reads=["msq", "ones_b"], writes=[bk])
            rstd_from_psum(C, bt[:, b, 0:nt], 128, nt, 1.0 / D, rstd[:, 0:nt], tmp[:, 0:nt], [bk], "mrstd", "mtmp", lnexp=True)
            for k in range(KD):
                S.add("dve", mk("scalar_tensor_tensor",
                    out=tmp[:, 0:nt], in0=h[:, k, 0:nt], scalar=m["gs"][:, k:k + 1], in1=rstd[:, 0:nt],
                    op0=ALU.mult, op1=ALU.mult), reads=[hk, "mrstd", m["key"]], writes=["mtmp"])
                S.add("act", mk("activation", out=xn[:, k, 0:nt], in_=tmp[:, 0:nt], func=AF.Identity,
                                                                   bias=m["sh"][:, k:k + 1], scale=1.0),
                      reads=["mtmp", m["key"]], writes=[("mxn", k)])

            def proj(col0, M):
                b, bt, bk = B.next()
                for k in range(KD):
                    S.add("pe", mk("matmul", bt[0:M, b, 0:nt], winx[:, k, col0:col0 + M], xn[:, k, 0:nt],
                                                             start=(k == 0), stop=(k == KD - 1)),
                          reads=[("mxn", k), ("winx", k)], writes=[bk])
                return bt[0:M, b, 0:nt], bk

            s3 = st3[0]
            for c in range(3):
                p, pk = proj(c * 128, 128)
                S.add("act", mk("activation", out=s3[:, c, 0:nt], in_=p, func=AF.Copy),
                      reads=[pk], writes=[("st3", 0)])
            S.add("sp", mk("dma_start", out=dst2("lx").rearrange("(c p) t -> p c t", p=128)[:, :, off:off + nt],
                                                     in_=s3[:, :, 0:nt]),
                  reads=[("st3", 0)], writes=[("L", "lx", q4)], dma=True)
            s3 = st3[1]
            for c in range(3):
                p, pk = proj(384 + c * 128, 128)
                S.add("act", mk("activation", out=s3[:, c, 0:nt], in_=p, func=AF.Gelu_apprx_tanh),
                      reads=[pk], writes=[("st3", 1)])
            S.add("sp", mk("dma_start", out=O["lgel"].rearrange("(c p) t -> p c t", p=128)[:, :, t0:t0 + nt],
                                                     in_=s3[:, :, 0:nt]),
                  reads=[("st3", 1)], writes=[("lgel", t0)], dma=True)
            pcq = []
            for c in range(2):
                p, pk = proj(768 + c * 128, 128)
                pcq.append((p, pk))
                S.add("act", mk("activation", out=sq2[:, c, 0:nt], in_=p, func=AF.Square),
                      reads=[pk], writes=["sq2"])
            b, bt, bk = B.next()
            for c in range(2):
                S.add("pe", mk("matmul", bt[:, b, 0:nt], ones_b[:], sq2[:, c, 0:nt], start=(c == 0), stop=(c == 1)),
                      reads=["sq2", "ones_b"], writes=[bk])
            rstd_from_psum(C, bt[:, b, 0:nt], 128, nt, 1.0 / 256, rstd[:, 0:nt], tmp[:, 0:nt], [bk], "mrstd", "mtmp", lnexp=True)
            for c in range(2):
                p, pk = pcq[c]
                S.add("dve", mk("scalar_tensor_tensor", out=cqn[:, c, 0:nt], in0=p, scalar=V(24 + c),
                                                                        in1=rstd[:, 0:nt], op0=ALU.mult, op1=ALU.mult),
                      reads=[pk, "mrstd", "vecs"], writes=["cqn"])
            p, pk = proj(1024, 128)
            S.add("act", mk("activation", out=sq2[:, 0, 0:nt], in_=p, func=AF.Square), reads=[pk], writes=["sq2"])
            b, bt, bk = B.next()
            S.add("pe", mk("matmul", bt[:, b, 0:nt], ones_b[:], sq2[:, 0, 0:nt], start=True, stop=True),
                  reads=["sq2", "ones_b"], writes=[bk])
            rstd_from_psum(C, bt[:, b, 0:nt], 128, nt, 1.0 / 128, rstd[:, 0:nt], tmp[:, 0:nt], [bk], "mrstd", "mtmp", lnexp=True)
            S.add("dve", mk("scalar_tensor_tensor", out=ckvn[:, 0:nt], in0=p, scalar=V(26), in1=rstd[:, 0:nt],
                                                               op0=ALU.mult, op1=ALU.mult),
                  reads=[pk, "mrstd", "vecs"], writes=["ckvn"])
            pkr, pkrk = proj(1152, 96)
            pks, pksk = proj(1248, 96)
            S.add("act", mk("activation", out=sqk[64:96, 0:nt], in_=pkr[64:96, :], func=AF.Square),
                  reads=[pkrk], writes=["sqk_r"])
            S.add("dve", mk("scalar_tensor_tensor", out=tA[64:96, 0:nt], in0=pkr[64:96, :], scalar=vecs[64:96, 29:30],
                                                          in1=rc[64:96, 0:nt], op0=ALU.mult, op1=ALU.mult),
                  reads=[pkrk, "rc", "vecs"], writes=["tA"])
            S.add("dve", mk("scalar_tensor_tensor", out=tmp[64:96, 0:nt], in0=pks[64:96, :], scalar=vecs[64:96, 30:31],
                                                          in1=rs[64:96, 0:nt], op0=ALU.mult, op1=ALU.mult),
                  reads=[pksk, "rs", "vecs"], writes=["mtmp"])
            S.add("dve", mk("tensor_tensor", out=tA[64:96, 0:nt], in0=tA[64:96, 0:nt], in1=tmp[64:96, 0:nt], op=ALU.add),
                  reads=["tA", "mtmp"], writes=["tA"])
            for which, col0, gcol, oname in (("q", 1344, 31, "nqT"), ("k", 1600, 32, "nkT")):
                ns = nst[0 if which == "q" else 1]
                nk_ = ("nst", which)
                for c in range(2):
                    p, pk = proj(col0 + c * 128, 128)
                    S.add("act", mk("activation", out=sq2[:, 0, 0:nt], in_=p, func=AF.Square), reads=[pk], writes=["sq2"])
                    b, bt, bk = B.next()
                    S.add("pe", mk("matmul", bt[:, b, 0:nt], ones_bd[:], sq2[:, 0, 0:nt], start=True, stop=True),
                          reads=["sq2", "ones_bd"], writes=[bk])
                    rstd_from_psum(C, bt[:, b, 0:nt], 128, nt, 1.0 / 64, rstd[:, 0:nt], tmp[:, 0:nt], [bk], "mrstd", "mtmp", lnexp=True)
                    S.add("dve", mk("scalar_tensor_tensor",
                        out=ns[:, c, 0:nt], in0=p, scalar=V(gcol), in1=rstd[:, 0:nt], op0=ALU.mult, op1=ALU.mult),
                          reads=[pk, "mrstd", "vecs"], writes=[nk_])
                odst = (O["nqT"].rearrange("(c p) t -> p c t", p=128)[:, :, t0:t0 + nt] if which == "q"
                        else dst2("nkT").rearrange("(c p) t -> p c t", p=128)[:, :, off:off + nt])
                S.add("sp", mk("dma_start", out=odst, in_=ns[:, :, 0:nt]),
                      reads=[nk_], writes=[("L", oname, q4) if which == "k" else (oname, t0)], dma=True)
            nsub = nt // 128
            for sb_ in range(nsub):
                b, bt, bk = B.next()
                for k in range(KD):
                    S.add("pe", mk("matmul", bt[:, b, 0:256], xn[:, k, sb_ * 128:(sb_ + 1) * 128],
                                                                           winx[:, k, 1856:2112], start=(k == 0), stop=(k == KD - 1)),
                          reads=[("mxn", k), ("winx", k)], writes=[bk])
                pv = bt[:, b, 0:256].rearrange("p (h d) -> p h d", h=4)
                S.add("act", mk("activation", out=nva[:, sb_, 0:4:2, 0:64], in_=pv[:, 0:4:2, :], func=AF.Copy),
                      reads=[bk], writes=["nva"])
                S.add("act", mk("activation", out=nva[:, sb_, 1:4:2, 64:128], in_=pv[:, 1:4:2, :], func=AF.Copy),
                      reads=[bk], writes=["nva"])
            S.add("sp", mk("dma_start",
                out=dst2("nvA")[off:off + nt, :].rearrange("(s p) (h d) -> p s h d", p=128, h=4), in_=nva[:, 0:nsub]),
                  reads=["nva"], writes=[("L", "nvA", q4)], dma=True)
            for sb_ in range(nsub):
                b, bt, bk = B.next()
                S.add("pe", mk("matmul", bt[:, b, 0:384], ckvn[:, sb_ * 128:(sb_ + 1) * 128],
                                                                  wukv[:, 384:768], start=True, stop=True),
                      reads=["ckvn", "wukv"], writes=[bk])
                pv = bt[:, b, 0:384].rearrange("p (h d) -> p h d", h=6)
                S.add("act", mk("activation", out=va[:, sb_, 0:6:2, 0:64], in_=pv[:, 0:6:2, :], func=AF.Copy),
                      reads=[bk], writes=["va"])
                S.add("act", mk("activation", out=va[:, sb_, 1:6:2, 64:128], in_=pv[:, 1:6:2, :], func=AF.Copy),
                      reads=[bk], writes=["va"])
            S.add("sp", mk("dma_start",
                out=dst2("vA")[off:off + nt, :].rearrange("(s p) (h d) -> p s h d", p=128, h=6), in_=va[:, 0:nsub]),
                  reads=["va"], writes=[("L", "vA", q4)], dma=True)
            for hd in range(6):
                i2 = hd % 2
                b, bt, bk = B.next()
                pkn = bt[0:64, b, 0:nt]
                S.add("pe", mk("matmul", pkn, wukv[:, hd * 64:(hd + 1) * 64], ckvn[:, 0:nt], start=True, stop=True),
                      reads=["ckvn", "wukv"], writes=[bk])
                S.add("act", mk("activation", out=sqk[0:64, 0:nt], in_=pkn, func=AF.Square),
                      reads=[bk], writes=["sqk_n"])
                b2, bt2, bk2 = B.next()
                pss = bt2[0:96, b2, 0:nt]
                S.add("pe", mk("matmul", pss, ones_b[0:96, 0:96], sqk[0:96, 0:nt], start=True, stop=True),
                      reads=["sqk_n", "sqk_r", "ones_b"], writes=[bk2])
                r_, t_ = rq[i2], tq[i2]
                rstd_from_psum(C, pss, 96, nt, 1.0 / 96, r_[0:96, 0:nt], t_[0:96, 0:nt], [bk2], ("rq", i2), ("tq", i2), lnexp=True)
                khh = kh[i2]
                S.add("dve", mk("scalar_tensor_tensor",
                    out=khh[0:64, 0:nt], in0=pkn, scalar=vecs[0:64, 29:30], in1=r_[0:64, 0:nt], op0=ALU.mult, op1=ALU.mult),
                      reads=[bk, ("rq", i2), "vecs"], writes=[("kh", i2)])
                S.add("dve", mk("tensor_tensor", out=khh[64:96, 0:nt], in0=tA[64:96, 0:nt],
                                                                        in1=r_[64:96, 0:nt], op=ALU.mult),
                      reads=["tA", ("rq", i2)], writes=[("kh", i2)])
                S.add("sp", mk("dma_start", out=dst2("kT")[hd * 96:(hd + 1) * 96, off:off + nt], in_=khh[0:96, 0:nt]),
                      reads=[("kh", i2)], writes=[("L", "kT", q4)], dma=True)
            for hd in range(6):
                i2 = hd % 2
                b, bt, bk = B.next()
                pq = bt[0:96, b, 0:nt]
                for k in range(2):
                    S.add("pe", mk("matmul", pq, wuq[:, k, hd * 192:hd * 192 + 96], cqn[:, k, 0:nt],
                                                                      start=(k == 0), stop=(k == 1)),
                          reads=["cqn", "wuq"], writes=[bk])
                b3, bt3, bk3 = B.next()
                pw = bt3[0:96, b3, 0:nt]
                for k in range(2):
                    S.add("pe", mk("matmul", pw, wuq[:, k, hd * 192 + 96:hd * 192 + 192], cqn[:, k, 0:nt],
                                                                      start=(k == 0), stop=(k == 1)),
                          reads=["cqn", "wuq"], writes=[bk3])
                sh_ = sqh[i2]
                S.add("act", mk("activation", out=sh_[0:96, 0:nt], in_=pq, func=AF.Square),
                      reads=[bk], writes=[("sqh", i2)])
                b2, bt2, bk2 = B.next()
                pss = bt2[0:96, b2, 0:nt]
                S.add("pe", mk("matmul", pss, ones_b[0:96, 0:96], sh_[0:96, 0:nt], start=True, stop=True),
                      reads=[("sqh", i2), "ones_b"], writes=[bk2])
                r_, t_ = rq[i2], tq[i2]
                rstd_from_psum(C, pss, 96, nt, 1.0 / 96, r_[0:96, 0:nt], t_[0:96, 0:nt], [bk2], ("rq", i2), ("tq", i2), lnexp=True)
                qhh, a1, a2 = qh[i2], t1[i2], t2[i2]
                S.add("dve", mk("scalar_tensor_tensor",
                    out=qhh[0:64, 0:nt], in0=pq[0:64, :], scalar=vecs[0:64, 27:28], in1=r_[0:64, 0:nt], op0=ALU.mult, op1=ALU.mult),
                      reads=[bk, ("rq", i2), "vecs"], writes=[("qh", i2)])
                S.add("dve", mk("scalar_tensor_tensor",
                    out=a1[64:96, 0:nt], in0=pq[64:96, :], scalar=vecs[64:96, 27:28], in1=rc[64:96, 0:nt], op0=ALU.mult, op1=ALU.mult),
                      reads=[bk, "rc", "vecs"], writes=[("t1", i2)])
                S.add("dve", mk("scalar_tensor_tensor",
                    out=a2[64:96, 0:nt], in0=pw[64:96, :], scalar=vecs[64:96, 28:29], in1=rs[64:96, 0:nt], op0=ALU.mult, op1=ALU.mult),
                      reads=[bk3, "rs", "vecs"], writes=[("t2", i2)])
                S.add("dve", mk("tensor_tensor", out=a1[64:96, 0:nt], in0=a1[64:96, 0:nt], in1=a2[64:96, 0:nt], op=ALU.add),
                      reads=[("t1", i2), ("t2", i2)], writes=[("t1", i2)])
                S.add("dve", mk("tensor_tensor", out=qhh[64:96, 0:nt], in0=a1[64:96, 0:nt],
                                                                              in1=r_[64:96, 0:nt], op=ALU.mult),
                      reads=[("t1", i2), ("rq", i2)], writes=[("qh", i2)])
                S.add("sp", mk("dma_start", out=O["qT"][hd, :, t0:t0 + nt], in_=qhh[0:96, 0:nt]),
                      reads=[("qh", i2)], writes=[("qT", hd, t0)], dma=True)
            if exch is not None:
                for qq in range(4):
                    if bi == min(2 * qq + 2, len(blocks) - 1) or (bi == len(blocks) - 1 and 2 * qq + 2 > bi):
                        if qq not in exch_done:
                            exch_done.add(qq)
                            exch(qq)


def lru_phase(C, I, vecs, lruw_d, halfmask_d, yT_d):
    S = C.S
    XW = 2 + NCX + 3 + 2 * NL + 2
    CT0 = 2
    LT0 = 2 + NCX + 3
    SEG = 512
    with C.scope():
        lw = C.sb("lw", [128, 1536], BF16)
        with C.scope():
            WL = WLoader(C, 1536, n=1)
            WL.load(lw[:], lruw_d, 1536, "lw")
        hm = C.sb("hm", [128, 2], F32)
        S.add("sp", mk("dma_start", out=hm[:], in_=halfmask_d), writes=["hm"], dma=True)
        par = C.sb("lpar", [128, 18], F32)
        e1 = C.sb("le1", [128, 6], F32)
        one_t = C.sb("one_t", [128, 1], F32)
        S.add("pool", mk("memset", one_t[:], 1.0), writes=["one_t"])
        S.add("act", mk("activation", out=e1[:], in_=vecs[:, 60:66], func=AF.Exp, scale=-1.0), reads=["vecs"], writes=["le1"])
        S.add("act", mk("activation", out=e1[:], in_=e1[:], func=AF.Ln, bias=one_t[:], scale=1.0), reads=["le1", "one_t"], writes=["le1"])
        S.add("dve", mk("tensor_scalar", out=par[:, 0:6], in0=e1[:], scalar1=-4.0, scalar2=None, op0=ALU.mult),
              reads=["le1"], writes=["lpar"])
        S.add("dve", mk("tensor_scalar", out=par[:, 6:18], in0=vecs[:, 48:60], scalar1=0.5, scalar2=None, op0=ALU.mult),
              reads=["vecs"], writes=["lpar"])
        xc = C.sb("xc", [128, XW], F32)
        xcb = C.sb("xcb", [128, XW], BF16)
        hsum = C.sb("hsum", [128, NT], F32)
        lg = C.sb("lg", [128, NT], F32)
        NB = 2
        tr = [C.sb("tr%d" % i, [128, SEG], F32) for i in range(NB)]
        ti = [C.sb("ti%d" % i, [128, SEG], F32) for i in range(NB)]
        aa = [C.sb("aa%d" % i, [128, SEG], F32) for i in range(NB)]
        a2 = [C.sb("a2%d" % i, [128, SEG], F32) for i in range(NB)]
        uu = [C.sb("uu%d" % i, [128, SEG], F32) for i in range(NB)]
        hh = [C.sb("hh%d" % i, [128, SEG], F32) for i in range(NB)]
        yb = C.sb("yb", [128, NT], BF16)
        stt = C.sb("lstate", [128, 1], F32)
        B = BankRR(C, 4)
        phalf = C.t["phalf"]
        for c in range(3):
            with C.scope():
                stg = C.sb("stg", [128, 2, NL], F32)
                xf = C.sb("xf", [128, XW], F32)
                S.add("dve", mk("memset", xf[:], 0.0), writes=["xf"])
                for hf in range(2):
                    for q4 in range(4):
                        S.add("sp", mk("dma_start", out=stg[:, hf, q4 * 1024:(q4 + 1) * 1024],
                                       in_=I["G_lx"][q4, hf, c * 128:(c + 1) * 128, :]),
                              reads=[("G", "lx", q4)], writes=[("stg", hf)], dma=True)
                S.add("sp", mk("dma_start", out=xf[:, CT0:CT0 + NCX], in_=I["L_lxc"][c * 128:(c + 1) * 128, :]),
                      reads=[("L", "lx", "c"), "xf"], writes=["xf"], dma=True)
                lat = xf[:, LT0:LT0 + 2 * NL].rearrange("p (r h c) -> p r h c", h=2, c=32)
                for hf in range(2):
                    eng = "act" if hf == 0 else "dve"
                    src = stg[:, hf, :].rearrange("p (r c) -> p r c", c=32)
                    if eng == "act":
                        S.add("act", mk("activation", out=lat[:, :, hf, :], in_=src, func=AF.Copy),
                              reads=[("stg", hf), "xf"], writes=["xf"])
                    else:
                        S.add("dve", mk("tensor_copy", out=lat[:, :, hf, :], in_=src),
                              reads=[("stg", hf), "xf"], writes=["xf"])
                n = XW - 3
                S.add("dve", mk("tensor_scalar", out=xc[:, 2:2 + n], in0=xf[:, 0:n], scalar1=vecs[:, 36 + 4 * c:37 + 4 * c],
                                                              scalar2=vecs[:, 33 + c:34 + c], op0=ALU.mult, op1=ALU.add),
                      reads=["xf", "vecs"], writes=["xc"])
                for j in range(1, 4):
                    S.add("dve", mk("scalar_tensor_tensor", out=xc[:, 2:2 + n], in0=xf[:, j:j + n],
                                                                              scalar=vecs[:, 36 + 4 * c + j:37 + 4 * c + j],
                                                                              in1=xc[:, 2:2 + n], op0=ALU.mult, op1=ALU.add),
                          reads=["xf", "vecs", "xc"], writes=["xc"])
                S.add("act", mk("activation", out=xcb[:, 2:2 + n], in_=xc[:, 2:2 + n], func=AF.Copy), reads=["xc"], writes=["xcb"])
            S.add("sp", mk("dma_start", out=lg[:], in_=I["lgel"][c * 128:(c + 1) * 128, :]),
                  reads=[("lgel", t) for t in list(range(0, NL, 512)) + [NL]], writes=["lg"], dma=True)
            segs = [(CT0, NCX, "ctx", 0)] + [(LT0 + i * SEG, SEG, "lat", i) for i in range(2 * NL // SEG)]
            for d in range(2):
                order = segs if d == 0 else [segs[0]] + segs[:0:-1]
                pidx = d * 3 + c
                hc = par[:, pidx:pidx + 1]
                hba = par[:, 6 + pidx:7 + pidx]
                hbx = par[:, 12 + pidx:13 + pidx]
                wa = lw[:, (0 * 6 + pidx) * 128:(0 * 6 + pidx + 1) * 128]
                wx = lw[:, (1 * 6 + pidx) * 128:(1 * 6 + pidx + 1) * 128]
                first = True
                for si, (x0, n, kind, li) in enumerate(order):
                    ib = si % NB
                    r_, i_, a_, q_, u_, h_ = tr[ib], ti[ib], aa[ib], a2[ib], uu[ib], hh[ib]
                    for p0 in range(0, n, 512):
                        pn = min(512, n - p0)
                        b, bt, bk = B.next()
                        S.add("pe", mk("matmul",
                            bt[:, b, 0:pn], wa, xcb[:, x0 + p0:x0 + p0 + pn], start=True, stop=True),
                              reads=["xcb", "lw"], writes=[bk])
                        S.add("act", mk("activation",
                            out=r_[:, p0:p0 + pn], in_=bt[:, b, 0:pn], func=AF.Tanh, bias=hba, scale=0.5),
                              reads=[bk, "lpar"], writes=[("tr", ib)])
                        b, bt, bk = B.next()
                        S.add("pe", mk("matmul",
                            bt[:, b, 0:pn], wx, xcb[:, x0 + p0:x0 + p0 + pn], start=True, stop=True),
                              reads=["xcb", "lw"], writes=[bk])
                        S.add("act", mk("activation",
                            out=i_[:, p0:p0 + pn], in_=bt[:, b, 0:pn], func=AF.Tanh, bias=hbx, scale=0.5),
                              reads=[bk, "lpar"], writes=[("ti", ib)])
                    S.add("act", mk("activation", out=a_[:, 0:n], in_=r_[:, 0:n], func=AF.Exp,
                                                                                 bias=hc, scale=hc),
                          reads=[("tr", ib), "lpar"], writes=[("aa", ib)])
                    S.add("dve", mk("tensor_tensor", out=q_[:, 0:n], in0=a_[:, 0:n], in1=a_[:, 0:n], op=ALU.mult),
                          reads=[("aa", ib)], writes=[("a2", ib)])
                    S.add("dve", mk("tensor_scalar", out=q_[:, 0:n], in0=q_[:, 0:n], scalar1=-0.25, scalar2=0.25,
                                                                      op0=ALU.mult, op1=ALU.add),
                          reads=[("a2", ib)], writes=[("a2", ib)])
                    S.add("act", mk("activation", out=q_[:, 0:n], in_=q_[:, 0:n], func=AF.Sqrt),
                          reads=[("a2", ib)], writes=[("a2", ib)])
                    S.add("dve", mk("scalar_tensor_tensor",
                        out=u_[:, 0:n], in0=i_[:, 0:n], scalar=1.0, in1=xc[:, x0:x0 + n], op0=ALU.add, op1=ALU.mult),
                          reads=[("ti", ib), "xc"], writes=[("uu", ib)])
                    S.add("dve", mk("tensor_tensor", out=u_[:, 0:n], in0=u_[:, 0:n], in1=q_[:, 0:n], op=ALU.mult),
                          reads=[("uu", ib), ("a2", ib)], writes=[("uu", ib)])
                    init = 0.0 if first else stt[:, 0:1]
                    if d == 0:
                        S.add("dve", mk("tensor_tensor_scan",
                            out=h_[:, 0:n], data0=a_[:, 0:n], data1=u_[:, 0:n], initial=init, op0=ALU.mult, op1=ALU.add),
                              reads=[("aa", ib), ("uu", ib), "lstate"], writes=[("hh", ib)])
                        S.add("dve", mk("tensor_copy", out=stt[:, 0:1], in_=h_[:, n - 1:n]),
                              reads=[("hh", ib)], writes=["lstate"])
                    else:
                        S.add("dve", mk("tensor_tensor_scan",
                            out=h_[:, 0:n][:, ::-1], data0=a_[:, 0:n][:, ::-1], data1=u_[:, 0:n][:, ::-1],
                            initial=init, op0=ALU.mult, op1=ALU.add),
                              reads=[("aa", ib), ("uu", ib), "lstate"], writes=[("hh", ib)])
                        S.add("dve", mk("tensor_copy", out=stt[:, 0:1], in_=h_[:, 0:1]),
                              reads=[("hh", ib)], writes=["lstate"])
                    first = False
                    if kind == "ctx":
                        if d == 0:
                            S.add("act", mk("activation", out=hsum[:, NL:NT], in_=h_[:, 0:NCX], func=AF.Copy),
                                  reads=[("hh", ib)], writes=[("hsum", "c")])
                        else:
                            S.add("dve", mk("tensor_tensor", out=hsum[:, NL:NT], in0=hsum[:, NL:NT], in1=h_[:, 0:NCX], op=ALU.add),
                                  reads=[("hh", ib), ("hsum", "c")], writes=[("hsum", "c")])
                    else:
                        rows = SEG // 64
                        hv = h_[:, 0:SEG].rearrange("p (r h c) -> p r h c", h=2, c=32)
                        ov = hsum[:, li * (SEG // 2):(li + 1) * (SEG // 2)].rearrange("p (r c) -> p r c", c=32)
                        hk_ = ("hsum", li)
                        if d == 0:
                            S.add("dve", mk("tensor_scalar", out=ov, in0=hv[:, :, 0, :], scalar1=hm[:, 0:1], scalar2=None,
                                                                                op0=ALU.mult),
                                  reads=[("hh", ib), "hm"], writes=[hk_])
                        else:
                            S.add("dve", mk("scalar_tensor_tensor", out=ov, in0=hv[:, :, 0, :], scalar=hm[:, 0:1], in1=ov,
                                                                                       op0=ALU.mult, op1=ALU.add),
                                  reads=[("hh", ib), "hm", hk_], writes=[hk_])
                        S.add("dve", mk("scalar_tensor_tensor", out=ov, in0=hv[:, :, 1, :], scalar=hm[:, 1:2], in1=ov,
                                                                                   op0=ALU.mult, op1=ALU.add),
                              reads=[("hh", ib), "hm", hk_], writes=[hk_])
            hkeys = [("hsum", "c")] + [("hsum", i) for i in range(2 * NL // SEG)]
            S.add("dve", mk("tensor_tensor", out=yb[:], in0=hsum[:], in1=lg[:], op=ALU.mult),
                  reads=hkeys + ["lg"], writes=["yb"])
            S.add("sp", mk("dma_start", out=yT_d[c * 128:(c + 1) * 128, :], in_=yb[:]),
                  reads=["yb"], writes=[("yT", c)], dma=True)


def mla_phase(C, I, yT_d, do_ctx=True):
    S = C.S
    NK = NCX + 2 * NL
    NJ = NK // 128
    with C.scope():
        kts = [C.sb("kt%d" % i, [128, NK], BF16) for i in range(2)]
        vas = [C.sb("vas%d" % i, [128, NJ, 128], BF16) for i in range(2)]
        qts = [C.sb("qt%d" % i, [128, NT], BF16) for i in range(2)]
        pts = [C.sb("pt%d" % i, [128, 2, 512], BF16) for i in range(3)]
        osb = [C.sb("osb%d" % i, [128, 512], F32) for i in range(2)]
        rcp = [C.sb("rcp%d" % i, [128, 512], F32) for i in range(2)]
        ysb = [C.sb("ysb%d" % i, [128, 512], BF16) for i in range(2)]
        sps = C.ps("sps", [128, 6, 512])
        ops = C.ps("ops", [128, 2, 512])
        kq_all = [(nm, t) for nm in ("kT0", "kT1") for t in range(0, NL, 512)]

        def loadh(hd):
            i2 = hd % 2
            kt, va, qt = kts[i2], vas[i2], qts[i2]
            S.add("sp", mk("dma_start", out=kt[0:96, 0:NCX], in_=I["L_kTc"][hd * 96:(hd + 1) * 96, :]),
                  reads=[("L", "kT", "c")], writes=[("kt", i2, "c")], dma=True)
            S.add("sp", mk("dma_start", out=va[:, 0:2, :],
                           in_=I["L_vAc"][:, hd * 128:(hd + 1) * 128].rearrange("(j p) d -> p j d", p=128)),
                  reads=[("L", "vA", "c")], writes=[("vas", i2, "c")], dma=True)
            for hf in range(2):
                for q4 in range(4):
                    k0 = NCX + hf * NL + q4 * 1024
                    S.add("sp", mk("dma_start", out=kt[0:96, k0:k0 + 1024], in_=I["G_kT"][q4, hf, hd * 96:(hd + 1) * 96, :]),
                          reads=[("G", "kT", q4)], writes=[("kt", i2, hf, q4)], dma=True)
                    j0 = 2 + hf * 32 + q4 * 8
                    S.add("sp", mk("dma_start", out=va[:, j0:j0 + 8, :],
                                   in_=I["G_vA"][q4, hf, :, hd * 128:(hd + 1) * 128].rearrange("(j p) d -> p j d", p=128)),
                          reads=[("G", "vA", q4)], writes=[("vas", i2, hf, q4)], dma=True)
            S.add("sp", mk("dma_start", out=qt[0:96, :], in_=I["qT"][hd, :, :]),
                  reads=[("qT", hd, t) for t in list(range(0, NL, 512)) + [NL]], writes=[("qts", i2)], dma=True)

        def jpart(j):
            return ("c",) if j < 2 else ((j - 2) // 32, ((j - 2) % 32) // 8)

        cnt = [0, 0]
        loadh(0)
        for hd in range(6):
            if hd + 1 < 6:
                loadh(hd + 1)
            i2 = hd % 2
            kt, va, qt = kts[i2], vas[i2], qts[i2]
            qblocks = [(q0, 512, 0, NJ) for q0 in range(0, NL, 512)] + ([(NL, 256, 0, 2)] if do_ctx else [])
            for (q0, nq, j0, j1) in qblocks:
                ob = cnt[1] % 2
                cnt[1] += 1
                oacc = ops[:, ob, 0:nq]
                pend = []
                prs = list(range(j0, j1, 2))
                for idx in range(len(prs) + 1):
                    if idx < len(prs):
                        j = prs[idx]
                        sb_ = cnt[0] % 3
                        cnt[0] += 1
                        sp2 = sps[:, 2 * sb_:2 * sb_ + 2, 0:nq]
                        for u in range(2):
                            S.add("pe", mk("matmul", sp2[:, u, :], kt[0:96, (j + u) * 128:(j + u + 1) * 128], qt[0:96, q0:q0 + nq],
                                           start=True, stop=True),
                                  reads=[("kt", i2) + jpart(j), ("qts", i2)], writes=[("sps", sb_)])
                        pt = pts[sb_]
                        S.add("act", mk("activation", out=pt[:, :, 0:nq], in_=sp2, func=AF.Exp, scale=MLA_SCALE),
                              reads=[("sps", sb_)], writes=[("pt", sb_)])
                        pend.append((j, sb_))
                    if idx >= 1:
                        j, sb_ = pend[idx - 1]
                        pt = pts[sb_]
                        for u in range(2):
                            S.add("pe", mk("matmul", oacc, va[:, j + u, :], pt[:, u, 0:nq],
                                           start=(idx == 1 and u == 0), stop=(idx == len(prs) and u == 1)),
                                  reads=[("pt", sb_), ("vas", i2) + jpart(j)], writes=[("ops", ob)])
                o_, r_, y_ = osb[ob], rcp[ob], ysb[ob]
                lo, hi = (0, 64) if hd % 2 == 0 else (64, 128)
                slo, shi = (64, 128) if hd % 2 == 0 else (0, 64)
                S.add("dve", mk("reciprocal", out=r_[slo:shi, 0:nq], in_=oacc[slo:shi, :]),
                      reads=[("ops", ob)], writes=[("rcp", ob)])
                S.add("act", mk("activation", out=o_[lo:hi, 0:nq], in_=oacc[lo:hi, :], func=AF.Copy),
                      reads=[("ops", ob)], writes=[("osb", ob)])
                S.add("dve", mk("tensor_copy", out=r_[lo:hi, 0:nq], in_=r_[slo:shi, 0:nq]),
                      reads=[("rcp", ob)], writes=[("rcp", ob)])
                S.add("dve", mk("tensor_tensor",
                    out=y_[lo:hi, 0:nq], in0=o_[lo:hi, 0:nq], in1=r_[lo:hi, 0:nq], op=ALU.mult),
                      reads=[("osb", ob), ("rcp", ob)], writes=[("ysb", ob)])
                row0 = 384 + hd * 64
                S.add("sp", mk("dma_start",
                    out=yT_d[row0:row0 + 64, q0:q0 + nq], in_=y_[lo:hi, 0:nq]),
                      reads=[("ysb", ob)], writes=[("yT", "m", hd, q0)], dma=True)


def na_pair_plan():
    cfg = {}
    plan = []
    r0f = lambda r: min(max(r - 4, 0), 120)
    for r in range(0, 128, 2):
        lo, hi = r0f(r), r0f(r + 1) + 8
        off0, off1 = r0f(r) - r, r0f(r + 1) - r
        chunks = []
        for ci in range(lo // 2, (hi - 1) // 2 + 1):
            key = (2 * ci - r, off0, off1)
            if key not in cfg:
                cfg[key] = len(cfg)
            chunks.append((ci, cfg[key]))
        plan.append(chunks)
    return plan, cfg


NA_PLAN, NA_CFG = na_pair_plan()
NTAB = len(NA_CFG)


def na_phase(C, I, natab_d, yT_d, do_ctx=True):
    S = C.S
    with C.scope():
        nk = C.sb("nk", [128, 2, 64, 2, 64], BF16)
        nkc = C.sb("nkc", [128, 2, NCX], BF16)
        nv = C.sb("nv", [128, 64, 4, 128], BF16)
        nvc = C.sb("nvc", [128, 2, 4, 128], BF16)
        nq = C.sb("nq", [128, 2, NT], BF16)
        tab = C.sb("natab", [128, 4, NTAB, 64], F32)
        ssb = [C.sb("nssb%d" % i, [128, 5, 64], F32) for i in range(8)]
        ptl = [C.sb("nptl%d" % i, [128, 5, 64], BF16) for i in range(8)]
        ptc = [C.sb("nptc%d" % i, [128, 2, 64], BF16) for i in range(8)]
        rcp = [C.sb("nrcp%d" % i, [128, 4, 64], F32) for i in range(2)]
        ysb = [C.sb("nysb%d" % i, [128, 2, 64], BF16) for i in range(2)]
        sps = C.ps("nsps", [128, 6, 512])
        ops = C.ps("nops", [128, 2, 512])
        S.add("sp", mk("dma_start", out=tab[:], in_=natab_d), writes=["natab"], dma=True)
        with C.scope():
            nks = C.sb("nks", [128, 2, 2, NL], BF16)
            for ck in range(2):
                for hf in range(2):
                    for q4 in range(4):
                        S.add("sp", mk("dma_start", out=nks[:, ck, hf, q4 * 1024:(q4 + 1) * 1024],
                                       in_=I["G_nkT"][q4, hf, ck * 128:(ck + 1) * 128, :]),
                              reads=[("G", "nkT", q4)], writes=[("nks", ck, hf)], dma=True)
                    src = nks[:, ck, hf, :].rearrange("p (ci t) -> p ci t", t=64)
                    if hf == 0:
                        S.add("act", mk("activation", out=nk[:, ck, :, hf, :], in_=src, func=AF.Copy),
                              reads=[("nks", ck, hf)], writes=["nk"])
                    else:
                        S.add("dve", mk("tensor_copy", out=nk[:, ck, :, hf, :], in_=src),
                              reads=[("nks", ck, hf)], writes=["nk"])
        for ck in range(2):
            S.add("sp", mk("dma_start", out=nkc[:, ck, :], in_=I["L_nkTc"][ck * 128:(ck + 1) * 128, :]),
                  reads=[("L", "nkT", "c")], writes=["nkc"], dma=True)
            S.add("sp", mk("dma_start", out=nq[:, ck, :], in_=I["nqT"][ck * 128:(ck + 1) * 128, :]),
                  reads=[("nqT", t) for t in list(range(0, NL, 512)) + [NL]], writes=["nq"], dma=True)
        for hf in range(2):
            for q4 in range(4):
                S.add("sp", mk("dma_start", out=nv[hf * 64:(hf + 1) * 64, q4 * 16:(q4 + 1) * 16],
                               in_=I["G_nvA"][q4, hf].rearrange("(ci q) (h d) -> q ci h d", q=64, h=4)),
                      reads=[("G", "nvA", q4)], writes=["nv"], dma=True)
        S.add("sp", mk("dma_start", out=nvc[:], in_=I["L_nvAc"].rearrange("(j p) (h d) -> p j h d", p=128, h=4)),
              reads=[("L", "nvA", "c")], writes=["nvc"], dma=True)
        cnt = [0, 0]
        NSB = 6

        def attend(q0, hd, loc, bset):
            ck, pl = hd // 2, (hd % 2) * 64
            sb_ = cnt[0] % NSB
            cnt[0] += 1
            nl = len(loc)
            sp_ = sps[:, sb_, 0:(nl + 2) * 64].rearrange("p (j q) -> p j q", q=64)
            qap = nq[pl:pl + 64, ck, q0:q0 + 64]
            for jl, (ci, ti_) in enumerate(loc):
                S.add("pe", mk("matmul", sp_[:, jl, :], nk[pl:pl + 64, ck, ci].rearrange("p h t -> p (h t)"), qap,
                               start=True, stop=True),
                      reads=["nk", "nq"], writes=[("nsps", sb_)])
            for jc in range(2):
                S.add("pe", mk("matmul", sp_[:, nl + jc, :], nkc[pl:pl + 64, ck, jc * 128:(jc + 1) * 128], qap,
                               start=True, stop=True),
                      reads=["nkc", "nq"], writes=[("nsps", sb_)])
            bi_ = bset * 4 + hd
            if nl:
                s_, p_ = ssb[bi_], ptl[bi_]
                ti0 = loc[0][1]
                assert [t for _, t in loc] == list(range(ti0, ti0 + nl))
                S.add("dve", mk("scalar_tensor_tensor", out=s_[:, 0:nl, :], in0=sp_[:, 0:nl, :], scalar=NA_SCALE,
                                in1=tab[:, hd, ti0:ti0 + nl, :], op0=ALU.mult, op1=ALU.add),
                      reads=[("nsps", sb_), "natab"], writes=[("nssb", bi_)])
                S.add("act", mk("activation", out=p_[:, 0:nl, :], in_=s_[:, 0:nl, :], func=AF.Exp),
                      reads=[("nssb", bi_)], writes=[("nptl", bi_)])
            pc_ = ptc[bi_]
            S.add("act", mk("activation", out=pc_[:, :, :], in_=sp_[:, nl:nl + 2, :], func=AF.Exp, scale=NA_SCALE),
                  reads=[("nsps", sb_)], writes=[("nptc", bi_)])

        def pv_and_store(q0, loc, bset, ob):
            ov = ops[:, ob, 0:256].rearrange("p (h q) -> p h q", q=64)
            nl = len(loc)
            for hd in range(4):
                bi_ = bset * 4 + hd
                p_, pc_ = ptl[bi_], ptc[bi_]
                for jl, (ci, ti_) in enumerate(loc):
                    S.add("pe", mk("matmul", ov[:, hd, :], nv[:, ci, hd, :], p_[:, jl, :], start=(jl == 0), stop=False),
                          reads=[("nptl", bi_), "nv"], writes=[("nops", ob)])
                for jc in range(2):
                    S.add("pe", mk("matmul", ov[:, hd, :], nvc[:, jc, hd, :], pc_[:, jc, :],
                                   start=(nl == 0 and jc == 0), stop=(jc == 1)),
                          reads=[("nptc", bi_), "nvc"], writes=[("nops", ob)])
            r_, y_ = rcp[ob], ysb[ob]
            S.add("dve", mk("reciprocal", out=r_[64:128, 0:4:2, :], in_=ov[64:128, 0:4:2, :]),
                  reads=[("nops", ob)], writes=[("nrcp", ob)])
            S.add("dve", mk("reciprocal", out=r_[0:64, 1:4:2, :], in_=ov[0:64, 1:4:2, :]),
                  reads=[("nops", ob)], writes=[("nrcp", ob)])
            S.add("dve", mk("tensor_copy", out=r_[0:64, 0:4:2, :], in_=r_[64:128, 0:4:2, :]),
                  reads=[("nrcp", ob)], writes=[("nrcp", ob)])
            S.add("dve", mk("tensor_copy", out=r_[64:128, 1:4:2, :], in_=r_[0:64, 1:4:2, :]),
                  reads=[("nrcp", ob)], writes=[("nrcp", ob)])
            S.add("dve", mk("tensor_tensor", out=y_[0:64, :, :], in0=ov[0:64, 0:4:2, :], in1=r_[0:64, 0:4:2, :], op=ALU.mult),
                  reads=[("nops", ob), ("nrcp", ob)], writes=[("nysb", ob)])
            S.add("dve", mk("tensor_tensor", out=y_[64:128, :, :], in0=ov[64:128, 1:4:2, :], in1=r_[64:128, 1:4:2, :], op=ALU.mult),
                  reads=[("nops", ob), ("nrcp", ob)], writes=[("nysb", ob)])
            S.add("sp", mk("dma_start", out=yT_d[768:1024, q0:q0 + 64].rearrange("(c p) q -> p c q", p=128), in_=y_[:, :, :]),
                  reads=[("nysb", ob)], writes=[("yT", "n", q0)], dma=True)

        items = [(pi * 64, chunks) for pi, chunks in enumerate(NA_PLAN)]
        if do_ctx:
            items += [(NL + qi * 64, []) for qi in range(NCX // 64)]
        for hd in range(4):
            attend(items[0][0], hd, items[0][1], 0)
        for ii, (q0, loc) in enumerate(items):
            if ii + 1 < len(items):
                for hd in range(4):
                    attend(items[ii + 1][0], hd, items[ii + 1][1], (ii + 1) % 2)
            pv_and_store(q0, loc, ii % 2, ii % 2)


def outproj_phase(C, hT_d, yT_d, wo_d, mods, do_ctx=True):
    S = C.S
    blocks = [(t0, 512, "lat") for t0 in range(0, NL, 512)] + ([(NL, 256, "ctx")] if do_ctx else [])
    with C.scope():
        wo = C.sb("wo", [128, KD, D], BF16)
        WL = WLoader(C, D)
        for k in range(KD):
            WL.load(wo[:, k, :], wo_d[k * 128:(k + 1) * 128, :], D, ("wo", k))
        yb = [C.sb("oyb%d" % i, [128, KD, 512], BF16) for i in range(2)]
        hb = [C.sb("ohb%d" % i, [128, KD, 512], F32) for i in range(2)]
        B = BankRR(C, 8)
        hview = hT_d.rearrange("(k p) t -> p k t", p=128)
        yview = yT_d.rearrange("(k p) t -> p k t", p=128)
        for bi, (t0, nt, kind) in enumerate(blocks):
            i2 = bi % 2
            m = mods[("mix", kind)]
            y_, h_ = yb[i2], hb[i2]
            S.add("sp", mk("dma_start", out=y_[:, :, 0:nt], in_=yview[:, :, t0:t0 + nt]),
                  reads=["yT_all"], writes=[("oyb", i2)], dma=True)
            S.add("sp", mk("dma_start", out=h_[:, :, 0:nt], in_=hview[:, :, t0:t0 + nt]),
                  reads=hT_keys(t0, nt), writes=[("ohb", i2)], dma=True)
            for n in range(KD):
                b, bt, bk = B.next()
                for k in range(KD):
                    S.add("pe", mk("matmul", bt[:, b, 0:nt], wo[:, k, n * 128:(n + 1) * 128], y_[:, k, 0:nt],
                                                                                     start=(k == 0), stop=(k == KD - 1)),
                          reads=[("oyb", i2), ("wo", k)], writes=[bk])
                S.add("dve", mk("scalar_tensor_tensor",
                    out=h_[:, n, 0:nt], in0=bt[:, b, 0:nt], scalar=m["gh"][:, n:n + 1], in1=h_[:, n, 0:nt], op0=ALU.mult, op1=ALU.add),
                      reads=[bk, ("ohb", i2), m["key"]], writes=[("ohb", i2)])
            S.add("sp", mk("dma_start", out=hview[:, :, t0:t0 + nt], in_=h_[:, :, 0:nt]),
                  reads=[("ohb", i2)], writes=hT_keys(t0, nt), dma=True)


def mark_yT_done(C):
    S = C.S
    keys = [k for k in list(S.last_writer.keys()) if isinstance(k, tuple) and k[0] == "yT"]
    S.add("sp", mk("nop", ), reads=keys, writes=["yT_all"])


PAIRS = [[0, 1], [2, 3], [4, 5], [6, 7]]
XCH = (("lx", 384, 1024, F32), ("kT", 576, 1024, BF16), ("vA", 1024, 768, BF16),
       ("nkT", 256, 1024, BF16), ("nvA", 1024, 512, BF16))
XCHC = dict(lx=(384, NCX), kT=(576, NCX), vA=(NCX, 768), nkT=(256, NCX), nvA=(NCX, 512))


def build_fused(nlayers=DEPTH, dbg=False):
    C = Ctx()
    S = C.S
    nl = nlayers
    hT_in = C.dram("hT_in", [D, NT], F32, "ExternalInput")
    scin = C.dram("scin", [128, KD, 2], F32, "ExternalInput")
    ropeC = C.dram("ropeC", [32, NT], F32, "ExternalInput")
    ropeS = C.dram("ropeS", [32, NT], F32, "ExternalInput")
    halfmask = C.dram("halfmask", [128, 2], F32, "ExternalInput")
    natab = C.dram("natab", [nl, 128, 4, NTAB, 64], F32, "ExternalInput")
    w_mod = C.dram("w_mod", [nl, D, 9 * D], F32, "ExternalInput")
    b_mod = C.dram("b_mod", [nl, 128, 72], F32, "ExternalInput")
    vecs_d = C.dram("vecs", [nl, 128, NVEC], F32, "ExternalInput")
    f1_in = C.dram("f1_in", [nl, D, 2 * DFF], F32, "ExternalInput")
    f1_out = C.dram("f1_out", [nl, DFF, D], F32, "ExternalInput")
    f2_in = C.dram("f2_in", [nl, D, 2 * DFF], F32, "ExternalInput")
    f2_out = C.dram("f2_out", [nl, DFF, D], F32, "ExternalInput")
    winx = C.dram("winx", [nl, D, WINX], F32, "ExternalInput")
    wuq = C.dram("wuq", [nl, 256, 1152], F32, "ExternalInput")
    wukv = C.dram("wukv", [nl, 128, 768], F32, "ExternalInput")
    lruw = C.dram("lruw", [nl, 128, 1536], F32, "ExternalInput")
    wo = C.dram("wo", [nl, D, D], F32, "ExternalInput")
    hT = C.dram("hT", [D, NT], F32, "ExternalOutput")
    yT = C.dram("yT", [D, NT], BF16, "ExternalOutput" if dbg else "Internal")
    O = dict(qT=C.dram("qT", [6, 96, NT], BF16, "Internal"), nqT=C.dram("nqT", [256, NT], BF16, "Internal"),
             lgel=C.dram("lgel", [384, NT], F32, "Internal"))
    for nm, r, c, dt in XCH:
        O["L_" + nm] = C.dram("L_" + nm, [4, r, c], dt, "Internal")
        O["L_" + nm + "c"] = C.dram("L_" + nm + "c", list(XCHC[nm]), dt, "Internal")
        O["G_" + nm] = C.dram("G_" + nm, [4, 2, r, c], dt, "Internal")
    setup_consts(C)
    vt = C.sb("vecs", [128, nl, NVEC], F32)
    S.add("sp", mk("dma_start", out=vt[:], in_=vecs_d.rearrange("l p v -> p l v")), writes=["vecs"], dma=True)
    for t0 in range(0, NT, TB):
        S.add("sp", mk("dma_start", out=hT[:, t0:t0 + TB], in_=hT_in[:, t0:t0 + TB]), writes=[("hT", t0)], dma=True)
    S.mark("mods")
    mods = [mod_phase(C, w_mod[l], b_mod[l], vt[:, l, :], scin, "L%d" % l) for l in range(nl)]

    def exch(qq):
        for nm, r, c, dt in XCH:
            S.add("pool", mk("collective_compute", "AllGather", ALU.bypass, replica_groups=PAIRS,
                             ins=[O["L_" + nm][qq]], outs=[O["G_" + nm][qq].rearrange("r a b -> (r a) b")]),
                  reads=[("L", nm, qq)], writes=[("G", nm, qq)], cc=True)

    for l in range(nl):
        last = (l == DEPTH - 1)
        vl = vt[:, l, :]
        S.mark("ffn1_%d" % l)
        ffn_phase(C, hT, f1_in[l], f1_out[l], ffn_blocks(), mods[l], "ffn1")
        S.mark("m1_%d" % l)
        m1_phase(C, hT, dict(winx=winx[l], wuq=wuq[l], wukv=wukv[l]), vl, mods[l], ropeC, ropeS, O, exch)
        S.mark("lru_%d" % l)
        lru_phase(C, O, vl, lruw[l], halfmask, yT)
        S.mark("mla_%d" % l)
        mla_phase(C, O, yT, do_ctx=not last)
        S.mark("na_%d" % l)
        na_phase(C, O, natab[l], yT, do_ctx=not last)
        mark_yT_done(C)
        S.mark("outp_%d" % l)
        outproj_phase(C, hT, yT, wo[l], mods[l], do_ctx=not last)
        S.mark("ffn2_%d" % l)
        ffn_phase(C, hT, f2_in[l], f2_out[l], ffn_blocks(do_ctx=not last), mods[l], "ffn2")
    S.mark("end")
    C.marks = S.marks
    nc = C.finish()
    nc._marks = S.marks if hasattr(nc, "__dict__") else None
    return nc


def rope_swap_index():
    i = np.arange(32)
    axis, half, f = i // 16, (i // 8) % 2, i % 8
    return axis * 16 + (1 - half) * 8 + f


def rope_tables(s):
    i = np.arange(NL)
    row = (i // 32).astype(np.float32)
    col = (32 * s + i % 32).astype(np.float32)
    inv = (np.float32(10000.0) ** (-np.arange(0, 16, 2, dtype=np.float32) / np.float32(16))).astype(np.float32)
    Cc = np.ones((32, NT), np.float32)
    Ss = np.zeros((32, NT), np.float32)
    for d in range(32):
        axis, half, f = d // 16, (d // 8) % 2, d % 8
        pos = row if axis == 0 else col
        ang = (pos * inv[f]).astype(np.float32)
        Cc[d, :NL] = np.cos(ang)
        Ss[d, :NL] = (-np.sin(ang)) if half == 0 else np.sin(ang)
    return Cc, Ss


def fm(v, k=None):
    v = np.asarray(v, np.float32)
    return np.ascontiguousarray(v.reshape(-1, 128).T)


def build_na_tables(rpb_l, s):
    rpb_l = np.asarray(rpb_l, np.float32)
    c = 32 * s + np.arange(32)
    kc = np.arange(64)
    w0 = np.clip(c - 8, 0, 48)
    col_in = (kc[:, None] >= w0[None, :]) & (kc[:, None] < w0[None, :] + 16)
    col_off = np.clip(kc[:, None] - c[None, :] + 15, 0, 30)
    G = rpb_l[:, :, col_off]
    G = np.where(col_in[None, None], G, np.float32(NEG)).astype(np.float32)
    T = np.full((128, 4, NTAB, 64), NEG, np.float32)
    for (k0mr, off0, off1), ti in NA_CFG.items():
        for hf in range(2):
            for e in range(2):
                for eq in range(2):
                    rel = k0mr + e
                    off = (off0, off1)[eq]
                    if not (off <= rel < off + 8):
                        continue
                    di = rel - eq + 7
                    p0 = hf * 64 + e * 32
                    T[p0:p0 + 32, :, ti, eq * 32:(eq + 1) * 32] = np.transpose(G[:, di, hf * 32:(hf + 1) * 32, :], (1, 0, 2))
    return T


def layer_shared(inp, l):
    sw = rope_swap_index()
    w_in = np.asarray(inp["w_in"][l], np.float32)
    winx = np.concatenate([w_in[:, 0:1152], w_in[:, 1088:1184], w_in[:, 1088:1152], w_in[:, 1152 + sw],
                           w_in[:, 1184:1952]], axis=1)
    assert winx.shape[1] == WINX
    wuq0 = np.asarray(inp["mla_w_uq"][l], np.float32).reshape(256, 6, 96)
    wuq_sw = np.concatenate([wuq0[:, :, 0:64], wuq0[:, :, 64 + sw]], axis=2)
    wuq = np.stack([wuq0, wuq_sw], axis=2).reshape(256, 1152)
    wukv0 = np.asarray(inp["mla_w_ukv"][l], np.float32).reshape(128, 6, 128)
    wukv = np.concatenate([wukv0[:, :, 0:64].reshape(128, 384), wukv0[:, :, 64:128].reshape(128, 384)], axis=1)
    vecs = np.zeros((128, NVEC), np.float32)
    vecs[:, 0:8] = fm(inp["norm_ffn1"][l])
    vecs[:, 8:16] = fm(inp["norm_mix"][l])
    vecs[:, 16:24] = fm(inp["norm_ffn2"][l])
    vecs[:, 24:26] = fm(inp["mla_q_norm"][l])
    vecs[:, 26] = np.asarray(inp["mla_kv_norm"][l], np.float32)
    gq = np.asarray(inp["mla_q_gain"][l], np.float32)
    gk = np.asarray(inp["mla_k_gain"][l], np.float32)
    vecs[0:96, 27] = gq
    vecs[64:96, 28] = gq[64 + sw]
    vecs[0:96, 29] = gk
    vecs[64:96, 30] = gk[64 + sw]
    vecs[:, 31] = np.tile(np.asarray(inp["na_q_gain"][l], np.float32), 2)
    vecs[:, 32] = np.tile(np.asarray(inp["na_k_gain"][l], np.float32), 2)
    vecs[:, 33:36] = fm(inp["lru_conv_b"][l])
    cw = np.asarray(inp["lru_conv_w"][l], np.float32)
    for c in range(3):
        for j in range(4):
            vecs[:, 36 + 4 * c + j] = cw[j, c * 128:(c + 1) * 128]
    for d in range(2):
        vecs[:, 48 + d * 3:51 + d * 3] = fm(inp["lru_b_a"][l][d])
        vecs[:, 54 + d * 3:57 + d * 3] = fm(inp["lru_b_x"][l][d])
        vecs[:, 60 + d * 3:63 + d * 3] = fm(inp["lru_lambda"][l][d])
    lruw = np.zeros((128, 2, 2, 3, 128), np.float32)
    for g, nm in enumerate(("lru_w_a", "lru_w_x")):
        w = np.asarray(inp[nm][l], np.float32)
        for d in range(2):
            for c in range(3):
                for i in range(2):
                    lruw[64 * i:64 * i + 64, g, d, c, 64 * i:64 * i + 64] = w[d, 2 * c + i]
    b_mod = fm(inp["b_mod"][l])
    return dict(winx=np.ascontiguousarray(winx), wuq=np.ascontiguousarray(wuq), wukv=np.ascontiguousarray(wukv),
                vecs=vecs, lruw=np.ascontiguousarray(lruw.reshape(128, 1536)), b_mod=b_mod,
                w_mod=np.asarray(inp["w_mod"][l], np.float32))


_PROGS = {}


def host_inputs(inp, nlayers=DEPTH, ncore=8):
    x = np.asarray(inp["x"], np.float32)
    ctx = np.asarray(inp["ctx"], np.float32)
    c = np.asarray(inp["c"], np.float32)
    c_ctx = np.asarray(inp["c_ctx"], np.float32)
    Ls = [layer_shared(inp, l) for l in range(nlayers)]
    shared = dict(
        w_mod=np.ascontiguousarray(np.asarray(inp["w_mod"], np.float32)[:nlayers]),
        b_mod=np.stack([L["b_mod"] for L in Ls]), vecs=np.stack([L["vecs"] for L in Ls]),
        f1_in=np.ascontiguousarray(np.asarray(inp["ffn1_w_in"], np.float32)[:nlayers]),
        f1_out=np.ascontiguousarray(np.asarray(inp["ffn1_w_out"], np.float32)[:nlayers]),
        f2_in=np.ascontiguousarray(np.asarray(inp["ffn2_w_in"], np.float32)[:nlayers]),
        f2_out=np.ascontiguousarray(np.asarray(inp["ffn2_w_out"], np.float32)[:nlayers]),
        winx=np.stack([L["winx"] for L in Ls]), wuq=np.stack([L["wuq"] for L in Ls]),
        wukv=np.stack([L["wukv"] for L in Ls]), lruw=np.stack([L["lruw"] for L in Ls]),
        wo=np.ascontiguousarray(np.asarray(inp["w_out"], np.float32)[:nlayers]))
    natabs = [np.stack([build_na_tables(inp["na_rpb"][l], s) for l in range(nlayers)]) for s in range(2)]
    ropes = [rope_tables(s) for s in range(2)]
    maps = []
    for core in range(ncore):
        b, s = core // 2, core % 2
        xl = x[b].reshape(128, 64, D)[:, 32 * s:32 * s + 32, :].reshape(NL, D)
        hm = np.zeros((128, 2), np.float32)
        hm[:, s] = 1.0
        m = dict(shared)
        m.update(hT_in=np.ascontiguousarray(np.concatenate([xl, ctx[b]], axis=0).T),
                 scin=np.ascontiguousarray(np.stack([fm(c[b]), fm(c_ctx)], axis=2)),
                 ropeC=ropes[s][0], ropeS=ropes[s][1], halfmask=hm, natab=natabs[s])
        maps.append(m)
    return maps


def kernel(**inp):
    ncore = 8
    if "fused" not in _PROGS:
        _PROGS["fused"] = build_fused()
    maps = host_inputs(inp)
    res = run_bass_kernel_spmd(_PROGS["fused"], maps, core_ids=list(range(ncore))).results
    out = np.zeros((4, 128, 64, D), np.float32)
    for core in range(ncore):
        b, s = core // 2, core % 2
        out[b, :, 32 * s:32 * s + 32, :] = res[core]["hT"][:, :NL].T.reshape(128, 32, D)
    return out.reshape(4, 8192, D)
```

```python
from contextlib import ExitStack
import numpy as np
import concourse.bass as bass
import concourse.mybir as mybir
from concourse.bass_utils import run_bass_kernel_spmd

F32 = mybir.dt.float32
BF16 = mybir.dt.bfloat16
AF = mybir.ActivationFunctionType
ALU = mybir.AluOpType

D = 1024
DFF = 2816
KD = 8
JF = 22
TB = 256
EPS = 1e-6
NL = 4096
NCX = 256
NT = NL + NCX
DEPTH = 4
NEG = -30000.0
NVEC = 66
WINX = 2112
MLA_SCALE = 96 ** -0.5
NA_SCALE = 0.125
USE_LNEXP = False

ENGS = ("pe", "act", "dve", "pool", "sp")
NDMA_SLOTS = 8


class Op:
    __slots__ = ("eng", "emit", "is_dma", "pos", "deps", "signal", "ticket", "waits", "slot", "slot_val", "idx")

    def __init__(self, eng, emit, is_dma):
        self.eng = eng
        self.emit = emit
        self.is_dma = is_dma
        self.deps = []
        self.signal = False
        self.ticket = 0
        self.waits = []
        self.slot = None
        self.slot_val = 0


class Sched:
    def __init__(self):
        self.ops = []
        self.streams = {e: [] for e in ENGS}
        self.last_writer = {}
        self.readers = {}
        self.dma_count = {e: 0 for e in ENGS}
        self.slot_last = {}
        self.barrier_ops = []
        self.barrier_pending = set()
        self.marks = []

    def mark(self, name):
        self.marks.append((name, {e: len(v) for e, v in self.streams.items()}))

    def barrier(self):
        ops = [s[-1] for s in self.streams.values() if s]
        ops += list(self.slot_last.values())
        self.barrier_ops = ops
        self.barrier_pending = set(ENGS)

    def add(self, eng, emit, reads=(), writes=(), dma=False, cc=False):
        op = Op(eng, emit, dma or cc)
        op.idx = len(self.ops)
        op.pos = len(self.streams[eng])
        deps = set()
        if eng in self.barrier_pending:
            deps.update(self.barrier_ops)
            self.barrier_pending.discard(eng)
        for k in reads:
            w = self.last_writer.get(k)
            if w is not None:
                deps.add(w)
        for k in writes:
            w = self.last_writer.get(k)
            if w is not None:
                deps.add(w)
            rd = self.readers.get(k)
            if rd:
                deps.update(rd.values())
        if cc:
            op.slot = ("cc", 0)
            prev = self.slot_last.get(op.slot)
            op.slot_val = (prev.slot_val + 1) if prev is not None else 1
            self.slot_last[op.slot] = op
        elif dma:
            n = self.dma_count[eng]
            self.dma_count[eng] = n + 1
            op.slot = (eng, n % NDMA_SLOTS)
            prev = self.slot_last.get(op.slot)
            if prev is not None:
                deps.add(prev)
                op.slot_val = prev.slot_val + 16
            else:
                op.slot_val = 16
            self.slot_last[op.slot] = op
        deps.discard(op)
        op.deps = sorted(deps, key=lambda o: o.idx)
        for k in reads:
            rk = ("d", op.idx) if dma else eng
            self.readers.setdefault(k, {})[rk] = op
        for k in writes:
            self.last_writer[k] = op
            self.readers[k] = {}
        self.ops.append(op)
        self.streams[eng].append(op)
        return op

    def finalize(self):
        waited = {e: {} for e in ENGS}
        for op in self.ops:
            w = waited[op.eng]
            for p in op.deps:
                if p.is_dma:
                    key = ("dma", p.slot)
                    if w.get(key, 0) >= p.slot_val:
                        continue
                    w[key] = p.slot_val
                    op.waits.append(p)
                else:
                    if p.eng == op.eng:
                        if p.eng == "pe":
                            continue
                        if not op.is_dma and op.pos - p.pos > 2:
                            continue
                    key = ("eng", p.eng)
                    if w.get(key, -1) >= p.pos:
                        continue
                    w[key] = p.pos
                    p.signal = True
                    op.waits.append(p)
        for e in ENGS:
            t = 0
            for op in self.streams[e]:
                if not op.is_dma and op.signal:
                    t += 1
                    op.ticket = t

    def emit_all(self, nc, stack):
        self.finalize()
        eng_sem = {e: stack.enter_context(nc.semaphore("s_" + e)) for e in ENGS}
        dma_sem = {}
        for e in ENGS:
            for i in range(min(NDMA_SLOTS, self.dma_count[e])):
                dma_sem[(e, i)] = stack.enter_context(nc.semaphore("d_%s%d" % (e, i)))
        if ("cc", 0) in self.slot_last:
            dma_sem[("cc", 0)] = stack.enter_context(nc.semaphore("cc_sem"))
        block = stack.enter_context(nc.Block())

        def run_stream(e, engobj):
            for op in self.streams[e]:
                for p in op.waits:
                    if p.is_dma:
                        engobj.wait_ge(dma_sem[p.slot], p.slot_val)
                    else:
                        engobj.wait_ge(eng_sem[p.eng], p.ticket)
                ins = op.emit(engobj)
                if op.is_dma:
                    if op.slot[0] == "cc":
                        ins.then_inc(dma_sem[op.slot])
                    else:
                        ins.then_inc(dma_sem[op.slot], 16)
                elif op.signal:
                    ins.then_inc(eng_sem[op.eng], 1)
            for slot, last in self.slot_last.items():
                if slot[0] == e or (slot[0] == "cc" and e == "pool"):
                    engobj.wait_ge(dma_sem[slot], last.slot_val)

        if self.streams["sp"]:
            @block.sync
            def _(eng):
                run_stream("sp", eng)
        if self.streams["act"]:
            @block.scalar
            def _(eng):
                run_stream("act", eng)
        if self.streams["dve"]:
            @block.vector
            def _(eng):
                run_stream("dve", eng)
        if self.streams["pool"]:
            @block.gpsimd
            def _(eng):
                run_stream("pool", eng)
        if self.streams["pe"]:
            @block.tensor
            def _(eng):
                run_stream("pe", eng)


def mk(name, *args, **kw):
    return lambda e: getattr(e, name)(*args, **kw)


class Ctx:
    def __init__(self):
        self.nc = bass.Bass("TRN2", target_bir_lowering=False)
        self.S = Sched()
        self.stack = ExitStack()
        self.t = {}
        self.cur = self.stack
        self.uid = 0
        self.bank_rr = 0

    def sb(self, name, shape, dtype):
        self.uid += 1
        t = self.cur.enter_context(self.nc.sbuf_tensor("%s_%d" % (name, self.uid), list(shape), dtype))
        self.t[name] = t
        return t

    def ps(self, name, shape, dtype=F32):
        self.uid += 1
        t = self.cur.enter_context(self.nc.psum_tensor("%s_%d" % (name, self.uid), list(shape), dtype))
        self.t[name] = t
        return t

    def dram(self, name, shape, dtype, kind):
        return self.nc.dram_tensor(name, list(shape), dtype, kind=kind).ap()

    def scope(self):
        return _Scope(self)

    def finish(self):
        self.S.emit_all(self.nc, self.stack)
        self.stack.close()
        return self.nc


class _Scope:
    def __init__(self, C):
        self.C = C

    def __enter__(self):
        self.prev = self.C.cur
        self.st = ExitStack()
        self.C.cur = self.st
        return self

    def __exit__(self, *a):
        self.st.close()
        self.C.cur = self.prev
        self.C.S.barrier()
        return False


def setup_consts(C):
    S = C.S
    ones_b = C.sb("ones_b", [128, 128], BF16)
    ones_bd = C.sb("ones_bd", [128, 128], BF16)
    nhalf = C.sb("nhalf", [128, 512], F32)
    phalf = C.sb("phalf", [128, 512], F32)
    S.add("pool", mk("memset", ones_b[:], 1.0), writes=["ones_b"])
    S.add("pool", mk("memset", ones_bd[:], 0.0), writes=["ones_bd"])
    S.add("pool", mk("memset", ones_bd[0:64, 0:64], 1.0), writes=["ones_bd"])
    S.add("pool", mk("memset", ones_bd[64:128, 64:128], 1.0), writes=["ones_bd"])
    S.add("pool", mk("memset", nhalf[:], -0.5), writes=["nhalf"])
    S.add("pool", mk("memset", phalf[:], 0.5), writes=["phalf"])
    eps_t = C.sb("eps_t", [128, 1], F32)
    S.add("pool", mk("memset", eps_t[:], EPS), writes=["eps_t"])


class WLoader:
    def __init__(self, C, width, n=3):
        self.C = C
        self.st = [C.sb("wstg%d" % i, [128, width], F32) for i in range(n)]
        self.cnt = 0

    def load(self, dst_ap, src_ap, ncols, wkey, nparts=128):
        S = self.C.S
        i = self.cnt % len(self.st)
        self.cnt += 1
        st = self.st[i]
        S.add("act", mk("dma_start", out=st[0:nparts, 0:ncols], in_=src_ap), writes=[("wstg", i)], dma=True)
        S.add("dve", mk("tensor_copy", out=dst_ap, in_=st[0:nparts, 0:ncols]), reads=[("wstg", i)], writes=[wkey])


def rstd_from_psum(C, ps_ap, M, nt, inv_n, out_ap, tmp_ap, rkeys, wkey, tmpkey, lnexp=False):
    S = C.S
    eps_t = C.t["eps_t"]
    if lnexp and USE_LNEXP:
        S.add("act", mk("activation", out=tmp_ap, in_=ps_ap, func=AF.Ln, bias=eps_t[0:M, :], scale=inv_n),
              reads=rkeys + ["eps_t"], writes=[tmpkey])
        S.add("act", mk("activation", out=out_ap, in_=tmp_ap, func=AF.Exp, scale=-0.5),
              reads=[tmpkey], writes=[wkey])
        return
    S.add("act", mk("activation", out=tmp_ap, in_=ps_ap, func=AF.Sqrt, bias=eps_t[0:M, :], scale=inv_n),
          reads=rkeys + ["eps_t"], writes=[tmpkey])
    S.add("dve", mk("reciprocal", out=out_ap, in_=tmp_ap),
          reads=[tmpkey], writes=[wkey])


def mod_phase(C, w_mod_d, b_mod_d, vecs, scin_d, name):
    S = C.S
    GS = C.sb("GS" + name, [128, 3, KD, 2], F32)
    SH = C.sb("SH" + name, [128, 3, KD, 2], F32)
    GT = C.sb("GT" + name, [128, 3, KD, 2], F32)
    key = "mods" + name
    with C.scope():
        sc = C.sb("sc", [128, KD, 2], F32)
        scs = C.sb("scs", [128, KD, 2], F32)
        bm = C.sb("bm", [128, 72], F32)
        modT = C.sb("modT", [128, 72, 2], F32)
        wm = [C.sb("wm%d" % i, [128, KD, 1024], F32) for i in range(2)]
        mp = C.ps("modps", [128, 72, 2])
        S.add("sp", mk("dma_start", out=sc[:], in_=scin_d), writes=["sc"], dma=True)
        S.add("sp", mk("dma_start", out=bm[:], in_=b_mod_d), writes=["bm"], dma=True)
        S.add("act", mk("activation", out=scs[:], in_=sc[:], func=AF.Silu), reads=["sc"], writes=["scs"])
        wv = w_mod_d.rearrange("(k p) n -> p k n", p=128)
        for mc in range(9):
            w = wm[mc % 2]
            for k in range(KD):
                S.add("sp", mk("dma_start", out=w[:, k, :],
                                                                   in_=w_mod_d[k * 128:(k + 1) * 128, mc * 1024:(mc + 1) * 1024]),
                      writes=[("wm", mc % 2, k)], dma=True)
            for j in range(8):
                ch = mc * 8 + j
                for k in range(KD):
                    S.add("pe", mk("matmul", mp[:, ch, :], w[:, k, j * 128:(j + 1) * 128],
                                                                         scs[:, k, :], start=(k == 0), stop=(k == KD - 1)),
                          reads=[("wm", mc % 2, k), "scs"], writes=["modps"])
        for c in range(2):
            S.add("dve", mk("tensor_tensor", out=modT[:, :, c], in0=mp[:, :, c], in1=bm[:], op=ALU.add),
                  reads=["modps", "bm"], writes=["modT"])
        for w3 in range(3):
            g = vecs[:, 8 * w3:8 * w3 + 8]
            for c in range(2):
                S.add("dve", mk("scalar_tensor_tensor",
                    out=GS[:, w3, :, c], in0=modT[:, (3 * w3 + 1) * 8:(3 * w3 + 2) * 8, c], scalar=1.0, in1=g,
                    op0=ALU.add, op1=ALU.mult), reads=["modT", "vecs"], writes=[key])
                S.add("dve", mk("tensor_copy", out=SH[:, w3, :, c], in_=modT[:, (3 * w3) * 8:(3 * w3 + 1) * 8, c]),
                      reads=["modT"], writes=[key])
                gsc = 1.0 if w3 == 1 else 0.5
                S.add("dve", mk("tensor_scalar",
                    out=GT[:, w3, :, c], in0=modT[:, (3 * w3 + 2) * 8:(3 * w3 + 3) * 8, c], scalar1=gsc, scalar2=None,
                    op0=ALU.mult), reads=["modT"], writes=[key])
    mods = {}
    for w3, wn in enumerate(("ffn1", "mix", "ffn2")):
        for c, cn in enumerate(("lat", "ctx")):
            mods[(wn, cn)] = dict(gs=GS[:, w3, :, c], sh=SH[:, w3, :, c], gh=GT[:, w3, :, c], key=key)
    return mods


def ffn_phase(C, hT_d, w_in_d, w_out_d, blocks, mods, wn, pre=None):
    S = C.S
    NSPL = 4
    W = 2 * DFF // NSPL
    with C.scope():
        win = C.sb("win", [128, KD, 2 * DFF], BF16)
        wout = C.sb("wout", [128, JF, D], BF16)
        WL = WLoader(C, W)
        loads = []
        for s in (0, 2):
            for k in range(KD):
                loads.append((win[:, k, s * W:(s + 1) * W], w_in_d[k * 128:(k + 1) * 128, s * W:(s + 1) * W], W, ("win", k, s)))
        for j in range(0, 11):
            loads.append((wout[:, j, :], w_out_d[j * 128:(j + 1) * 128, :], D, ("wout", j)))
        for s in (1, 3):
            for k in range(KD):
                loads.append((win[:, k, s * W:(s + 1) * W], w_in_d[k * 128:(k + 1) * 128, s * W:(s + 1) * W], W, ("win", k, s)))
        for j in range(11, JF):
            loads.append((wout[:, j, :], w_out_d[j * 128:(j + 1) * 128, :], D, ("wout", j)))
        pos = [0]

        def tick(n):
            for _ in range(n):
                if pos[0] < len(loads):
                    WL.load(*loads[pos[0]])
                    pos[0] += 1

        if pre is not None:
            pre(tick)
        tick(len(loads))
        hbs = [C.sb("hb%d" % i, [128, KD, TB], F32) for i in range(3)]
        xns = [C.sb("xn%d" % i, [128, KD, TB], BF16) for i in range(2)]
        sqs = [C.sb("sq%d" % i, [128, KD, TB], BF16) for i in range(2)]
        rstds = [C.sb("rstd%d" % i, [128, TB], F32) for i in range(2)]
        tmps = [C.sb("tmp%d" % i, [128, TB], F32) for i in range(2)]
        sgs = [C.sb("sg%d" % i, [128, TB], F32) for i in range(4)]
        gjs = [C.sb("gj%d" % i, [128, TB], BF16) for i in range(4)]
        acc = C.ps("acc", [128, 8, TB])
        gu = C.ps("gu", [128, 8, TB])
        ones_b = C.t["ones_b"]
        nb = len(blocks)
        hview = hT_d.rearrange("(k p) t -> p k t", p=128)
        gu_slot = [0]

        def next_gu():
            s = gu_slot[0]
            gu_slot[0] = (s + 1) % 4
            return s

        def load(b):
            t0 = blocks[b][0]
            hb = hbs[b % 3]
            S.add("sp", mk("dma_start", out=hb[:], in_=hview[:, :, t0:t0 + TB]),
                  reads=[("hT", t0)], writes=[("hb", b % 3)], dma=True)

        def norm_a(b):
            hb, sq = hbs[b % 3], sqs[b % 2]
            S.add("act", mk("activation", out=sq[:], in_=hb[:], func=AF.Square),
                  reads=[("hb", b % 3)], writes=[("sq", b % 2)])

        def norm_b(b):
            m = mods[(wn, blocks[b][1])]
            hb, sq, xn, rstd, tmp = hbs[b % 3], sqs[b % 2], xns[b % 2], rstds[b % 2], tmps[b % 2]
            s = next_gu()
            ssp = gu[:, 2 * s, :]
            for k in range(KD):
                S.add("pe", mk("matmul", ssp, ones_b[:], sq[:, k, :], start=(k == 0), stop=(k == KD - 1)),
                      reads=[("sq", b % 2), "ones_b"], writes=[("gu", s)])
            rstd_from_psum(C, ssp, 128, TB, 1.0 / D, rstd[:], tmp[:], [("gu", s)], ("rstd", b % 2), ("tmp", b % 2))
            for k in range(KD):
                S.add("dve", mk("scalar_tensor_tensor", out=tmp[:], in0=hb[:, k, :], scalar=m["gs"][:, k:k + 1],
                                                                   in1=rstd[:], op0=ALU.mult, op1=ALU.mult),
                      reads=[("hb", b % 3), ("rstd", b % 2), m["key"]], writes=[("tmp", b % 2)])
                S.add("act", mk("activation", out=xn[:, k, :], in_=tmp[:], func=AF.Identity,
                                                         bias=m["sh"][:, k:k + 1], scale=1.0),
                      reads=[("tmp", b % 2), m["key"]], writes=[("xn", b % 2, k)])

        def main(b):
            m = mods[(wn, blocks[b][1])]
            t0 = blocks[b][0]
            hb, xn = hbs[b % 3], xns[b % 2]
            xkeys = [("xn", b % 2, k) for k in range(KD)]
            for jj in range(JF + 2):
                if jj < JF:
                    j = jj
                    s = next_gu()
                    gp = gu[:, 2 * s, :]
                    up = gu[:, 2 * s + 1, :]
                    for k in range(KD):
                        S.add("pe", mk("matmul", gp, win[:, k, j * 128:(j + 1) * 128], xn[:, k, :],
                                                                       start=(k == 0), stop=(k == KD - 1)),
                              reads=[xkeys[k], ("win", k, (j * 128) // W), ("win", k, ((j + 1) * 128 - 1) // W)],
                              writes=[("gu", s)])
                    for k in range(KD):
                        c0 = DFF + j * 128
                        S.add("pe", mk("matmul", up, win[:, k, c0:c0 + 128], xn[:, k, :],
                                                                          start=False, stop=(k == KD - 1),
                                                                          skip_group_check=True),
                              reads=[xkeys[k], ("win", k, c0 // W), ("win", k, (c0 + 127) // W)],
                              writes=[("gu", s)])
                    sg, gj = sgs[j % 4], gjs[j % 4]
                    S.add("act", mk("activation", out=sg[:], in_=gp, func=AF.Silu),
                          reads=[("gu", s)], writes=[("sg", j % 4)])
                    S.add("dve", mk("tensor_tensor", out=gj[:], in0=up, in1=sg[:], op=ALU.mult),
                          reads=[("gu", s), ("sg", j % 4)], writes=[("gj", j % 4)])
                if jj >= 2:
                    j = jj - 2
                    gj = gjs[j % 4]
                    for n in range(KD):
                        f = (j == 0 and n % 2 == 0)
                        S.add("pe", mk("matmul",
                            acc[:, n, :], wout[:, j, n * 128:(n + 1) * 128], gj[:],
                            start=f, stop=(j == JF - 1), skip_group_check=True),
                              reads=[("gj", j % 4), ("wout", j)], writes=[("acc", n // 2)])
                if jj == 6 and b + 1 < nb:
                    norm_b(b + 1)
            for n in range(KD):
                S.add("dve", mk("scalar_tensor_tensor", out=hb[:, n, :], in0=acc[:, n, :],
                                                                   scalar=m["gh"][:, n:n + 1], in1=hb[:, n, :],
                                                                   op0=ALU.mult, op1=ALU.add),
                      reads=[("acc", n // 2), m["key"], ("hb", b % 3)], writes=[("hb", b % 3)])
            S.add("sp", mk("dma_start", out=hview[:, :, t0:t0 + TB], in_=hb[:]),
                  reads=[("hb", b % 3)], writes=[("hT", t0)], dma=True)

        load(0)
        if nb > 1:
            load(1)
        norm_a(0)
        norm_b(0)
        for b in range(nb):
            if b + 2 < nb:
                load(b + 2)
            if b + 1 < nb:
                norm_a(b + 1)
            main(b)


def ffn_blocks(do_ctx=True):
    return [(t0, "lat" if t0 < NL else "ctx") for t0 in range(0, NT if do_ctx else NL, TB)]


def hT_keys(t0, nt):
    return [("hT", t) for t in range(t0, t0 + nt, TB)]


class BankRR:
    def __init__(self, C, n=8):
        self.t = C.ps("bank", [128, n, 512])
        self.n = n
        self.i = 0

    def next(self):
        b = self.i
        self.i = (self.i + 1) % self.n
        return b, self.t, ("bank", b)


def m1_phase(C, hT_d, Wd, vecs, mods, ropeC_d, ropeS_d, O, exch=None):
    S = C.S
    blocks = [(t0, 512, "lat") for t0 in range(0, NL, 512)] + [(NL, 256, "ctx")]
    with C.scope():
        winx = C.sb("winx", [128, KD, WINX], BF16)
        wuq = C.sb("wuq", [128, 2, 1152], BF16)
        wukv = C.sb("wukv", [128, 768], BF16)
        WL = WLoader(C, 1152)
        for k in range(KD):
            for s2 in range(2):
                c0, c1 = s2 * 1056, (s2 + 1) * 1056
                WL.load(winx[:, k, c0:c1], Wd["winx"][k * 128:(k + 1) * 128, c0:c1], 1056, ("winx", k))
        for k in range(2):
            WL.load(wuq[:, k, :], Wd["wuq"][k * 128:(k + 1) * 128, :], 1152, "wuq")
        WL.load(wukv[:], Wd["wukv"], 768, "wukv")
        hb = [C.sb("mhb%d" % i, [128, KD, 512], F32) for i in range(2)]
        xn = C.sb("mxn", [128, KD, 512], BF16)
        sq = C.sb("msq", [128, KD, 512], BF16)
        rstd = C.sb("mrstd", [128, 512], F32)
        tmp = C.sb("mtmp", [128, 512], F32)
        rc = C.sb("rc", [128, 512], F32)
        rs = C.sb("rs", [128, 512], F32)
        st3 = [C.sb("st3_%d" % i, [128, 3, 512], F32) for i in range(2)]
        sq2 = C.sb("sq2", [128, 2, 512], BF16)
        cqn = C.sb("cqn", [128, 2, 512], BF16)
        ckvn = C.sb("ckvn", [128, 512], BF16)
        sqk = C.sb("sqk", [128, 512], BF16)
        tA = C.sb("tA", [128, 512], F32)
        t1 = [C.sb("t1_%d" % i, [128, 512], F32) for i in range(2)]
        t2 = [C.sb("t2_%d" % i, [128, 512], F32) for i in range(2)]
        rq = [C.sb("rq_%d" % i, [128, 512], F32) for i in range(2)]
        tq = [C.sb("tq_%d" % i, [128, 512], F32) for i in range(2)]
        sqh = [C.sb("sqh_%d" % i, [128, 512], BF16) for i in range(2)]
        qh = [C.sb("qh_%d" % i, [128, 512], BF16) for i in range(2)]
        kh = [C.sb("kh_%d" % i, [128, 512], BF16) for i in range(2)]
        nst = [C.sb("nst_%d" % i, [128, 2, 512], BF16) for i in range(2)]
        nva = C.sb("nva", [128, 4, 4, 128], BF16)
        va = C.sb("va", [128, 4, 6, 128], BF16)
        B = BankRR(C, 8)
        ones_b, ones_bd = C.t["ones_b"], C.t["ones_bd"]
        S.add("pool", mk("memset", nva[:], 1.0), writes=["nva"])
        S.add("pool", mk("memset", va[:], 1.0), writes=["va"])
        hview = hT_d.rearrange("(k p) t -> p k t", p=128)
        V = lambda c: vecs[:, c:c + 1]
        exch_done = set()

        def load(bi):
            t0, nt, kind = blocks[bi]
            h = hb[bi % 2]
            S.add("sp", mk("dma_start", out=h[:, :, 0:nt], in_=hview[:, :, t0:t0 + nt]),
                  reads=hT_keys(t0, nt), writes=[("mhb", bi % 2)], dma=True)

        load(0)
        for bi, (t0, nt, kind) in enumerate(blocks):
            if bi + 1 < len(blocks):
                load(bi + 1)
            m = mods[("mix", kind)]
            h = hb[bi % 2]
            hk = ("mhb", bi % 2)
            if kind == "lat":
                q4, off = t0 // 1024, t0 % 1024
                dst2 = lambda nm, q4=q4: O["L_" + nm][q4]
            else:
                q4, off = "c", 0
                dst2 = lambda nm: O["L_" + nm + "c"]
            S.add("sp", mk("dma_start", out=rc[64:96, 0:nt], in_=ropeC_d[:, t0:t0 + nt]),
                  writes=["rc"], dma=True)
            S.add("sp", mk("dma_start", out=rs[64:96, 0:nt], in_=ropeS_d[:, t0:t0 + nt]),
                  writes=["rs"], dma=True)
            S.add("act", mk("activation", out=sq[:, :, 0:nt], in_=h[:, :, 0:nt], func=AF.Square),
                  reads=[hk], writes=["msq"])
            b, bt, bk = B.next()
            for k in range(KD):
                S.add("pe", mk("matmul", bt[:, b, 0:nt], ones_b[:], sq[:, k, 0:nt],
                                                               start=(k == 0), stop=(k == KD - 1)),
                      reads=["msq", "ones_b"], writes=[bk])
            rstd_from_psum(C, bt[:, b, 0:nt], 128, nt, 1.0 / D, rstd[:, 0:nt], tmp[:, 0:nt], [bk], "mrstd", "mtmp", lnexp=True)
            for k in range(KD):
                S.add("dve", mk("scalar_tensor_tensor",
                    out=tmp[:, 0:nt], in0=h[:, k, 0:nt], scalar=m["gs"][:, k:k + 1], in1=rstd[:, 0:nt],
                    op0=ALU.mult, op1=ALU.mult), reads=[hk, "mrstd", m["key"]], writes=["mtmp"])
                S.add("act", mk("activation", out=xn[:, k, 0:nt], in_=tmp[:, 0:nt], func=AF.Identity,
                                                                   bias=m["sh"][:, k:k + 1], scale=1.0),
                      reads=["mtmp", m["key"]], writes=[("mxn", k)])

            def proj(col0, M):
                b, bt, bk = B.next()
                for k in range(KD):
                    S.add("pe", mk("matmul", bt[0:M, b, 0:nt], winx[:, k, col0:col0 + M], xn[:, k, 0:nt],
                                                             start=(k == 0), stop=(k == KD - 1)),
                          reads=[("mxn", k), ("winx", k)], writes=[bk])
                return bt[0:M, b, 0:nt], bk

            s3 = st3[0]
            for c in range(3):
                p, pk = proj(c * 128, 128)
                S.add("act", mk("activation", out=s3[:, c, 0:nt], in_=p, func=AF.Copy),
                      reads=[pk], writes=[("st3", 0)])
            S.add("sp", mk("dma_start", out=dst2("lx").rearrange("(c p) t -> p c t", p=128)[:, :, off:off + nt],
                                                     in_=s3[:, :, 0:nt]),
                  reads=[("st3", 0)], writes=[("L", "lx", q4)], dma=True)
            s3 = st3[1]
            for c in range(3):
                p, pk = proj(384 + c * 128, 128)
                S.add("act", mk("activation", out=s3[:, c, 0:nt], in_=p, func=AF.Gelu_apprx_tanh),
                      reads=[pk], writes=[("st3", 1)])
            S.add("sp", mk("dma_start", out=O["lgel"].rearrange("(c p) t -> p c t", p=128)[:, :, t0:t0 + nt],
                                                     in_=s3[:, :, 0:nt]),
                  reads=[("st3", 1)], writes=[("lgel", t0)], dma=True)
            pcq = []
            for c in range(2):
                p, pk = proj(768 + c * 128, 128)
                pcq.append((p, pk))
                S.add("act", mk("activation", out=sq2[:, c, 0:nt], in_=p, func=AF.Square),
                      reads=[pk], writes=["sq2"])
            b, bt, bk = B.next()
            for c in range(2):
                S.add("pe", mk("matmul", bt[:, b, 0:nt], ones_b[:], sq2[:, c, 0:nt], start=(c == 0), stop=(c == 1)),
                      reads=["sq2", "ones_b"], writes=[bk])
            rstd_from_psum(C, bt[:, b, 0:nt], 128, nt, 1.0 / 256, rstd[:, 0:nt], tmp[:, 0:nt], [bk], "mrstd", "mtmp", lnexp=True)
            for c in range(2):
                p, pk = pcq[c]
                S.add("dve", mk("scalar_tensor_tensor", out=cqn[:, c, 0:nt], in0=p, scalar=V(24 + c),
                                                                        in1=rstd[:, 0:nt], op0=ALU.mult, op1=ALU.mult),
                      reads=[pk, "mrstd", "vecs"], writes=["cqn"])
            p, pk = proj(1024, 128)
            S.add("act", mk("activation", out=sq2[:, 0, 0:nt], in_=p, func=AF.Square), reads=[pk], writes=["sq2"])
            b, bt, bk = B.next()
            S.add("pe", mk("matmul", bt[:, b, 0:nt], ones_b[:], sq2[:, 0, 0:nt], start=True, stop=True),
                  reads=["sq2", "ones_b"], writes=[bk])
            rstd_from_psum(C, bt[:, b, 0:nt], 128, nt, 1.0 / 128, rstd[:, 0:nt], tmp[:, 0:nt], [bk], "mrstd", "mtmp", lnexp=True)
            S.add("dve", mk("scalar_tensor_tensor", out=ckvn[:, 0:nt], in0=p, scalar=V(26), in1=rstd[:, 0:nt],
                                                               op0=ALU.mult, op1=ALU.mult),
                  reads=[pk, "mrstd", "vecs"], writes=["ckvn"])
            pkr, pkrk = proj(1152, 96)
            pks, pksk = proj(1248, 96)
            S.add("act", mk("activation", out=sqk[64:96, 0:nt], in_=pkr[64:96, :], func=AF.Square),
                  reads=[pkrk], writes=["sqk_r"])
            S.add("dve", mk("scalar_tensor_tensor", out=tA[64:96, 0:nt], in0=pkr[64:96, :], scalar=vecs[64:96, 29:30],
                                                          in1=rc[64:96, 0:nt], op0=ALU.mult, op1=ALU.mult),
                  reads=[pkrk, "rc", "vecs"], writes=["tA"])
            S.add("dve", mk("scalar_tensor_tensor", out=tmp[64:96, 0:nt], in0=pks[64:96, :], scalar=vecs[64:96, 30:31],
                                                          in1=rs[64:96, 0:nt], op0=ALU.mult, op1=ALU.mult),
                  reads=[pksk, "rs", "vecs"], writes=["mtmp"])
            S.add("dve", mk("tensor_tensor", out=tA[64:96, 0:nt], in0=tA[64:96, 0:nt], in1=tmp[64:96, 0:nt], op=ALU.add),
                  reads=["tA", "mtmp"], writes=["tA"])
            for which, col0, gcol, oname in (("q", 1344, 31, "nqT"), ("k", 1600, 32, "nkT")):
                ns = nst[0 if which == "q" else 1]
                nk_ = ("nst", which)
                for c in range(2):
                    p, pk = proj(col0 + c * 128, 128)
                    S.add("act", mk("activation", out=sq2[:, 0, 0:nt], in_=p, func=AF.Square), reads=[pk], writes=["sq2"])
                    b, bt, bk = B.next()
                    S.add("pe", mk("matmul", bt[:, b, 0:nt], ones_bd[:], sq2[:, 0, 0:nt], start=True, stop=True),
                          reads=["sq2", "ones_bd"], writes=[bk])
                    rstd_from_psum(C, bt[:, b, 0:nt], 128, nt, 1.0 / 64, rstd[:, 0:nt], tmp[:, 0:nt], [bk], "mrstd", "mtmp", lnexp=True)
                    S.add("dve", mk("scalar_tensor_tensor",
                        out=ns[:, c, 0:nt], in0=p, scalar=V(gcol), in1=rstd[:, 0:nt], op0=ALU.mult, op1=ALU.mult),
                          reads=[pk, "mrstd", "vecs"], writes=[nk_])
                odst = (O["nqT"].rearrange("(c p) t -> p c t", p=128)[:, :, t0:t0 + nt] if which == "q"
                        else dst2("nkT").rearrange("(c p) t -> p c t", p=128)[:, :, off:off + nt])
                S.add("sp", mk("dma_start", out=odst, in_=ns[:, :, 0:nt]),
                      reads=[nk_], writes=[("L", oname, q4) if which == "k" else (oname, t0)], dma=True)
            nsub = nt // 128
            for sb_ in range(nsub):
                b, bt, bk = B.next()
                for k in range(KD):
                    S.add("pe", mk("matmul", bt[:, b, 0:256], xn[:, k, sb_ * 128:(sb_ + 1) * 128],
                                                                           winx[:, k, 1856:2112], start=(k == 0), stop=(k == KD - 1)),
                          reads=[("mxn", k), ("winx", k)], writes=[bk])
                pv = bt[:, b, 0:256].rearrange("p (h d) -> p h d", h=4)
                S.add("act", mk("activation", out=nva[:, sb_, 0:4:2, 0:64], in_=pv[:, 0:4:2, :], func=AF.Copy),
                      reads=[bk], writes=["nva"])
                S.add("act", mk("activation", out=nva[:, sb_, 1:4:2, 64:128], in_=pv[:, 1:4:2, :], func=AF.Copy),
                      reads=[bk], writes=["nva"])
            S.add("sp", mk("dma_start",
                out=dst2("nvA")[off:off + nt, :].rearrange("(s p) (h d) -> p s h d", p=128, h=4), in_=nva[:, 0:nsub]),
                  reads=["nva"], writes=[("L", "nvA", q4)], dma=True)
            for sb_ in range(nsub):
                b, bt, bk = B.next()
                S.add("pe", mk("matmul", bt[:, b, 0:384], ckvn[:, sb_ * 128:(sb_ + 1) * 128],
                                                                  wukv[:, 384:768], start=True, stop=True),
                      reads=["ckvn", "wukv"], writes=[bk])
                pv = bt[:, b, 0:384].rearrange("p (h d) -> p h d", h=6)
                S.add("act", mk("activation", out=va[:, sb_, 0:6:2, 0:64], in_=pv[:, 0:6:2, :], func=AF.Copy),
                      reads=[bk], writes=["va"])
                S.add("act", mk("activation", out=va[:, sb_, 1:6:2, 64:128], in_=pv[:, 1:6:2, :], func=AF.Copy),
                      reads=[bk], writes=["va"])
            S.add("sp", mk("dma_start",
                out=dst2("vA")[off:off + nt, :].rearrange("(s p) (h d) -> p s h d", p=128, h=6), in_=va[:, 0:nsub]),
                  reads=["va"], writes=[("L", "vA", q4)], dma=True)
            for hd in range(6):
                i2 = hd % 2
                b, bt, bk = B.next()
                pkn = bt[0:64, b, 0:nt]
                S.add("pe", mk("matmul", pkn, wukv[:, hd * 64:(hd + 1) * 64], ckvn[:, 0:nt], start=True, stop=True),
                      reads=["ckvn", "wukv"], writes=[bk])
                S.add("act", mk("activation", out=sqk[0:64, 0:nt], in_=pkn, func=AF.Square),
                      reads=[bk], writes=["sqk_n"])
                b2, bt2, bk2 = B.next()
                pss = bt2[0:96, b2, 0:nt]
                S.add("pe", mk("matmul", pss, ones_b[0:96, 0:96], sqk[0:96, 0:nt], start=True, stop=True),
                      reads=["sqk_n", "sqk_r", "ones_b"], writes=[bk2])
                r_, t_ = rq[i2], tq[i2]
                rstd_from_psum(C, pss, 96, nt, 1.0 / 96, r_[0:96, 0:nt], t_[0:96, 0:nt], [bk2], ("rq", i2), ("tq", i2), lnexp=True)
                khh = kh[i2]
                S.add("dve", mk("scalar_tensor_tensor",
                    out=khh[0:64, 0:nt], in0=pkn, scalar=vecs[0:64, 29:30], in1=r_[0:64, 0:nt], op0=ALU.mult, op1=ALU.mult),
                      reads=[bk, ("rq", i2), "vecs"], writes=[("kh", i2)])
                S.add("dve", mk("tensor_tensor", out=khh[64:96, 0:nt], in0=tA[64:96, 0:nt],
                                                                        in1=r_[64:96, 0:nt], op=ALU.mult),
                      reads=["tA", ("rq", i2)], writes=[("kh", i2)])
                S.add("sp", mk("dma_start", out=dst2("kT")[hd * 96:(hd + 1) * 96, off:off + nt], in_=khh[0:96, 0:nt]),
                      reads=[("kh", i2)], writes=[("L", "kT", q4)], dma=True)
            for hd in range(6):
                i2 = hd % 2
                b, bt, bk = B.next()
                pq = bt[0:96, b, 0:nt]
                for k in range(2):
                    S.add("pe", mk("matmul", pq, wuq[:, k, hd * 192:hd * 192 + 96], cqn[:, k, 0:nt],
                                                                      start=(k == 0), stop=(k == 1)),
                          reads=["cqn", "wuq"], writes=[bk])
                b3, bt3, bk3 = B.next()
                pw = bt3[0:96, b3, 0:nt]
                for k in range(2):
                    S.add("pe", mk("matmul", pw, wuq[:, k, hd * 192 + 96:hd * 192 + 192], cqn[:, k, 0:nt],
                                                                      start=(k == 0), stop=(k == 1)),
                          reads=["cqn", "wuq"], writes=[bk3])
                sh_ = sqh[i2]
                S.add("act", mk("activation", out=sh_[0:96, 0:nt], in_=pq, func=AF.Square),
                      reads=[bk], writes=[("sqh", i2)])
                b2, bt2, bk2 = B.next()
                pss = bt2[0:96, b2, 0:nt]
                S.add("pe", mk("matmul", pss, ones_b[0:96, 0:96], sh_[0:96, 0:nt], start=True, stop=True),
                      reads=[("sqh", i2), "ones_b"], writes=[bk2])
                r_, t_ = rq[i2], tq[i2]
                rstd_from_psum(C, pss, 96, nt, 1.0 / 96, r_[0:96, 0:nt], t_[0:96, 0:nt], [bk2], ("rq", i2), ("tq", i2), lnexp=True)
                qhh, a1, a2 = qh[i2], t1[i2], t2[i2]
                S.add("dve", mk("scalar_tensor_tensor",
                    out=qhh[0:64, 0:nt], in0=pq[0:64, :], scalar=vecs[0:64, 27:28], in1=r_[0:64, 0:nt], op0=ALU.mult, op1=ALU.mult),
                      reads=[bk, ("rq", i2), "vecs"], writes=[("qh", i2)])
                S.add("dve", mk("scalar_tensor_tensor",
                    out=a1[64:96, 0:nt], in0=pq[64:96, :], scalar=vecs[64:96, 27:28], in1=rc[64:96, 0:nt], op0=ALU.mult, op1=ALU.mult),
                      reads=[bk, "rc", "vecs"], writes=[("t1", i2)])
                S.add("dve", mk("scalar_tensor_tensor",
                    out=a2[64:96, 0:nt], in0=pw[64:96, :], scalar=vecs[64:96, 28:29], in1=rs[64:96, 0:nt], op0=ALU.mult, op1=ALU.mult),
                      reads=[bk3, "rs", "vecs"], writes=[("t2", i2)])
                S.add("dve", mk("tensor_tensor", out=a1[64:96, 0:nt], in0=a1[64:96, 0:nt], in1=a2[64:96, 0:nt], op=ALU.add),
                      reads=[("t1", i2), ("t2", i2)], writes=[("t1", i2)])
                S.add("dve", mk("tensor_tensor", out=qhh[64:96, 0:nt], in0=a1[64:96, 0:nt],
                                                                              in1=r_[64:96, 0:nt], op=ALU.mult),
                      reads=[("t1", i2), ("rq", i2)], writes=[("qh", i2)])
                S.add("sp", mk("dma_start", out=O["qT"][hd, :, t0:t0 + nt], in_=qhh[0:96, 0:nt]),
                      reads=[("qh", i2)], writes=[("qT", hd, t0)], dma=True)
            if exch is not None:
                for qq in range(4):
                    if bi == min(2 * qq + 2, len(blocks) - 1) or (bi == len(blocks) - 1 and 2 * qq + 2 > bi):
                        if qq not in exch_done:
                            exch_done.add(qq)
                            exch(qq)


def lru_phase(C, I, vecs, lruw_d, halfmask_d, yT_d):
    S = C.S
    XW = 2 + NCX + 3 + 2 * NL + 2
    CT0 = 2
    LT0 = 2 + NCX + 3
    SEG = 512
    with C.scope():
        lw = C.sb("lw", [128, 1536], BF16)
        with C.scope():
            WL = WLoader(C, 1536, n=1)
            WL.load(lw[:], lruw_d, 1536, "lw")
        hm = C.sb("hm", [128, 2], F32)
        S.add("sp", mk("dma_start", out=hm[:], in_=halfmask_d), writes=["hm"], dma=True)
        par = C.sb("lpar", [128, 18], F32)
        e1 = C.sb("le1", [128, 6], F32)
        one_t = C.sb("one_t", [128, 1], F32)
        S.add("pool", mk("memset", one_t[:], 1.0), writes=["one_t"])
        S.add("act", mk("activation", out=e1[:], in_=vecs[:, 60:66], func=AF.Exp, scale=-1.0), reads=["vecs"], writes=["le1"])
        S.add("act", mk("activation", out=e1[:], in_=e1[:], func=AF.Ln, bias=one_t[:], scale=1.0), reads=["le1", "one_t"], writes=["le1"])
        S.add("dve", mk("tensor_scalar", out=par[:, 0:6], in0=e1[:], scalar1=-4.0, scalar2=None, op0=ALU.mult),
              reads=["le1"], writes=["lpar"])
        S.add("dve", mk("tensor_scalar", out=par[:, 6:18], in0=vecs[:, 48:60], scalar1=0.5, scalar2=None, op0=ALU.mult),
              reads=["vecs"], writes=["lpar"])
        xc = C.sb("xc", [128, XW], F32)
        xcb = C.sb("xcb", [128, XW], BF16)
        hsum = C.sb("hsum", [128, NT], F32)
        lg = C.sb("lg", [128, NT], F32)
        NB = 2
        tr = [C.sb("tr%d" % i, [128, SEG], F32) for i in range(NB)]
        ti = [C.sb("ti%d" % i, [128, SEG], F32) for i in range(NB)]
        aa = [C.sb("aa%d" % i, [128, SEG], F32) for i in range(NB)]
        a2 = [C.sb("a2%d" % i, [128, SEG], F32) for i in range(NB)]
        uu = [C.sb("uu%d" % i, [128, SEG], F32) for i in range(NB)]
        hh = [C.sb("hh%d" % i, [128, SEG], F32) for i in range(NB)]
        yb = C.sb("yb", [128, NT], BF16)
        stt = C.sb("lstate", [128, 1], F32)
        B = BankRR(C, 4)
        phalf = C.t["phalf"]
        for c in range(3):
            with C.scope():
                stg = C.sb("stg", [128, 2, NL], F32)
                xf = C.sb("xf", [128, XW], F32)
                S.add("dve", mk("memset", xf[:], 0.0), writes=["xf"])
                for hf in range(2):
                    for q4 in range(4):
                        S.add("sp", mk("dma_start", out=stg[:, hf, q4 * 1024:(q4 + 1) * 1024],
                                       in_=I["G_lx"][q4, hf, c * 128:(c + 1) * 128, :]),
                              reads=[("G", "lx", q4)], writes=[("stg", hf)], dma=True)
                S.add("sp", mk("dma_start", out=xf[:, CT0:CT0 + NCX], in_=I["L_lxc"][c * 128:(c + 1) * 128, :]),
                      reads=[("L", "lx", "c"), "xf"], writes=["xf"], dma=True)
                lat = xf[:, LT0:LT0 + 2 * NL].rearrange("p (r h c) -> p r h c", h=2, c=32)
                for hf in range(2):
                    eng = "act" if hf == 0 else "dve"
                    src = stg[:, hf, :].rearrange("p (r c) -> p r c", c=32)
                    if eng == "act":
                        S.add("act", mk("activation", out=lat[:, :, hf, :], in_=src, func=AF.Copy),
                              reads=[("stg", hf), "xf"], writes=["xf"])
                    else:
                        S.add("dve", mk("tensor_copy", out=lat[:, :, hf, :], in_=src),
                              reads=[("stg", hf), "xf"], writes=["xf"])
                n = XW - 3
                S.add("dve", mk("tensor_scalar", out=xc[:, 2:2 + n], in0=xf[:, 0:n], scalar1=vecs[:, 36 + 4 * c:37 + 4 * c],
                                                              scalar2=vecs[:, 33 + c:34 + c], op0=ALU.mult, op1=ALU.add),
                      reads=["xf", "vecs"], writes=["xc"])
                for j in range(1, 4):
                    S.add("dve", mk("scalar_tensor_tensor", out=xc[:, 2:2 + n], in0=xf[:, j:j + n],
                                                                              scalar=vecs[:, 36 + 4 * c + j:37 + 4 * c + j],
                                                                              in1=xc[:, 2:2 + n], op0=ALU.mult, op1=ALU.add),
                          reads=["xf", "vecs", "xc"], writes=["xc"])
                S.add("act", mk("activation", out=xcb[:, 2:2 + n], in_=xc[:, 2:2 + n], func=AF.Copy), reads=["xc"], writes=["xcb"])
            S.add("sp", mk("dma_start", out=lg[:], in_=I["lgel"][c * 128:(c + 1) * 128, :]),
                  reads=[("lgel", t) for t in list(range(0, NL, 512)) + [NL]], writes=["lg"], dma=True)
            segs = [(CT0, NCX, "ctx", 0)] + [(LT0 + i * SEG, SEG, "lat", i) for i in range(2 * NL // SEG)]
            for d in range(2):
                order = segs if d == 0 else [segs[0]] + segs[:0:-1]
                pidx = d * 3 + c
                hc = par[:, pidx:pidx + 1]
                hba = par[:, 6 + pidx:7 + pidx]
                hbx = par[:, 12 + pidx:13 + pidx]
                wa = lw[:, (0 * 6 + pidx) * 128:(0 * 6 + pidx + 1) * 128]
                wx = lw[:, (1 * 6 + pidx) * 128:(1 * 6 + pidx + 1) * 128]
                first = True
                for si, (x0, n, kind, li) in enumerate(order):
                    ib = si % NB
                    r_, i_, a_, q_, u_, h_ = tr[ib], ti[ib], aa[ib], a2[ib], uu[ib], hh[ib]
                    for p0 in range(0, n, 512):
                        pn = min(512, n - p0)
                        b, bt, bk = B.next()
                        S.add("pe", mk("matmul",
                            bt[:, b, 0:pn], wa, xcb[:, x0 + p0:x0 + p0 + pn], start=True, stop=True),
                              reads=["xcb", "lw"], writes=[bk])
                        S.add("act", mk("activation",
                            out=r_[:, p0:p0 + pn], in_=bt[:, b, 0:pn], func=AF.Tanh, bias=hba, scale=0.5),
                              reads=[bk, "lpar"], writes=[("tr", ib)])
                        b, bt, bk = B.next()
                        S.add("pe", mk("matmul",
                            bt[:, b, 0:pn], wx, xcb[:, x0 + p0:x0 + p0 + pn], start=True, stop=True),
                              reads=["xcb", "lw"], writes=[bk])
                        S.add("act", mk("activation",
                            out=i_[:, p0:p0 + pn], in_=bt[:, b, 0:pn], func=AF.Tanh, bias=hbx, scale=0.5),
                              reads=[bk, "lpar"], writes=[("ti", ib)])
                    S.add("act", mk("activation", out=a_[:, 0:n], in_=r_[:, 0:n], func=AF.Exp,
                                                                                 bias=hc, scale=hc),
                          reads=[("tr", ib), "lpar"], writes=[("aa", ib)])
                    S.add("dve", mk("tensor_tensor", out=q_[:, 0:n], in0=a_[:, 0:n], in1=a_[:, 0:n], op=ALU.mult),
                          reads=[("aa", ib)], writes=[("a2", ib)])
                    S.add("dve", mk("tensor_scalar", out=q_[:, 0:n], in0=q_[:, 0:n], scalar1=-0.25, scalar2=0.25,
                                                                      op0=ALU.mult, op1=ALU.add),
                          reads=[("a2", ib)], writes=[("a2", ib)])
                    S.add("act", mk("activation", out=q_[:, 0:n], in_=q_[:, 0:n], func=AF.Sqrt),
                          reads=[("a2", ib)], writes=[("a2", ib)])
                    S.add("dve", mk("scalar_tensor_tensor",
                        out=u_[:, 0:n], in0=i_[:, 0:n], scalar=1.0, in1=xc[:, x0:x0 + n], op0=ALU.add, op1=ALU.mult),
                          reads=[("ti", ib), "xc"], writes=[("uu", ib)])
                    S.add("dve", mk("tensor_tensor", out=u_[:, 0:n], in0=u_[:, 0:n], in1=q_[:, 0:n], op=ALU.mult),
                          reads=[("uu", ib), ("a2", ib)], writes=[("uu", ib)])
                    init = 0.0 if first else stt[:, 0:1]
                    if d == 0:
                        S.add("dve", mk("tensor_tensor_scan",
                            out=h_[:, 0:n], data0=a_[:, 0:n], data1=u_[:, 0:n], initial=init, op0=ALU.mult, op1=ALU.add),
                              reads=[("aa", ib), ("uu", ib), "lstate"], writes=[("hh", ib)])
                        S.add("dve", mk("tensor_copy", out=stt[:, 0:1], in_=h_[:, n - 1:n]),
                              reads=[("hh", ib)], writes=["lstate"])
                    else:
                        S.add("dve", mk("tensor_tensor_scan",
                            out=h_[:, 0:n][:, ::-1], data0=a_[:, 0:n][:, ::-1], data1=u_[:, 0:n][:, ::-1],
                            initial=init, op0=ALU.mult, op1=ALU.add),
                              reads=[("aa", ib), ("uu", ib), "lstate"], writes=[("hh", ib)])
                        S.add("dve", mk("tensor_copy", out=stt[:, 0:1], in_=h_[:, 0:1]),
                              reads=[("hh", ib)], writes=["lstate"])
                    first = False
                    if kind == "ctx":
                        if d == 0:
                            S.add("act", mk("activation", out=hsum[:, NL:NT], in_=h_[:, 0:NCX], func=AF.Copy),
                                  reads=[("hh", ib)], writes=[("hsum", "c")])
                        else:
                            S.add("dve", mk("tensor_tensor", out=hsum[:, NL:NT], in0=hsum[:, NL:NT], in1=h_[:, 0:NCX], op=ALU.add),
                                  reads=[("hh", ib), ("hsum", "c")], writes=[("hsum", "c")])
                    else:
                        rows = SEG // 64
                        hv = h_[:, 0:SEG].rearrange("p (r h c) -> p r h c", h=2, c=32)
                        ov = hsum[:, li * (SEG // 2):(li + 1) * (SEG // 2)].rearrange("p (r c) -> p r c", c=32)
                        hk_ = ("hsum", li)
                        if d == 0:
                            S.add("dve", mk("tensor_scalar", out=ov, in0=hv[:, :, 0, :], scalar1=hm[:, 0:1], scalar2=None,
                                                                                op0=ALU.mult),
                                  reads=[("hh", ib), "hm"], writes=[hk_])
                        else:
                            S.add("dve", mk("scalar_tensor_tensor", out=ov, in0=hv[:, :, 0, :], scalar=hm[:, 0:1], in1=ov,
                                                                                       op0=ALU.mult, op1=ALU.add),
                                  reads=[("hh", ib), "hm", hk_], writes=[hk_])
                        S.add("dve", mk("scalar_tensor_tensor", out=ov, in0=hv[:, :, 1, :], scalar=hm[:, 1:2], in1=ov,
                                                                                   op0=ALU.mult, op1=ALU.add),
                              reads=[("hh", ib), "hm", hk_], writes=[hk_])
            hkeys = [("hsum", "c")] + [("hsum", i) for i in range(2 * NL // SEG)]
            S.add("dve", mk("tensor_tensor", out=yb[:], in0=hsum[:], in1=lg[:], op=ALU.mult),
                  reads=hkeys + ["lg"], writes=["yb"])
            S.add("sp", mk("dma_start", out=yT_d[c * 128:(c + 1) * 128, :], in_=yb[:]),
                  reads=["yb"], writes=[("yT", c)], dma=True)


def mla_phase(C, I, yT_d, do_ctx=True):
    S = C.S
    NK = NCX + 2 * NL
    NJ = NK // 128
    with C.scope():
        kts = [C.sb("kt%d" % i, [128, NK], BF16) for i in range(2)]
        vas = [C.sb("vas%d" % i, [128, NJ, 128], BF16) for i in range(2)]
        qts = [C.sb("qt%d" % i, [128, NT], BF16) for i in range(2)]
        pts = [C.sb("pt%d" % i, [128, 2, 512], BF16) for i in range(3)]
        osb = [C.sb("osb%d" % i, [128, 512], F32) for i in range(2)]
        rcp = [C.sb("rcp%d" % i, [128, 512], F32) for i in range(2)]
        ysb = [C.sb("ysb%d" % i, [128, 512], BF16) for i in range(2)]
        sps = C.ps("sps", [128, 6, 512])
        ops = C.ps("ops", [128, 2, 512])
        kq_all = [(nm, t) for nm in ("kT0", "kT1") for t in range(0, NL, 512)]

        def loadh(hd):
            i2 = hd % 2
            kt, va, qt = kts[i2], vas[i2], qts[i2]
            S.add("sp", mk("dma_start", out=kt[0:96, 0:NCX], in_=I["L_kTc"][hd * 96:(hd + 1) * 96, :]),
                  reads=[("L", "kT", "c")], writes=[("kt", i2, "c")], dma=True)
            S.add("sp", mk("dma_start", out=va[:, 0:2, :],
                           in_=I["L_vAc"][:, hd * 128:(hd + 1) * 128].rearrange("(j p) d -> p j d", p=128)),
                  reads=[("L", "vA", "c")], writes=[("vas", i2, "c")], dma=True)
            for hf in range(2):
                for q4 in range(4):
                    k0 = NCX + hf * NL + q4 * 1024
                    S.add("sp", mk("dma_start", out=kt[0:96, k0:k0 + 1024], in_=I["G_kT"][q4, hf, hd * 96:(hd + 1) * 96, :]),
                          reads=[("G", "kT", q4)], writes=[("kt", i2, hf, q4)], dma=True)
                    j0 = 2 + hf * 32 + q4 * 8
                    S.add("sp", mk("dma_start", out=va[:, j0:j0 + 8, :],
                                   in_=I["G_vA"][q4, hf, :, hd * 128:(hd + 1) * 128].rearrange("(j p) d -> p j d", p=128)),
                          reads=[("G", "vA", q4)], writes=[("vas", i2, hf, q4)], dma=True)
            S.add("sp", mk("dma_start", out=qt[0:96, :], in_=I["qT"][hd, :, :]),
                  reads=[("qT", hd, t) for t in list(range(0, NL, 512)) + [NL]], writes=[("qts", i2)], dma=True)

        def jpart(j):
            return ("c",) if j < 2 else ((j - 2) // 32, ((j - 2) % 32) // 8)

        cnt = [0, 0]
        loadh(0)
        for hd in range(6):
            if hd + 1 < 6:
                loadh(hd + 1)
            i2 = hd % 2
            kt, va, qt = kts[i2], vas[i2], qts[i2]
            qblocks = [(q0, 512, 0, NJ) for q0 in range(0, NL, 512)] + ([(NL, 256, 0, 2)] if do_ctx else [])
            for (q0, nq, j0, j1) in qblocks:
                ob = cnt[1] % 2
                cnt[1] += 1
                oacc = ops[:, ob, 0:nq]
                pend = []
                prs = list(range(j0, j1, 2))
                for idx in range(len(prs) + 1):
                    if idx < len(prs):
                        j = prs[idx]
                        sb_ = cnt[0] % 3
                        cnt[0] += 1
                        sp2 = sps[:, 2 * sb_:2 * sb_ + 2, 0:nq]
                        for u in range(2):
                            S.add("pe", mk("matmul", sp2[:, u, :], kt[0:96, (j + u) * 128:(j + u + 1) * 128], qt[0:96, q0:q0 + nq],
                                           start=True, stop=True),
                                  reads=[("kt", i2) + jpart(j), ("qts", i2)], writes=[("sps", sb_)])
                        pt = pts[sb_]
                        S.add("act", mk("activation", out=pt[:, :, 0:nq], in_=sp2, func=AF.Exp, scale=MLA_SCALE),
                              reads=[("sps", sb_)], writes=[("pt", sb_)])
                        pend.append((j, sb_))
                    if idx >= 1:
                        j, sb_ = pend[idx - 1]
                        pt = pts[sb_]
                        for u in range(2):
                            S.add("pe", mk("matmul", oacc, va[:, j + u, :], pt[:, u, 0:nq],
                                           start=(idx == 1 and u == 0), stop=(idx == len(prs) and u == 1)),
                                  reads=[("pt", sb_), ("vas", i2) + jpart(j)], writes=[("ops", ob)])
                o_, r_, y_ = osb[ob], rcp[ob], ysb[ob]
                lo, hi = (0, 64) if hd % 2 == 0 else (64, 128)
                slo, shi = (64, 128) if hd % 2 == 0 else (0, 64)
                S.add("dve", mk("reciprocal", out=r_[slo:shi, 0:nq], in_=oacc[slo:shi, :]),
                      reads=[("ops", ob)], writes=[("rcp", ob)])
                S.add("act", mk("activation", out=o_[lo:hi, 0:nq], in_=oacc[lo:hi, :], func=AF.Copy),
                      reads=[("ops", ob)], writes=[("osb", ob)])
                S.add("dve", mk("tensor_copy", out=r_[lo:hi, 0:nq], in_=r_[slo:shi, 0:nq]),
                      reads=[("rcp", ob)], writes=[("rcp", ob)])
                S.add("dve", mk("tensor_tensor",
                    out=y_[lo:hi, 0:nq], in0=o_[lo:hi, 0:nq], in1=r_[lo:hi, 0:nq], op=ALU.mult),
                      reads=[("osb", ob), ("rcp", ob)], writes=[("ysb", ob)])
                row0 = 384 + hd * 64
                S.add("sp", mk("dma_start",
                    out=yT_d[row0:row0 + 64, q0:q0 + nq], in_=y_[lo:hi, 0:nq]),
                      reads=[("ysb", ob)], writes=[("yT", "m", hd, q0)], dma=True)


def na_pair_plan():
    cfg = {}
    plan = []
    r0f = lambda r: min(max(r - 4, 0), 120)
    for r in range(0, 128, 2):
        lo, hi = r0f(r), r0f(r + 1) + 8
        off0, off1 = r0f(r) - r, r0f(r + 1) - r
        chunks = []
        for ci in range(lo // 2, (hi - 1) // 2 + 1):
            key = (2 * ci - r, off0, off1)
            if key not in cfg:
                cfg[key] = len(cfg)
            chunks.append((ci, cfg[key]))
        plan.append(chunks)
    return plan, cfg


NA_PLAN, NA_CFG = na_pair_plan()
NTAB = len(NA_CFG)


def na_phase(C, I, natab_d, yT_d, do_ctx=True):
    S = C.S
    with C.scope():
        nk = C.sb("nk", [128, 2, 64, 2, 64], BF16)
        nkc = C.sb("nkc", [128, 2, NCX], BF16)
        nv = C.sb("nv", [128, 64, 4, 128], BF16)
        nvc = C.sb("nvc", [128, 2, 4, 128], BF16)
        nq = C.sb("nq", [128, 2, NT], BF16)
        tab = C.sb("natab", [128, 4, NTAB, 64], F32)
        ssb = [C.sb("nssb%d" % i, [128, 5, 64], F32) for i in range(8)]
        ptl = [C.sb("nptl%d" % i, [128, 5, 64], BF16) for i in range(8)]
        ptc = [C.sb("nptc%d" % i, [128, 2, 64], BF16) for i in range(8)]
        rcp = [C.sb("nrcp%d" % i, [128, 4, 64], F32) for i in range(2)]
        ysb = [C.sb("nysb%d" % i, [128, 2, 64], BF16) for i in range(2)]
        sps = C.ps("nsps", [128, 6, 512])
        ops = C.ps("nops", [128, 2, 512])
        S.add("sp", mk("dma_start", out=tab[:], in_=natab_d), writes=["natab"], dma=True)
        with C.scope():
            nks = C.sb("nks", [128, 2, 2, NL], BF16)
            for ck in range(2):
                for hf in range(2):
                    for q4 in range(4):
                        S.add("sp", mk("dma_start", out=nks[:, ck, hf, q4 * 1024:(q4 + 1) * 1024],
                                       in_=I["G_nkT"][q4, hf, ck * 128:(ck + 1) * 128, :]),
                              reads=[("G", "nkT", q4)], writes=[("nks", ck, hf)], dma=True)
                    src = nks[:, ck, hf, :].rearrange("p (ci t) -> p ci t", t=64)
                    if hf == 0:
                        S.add("act", mk("activation", out=nk[:, ck, :, hf, :], in_=src, func=AF.Copy),
                              reads=[("nks", ck, hf)], writes=["nk"])
                    else:
                        S.add("dve", mk("tensor_copy", out=nk[:, ck, :, hf, :], in_=src),
                              reads=[("nks", ck, hf)], writes=["nk"])
        for ck in range(2):
            S.add("sp", mk("dma_start", out=nkc[:, ck, :], in_=I["L_nkTc"][ck * 128:(ck + 1) * 128, :]),
                  reads=[("L", "nkT", "c")], writes=["nkc"], dma=True)
            S.add("sp", mk("dma_start", out=nq[:, ck, :], in_=I["nqT"][ck * 128:(ck + 1) * 128, :]),
                  reads=[("nqT", t) for t in list(range(0, NL, 512)) + [NL]], writes=["nq"], dma=True)
        for hf in range(2):
            for q4 in range(4):
                S.add("sp", mk("dma_start", out=nv[hf * 64:(hf + 1) * 64, q4 * 16:(q4 + 1) * 16],
                               in_=I["G_nvA"][q4, hf].rearrange("(ci q) (h d) -> q ci h d", q=64, h=4)),
                      reads=[("G", "nvA", q4)], writes=["nv"], dma=True)
        S.add("sp", mk("dma_start", out=nvc[:], in_=I["L_nvAc"].rearrange("(j p) (h d) -> p j h d", p=128, h=4)),
              reads=[("L", "nvA", "c")], writes=["nvc"], dma=True)
        cnt = [0, 0]
        NSB = 6

        def attend(q0, hd, loc, bset):
            ck, pl = hd // 2, (hd % 2) * 64
            sb_ = cnt[0] % NSB
            cnt[0] += 1
            nl = len(loc)
            sp_ = sps[:, sb_, 0:(nl + 2) * 64].rearrange("p (j q) -> p j q", q=64)
            qap = nq[pl:pl + 64, ck, q0:q0 + 64]
            for jl, (ci, ti_) in enumerate(loc):
                S.add("pe", mk("matmul", sp_[:, jl, :], nk[pl:pl + 64, ck, ci].rearrange("p h t -> p (h t)"), qap,
                               start=True, stop=True),
                      reads=["nk", "nq"], writes=[("nsps", sb_)])
            for jc in range(2):
                S.add("pe", mk("matmul", sp_[:, nl + jc, :], nkc[pl:pl + 64, ck, jc * 128:(jc + 1) * 128], qap,
                               start=True, stop=True),
                      reads=["nkc", "nq"], writes=[("nsps", sb_)])
            bi_ = bset * 4 + hd
            if nl:
                s_, p_ = ssb[bi_], ptl[bi_]
                ti0 = loc[0][1]
                assert [t for _, t in loc] == list(range(ti0, ti0 + nl))
                S.add("dve", mk("scalar_tensor_tensor", out=s_[:, 0:nl, :], in0=sp_[:, 0:nl, :], scalar=NA_SCALE,
                                in1=tab[:, hd, ti0:ti0 + nl, :], op0=ALU.mult, op1=ALU.add),
                      reads=[("nsps", sb_), "natab"], writes=[("nssb", bi_)])
                S.add("act", mk("activation", out=p_[:, 0:nl, :], in_=s_[:, 0:nl, :], func=AF.Exp),
                      reads=[("nssb", bi_)], writes=[("nptl", bi_)])
            pc_ = ptc[bi_]
            S.add("act", mk("activation", out=pc_[:, :, :], in_=sp_[:, nl:nl + 2, :], func=AF.Exp, scale=NA_SCALE),
                  reads=[("nsps", sb_)], writes=[("nptc", bi_)])

        def pv_and_store(q0, loc, bset, ob):
            ov = ops[:, ob, 0:256].rearrange("p (h q) -> p h q", q=64)
            nl = len(loc)
            for hd in range(4):
                bi_ = bset * 4 + hd
                p_, pc_ = ptl[bi_], ptc[bi_]
                for jl, (ci, ti_) in enumerate(loc):
                    S.add("pe", mk("matmul", ov[:, hd, :], nv[:, ci, hd, :], p_[:, jl, :], start=(jl == 0), stop=False),
                          reads=[("nptl", bi_), "nv"], writes=[("nops", ob)])
                for jc in range(2):
                    S.add("pe", mk("matmul", ov[:, hd, :], nvc[:, jc, hd, :], pc_[:, jc, :],
                                   start=(nl == 0 and jc == 0), stop=(jc == 1)),
                          reads=[("nptc", bi_), "nvc"], writes=[("nops", ob)])
            r_, y_ = rcp[ob], ysb[ob]
            S.add("dve", mk("reciprocal", out=r_[64:128, 0:4:2, :], in_=ov[64:128, 0:4:2, :]),
                  reads=[("nops", ob)], writes=[("nrcp", ob)])
            S.add("dve", mk("reciprocal", out=r_[0:64, 1:4:2, :], in_=ov[0:64, 1:4:2, :]),
                  reads=[("nops", ob)], writes=[("nrcp", ob)])
            S.add("dve", mk("tensor_copy", out=r_[0:64, 0:4:2, :], in_=r_[64:128, 0:4:2, :]),
                  reads=[("nrcp", ob)], writes=[("nrcp", ob)])
            S.add("dve", mk("tensor_copy", out=r_[64:128, 1:4:2, :], in_=r_[0:64, 1:4:2, :]),
                  reads=[("nrcp", ob)], writes=[("nrcp", ob)])
            S.add("dve", mk("tensor_tensor", out=y_[0:64, :, :], in0=ov[0:64, 0:4:2, :], in1=r_[0:64, 0:4:2, :], op=ALU.mult),
                  reads=[("nops", ob), ("nrcp", ob)], writes=[("nysb", ob)])
            S.add("dve", mk("tensor_tensor", out=y_[64:128, :, :], in0=ov[64:128, 1:4:2, :], in1=r_[64:128, 1:4:2, :], op=ALU.mult),
                  reads=[("nops", ob), ("nrcp", ob)], writes=[("nysb", ob)])
            S.add("sp", mk("dma_start", out=yT_d[768:1024, q0:q0 + 64].rearrange("(c p) q -> p c q", p=128), in_=y_[:, :, :]),
                  reads=[("nysb", ob)], writes=[("yT", "n", q0)], dma=True)

        items = [(pi * 64, chunks) for pi, chunks in enumerate(NA_PLAN)]
        if do_ctx:
            items += [(NL + qi * 64, []) for qi in range(NCX // 64)]
        for hd in range(4):
            attend(items[0][0], hd, items[0][1], 0)
        for ii, (q0, loc) in enumerate(items):
            if ii + 1 < len(items):
                for hd in range(4):
                    attend(items[ii + 1][0], hd, items[ii + 1][1], (ii + 1) % 2)
            pv_and_store(q0, loc, ii % 2, ii % 2)


def outproj_phase(C, hT_d, yT_d, wo_d, mods, do_ctx=True, tick=None):
    S = C.S
    blocks = [(t0, 256, "lat") for t0 in range(0, NL, 256)] + ([(NL, 256, "ctx")] if do_ctx else [])
    with C.scope():
        wo = C.sb("wo", [128, KD, D], BF16)
        WL = WLoader(C, D, n=1)
        for k in range(KD):
            WL.load(wo[:, k, :], wo_d[k * 128:(k + 1) * 128, :], D, ("wo", k))
        yb = [C.sb("oyb%d" % i, [128, KD, 256], BF16) for i in range(2)]
        hb = [C.sb("ohb%d" % i, [128, KD, 256], F32) for i in range(2)]
        B = BankRR(C, 8)
        hview = hT_d.rearrange("(k p) t -> p k t", p=128)
        yview = yT_d.rearrange("(k p) t -> p k t", p=128)
        for bi, (t0, nt, kind) in enumerate(blocks):
            i2 = bi % 2
            m = mods[("mix", kind)]
            y_, h_ = yb[i2], hb[i2]
            if tick is not None:
                tick(4)
            S.add("sp", mk("dma_start", out=y_[:, :, 0:nt], in_=yview[:, :, t0:t0 + nt]),
                  reads=["yT_all"], writes=[("oyb", i2)], dma=True)
            S.add("sp", mk("dma_start", out=h_[:, :, 0:nt], in_=hview[:, :, t0:t0 + nt]),
                  reads=hT_keys(t0, nt), writes=[("ohb", i2)], dma=True)
            for n in range(KD):
                b, bt, bk = B.next()
                for k in range(KD):
                    S.add("pe", mk("matmul", bt[:, b, 0:nt], wo[:, k, n * 128:(n + 1) * 128], y_[:, k, 0:nt],
                                                                                     start=(k == 0), stop=(k == KD - 1)),
                          reads=[("oyb", i2), ("wo", k)], writes=[bk])
                S.add("dve", mk("scalar_tensor_tensor",
                    out=h_[:, n, 0:nt], in0=bt[:, b, 0:nt], scalar=m["gh"][:, n:n + 1], in1=h_[:, n, 0:nt], op0=ALU.mult, op1=ALU.add),
                      reads=[bk, ("ohb", i2), m["key"]], writes=[("ohb", i2)])
            S.add("sp", mk("dma_start", out=hview[:, :, t0:t0 + nt], in_=h_[:, :, 0:nt]),
                  reads=[("ohb", i2)], writes=hT_keys(t0, nt), dma=True)


def mark_yT_done(C):
    S = C.S
    keys = [k for k in list(S.last_writer.keys()) if isinstance(k, tuple) and k[0] == "yT"]
    S.add("sp", mk("nop", ), reads=keys, writes=["yT_all"])


PAIRS = [[0, 1], [2, 3], [4, 5], [6, 7]]
XCH = (("lx", 384, 1024, F32), ("kT", 576, 1024, BF16), ("vA", 1024, 768, BF16),
       ("nkT", 256, 1024, BF16), ("nvA", 1024, 512, BF16))
XCHC = dict(lx=(384, NCX), kT=(576, NCX), vA=(NCX, 768), nkT=(256, NCX), nvA=(NCX, 512))


def build_fused(nlayers=DEPTH, dbg=False):
    C = Ctx()
    S = C.S
    nl = nlayers
    hT_in = C.dram("hT_in", [D, NT], F32, "ExternalInput")
    scin = C.dram("scin", [128, KD, 2], F32, "ExternalInput")
    ropeC = C.dram("ropeC", [32, NT], F32, "ExternalInput")
    ropeS = C.dram("ropeS", [32, NT], F32, "ExternalInput")
    halfmask = C.dram("halfmask", [128, 2], F32, "ExternalInput")
    natab = C.dram("natab", [nl, 128, 4, NTAB, 64], F32, "ExternalInput")
    w_mod = C.dram("w_mod", [nl, D, 9 * D], F32, "ExternalInput")
    b_mod = C.dram("b_mod", [nl, 128, 72], F32, "ExternalInput")
    vecs_d = C.dram("vecs", [nl, 128, NVEC], F32, "ExternalInput")
    f1_in = C.dram("f1_in", [nl, D, 2 * DFF], F32, "ExternalInput")
    f1_out = C.dram("f1_out", [nl, DFF, D], F32, "ExternalInput")
    f2_in = C.dram("f2_in", [nl, D, 2 * DFF], F32, "ExternalInput")
    f2_out = C.dram("f2_out", [nl, DFF, D], F32, "ExternalInput")
    winx = C.dram("winx", [nl, D, WINX], F32, "ExternalInput")
    wuq = C.dram("wuq", [nl, 256, 1152], F32, "ExternalInput")
    wukv = C.dram("wukv", [nl, 128, 768], F32, "ExternalInput")
    lruw = C.dram("lruw", [nl, 128, 1536], F32, "ExternalInput")
    wo = C.dram("wo", [nl, D, D], F32, "ExternalInput")
    hT = C.dram("hT", [D, NT], F32, "ExternalOutput")
    yT = C.dram("yT", [D, NT], BF16, "ExternalOutput" if dbg else "Internal")
    O = dict(qT=C.dram("qT", [6, 96, NT], BF16, "Internal"), nqT=C.dram("nqT", [256, NT], BF16, "Internal"),
             lgel=C.dram("lgel", [384, NT], F32, "Internal"))
    for nm, r, c, dt in XCH:
        O["L_" + nm] = C.dram("L_" + nm, [4, r, c], dt, "Internal")
        O["L_" + nm + "c"] = C.dram("L_" + nm + "c", list(XCHC[nm]), dt, "Internal")
        O["G_" + nm] = C.dram("G_" + nm, [4, 2, r, c], dt, "Internal")
    setup_consts(C)
    vt = C.sb("vecs", [128, nl, NVEC], F32)
    S.add("sp", mk("dma_start", out=vt[:], in_=vecs_d.rearrange("l p v -> p l v")), writes=["vecs"], dma=True)
    for t0 in range(0, NT, TB):
        S.add("sp", mk("dma_start", out=hT[:, t0:t0 + TB], in_=hT_in[:, t0:t0 + TB]), writes=[("hT", t0)], dma=True)
    S.mark("mods")
    mods = [mod_phase(C, w_mod[l], b_mod[l], vt[:, l, :], scin, "L%d" % l) for l in range(nl)]

    def exch(qq):
        for nm, r, c, dt in XCH:
            S.add("pool", mk("collective_compute", "AllGather", ALU.bypass, replica_groups=PAIRS,
                             ins=[O["L_" + nm][qq]], outs=[O["G_" + nm][qq].rearrange("r a b -> (r a) b")]),
                  reads=[("L", nm, qq)], writes=[("G", nm, qq)], cc=True)

    for l in range(nl):
        last = (l == DEPTH - 1)
        vl = vt[:, l, :]
        S.mark("ffn1_%d" % l)
        ffn_phase(C, hT, f1_in[l], f1_out[l], ffn_blocks(), mods[l], "ffn1")
        S.mark("m1_%d" % l)
        m1_phase(C, hT, dict(winx=winx[l], wuq=wuq[l], wukv=wukv[l]), vl, mods[l], ropeC, ropeS, O, exch)
        S.mark("lru_%d" % l)
        lru_phase(C, O, vl, lruw[l], halfmask, yT)
        S.mark("mla_%d" % l)
        mla_phase(C, O, yT, do_ctx=not last)
        S.mark("na_%d" % l)
        na_phase(C, O, natab[l], yT, do_ctx=not last)
        mark_yT_done(C)
        S.mark("outp_%d" % l)
        ffn_phase(C, hT, f2_in[l], f2_out[l], ffn_blocks(do_ctx=not last), mods[l], "ffn2",
                  pre=lambda tick: outproj_phase(C, hT, yT, wo[l], mods[l], do_ctx=not last, tick=tick))
    S.mark("end")
    C.marks = S.marks
    nc = C.finish()
    nc._marks = S.marks if hasattr(nc, "__dict__") else None
    return nc


def rope_swap_index():
    i = np.arange(32)
    axis, half, f = i // 16, (i // 8) % 2, i % 8
    return axis * 16 + (1 - half) * 8 + f


def rope_tables(s):
    i = np.arange(NL)
    row = (i // 32).astype(np.float32)
    col = (32 * s + i % 32).astype(np.float32)
    inv = (np.float32(10000.0) ** (-np.arange(0, 16, 2, dtype=np.float32) / np.float32(16))).astype(np.float32)
    Cc = np.ones((32, NT), np.float32)
    Ss = np.zeros((32, NT), np.float32)
    for d in range(32):
        axis, half, f = d // 16, (d // 8) % 2, d % 8
        pos = row if axis == 0 else col
        ang = (pos * inv[f]).astype(np.float32)
        Cc[d, :NL] = np.cos(ang)
        Ss[d, :NL] = (-np.sin(ang)) if half == 0 else np.sin(ang)
    return Cc, Ss


def fm(v, k=None):
    v = np.asarray(v, np.float32)
    return np.ascontiguousarray(v.reshape(-1, 128).T)


def build_na_tables(rpb_l, s):
    rpb_l = np.asarray(rpb_l, np.float32)
    c = 32 * s + np.arange(32)
    kc = np.arange(64)
    w0 = np.clip(c - 8, 0, 48)
    col_in = (kc[:, None] >= w0[None, :]) & (kc[:, None] < w0[None, :] + 16)
    col_off = np.clip(kc[:, None] - c[None, :] + 15, 0, 30)
    G = rpb_l[:, :, col_off]
    G = np.where(col_in[None, None], G, np.float32(NEG)).astype(np.float32)
    T = np.full((128, 4, NTAB, 64), NEG, np.float32)
    for (k0mr, off0, off1), ti in NA_CFG.items():
        for hf in range(2):
            for e in range(2):
                for eq in range(2):
                    rel = k0mr + e
                    off = (off0, off1)[eq]
                    if not (off <= rel < off + 8):
                        continue
                    di = rel - eq + 7
                    p0 = hf * 64 + e * 32
                    T[p0:p0 + 32, :, ti, eq * 32:(eq + 1) * 32] = np.transpose(G[:, di, hf * 32:(hf + 1) * 32, :], (1, 0, 2))
    return T


def layer_shared(inp, l):
    sw = rope_swap_index()
    w_in = np.asarray(inp["w_in"][l], np.float32)
    winx = np.concatenate([w_in[:, 0:1152], w_in[:, 1088:1184], w_in[:, 1088:1152], w_in[:, 1152 + sw],
                           w_in[:, 1184:1952]], axis=1)
    assert winx.shape[1] == WINX
    wuq0 = np.asarray(inp["mla_w_uq"][l], np.float32).reshape(256, 6, 96)
    wuq_sw = np.concatenate([wuq0[:, :, 0:64], wuq0[:, :, 64 + sw]], axis=2)
    wuq = np.stack([wuq0, wuq_sw], axis=2).reshape(256, 1152)
    wukv0 = np.asarray(inp["mla_w_ukv"][l], np.float32).reshape(128, 6, 128)
    wukv = np.concatenate([wukv0[:, :, 0:64].reshape(128, 384), wukv0[:, :, 64:128].reshape(128, 384)], axis=1)
    vecs = np.zeros((128, NVEC), np.float32)
    vecs[:, 0:8] = fm(inp["norm_ffn1"][l])
    vecs[:, 8:16] = fm(inp["norm_mix"][l])
    vecs[:, 16:24] = fm(inp["norm_ffn2"][l])
    vecs[:, 24:26] = fm(inp["mla_q_norm"][l])
    vecs[:, 26] = np.asarray(inp["mla_kv_norm"][l], np.float32)
    gq = np.asarray(inp["mla_q_gain"][l], np.float32)
    gk = np.asarray(inp["mla_k_gain"][l], np.float32)
    vecs[0:96, 27] = gq
    vecs[64:96, 28] = gq[64 + sw]
    vecs[0:96, 29] = gk
    vecs[64:96, 30] = gk[64 + sw]
    vecs[:, 31] = np.tile(np.asarray(inp["na_q_gain"][l], np.float32), 2)
    vecs[:, 32] = np.tile(np.asarray(inp["na_k_gain"][l], np.float32), 2)
    vecs[:, 33:36] = fm(inp["lru_conv_b"][l])
    cw = np.asarray(inp["lru_conv_w"][l], np.float32)
    for c in range(3):
        for j in range(4):
            vecs[:, 36 + 4 * c + j] = cw[j, c * 128:(c + 1) * 128]
    for d in range(2):
        vecs[:, 48 + d * 3:51 + d * 3] = fm(inp["lru_b_a"][l][d])
        vecs[:, 54 + d * 3:57 + d * 3] = fm(inp["lru_b_x"][l][d])
        vecs[:, 60 + d * 3:63 + d * 3] = fm(inp["lru_lambda"][l][d])
    lruw = np.zeros((128, 2, 2, 3, 128), np.float32)
    for g, nm in enumerate(("lru_w_a", "lru_w_x")):
        w = np.asarray(inp[nm][l], np.float32)
        for d in range(2):
            for c in range(3):
                for i in range(2):
                    lruw[64 * i:64 * i + 64, g, d, c, 64 * i:64 * i + 64] = w[d, 2 * c + i]
    b_mod = fm(inp["b_mod"][l])
    return dict(winx=np.ascontiguousarray(winx), wuq=np.ascontiguousarray(wuq), wukv=np.ascontiguousarray(wukv),
                vecs=vecs, lruw=np.ascontiguousarray(lruw.reshape(128, 1536)), b_mod=b_mod,
                w_mod=np.asarray(inp["w_mod"][l], np.float32))


_PROGS = {}


def host_inputs(inp, nlayers=DEPTH, ncore=8):
    x = np.asarray(inp["x"], np.float32)
    ctx = np.asarray(inp["ctx"], np.float32)
    c = np.asarray(inp["c"], np.float32)
    c_ctx = np.asarray(inp["c_ctx"], np.float32)
    Ls = [layer_shared(inp, l) for l in range(nlayers)]
    shared = dict(
        w_mod=np.ascontiguousarray(np.asarray(inp["w_mod"], np.float32)[:nlayers]),
        b_mod=np.stack([L["b_mod"] for L in Ls]), vecs=np.stack([L["vecs"] for L in Ls]),
        f1_in=np.ascontiguousarray(np.asarray(inp["ffn1_w_in"], np.float32)[:nlayers]),
        f1_out=np.ascontiguousarray(np.asarray(inp["ffn1_w_out"], np.float32)[:nlayers]),
        f2_in=np.ascontiguousarray(np.asarray(inp["ffn2_w_in"], np.float32)[:nlayers]),
        f2_out=np.ascontiguousarray(np.asarray(inp["ffn2_w_out"], np.float32)[:nlayers]),
        winx=np.stack([L["winx"] for L in Ls]), wuq=np.stack([L["wuq"] for L in Ls]),
        wukv=np.stack([L["wukv"] for L in Ls]), lruw=np.stack([L["lruw"] for L in Ls]),
        wo=np.ascontiguousarray(np.asarray(inp["w_out"], np.float32)[:nlayers]))
    natabs = [np.stack([build_na_tables(inp["na_rpb"][l], s) for l in range(nlayers)]) for s in range(2)]
    ropes = [rope_tables(s) for s in range(2)]
    maps = []
    for core in range(ncore):
        b, s = core // 2, core % 2
        xl = x[b].reshape(128, 64, D)[:, 32 * s:32 * s + 32, :].reshape(NL, D)
        hm = np.zeros((128, 2), np.float32)
        hm[:, s] = 1.0
        m = dict(shared)
        m.update(hT_in=np.ascontiguousarray(np.concatenate([xl, ctx[b]], axis=0).T),
                 scin=np.ascontiguousarray(np.stack([fm(c[b]), fm(c_ctx)], axis=2)),
                 ropeC=ropes[s][0], ropeS=ropes[s][1], halfmask=hm, natab=natabs[s])
        maps.append(m)
    return maps


def kernel(**inp):
    ncore = 8
    if "fused" not in _PROGS:
        _PROGS["fused"] = build_fused()
    maps = host_inputs(inp)
    res = run_bass_kernel_spmd(_PROGS["fused"], maps, core_ids=list(range(ncore))).results
    out = np.zeros((4, 128, 64, D), np.float32)
    for core in range(ncore):
        b, s = core // 2, core % 2
        out[b, :, 32 * s:32 * s + 32, :] = res[core]["hT"][:, :NL].T.reshape(128, 32, D)
    return out.reshape(4, 8192, D)
```

```python
from contextlib import ExitStack
import numpy as np
import concourse.bass as bass
import concourse.mybir as mybir
from concourse.bass_utils import run_bass_kernel_spmd

F32 = mybir.dt.float32
BF16 = mybir.dt.bfloat16
AF = mybir.ActivationFunctionType
ALU = mybir.AluOpType

D = 1024
DFF = 2816
KD = 8
JF = 22
TB = 256
EPS = 1e-6
NL = 4096
NCX = 256
NT = NL + NCX
DEPTH = 4
NEG = -30000.0
NVEC = 66
WINX = 2112
MLA_SCALE = 96 ** -0.5
NA_SCALE = 0.125
USE_LNEXP = False

ENGS = ("pe", "act", "dve", "pool", "sp")
NDMA_SLOTS = 8


class Op:
    __slots__ = ("eng", "emit", "is_dma", "pos", "deps", "signal", "ticket", "waits", "slot", "slot_val", "idx")

    def __init__(self, eng, emit, is_dma):
        self.eng = eng
        self.emit = emit
        self.is_dma = is_dma
        self.deps = []
        self.signal = False
        self.ticket = 0
        self.waits = []
        self.slot = None
        self.slot_val = 0


class Sched:
    def __init__(self):
        self.ops = []
        self.streams = {e: [] for e in ENGS}
        self.last_writer = {}
        self.readers = {}
        self.dma_count = {e: 0 for e in ENGS}
        self.slot_last = {}
        self.barrier_ops = []
        self.barrier_pending = set()
        self.marks = []

    def mark(self, name):
        self.marks.append((name, {e: len(v) for e, v in self.streams.items()}))

    def barrier(self):
        ops = [s[-1] for s in self.streams.values() if s]
        ops += list(self.slot_last.values())
        self.barrier_ops = ops
        self.barrier_pending = set(ENGS)

    def add(self, eng, emit, reads=(), writes=(), dma=False, cc=False):
        op = Op(eng, emit, dma or cc)
        op.idx = len(self.ops)
        op.pos = len(self.streams[eng])
        deps = set()
        if eng in self.barrier_pending:
            deps.update(self.barrier_ops)
            self.barrier_pending.discard(eng)
        for k in reads:
            w = self.last_writer.get(k)
            if w is not None:
                deps.add(w)
        for k in writes:
            w = self.last_writer.get(k)
            if w is not None:
                deps.add(w)
            rd = self.readers.get(k)
            if rd:
                deps.update(rd.values())
        if cc:
            op.slot = ("cc", 0)
            prev = self.slot_last.get(op.slot)
            op.slot_val = (prev.slot_val + 1) if prev is not None else 1
            self.slot_last[op.slot] = op
        elif dma:
            n = self.dma_count[eng]
            self.dma_count[eng] = n + 1
            op.slot = (eng, n % NDMA_SLOTS)
            prev = self.slot_last.get(op.slot)
            if prev is not None:
                deps.add(prev)
                op.slot_val = prev.slot_val + 16
            else:
                op.slot_val = 16
            self.slot_last[op.slot] = op
        deps.discard(op)
        op.deps = sorted(deps, key=lambda o: o.idx)
        for k in reads:
            rk = ("d", op.idx) if dma else eng
            self.readers.setdefault(k, {})[rk] = op
        for k in writes:
            self.last_writer[k] = op
            self.readers[k] = {}
        self.ops.append(op)
        self.streams[eng].append(op)
        return op

    def finalize(self):
        waited = {e: {} for e in ENGS}
        for op in self.ops:
            w = waited[op.eng]
            for p in op.deps:
                if p.is_dma:
                    key = ("dma", p.slot)
                    if w.get(key, 0) >= p.slot_val:
                        continue
                    w[key] = p.slot_val
                    op.waits.append(p)
                else:
                    if p.eng == op.eng:
                        if p.eng == "pe":
                            continue
                        if not op.is_dma and op.pos - p.pos > 2:
                            continue
                    key = ("eng", p.eng)
                    if w.get(key, -1) >= p.pos:
                        continue
                    w[key] = p.pos
                    p.signal = True
                    op.waits.append(p)
        for e in ENGS:
            t = 0
            for op in self.streams[e]:
                if not op.is_dma and op.signal:
                    t += 1
                    op.ticket = t

    def emit_all(self, nc, stack):
        self.finalize()
        eng_sem = {e: stack.enter_context(nc.semaphore("s_" + e)) for e in ENGS}
        dma_sem = {}
        for e in ENGS:
            for i in range(min(NDMA_SLOTS, self.dma_count[e])):
                dma_sem[(e, i)] = stack.enter_context(nc.semaphore("d_%s%d" % (e, i)))
        if ("cc", 0) in self.slot_last:
            dma_sem[("cc", 0)] = stack.enter_context(nc.semaphore("cc_sem"))
        block = stack.enter_context(nc.Block())

        def run_stream(e, engobj):
            for op in self.streams[e]:
                for p in op.waits:
                    if p.is_dma:
                        engobj.wait_ge(dma_sem[p.slot], p.slot_val)
                    else:
                        engobj.wait_ge(eng_sem[p.eng], p.ticket)
                ins = op.emit(engobj)
                if op.is_dma:
                    if op.slot[0] == "cc":
                        ins.then_inc(dma_sem[op.slot])
                    else:
                        ins.then_inc(dma_sem[op.slot], 16)
                elif op.signal:
                    ins.then_inc(eng_sem[op.eng], 1)
            for slot, last in self.slot_last.items():
                if slot[0] == e or (slot[0] == "cc" and e == "pool"):
                    engobj.wait_ge(dma_sem[slot], last.slot_val)

        if self.streams["sp"]:
            @block.sync
            def _(eng):
                run_stream("sp", eng)
        if self.streams["act"]:
            @block.scalar
            def _(eng):
                run_stream("act", eng)
        if self.streams["dve"]:
            @block.vector
            def _(eng):
                run_stream("dve", eng)
        if self.streams["pool"]:
            @block.gpsimd
            def _(eng):
                run_stream("pool", eng)
        if self.streams["pe"]:
            @block.tensor
            def _(eng):
                run_stream("pe", eng)


def mk(name, *args, **kw):
    return lambda e: getattr(e, name)(*args, **kw)


class Ctx:
    def __init__(self):
        self.nc = bass.Bass("TRN2", target_bir_lowering=False)
        self.S = Sched()
        self.stack = ExitStack()
        self.t = {}
        self.cur = self.stack
        self.uid = 0
        self.bank_rr = 0

    def sb(self, name, shape, dtype):
        self.uid += 1
        t = self.cur.enter_context(self.nc.sbuf_tensor("%s_%d" % (name, self.uid), list(shape), dtype))
        self.t[name] = t
        return t

    def ps(self, name, shape, dtype=F32):
        self.uid += 1
        t = self.cur.enter_context(self.nc.psum_tensor("%s_%d" % (name, self.uid), list(shape), dtype))
        self.t[name] = t
        return t

    def dram(self, name, shape, dtype, kind):
        return self.nc.dram_tensor(name, list(shape), dtype, kind=kind).ap()

    def scope(self):
        return _Scope(self)

    def finish(self):
        self.S.emit_all(self.nc, self.stack)
        self.stack.close()
        return self.nc


class _Scope:
    def __init__(self, C):
        self.C = C

    def __enter__(self):
        self.prev = self.C.cur
        self.st = ExitStack()
        self.C.cur = self.st
        return self

    def __exit__(self, *a):
        self.st.close()
        self.C.cur = self.prev
        self.C.S.barrier()
        return False


def setup_consts(C):
    S = C.S
    ones_b = C.sb("ones_b", [128, 128], BF16)
    ones_bd = C.sb("ones_bd", [128, 128], BF16)
    nhalf = C.sb("nhalf", [128, 512], F32)
    phalf = C.sb("phalf", [128, 512], F32)
    S.add("pool", mk("memset", ones_b[:], 1.0), writes=["ones_b"])
    S.add("pool", mk("memset", ones_bd[:], 0.0), writes=["ones_bd"])
    S.add("pool", mk("memset", ones_bd[0:64, 0:64], 1.0), writes=["ones_bd"])
    S.add("pool", mk("memset", ones_bd[64:128, 64:128], 1.0), writes=["ones_bd"])
    S.add("pool", mk("memset", nhalf[:], -0.5), writes=["nhalf"])
    S.add("pool", mk("memset", phalf[:], 0.5), writes=["phalf"])
    eps_t = C.sb("eps_t", [128, 1], F32)
    S.add("pool", mk("memset", eps_t[:], EPS), writes=["eps_t"])


class WLoader:
    def __init__(self, C, width, n=3):
        self.C = C
        self.st = [C.sb("wstg%d" % i, [128, width], F32) for i in range(n)]
        self.cnt = 0

    def load(self, dst_ap, src_ap, ncols, wkey, nparts=128):
        S = self.C.S
        i = self.cnt % len(self.st)
        self.cnt += 1
        st = self.st[i]
        S.add("act", mk("dma_start", out=st[0:nparts, 0:ncols], in_=src_ap), writes=[("wstg", i)], dma=True)
        S.add("dve", mk("tensor_copy", out=dst_ap, in_=st[0:nparts, 0:ncols]), reads=[("wstg", i)], writes=[wkey])


def rstd_from_psum(C, ps_ap, M, nt, inv_n, out_ap, tmp_ap, rkeys, wkey, tmpkey, lnexp=False):
    S = C.S
    eps_t = C.t["eps_t"]
    if lnexp and USE_LNEXP:
        S.add("act", mk("activation", out=tmp_ap, in_=ps_ap, func=AF.Ln, bias=eps_t[0:M, :], scale=inv_n),
              reads=rkeys + ["eps_t"], writes=[tmpkey])
        S.add("act", mk("activation", out=out_ap, in_=tmp_ap, func=AF.Exp, scale=-0.5),
              reads=[tmpkey], writes=[wkey])
        return
    S.add("act", mk("activation", out=tmp_ap, in_=ps_ap, func=AF.Sqrt, bias=eps_t[0:M, :], scale=inv_n),
          reads=rkeys + ["eps_t"], writes=[tmpkey])
    S.add("dve", mk("reciprocal", out=out_ap, in_=tmp_ap),
          reads=[tmpkey], writes=[wkey])


def mod_phase(C, w_mod_d, b_mod_d, vecs, scin_d, name):
    S = C.S
    GS = C.sb("GS" + name, [128, 3, KD, 2], F32)
    SH = C.sb("SH" + name, [128, 3, KD, 2], F32)
    GT = C.sb("GT" + name, [128, 3, KD, 2], F32)
    key = "mods" + name
    with C.scope():
        sc = C.sb("sc", [128, KD, 2], F32)
        scs = C.sb("scs", [128, KD, 2], F32)
        bm = C.sb("bm", [128, 72], F32)
        modT = C.sb("modT", [128, 72, 2], F32)
        wm = [C.sb("wm%d" % i, [128, KD, 1024], F32) for i in range(2)]
        mp = C.ps("modps", [128, 72, 2])
        S.add("sp", mk("dma_start", out=sc[:], in_=scin_d), writes=["sc"], dma=True)
        S.add("sp", mk("dma_start", out=bm[:], in_=b_mod_d), writes=["bm"], dma=True)
        S.add("act", mk("activation", out=scs[:], in_=sc[:], func=AF.Silu), reads=["sc"], writes=["scs"])
        wv = w_mod_d.rearrange("(k p) n -> p k n", p=128)
        for mc in range(9):
            w = wm[mc % 2]
            for k in range(KD):
                S.add("sp", mk("dma_start", out=w[:, k, :],
                                                                   in_=w_mod_d[k * 128:(k + 1) * 128, mc * 1024:(mc + 1) * 1024]),
                      writes=[("wm", mc % 2, k)], dma=True)
            for j in range(8):
                ch = mc * 8 + j
                for k in range(KD):
                    S.add("pe", mk("matmul", mp[:, ch, :], w[:, k, j * 128:(j + 1) * 128],
                                                                         scs[:, k, :], start=(k == 0), stop=(k == KD - 1)),
                          reads=[("wm", mc % 2, k), "scs"], writes=["modps"])
        for c in range(2):
            S.add("dve", mk("tensor_tensor", out=modT[:, :, c], in0=mp[:, :, c], in1=bm[:], op=ALU.add),
                  reads=["modps", "bm"], writes=["modT"])
        for w3 in range(3):
            g = vecs[:, 8 * w3:8 * w3 + 8]
            for c in range(2):
                S.add("dve", mk("scalar_tensor_tensor",
                    out=GS[:, w3, :, c], in0=modT[:, (3 * w3 + 1) * 8:(3 * w3 + 2) * 8, c], scalar=1.0, in1=g,
                    op0=ALU.add, op1=ALU.mult), reads=["modT", "vecs"], writes=[key])
                S.add("dve", mk("tensor_copy", out=SH[:, w3, :, c], in_=modT[:, (3 * w3) * 8:(3 * w3 + 1) * 8, c]),
                      reads=["modT"], writes=[key])
                gsc = 1.0 if w3 == 1 else 0.5
                S.add("dve", mk("tensor_scalar",
                    out=GT[:, w3, :, c], in0=modT[:, (3 * w3 + 2) * 8:(3 * w3 + 3) * 8, c], scalar1=gsc, scalar2=None,
                    op0=ALU.mult), reads=["modT"], writes=[key])
    mods = {}
    for w3, wn in enumerate(("ffn1", "mix", "ffn2")):
        for c, cn in enumerate(("lat", "ctx")):
            mods[(wn, cn)] = dict(gs=GS[:, w3, :, c], sh=SH[:, w3, :, c], gh=GT[:, w3, :, c], key=key)
    return mods


def ffn_phase(C, hT_d, w_in_d, w_out_d, blocks, mods, wn):
    S = C.S
    NSPL = 4
    W = 2 * DFF // NSPL
    with C.scope():
        win = C.sb("win", [128, KD, 2 * DFF], BF16)
        wout = C.sb("wout", [128, JF, D], BF16)
        hbs = [C.sb("hb%d" % i, [128, KD, TB], F32) for i in range(3)]
        xns = [C.sb("xn%d" % i, [128, KD, TB], BF16) for i in range(2)]
        sqs = [C.sb("sq%d" % i, [128, KD, TB], BF16) for i in range(2)]
        rstds = [C.sb("rstd%d" % i, [128, TB], F32) for i in range(2)]
        tmps = [C.sb("tmp%d" % i, [128, TB], F32) for i in range(2)]
        sgs = [C.sb("sg%d" % i, [128, TB], F32) for i in range(4)]
        gjs = [C.sb("gj%d" % i, [128, TB], BF16) for i in range(4)]
        acc = C.ps("acc", [128, 8, TB])
        gu = C.ps("gu", [128, 8, TB])
        ones_b = C.t["ones_b"]
        WL = WLoader(C, W)
        for s in (0, 2):
            for k in range(KD):
                WL.load(win[:, k, s * W:(s + 1) * W], w_in_d[k * 128:(k + 1) * 128, s * W:(s + 1) * W], W, ("win", k, s))
        for j in range(0, 11):
            WL.load(wout[:, j, :], w_out_d[j * 128:(j + 1) * 128, :], D, ("wout", j))
        for s in (1, 3):
            for k in range(KD):
                WL.load(win[:, k, s * W:(s + 1) * W], w_in_d[k * 128:(k + 1) * 128, s * W:(s + 1) * W], W, ("win", k, s))
        for j in range(11, JF):
            WL.load(wout[:, j, :], w_out_d[j * 128:(j + 1) * 128, :], D, ("wout", j))
        nb = len(blocks)
        hview = hT_d.rearrange("(k p) t -> p k t", p=128)
        gu_slot = [0]

        def next_gu():
            s = gu_slot[0]
            gu_slot[0] = (s + 1) % 4
            return s

        def load(b):
            t0 = blocks[b][0]
            hb = hbs[b % 3]
            S.add("sp", mk("dma_start", out=hb[:], in_=hview[:, :, t0:t0 + TB]),
                  reads=[("hT", t0)], writes=[("hb", b % 3)], dma=True)

        def norm_a(b):
            hb, sq = hbs[b % 3], sqs[b % 2]
            S.add("act", mk("activation", out=sq[:], in_=hb[:], func=AF.Square),
                  reads=[("hb", b % 3)], writes=[("sq", b % 2)])

        def norm_b(b):
            m = mods[(wn, blocks[b][1])]
            hb, sq, xn, rstd, tmp = hbs[b % 3], sqs[b % 2], xns[b % 2], rstds[b % 2], tmps[b % 2]
            s = next_gu()
            ssp = gu[:, 2 * s, :]
            for k in range(KD):
                S.add("pe", mk("matmul", ssp, ones_b[:], sq[:, k, :], start=(k == 0), stop=(k == KD - 1)),
                      reads=[("sq", b % 2), "ones_b"], writes=[("gu", s)])
            rstd_from_psum(C, ssp, 128, TB, 1.0 / D, rstd[:], tmp[:], [("gu", s)], ("rstd", b % 2), ("tmp", b % 2))
            for k in range(KD):
                S.add("dve", mk("scalar_tensor_tensor", out=tmp[:], in0=hb[:, k, :], scalar=m["gs"][:, k:k + 1],
                                                                   in1=rstd[:], op0=ALU.mult, op1=ALU.mult),
                      reads=[("hb", b % 3), ("rstd", b % 2), m["key"]], writes=[("tmp", b % 2)])
                S.add("act", mk("activation", out=xn[:, k, :], in_=tmp[:], func=AF.Identity,
                                                         bias=m["sh"][:, k:k + 1], scale=1.0),
                      reads=[("tmp", b % 2), m["key"]], writes=[("xn", b % 2, k)])

        def main(b):
            m = mods[(wn, blocks[b][1])]
            t0 = blocks[b][0]
            hb, xn = hbs[b % 3], xns[b % 2]
            xkeys = [("xn", b % 2, k) for k in range(KD)]
            for jj in range(JF + 2):
                if jj < JF:
                    j = jj
                    s = next_gu()
                    gp = gu[:, 2 * s, :]
                    up = gu[:, 2 * s + 1, :]
                    for k in range(KD):
                        S.add("pe", mk("matmul", gp, win[:, k, j * 128:(j + 1) * 128], xn[:, k, :],
                                                                       start=(k == 0), stop=(k == KD - 1)),
                              reads=[xkeys[k], ("win", k, (j * 128) // W), ("win", k, ((j + 1) * 128 - 1) // W)],
                              writes=[("gu", s)])
                    for k in range(KD):
                        c0 = DFF + j * 128
                        S.add("pe", mk("matmul", up, win[:, k, c0:c0 + 128], xn[:, k, :],
                                                                          start=False, stop=(k == KD - 1),
                                                                          skip_group_check=True),
                              reads=[xkeys[k], ("win", k, c0 // W), ("win", k, (c0 + 127) // W)],
                              writes=[("gu", s)])
                    sg, gj = sgs[j % 4], gjs[j % 4]
                    S.add("act", mk("activation", out=sg[:], in_=gp, func=AF.Silu),
                          reads=[("gu", s)], writes=[("sg", j % 4)])
                    S.add("dve", mk("tensor_tensor", out=gj[:], in0=up, in1=sg[:], op=ALU.mult),
                          reads=[("gu", s), ("sg", j % 4)], writes=[("gj", j % 4)])
                if jj >= 2:
                    j = jj - 2
                    gj = gjs[j % 4]
                    for n in range(KD):
                        f = (j == 0 and n % 2 == 0)
                        S.add("pe", mk("matmul",
                            acc[:, n, :], wout[:, j, n * 128:(n + 1) * 128], gj[:],
                            start=f, stop=(j == JF - 1), skip_group_check=True),
                              reads=[("gj", j % 4), ("wout", j)], writes=[("acc", n // 2)])
                if jj == 6 and b + 1 < nb:
                    norm_b(b + 1)
            for n in range(KD):
                S.add("dve", mk("scalar_tensor_tensor", out=hb[:, n, :], in0=acc[:, n, :],
                                                                   scalar=m["gh"][:, n:n + 1], in1=hb[:, n, :],
                                                                   op0=ALU.mult, op1=ALU.add),
                      reads=[("acc", n // 2), m["key"], ("hb", b % 3)], writes=[("hb", b % 3)])
            S.add("sp", mk("dma_start", out=hview[:, :, t0:t0 + TB], in_=hb[:]),
                  reads=[("hb", b % 3)], writes=[("hT", t0)], dma=True)

        load(0)
        if nb > 1:
            load(1)
        norm_a(0)
        norm_b(0)
        for b in range(nb):
            if b + 2 < nb:
                load(b + 2)
            if b + 1 < nb:
                norm_a(b + 1)
            main(b)


def ffn_blocks(do_ctx=True):
    return [(t0, "lat" if t0 < NL else "ctx") for t0 in range(0, NT if do_ctx else NL, TB)]


def hT_keys(t0, nt):
    return [("hT", t) for t in range(t0, t0 + nt, TB)]


class BankRR:
    def __init__(self, C, n=8):
        self.t = C.ps("bank", [128, n, 512])
        self.n = n
        self.i = 0

    def next(self):
        b = self.i
        self.i = (self.i + 1) % self.n
        return b, self.t, ("bank", b)


def m1_phase(C, hT_d, Wd, vecs, mods, ropeC_d, ropeS_d, O, exch=None):
    S = C.S
    blocks = [(t0, 512, "lat") for t0 in range(0, NL, 512)] + [(NL, 256, "ctx")]
    with C.scope():
        winx = C.sb("winx", [128, KD, WINX], BF16)
        wuq = C.sb("wuq", [128, 2, 1152], BF16)
        wukv = C.sb("wukv", [128, 768], BF16)
        WL = WLoader(C, 1152)
        for k in range(KD):
            for s2 in range(2):
                c0, c1 = s2 * 1056, (s2 + 1) * 1056
                WL.load(winx[:, k, c0:c1], Wd["winx"][k * 128:(k + 1) * 128, c0:c1], 1056, ("winx", k))
        for k in range(2):
            WL.load(wuq[:, k, :], Wd["wuq"][k * 128:(k + 1) * 128, :], 1152, "wuq")
        WL.load(wukv[:], Wd["wukv"], 768, "wukv")
        hb = [C.sb("mhb%d" % i, [128, KD, 512], F32) for i in range(2)]
        xn = C.sb("mxn", [128, KD, 512], BF16)
        sq = C.sb("msq", [128, KD, 512], BF16)
        rstd = C.sb("mrstd", [128, 512], F32)
        tmp = C.sb("mtmp", [128, 512], F32)
        rc = C.sb("rc", [128, 512], F32)
        rs = C.sb("rs", [128, 512], F32)
        st3 = [C.sb("st3_%d" % i, [128, 3, 512], F32) for i in range(2)]
        sq2 = C.sb("sq2", [128, 2, 512], BF16)
        cqn = C.sb("cqn", [128, 2, 512], BF16)
        ckvn = C.sb("ckvn", [128, 512], BF16)
        sqk = C.sb("sqk", [128, 512], BF16)
        tA = C.sb("tA", [128, 512], F32)
        t1 = [C.sb("t1_%d" % i, [128, 512], F32) for i in range(2)]
        t2 = [C.sb("t2_%d" % i, [128, 512], F32) for i in range(2)]
        rq = [C.sb("rq_%d" % i, [128, 512], F32) for i in range(2)]
        tq = [C.sb("tq_%d" % i, [128, 512], F32) for i in range(2)]
        sqh = [C.sb("sqh_%d" % i, [128, 512], BF16) for i in range(2)]
        qh = [C.sb("qh_%d" % i, [128, 512], BF16) for i in range(2)]
        kh = [C.sb("kh_%d" % i, [128, 512], BF16) for i in range(2)]
        nst = [C.sb("nst_%d" % i, [128, 2, 512], BF16) for i in range(2)]
        nva = C.sb("nva", [128, 4, 4, 128], BF16)
        va = C.sb("va", [128, 4, 6, 128], BF16)
        B = BankRR(C, 8)
        ones_b, ones_bd = C.t["ones_b"], C.t["ones_bd"]
        S.add("pool", mk("memset", nva[:], 1.0), writes=["nva"])
        S.add("pool", mk("memset", va[:], 1.0), writes=["va"])
        hview = hT_d.rearrange("(k p) t -> p k t", p=128)
        V = lambda c: vecs[:, c:c + 1]
        exch_done = set()

        def load(bi):
            t0, nt, kind = blocks[bi]
            h = hb[bi % 2]
            S.add("sp", mk("dma_start", out=h[:, :, 0:nt], in_=hview[:, :, t0:t0 + nt]),
                  reads=hT_keys(t0, nt), writes=[("mhb", bi % 2)], dma=True)

        load(0)
        for bi, (t0, nt, kind) in enumerate(blocks):
            if bi + 1 < len(blocks):
                load(bi + 1)
            m = mods[("mix", kind)]
            h = hb[bi % 2]
            hk = ("mhb", bi % 2)
            if kind == "lat":
                q4, off = t0 // 1024, t0 % 1024
                dst2 = lambda nm, q4=q4: O["L_" + nm][q4]
            else:
                q4, off = "c", 0
                dst2 = lambda nm: O["L_" + nm + "c"]
            S.add("sp", mk("dma_start", out=rc[64:96, 0:nt], in_=ropeC_d[:, t0:t0 + nt]),
                  writes=["rc"], dma=True)
            S.add("sp", mk("dma_start", out=rs[64:96, 0:nt], in_=ropeS_d[:, t0:t0 + nt]),
                  writes=["rs"], dma=True)
            S.add("act", mk("activation", out=sq[:, :, 0:nt], in_=h[:, :, 0:nt], func=AF.Square),
                  reads=[hk], writes=["msq"])
            b, bt, bk = B.next()
            for k in range(KD):
                S.add("pe", mk("matmul", bt[:, b, 0:nt], ones_b[:], sq[:, k, 0:nt],
                                                               start=(k == 0), stop=(k == KD - 1)),
                      reads=["msq", "ones_b"], writes=[bk])
            rstd_from_psum(C, bt[:, b, 0:nt], 128, nt, 1.0 / D, rstd[:, 0:nt], tmp[:, 0:nt], [bk], "mrstd", "mtmp", lnexp=True)
            for k in range(KD):
                S.add("dve", mk("scalar_tensor_tensor",
                    out=tmp[:, 0:nt], in0=h[:, k, 0:nt], scalar=m["gs"][:, k:k + 1], in1=rstd[:, 0:nt],
                    op0=ALU.mult, op1=ALU.mult), reads=[hk, "mrstd", m["key"]], writes=["mtmp"])
                S.add("act", mk("activation", out=xn[:, k, 0:nt], in_=tmp[:, 0:nt], func=AF.Identity,
                                                                   bias=m["sh"][:, k:k + 1], scale=1.0),
                      reads=["mtmp", m["key"]], writes=[("mxn", k)])

            def proj(col0, M):
                b, bt, bk = B.next()
                for k in range(KD):
                    S.add("pe", mk("matmul", bt[0:M, b, 0:nt], winx[:, k, col0:col0 + M], xn[:, k, 0:nt],
                                                             start=(k == 0), stop=(k == KD - 1)),
                          reads=[("mxn", k), ("winx", k)], writes=[bk])
                return bt[0:M, b, 0:nt], bk

            s3 = st3[0]
            for c in range(3):
                p, pk = proj(c * 128, 128)
                S.add("act", mk("activation", out=s3[:, c, 0:nt], in_=p, func=AF.Copy),
                      reads=[pk], writes=[("st3", 0)])
            S.add("sp", mk("dma_start", out=dst2("lx").rearrange("(c p) t -> p c t", p=128)[:, :, off:off + nt],
                                                     in_=s3[:, :, 0:nt]),
                  reads=[("st3", 0)], writes=[("L", "lx", q4)], dma=True)
            s3 = st3[1]
            for c in range(3):
                p, pk = proj(384 + c * 128, 128)
                S.add("act", mk("activation", out=s3[:, c, 0:nt], in_=p, func=AF.Gelu_apprx_tanh),
                      reads=[pk], writes=[("st3", 1)])
            S.add("sp", mk("dma_start", out=O["lgel"].rearrange("(c p) t -> p c t", p=128)[:, :, t0:t0 + nt],
                                                     in_=s3[:, :, 0:nt]),
                  reads=[("st3", 1)], writes=[("lgel", t0)], dma=True)
            pcq = []
            for c in range(2):
                p, pk = proj(768 + c * 128, 128)
                pcq.append((p, pk))
                S.add("act", mk("activation", out=sq2[:, c, 0:nt], in_=p, func=AF.Square),
                      reads=[pk], writes=["sq2"])
            b, bt, bk = B.next()
            for c in range(2):
                S.add("pe", mk("matmul", bt[:, b, 0:nt], ones_b[:], sq2[:, c, 0:nt], start=(c == 0), stop=(c == 1)),
                      reads=["sq2", "ones_b"], writes=[bk])
            rstd_from_psum(C, bt[:, b, 0:nt], 128, nt, 1.0 / 256, rstd[:, 0:nt], tmp[:, 0:nt], [bk], "mrstd", "mtmp", lnexp=True)
            for c in range(2):
                p, pk = pcq[c]
                S.add("dve", mk("scalar_tensor_tensor", out=cqn[:, c, 0:nt], in0=p, scalar=V(24 + c),
                                                                        in1=rstd[:, 0:nt], op0=ALU.mult, op1=ALU.mult),
                      reads=[pk, "mrstd", "vecs"], writes=["cqn"])
            p, pk = proj(1024, 128)
            S.add("act", mk("activation", out=sq2[:, 0, 0:nt], in_=p, func=AF.Square), reads=[pk], writes=["sq2"])
            b, bt, bk = B.next()
            S.add("pe", mk("matmul", bt[:, b, 0:nt], ones_b[:], sq2[:, 0, 0:nt], start=True, stop=True),
                  reads=["sq2", "ones_b"], writes=[bk])
            rstd_from_psum(C, bt[:, b, 0:nt], 128, nt, 1.0 / 128, rstd[:, 0:nt], tmp[:, 0:nt], [bk], "mrstd", "mtmp", lnexp=True)
            S.add("dve", mk("scalar_tensor_tensor", out=ckvn[:, 0:nt], in0=p, scalar=V(26), in1=rstd[:, 0:nt],
                                                               op0=ALU.mult, op1=ALU.mult),
                  reads=[pk, "mrstd", "vecs"], writes=["ckvn"])
            pkr, pkrk = proj(1152, 96)
            pks, pksk = proj(1248, 96)
            S.add("act", mk("activation", out=sqk[64:96, 0:nt], in_=pkr[64:96, :], func=AF.Square),
                  reads=[pkrk], writes=["sqk_r"])
            S.add("dve", mk("scalar_tensor_tensor", out=tA[64:96, 0:nt], in0=pkr[64:96, :], scalar=vecs[64:96, 29:30],
                                                          in1=rc[64:96, 0:nt], op0=ALU.mult, op1=ALU.mult),
                  reads=[pkrk, "rc", "vecs"], writes=["tA"])
            S.add("dve", mk("scalar_tensor_tensor", out=tmp[64:96, 0:nt], in0=pks[64:96, :], scalar=vecs[64:96, 30:31],
                                                          in1=rs[64:96, 0:nt], op0=ALU.mult, op1=ALU.mult),
                  reads=[pksk, "rs", "vecs"], writes=["mtmp"])
            S.add("dve", mk("tensor_tensor", out=tA[64:96, 0:nt], in0=tA[64:96, 0:nt], in1=tmp[64:96, 0:nt], op=ALU.add),
                  reads=["tA", "mtmp"], writes=["tA"])
            for which, col0, gcol, oname in (("q", 1344, 31, "nqT"), ("k", 1600, 32, "nkT")):
                ns = nst[0 if which == "q" else 1]
                nk_ = ("nst", which)
                for c in range(2):
                    p, pk = proj(col0 + c * 128, 128)
                    S.add("act", mk("activation", out=sq2[:, 0, 0:nt], in_=p, func=AF.Square), reads=[pk], writes=["sq2"])
                    b, bt, bk = B.next()
                    S.add("pe", mk("matmul", bt[:, b, 0:nt], ones_bd[:], sq2[:, 0, 0:nt], start=True, stop=True),
                          reads=["sq2", "ones_bd"], writes=[bk])
                    rstd_from_psum(C, bt[:, b, 0:nt], 128, nt, 1.0 / 64, rstd[:, 0:nt], tmp[:, 0:nt], [bk], "mrstd", "mtmp", lnexp=True)
                    S.add("dve", mk("scalar_tensor_tensor",
                        out=ns[:, c, 0:nt], in0=p, scalar=V(gcol), in1=rstd[:, 0:nt], op0=ALU.mult, op1=ALU.mult),
                          reads=[pk, "mrstd", "vecs"], writes=[nk_])
                odst = (O["nqT"].rearrange("(c p) t -> p c t", p=128)[:, :, t0:t0 + nt] if which == "q"
                        else dst2("nkT").rearrange("(c p) t -> p c t", p=128)[:, :, off:off + nt])
                S.add("sp", mk("dma_start", out=odst, in_=ns[:, :, 0:nt]),
                      reads=[nk_], writes=[("L", oname, q4) if which == "k" else (oname, t0)], dma=True)
            nsub = nt // 128
            for sb_ in range(nsub):
                b, bt, bk = B.next()
                for k in range(KD):
                    S.add("pe", mk("matmul", bt[:, b, 0:256], xn[:, k, sb_ * 128:(sb_ + 1) * 128],
                                                                           winx[:, k, 1856:2112], start=(k == 0), stop=(k == KD - 1)),
                          reads=[("mxn", k), ("winx", k)], writes=[bk])
                pv = bt[:, b, 0:256].rearrange("p (h d) -> p h d", h=4)
                S.add("act", mk("activation", out=nva[:, sb_, 0:4:2, 0:64], in_=pv[:, 0:4:2, :], func=AF.Copy),
                      reads=[bk], writes=["nva"])
                S.add("act", mk("activation", out=nva[:, sb_, 1:4:2, 64:128], in_=pv[:, 1:4:2, :], func=AF.Copy),
                      reads=[bk], writes=["nva"])
            S.add("sp", mk("dma_start",
                out=dst2("nvA")[off:off + nt, :].rearrange("(s p) (h d) -> p s h d", p=128, h=4), in_=nva[:, 0:nsub]),
                  reads=["nva"], writes=[("L", "nvA", q4)], dma=True)
            for sb_ in range(nsub):
                b, bt, bk = B.next()
                S.add("pe", mk("matmul", bt[:, b, 0:384], ckvn[:, sb_ * 128:(sb_ + 1) * 128],
                                                                  wukv[:, 384:768], start=True, stop=True),
                      reads=["ckvn", "wukv"], writes=[bk])
                pv = bt[:, b, 0:384].rearrange("p (h d) -> p h d", h=6)
                S.add("act", mk("activation", out=va[:, sb_, 0:6:2, 0:64], in_=pv[:, 0:6:2, :], func=AF.Copy),
                      reads=[bk], writes=["va"])
                S.add("act", mk("activation", out=va[:, sb_, 1:6:2, 64:128], in_=pv[:, 1:6:2, :], func=AF.Copy),
                      reads=[bk], writes=["va"])
            S.add("sp", mk("dma_start",
                out=dst2("vA")[off:off + nt, :].rearrange("(s p) (h d) -> p s h d", p=128, h=6), in_=va[:, 0:nsub]),
                  reads=["va"], writes=[("L", "vA", q4)], dma=True)
            for hd in range(6):
                i2 = hd % 2
                b, bt, bk = B.next()
                pkn = bt[0:64, b, 0:nt]
                S.add("pe", mk("matmul", pkn, wukv[:, hd * 64:(hd + 1) * 64], ckvn[:, 0:nt], start=True, stop=True),
                      reads=["ckvn", "wukv"], writes=[bk])
                S.add("act", mk("activation", out=sqk[0:64, 0:nt], in_=pkn, func=AF.Square),
                      reads=[bk], writes=["sqk_n"])
                b2, bt2, bk2 = B.next()
                pss = bt2[0:96, b2, 0:nt]
                S.add("pe", mk("matmul", pss, ones_b[0:96, 0:96], sqk[0:96, 0:nt], start=True, stop=True),
                      reads=["sqk_n", "sqk_r", "ones_b"], writes=[bk2])
                r_, t_ = rq[i2], tq[i2]
                rstd_from_psum(C, pss, 96, nt, 1.0 / 96, r_[0:96, 0:nt], t_[0:96, 0:nt], [bk2], ("rq", i2), ("tq", i2), lnexp=True)
                khh = kh[i2]
                S.add("dve", mk("scalar_tensor_tensor",
                    out=khh[0:64, 0:nt], in0=pkn, scalar=vecs[0:64, 29:30], in1=r_[0:64, 0:nt], op0=ALU.mult, op1=ALU.mult),
                      reads=[bk, ("rq", i2), "vecs"], writes=[("kh", i2)])
                S.add("dve", mk("tensor_tensor", out=khh[64:96, 0:nt], in0=tA[64:96, 0:nt],
                                                                        in1=r_[64:96, 0:nt], op=ALU.mult),
                      reads=["tA", ("rq", i2)], writes=[("kh", i2)])
                S.add("sp", mk("dma_start", out=dst2("kT")[hd * 96:(hd + 1) * 96, off:off + nt], in_=khh[0:96, 0:nt]),
                      reads=[("kh", i2)], writes=[("L", "kT", q4)], dma=True)
            for hd in range(6):
                i2 = hd % 2
                b, bt, bk = B.next()
                pq = bt[0:96, b, 0:nt]
                for k in range(2):
                    S.add("pe", mk("matmul", pq, wuq[:, k, hd * 192:hd * 192 + 96], cqn[:, k, 0:nt],
                                                                      start=(k == 0), stop=(k == 1)),
                          reads=["cqn", "wuq"], writes=[bk])
                b3, bt3, bk3 = B.next()
                pw = bt3[0:96, b3, 0:nt]
                for k in range(2):
                    S.add("pe", mk("matmul", pw, wuq[:, k, hd * 192 + 96:hd * 192 + 192], cqn[:, k, 0:nt],
                                                                      start=(k == 0), stop=(k == 1)),
                          reads=["cqn", "wuq"], writes=[bk3])
                sh_ = sqh[i2]
                S.add("act", mk("activation", out=sh_[0:96, 0:nt], in_=pq, func=AF.Square),
                      reads=[bk], writes=[("sqh", i2)])
                b2, bt2, bk2 = B.next()
                pss = bt2[0:96, b2, 0:nt]
                S.add("pe", mk("matmul", pss, ones_b[0:96, 0:96], sh_[0:96, 0:nt], start=True, stop=True),
                      reads=[("sqh", i2), "ones_b"], writes=[bk2])
                r_, t_ = rq[i2], tq[i2]
                rstd_from_psum(C, pss, 96, nt, 1.0 / 96, r_[0:96, 0:nt], t_[0:96, 0:nt], [bk2], ("rq", i2), ("tq", i2), lnexp=True)
                qhh, a1, a2 = qh[i2], t1[i2], t2[i2]
                S.add("dve", mk("scalar_tensor_tensor",
                    out=qhh[0:64, 0:nt], in0=pq[0:64, :], scalar=vecs[0:64, 27:28], in1=r_[0:64, 0:nt], op0=ALU.mult, op1=ALU.mult),
                      reads=[bk, ("rq", i2), "vecs"], writes=[("qh", i2)])
                S.add("dve", mk("scalar_tensor_tensor",
                    out=a1[64:96, 0:nt], in0=pq[64:96, :], scalar=vecs[64:96, 27:28], in1=rc[64:96, 0:nt], op0=ALU.mult, op1=ALU.mult),
                      reads=[bk, "rc", "vecs"], writes=[("t1", i2)])
                S.add("dve", mk("scalar_tensor_tensor",
                    out=a2[64:96, 0:nt], in0=pw[64:96, :], scalar=vecs[64:96, 28:29], in1=rs[64:96, 0:nt], op0=ALU.mult, op1=ALU.mult),
                      reads=[bk3, "rs", "vecs"], writes=[("t2", i2)])
                S.add("dve", mk("tensor_tensor", out=a1[64:96, 0:nt], in0=a1[64:96, 0:nt], in1=a2[64:96, 0:nt], op=ALU.add),
                      reads=[("t1", i2), ("t2", i2)], writes=[("t1", i2)])
                S.add("dve", mk("tensor_tensor", out=qhh[64:96, 0:nt], in0=a1[64:96, 0:nt],
                                                                              in1=r_[64:96, 0:nt], op=ALU.mult),
                      reads=[("t1", i2), ("rq", i2)], writes=[("qh", i2)])
                S.add("sp", mk("dma_start", out=O["qT"][hd, :, t0:t0 + nt], in_=qhh[0:96, 0:nt]),
                      reads=[("qh", i2)], writes=[("qT", hd, t0)], dma=True)
            if exch is not None:
                for qq in range(4):
                    if bi == min(2 * qq + 2, len(blocks) - 1) or (bi == len(blocks) - 1 and 2 * qq + 2 > bi):
                        if qq not in exch_done:
                            exch_done.add(qq)
                            exch(qq)


def lru_phase(C, I, vecs, lruw_d, halfmask_d, yT_d):
    S = C.S
    XW = 2 + NCX + 3 + 2 * NL + 2
    CT0 = 2
    LT0 = 2 + NCX + 3
    SEG = 512
    with C.scope():
        lw = C.sb("lw", [128, 1536], BF16)
        with C.scope():
            WL = WLoader(C, 1536, n=1)
            WL.load(lw[:], lruw_d, 1536, "lw")
        hm = C.sb("hm", [128, 2], F32)
        S.add("sp", mk("dma_start", out=hm[:], in_=halfmask_d), writes=["hm"], dma=True)
        par = C.sb("lpar", [128, 18], F32)
        e1 = C.sb("le1", [128, 6], F32)
        one_t = C.sb("one_t", [128, 1], F32)
        S.add("pool", mk("memset", one_t[:], 1.0), writes=["one_t"])
        S.add("act", mk("activation", out=e1[:], in_=vecs[:, 60:66], func=AF.Exp, scale=-1.0), reads=["vecs"], writes=["le1"])
        S.add("act", mk("activation", out=e1[:], in_=e1[:], func=AF.Ln, bias=one_t[:], scale=1.0), reads=["le1", "one_t"], writes=["le1"])
        S.add("dve", mk("tensor_scalar", out=par[:, 0:6], in0=e1[:], scalar1=-4.0, scalar2=None, op0=ALU.mult),
              reads=["le1"], writes=["lpar"])
        S.add("dve", mk("tensor_scalar", out=par[:, 6:18], in0=vecs[:, 48:60], scalar1=0.5, scalar2=None, op0=ALU.mult),
              reads=["vecs"], writes=["lpar"])
        xc = C.sb("xc", [128, XW], F32)
        xcb = C.sb("xcb", [128, XW], BF16)
        hsum = C.sb("hsum", [128, NT], F32)
        lg = C.sb("lg", [128, NT], F32)
        NB = 2
        tr = [C.sb("tr%d" % i, [128, SEG], F32) for i in range(NB)]
        ti = [C.sb("ti%d" % i, [128, SEG], F32) for i in range(NB)]
        aa = [C.sb("aa%d" % i, [128, SEG], F32) for i in range(NB)]
        a2 = [C.sb("a2%d" % i, [128, SEG], F32) for i in range(NB)]
        uu = [C.sb("uu%d" % i, [128, SEG], F32) for i in range(NB)]
        hh = [C.sb("hh%d" % i, [128, SEG], F32) for i in range(NB)]
        yb = C.sb("yb", [128, NT], BF16)
        stt = C.sb("lstate", [128, 1], F32)
        B = BankRR(C, 4)
        phalf = C.t["phalf"]
        for c in range(3):
            with C.scope():
                stg = C.sb("stg", [128, 2, NL], F32)
                xf = C.sb("xf", [128, XW], F32)
                S.add("dve", mk("memset", xf[:], 0.0), writes=["xf"])
                for hf in range(2):
                    for q4 in range(4):
                        S.add("sp", mk("dma_start", out=stg[:, hf, q4 * 1024:(q4 + 1) * 1024],
                                       in_=I["G_lx"][q4, hf, c * 128:(c + 1) * 128, :]),
                              reads=[("G", "lx", q4)], writes=[("stg", hf)], dma=True)
                S.add("sp", mk("dma_start", out=xf[:, CT0:CT0 + NCX], in_=I["L_lxc"][c * 128:(c + 1) * 128, :]),
                      reads=[("L", "lx", "c"), "xf"], writes=["xf"], dma=True)
                lat = xf[:, LT0:LT0 + 2 * NL].rearrange("p (r h c) -> p r h c", h=2, c=32)
                for hf in range(2):
                    eng = "act" if hf == 0 else "dve"
                    src = stg[:, hf, :].rearrange("p (r c) -> p r c", c=32)
                    if eng == "act":
                        S.add("act", mk("activation", out=lat[:, :, hf, :], in_=src, func=AF.Copy),
                              reads=[("stg", hf), "xf"], writes=["xf"])
                    else:
                        S.add("dve", mk("tensor_copy", out=lat[:, :, hf, :], in_=src),
                              reads=[("stg", hf), "xf"], writes=["xf"])
                n = XW - 3
                S.add("dve", mk("tensor_scalar", out=xc[:, 2:2 + n], in0=xf[:, 0:n], scalar1=vecs[:, 36 + 4 * c:37 + 4 * c],
                                                              scalar2=vecs[:, 33 + c:34 + c], op0=ALU.mult, op1=ALU.add),
                      reads=["xf", "vecs"], writes=["xc"])
                for j in range(1, 4):
                    S.add("dve", mk("scalar_tensor_tensor", out=xc[:, 2:2 + n], in0=xf[:, j:j + n],
                                                                              scalar=vecs[:, 36 + 4 * c + j:37 + 4 * c + j],
                                                                              in1=xc[:, 2:2 + n], op0=ALU.mult, op1=ALU.add),
                          reads=["xf", "vecs", "xc"], writes=["xc"])
                S.add("act", mk("activation", out=xcb[:, 2:2 + n], in_=xc[:, 2:2 + n], func=AF.Copy), reads=["xc"], writes=["xcb"])
            S.add("sp", mk("dma_start", out=lg[:], in_=I["lgel"][c * 128:(c + 1) * 128, :]),
                  reads=[("lgel", t) for t in list(range(0, NL, 512)) + [NL]], writes=["lg"], dma=True)
            segs = [(CT0, NCX, "ctx", 0)] + [(LT0 + i * SEG, SEG, "lat", i) for i in range(2 * NL // SEG)]
            for d in range(2):
                order = segs if d == 0 else [segs[0]] + segs[:0:-1]
                pidx = d * 3 + c
                hc = par[:, pidx:pidx + 1]
                hba = par[:, 6 + pidx:7 + pidx]
                hbx = par[:, 12 + pidx:13 + pidx]
                wa = lw[:, (0 * 6 + pidx) * 128:(0 * 6 + pidx + 1) * 128]
                wx = lw[:, (1 * 6 + pidx) * 128:(1 * 6 + pidx + 1) * 128]
                first = True
                for si, (x0, n, kind, li) in enumerate(order):
                    ib = si % NB
                    r_, i_, a_, q_, u_, h_ = tr[ib], ti[ib], aa[ib], a2[ib], uu[ib], hh[ib]
                    for p0 in range(0, n, 512):
                        pn = min(512, n - p0)
                        b, bt, bk = B.next()
                        S.add("pe", mk("matmul",
                            bt[:, b, 0:pn], wa, xcb[:, x0 + p0:x0 + p0 + pn], start=True, stop=True),
                              reads=["xcb", "lw"], writes=[bk])
                        S.add("act", mk("activation",
                            out=r_[:, p0:p0 + pn], in_=bt[:, b, 0:pn], func=AF.Tanh, bias=hba, scale=0.5),
                              reads=[bk, "lpar"], writes=[("tr", ib)])
                        b, bt, bk = B.next()
                        S.add("pe", mk("matmul",
                            bt[:, b, 0:pn], wx, xcb[:, x0 + p0:x0 + p0 + pn], start=True, stop=True),
                              reads=["xcb", "lw"], writes=[bk])
                        S.add("act", mk("activation",
                            out=i_[:, p0:p0 + pn], in_=bt[:, b, 0:pn], func=AF.Tanh, bias=hbx, scale=0.5),
                              reads=[bk, "lpar"], writes=[("ti", ib)])
                    S.add("act", mk("activation", out=a_[:, 0:n], in_=r_[:, 0:n], func=AF.Exp,
                                                                                 bias=hc, scale=hc),
                          reads=[("tr", ib), "lpar"], writes=[("aa", ib)])
                    S.add("dve", mk("tensor_tensor", out=q_[:, 0:n], in0=a_[:, 0:n], in1=a_[:, 0:n], op=ALU.mult),
                          reads=[("aa", ib)], writes=[("a2", ib)])
                    S.add("dve", mk("tensor_scalar", out=q_[:, 0:n], in0=q_[:, 0:n], scalar1=-0.25, scalar2=0.25,
                                                                      op0=ALU.mult, op1=ALU.add),
                          reads=[("a2", ib)], writes=[("a2", ib)])
                    S.add("act", mk("activation", out=q_[:, 0:n], in_=q_[:, 0:n], func=AF.Sqrt),
                          reads=[("a2", ib)], writes=[("a2", ib)])
                    S.add("dve", mk("scalar_tensor_tensor",
                        out=u_[:, 0:n], in0=i_[:, 0:n], scalar=1.0, in1=xc[:, x0:x0 + n], op0=ALU.add, op1=ALU.mult),
                          reads=[("ti", ib), "xc"], writes=[("uu", ib)])
                    S.add("dve", mk("tensor_tensor", out=u_[:, 0:n], in0=u_[:, 0:n], in1=q_[:, 0:n], op=ALU.mult),
                          reads=[("uu", ib), ("a2", ib)], writes=[("uu", ib)])
                    init = 0.0 if first else stt[:, 0:1]
                    if d == 0:
                        S.add("dve", mk("tensor_tensor_scan",
                            out=h_[:, 0:n], data0=a_[:, 0:n], data1=u_[:, 0:n], initial=init, op0=ALU.mult, op1=ALU.add),
                              reads=[("aa", ib), ("uu", ib), "lstate"], writes=[("hh", ib)])
                        S.add("dve", mk("tensor_copy", out=stt[:, 0:1], in_=h_[:, n - 1:n]),
                              reads=[("hh", ib)], writes=["lstate"])
                    else:
                        S.add("dve", mk("tensor_tensor_scan",
                            out=h_[:, 0:n][:, ::-1], data0=a_[:, 0:n][:, ::-1], data1=u_[:, 0:n][:, ::-1],
                            initial=init, op0=ALU.mult, op1=ALU.add),
                              reads=[("aa", ib), ("uu", ib), "lstate"], writes=[("hh", ib)])
                        S.add("dve", mk("tensor_copy", out=stt[:, 0:1], in_=h_[:, 0:1]),
                              reads=[("hh", ib)], writes=["lstate"])
                    first = False
                    if kind == "ctx":
                        if d == 0:
                            S.add("act", mk("activation", out=hsum[:, NL:NT], in_=h_[:, 0:NCX], func=AF.Copy),
                                  reads=[("hh", ib)], writes=[("hsum", "c")])
                        else:
                            S.add("dve", mk("tensor_tensor", out=hsum[:, NL:NT], in0=hsum[:, NL:NT], in1=h_[:, 0:NCX], op=ALU.add),
                                  reads=[("hh", ib), ("hsum", "c")], writes=[("hsum", "c")])
                    else:
                        rows = SEG // 64
                        hv = h_[:, 0:SEG].rearrange("p (r h c) -> p r h c", h=2, c=32)
                        ov = hsum[:, li * (SEG // 2):(li + 1) * (SEG // 2)].rearrange("p (r c) -> p r c", c=32)
                        hk_ = ("hsum", li)
                        if d == 0:
                            S.add("dve", mk("tensor_scalar", out=ov, in0=hv[:, :, 0, :], scalar1=hm[:, 0:1], scalar2=None,
                                                                                op0=ALU.mult),
                                  reads=[("hh", ib), "hm"], writes=[hk_])
                        else:
                            S.add("dve", mk("scalar_tensor_tensor", out=ov, in0=hv[:, :, 0, :], scalar=hm[:, 0:1], in1=ov,
                                                                                       op0=ALU.mult, op1=ALU.add),
                                  reads=[("hh", ib), "hm", hk_], writes=[hk_])
                        S.add("dve", mk("scalar_tensor_tensor", out=ov, in0=hv[:, :, 1, :], scalar=hm[:, 1:2], in1=ov,
                                                                                   op0=ALU.mult, op1=ALU.add),
                              reads=[("hh", ib), "hm", hk_], writes=[hk_])
            hkeys = [("hsum", "c")] + [("hsum", i) for i in range(2 * NL // SEG)]
            S.add("dve", mk("tensor_tensor", out=yb[:], in0=hsum[:], in1=lg[:], op=ALU.mult),
                  reads=hkeys + ["lg"], writes=["yb"])
            S.add("sp", mk("dma_start", out=yT_d[c * 128:(c + 1) * 128, :], in_=yb[:]),
                  reads=["yb"], writes=[("yT", c)], dma=True)


def mla_phase(C, I, yT_d, do_ctx=True):
    S = C.S
    NK = NCX + 2 * NL
    NJ = NK // 128
    with C.scope():
        kts = [C.sb("kt%d" % i, [128, NK], BF16) for i in range(2)]
        vas = [C.sb("vas%d" % i, [128, NJ, 128], BF16) for i in range(2)]
        qts = [C.sb("qt%d" % i, [128, NT], BF16) for i in range(2)]
        pts = [C.sb("pt%d" % i, [128, 2, 512], BF16) for i in range(3)]
        osb = [C.sb("osb%d" % i, [128, 512], F32) for i in range(2)]
        rcp = [C.sb("rcp%d" % i, [128, 512], F32) for i in range(2)]
        ysb = [C.sb("ysb%d" % i, [128, 512], BF16) for i in range(2)]
        sps = C.ps("sps", [128, 6, 512])
        ops = C.ps("ops", [128, 2, 512])
        kq_all = [(nm, t) for nm in ("kT0", "kT1") for t in range(0, NL, 512)]

        def loadh(hd):
            i2 = hd % 2
            kt, va, qt = kts[i2], vas[i2], qts[i2]
            S.add("sp", mk("dma_start", out=kt[0:96, 0:NCX], in_=I["L_kTc"][hd * 96:(hd + 1) * 96, :]),
                  reads=[("L", "kT", "c")], writes=[("kt", i2, "c")], dma=True)
            S.add("sp", mk("dma_start", out=va[:, 0:2, :],
                           in_=I["L_vAc"][:, hd * 128:(hd + 1) * 128].rearrange("(j p) d -> p j d", p=128)),
                  reads=[("L", "vA", "c")], writes=[("vas", i2, "c")], dma=True)
            for hf in range(2):
                for q4 in range(4):
                    k0 = NCX + hf * NL + q4 * 1024
                    S.add("sp", mk("dma_start", out=kt[0:96, k0:k0 + 1024], in_=I["G_kT"][q4, hf, hd * 96:(hd + 1) * 96, :]),
                          reads=[("G", "kT", q4)], writes=[("kt", i2, hf, q4)], dma=True)
                    j0 = 2 + hf * 32 + q4 * 8
                    S.add("sp", mk("dma_start", out=va[:, j0:j0 + 8, :],
                                   in_=I["G_vA"][q4, hf, :, hd * 128:(hd + 1) * 128].rearrange("(j p) d -> p j d", p=128)),
                          reads=[("G", "vA", q4)], writes=[("vas", i2, hf, q4)], dma=True)
            S.add("sp", mk("dma_start", out=qt[0:96, :], in_=I["qT"][hd, :, :]),
                  reads=[("qT", hd, t) for t in list(range(0, NL, 512)) + [NL]], writes=[("qts", i2)], dma=True)

        def jpart(j):
            return ("c",) if j < 2 else ((j - 2) // 32, ((j - 2) % 32) // 8)

        cnt = [0, 0]
        loadh(0)
        for hd in range(6):
            if hd + 1 < 6:
                loadh(hd + 1)
            i2 = hd % 2
            kt, va, qt = kts[i2], vas[i2], qts[i2]
            qblocks = [(q0, 512, 0, NJ) for q0 in range(0, NL, 512)] + ([(NL, 256, 0, 2)] if do_ctx else [])
            for (q0, nq, j0, j1) in qblocks:
                ob = cnt[1] % 2
                cnt[1] += 1
                oacc = ops[:, ob, 0:nq]
                pend = []
                prs = list(range(j0, j1, 2))
                for idx in range(len(prs) + 2):
                    if idx < len(prs):
                        j = prs[idx]
                        sb_ = cnt[0] % 3
                        cnt[0] += 1
                        sp2 = sps[:, 2 * sb_:2 * sb_ + 2, 0:nq]
                        for u in range(2):
                            S.add("pe", mk("matmul", sp2[:, u, :], kt[0:96, (j + u) * 128:(j + u + 1) * 128], qt[0:96, q0:q0 + nq],
                                           start=True, stop=True),
                                  reads=[("kt", i2) + jpart(j), ("qts", i2)], writes=[("sps", sb_)])
                        pt = pts[sb_]
                        S.add("act", mk("activation", out=pt[:, :, 0:nq], in_=sp2, func=AF.Exp, scale=MLA_SCALE),
                              reads=[("sps", sb_)], writes=[("pt", sb_)])
                        pend.append((j, sb_))
                    if idx >= 2:
                        j, sb_ = pend[idx - 2]
                        pt = pts[sb_]
                        for u in range(2):
                            S.add("pe", mk("matmul", oacc, va[:, j + u, :], pt[:, u, 0:nq],
                                           start=(idx == 2 and u == 0), stop=(idx == len(prs) + 1 and u == 1)),
                                  reads=[("pt", sb_), ("vas", i2) + jpart(j)], writes=[("ops", ob)])
                o_, r_, y_ = osb[ob], rcp[ob], ysb[ob]
                lo, hi = (0, 64) if hd % 2 == 0 else (64, 128)
                slo, shi = (64, 128) if hd % 2 == 0 else (0, 64)
                S.add("dve", mk("reciprocal", out=r_[slo:shi, 0:nq], in_=oacc[slo:shi, :]),
                      reads=[("ops", ob)], writes=[("rcp", ob)])
                S.add("act", mk("activation", out=o_[lo:hi, 0:nq], in_=oacc[lo:hi, :], func=AF.Copy),
                      reads=[("ops", ob)], writes=[("osb", ob)])
                S.add("dve", mk("tensor_copy", out=r_[lo:hi, 0:nq], in_=r_[slo:shi, 0:nq]),
                      reads=[("rcp", ob)], writes=[("rcp", ob)])
                S.add("dve", mk("tensor_tensor",
                    out=y_[lo:hi, 0:nq], in0=o_[lo:hi, 0:nq], in1=r_[lo:hi, 0:nq], op=ALU.mult),
                      reads=[("osb", ob), ("rcp", ob)], writes=[("ysb", ob)])
                row0 = 384 + hd * 64
                S.add("sp", mk("dma_start",
                    out=yT_d[row0:row0 + 64, q0:q0 + nq], in_=y_[lo:hi, 0:nq]),
                      reads=[("ysb", ob)], writes=[("yT", "m", hd, q0)], dma=True)


def na_pair_plan():
    cfg = {}
    plan = []
    r0f = lambda r: min(max(r - 4, 0), 120)
    for r in range(0, 128, 2):
        lo, hi = r0f(r), r0f(r + 1) + 8
        off0, off1 = r0f(r) - r, r0f(r + 1) - r
        chunks = []
        for ci in range(lo // 2, (hi - 1) // 2 + 1):
            key = (2 * ci - r, off0, off1)
            if key not in cfg:
                cfg[key] = len(cfg)
            chunks.append((ci, cfg[key]))
        plan.append(chunks)
    return plan, cfg


NA_PLAN, NA_CFG = na_pair_plan()
NTAB = len(NA_CFG)


def na_phase(C, I, natab_d, yT_d, do_ctx=True):
    S = C.S
    with C.scope():
        nk = C.sb("nk", [128, 2, 64, 2, 64], BF16)
        nkc = C.sb("nkc", [128, 2, NCX], BF16)
        nv = C.sb("nv", [128, 64, 4, 128], BF16)
        nvc = C.sb("nvc", [128, 2, 4, 128], BF16)
        nq = C.sb("nq", [128, 2, NT], BF16)
        tab = C.sb("natab", [128, 4, NTAB, 64], F32)
        ssb = [C.sb("nssb%d" % i, [128, 5, 64], F32) for i in range(8)]
        ptl = [C.sb("nptl%d" % i, [128, 5, 64], BF16) for i in range(8)]
        ptc = [C.sb("nptc%d" % i, [128, 2, 64], BF16) for i in range(8)]
        rcp = [C.sb("nrcp%d" % i, [128, 4, 64], F32) for i in range(2)]
        ysb = [C.sb("nysb%d" % i, [128, 2, 64], BF16) for i in range(2)]
        sps = C.ps("nsps", [128, 6, 512])
        ops = C.ps("nops", [128, 2, 512])
        S.add("sp", mk("dma_start", out=tab[:], in_=natab_d), writes=["natab"], dma=True)
        with C.scope():
            nks = C.sb("nks", [128, 2, 2, NL], BF16)
            for ck in range(2):
                for hf in range(2):
                    for q4 in range(4):
                        S.add("sp", mk("dma_start", out=nks[:, ck, hf, q4 * 1024:(q4 + 1) * 1024],
                                       in_=I["G_nkT"][q4, hf, ck * 128:(ck + 1) * 128, :]),
                              reads=[("G", "nkT", q4)], writes=[("nks", ck, hf)], dma=True)
                    src = nks[:, ck, hf, :].rearrange("p (ci t) -> p ci t", t=64)
                    if hf == 0:
                        S.add("act", mk("activation", out=nk[:, ck, :, hf, :], in_=src, func=AF.Copy),
                              reads=[("nks", ck, hf)], writes=["nk"])
                    else:
                        S.add("dve", mk("tensor_copy", out=nk[:, ck, :, hf, :], in_=src),
                              reads=[("nks", ck, hf)], writes=["nk"])
        for ck in range(2):
            S.add("sp", mk("dma_start", out=nkc[:, ck, :], in_=I["L_nkTc"][ck * 128:(ck + 1) * 128, :]),
                  reads=[("L", "nkT", "c")], writes=["nkc"], dma=True)
            S.add("sp", mk("dma_start", out=nq[:, ck, :], in_=I["nqT"][ck * 128:(ck + 1) * 128, :]),
                  reads=[("nqT", t) for t in list(range(0, NL, 512)) + [NL]], writes=["nq"], dma=True)
        for hf in range(2):
            for q4 in range(4):
                S.add("sp", mk("dma_start", out=nv[hf * 64:(hf + 1) * 64, q4 * 16:(q4 + 1) * 16],
                               in_=I["G_nvA"][q4, hf].rearrange("(ci q) (h d) -> q ci h d", q=64, h=4)),
                      reads=[("G", "nvA", q4)], writes=["nv"], dma=True)
        S.add("sp", mk("dma_start", out=nvc[:], in_=I["L_nvAc"].rearrange("(j p) (h d) -> p j h d", p=128, h=4)),
              reads=[("L", "nvA", "c")], writes=["nvc"], dma=True)
        cnt = [0, 0]
        NSB = 6

        def attend(q0, hd, loc, bset):
            ck, pl = hd // 2, (hd % 2) * 64
            sb_ = cnt[0] % NSB
            cnt[0] += 1
            nl = len(loc)
            sp_ = sps[:, sb_, 0:(nl + 2) * 64].rearrange("p (j q) -> p j q", q=64)
            qap = nq[pl:pl + 64, ck, q0:q0 + 64]
            for jl, (ci, ti_) in enumerate(loc):
                S.add("pe", mk("matmul", sp_[:, jl, :], nk[pl:pl + 64, ck, ci].rearrange("p h t -> p (h t)"), qap,
                               start=True, stop=True),
                      reads=["nk", "nq"], writes=[("nsps", sb_)])
            for jc in range(2):
                S.add("pe", mk("matmul", sp_[:, nl + jc, :], nkc[pl:pl + 64, ck, jc * 128:(jc + 1) * 128], qap,
                               start=True, stop=True),
                      reads=["nkc", "nq"], writes=[("nsps", sb_)])
            bi_ = bset * 4 + hd
            if nl:
                s_, p_ = ssb[bi_], ptl[bi_]
                ti0 = loc[0][1]
                assert [t for _, t in loc] == list(range(ti0, ti0 + nl))
                S.add("dve", mk("scalar_tensor_tensor", out=s_[:, 0:nl, :], in0=sp_[:, 0:nl, :], scalar=NA_SCALE,
                                in1=tab[:, hd, ti0:ti0 + nl, :], op0=ALU.mult, op1=ALU.add),
                      reads=[("nsps", sb_), "natab"], writes=[("nssb", bi_)])
                S.add("act", mk("activation", out=p_[:, 0:nl, :], in_=s_[:, 0:nl, :], func=AF.Exp),
                      reads=[("nssb", bi_)], writes=[("nptl", bi_)])
            pc_ = ptc[bi_]
            S.add("act", mk("activation", out=pc_[:, :, :], in_=sp_[:, nl:nl + 2, :], func=AF.Exp, scale=NA_SCALE),
                  reads=[("nsps", sb_)], writes=[("nptc", bi_)])

        def pv_and_store(q0, loc, bset, ob):
            ov = ops[:, ob, 0:256].rearrange("p (h q) -> p h q", q=64)
            nl = len(loc)
            for hd in range(4):
                bi_ = bset * 4 + hd
                p_, pc_ = ptl[bi_], ptc[bi_]
                for jl, (ci, ti_) in enumerate(loc):
                    S.add("pe", mk("matmul", ov[:, hd, :], nv[:, ci, hd, :], p_[:, jl, :], start=(jl == 0), stop=False),
                          reads=[("nptl", bi_), "nv"], writes=[("nops", ob)])
                for jc in range(2):
                    S.add("pe", mk("matmul", ov[:, hd, :], nvc[:, jc, hd, :], pc_[:, jc, :],
                                   start=(nl == 0 and jc == 0), stop=(jc == 1)),
                          reads=[("nptc", bi_), "nvc"], writes=[("nops", ob)])
            r_, y_ = rcp[ob], ysb[ob]
            S.add("dve", mk("reciprocal", out=r_[64:128, 0:4:2, :], in_=ov[64:128, 0:4:2, :]),
                  reads=[("nops", ob)], writes=[("nrcp", ob)])
            S.add("dve", mk("reciprocal", out=r_[0:64, 1:4:2, :], in_=ov[0:64, 1:4:2, :]),
                  reads=[("nops", ob)], writes=[("nrcp", ob)])
            S.add("dve", mk("tensor_copy", out=r_[0:64, 0:4:2, :], in_=r_[64:128, 0:4:2, :]),
                  reads=[("nrcp", ob)], writes=[("nrcp", ob)])
            S.add("dve", mk("tensor_copy", out=r_[64:128, 1:4:2, :], in_=r_[0:64, 1:4:2, :]),
                  reads=[("nrcp", ob)], writes=[("nrcp", ob)])
            S.add("dve", mk("tensor_tensor", out=y_[0:64, :, :], in0=ov[0:64, 0:4:2, :], in1=r_[0:64, 0:4:2, :], op=ALU.mult),
                  reads=[("nops", ob), ("nrcp", ob)], writes=[("nysb", ob)])
            S.add("dve", mk("tensor_tensor", out=y_[64:128, :, :], in0=ov[64:128, 1:4:2, :], in1=r_[64:128, 1:4:2, :], op=ALU.mult),
                  reads=[("nops", ob), ("nrcp", ob)], writes=[("nysb", ob)])
            S.add("sp", mk("dma_start", out=yT_d[768:1024, q0:q0 + 64].rearrange("(c p) q -> p c q", p=128), in_=y_[:, :, :]),
                  reads=[("nysb", ob)], writes=[("yT", "n", q0)], dma=True)

        items = [(pi * 64, chunks) for pi, chunks in enumerate(NA_PLAN)]
        if do_ctx:
            items += [(NL + qi * 64, []) for qi in range(NCX // 64)]
        for hd in range(4):
            attend(items[0][0], hd, items[0][1], 0)
        for ii, (q0, loc) in enumerate(items):
            if ii + 1 < len(items):
                for hd in range(4):
                    attend(items[ii + 1][0], hd, items[ii + 1][1], (ii + 1) % 2)
            pv_and_store(q0, loc, ii % 2, ii % 2)


def outproj_phase(C, hT_d, yT_d, wo_d, mods, do_ctx=True):
    S = C.S
    blocks = [(t0, 512, "lat") for t0 in range(0, NL, 512)] + ([(NL, 256, "ctx")] if do_ctx else [])
    with C.scope():
        wo = C.sb("wo", [128, KD, D], BF16)
        WL = WLoader(C, D)
        for k in range(KD):
            WL.load(wo[:, k, :], wo_d[k * 128:(k + 1) * 128, :], D, ("wo", k))
        yb = [C.sb("oyb%d" % i, [128, KD, 512], BF16) for i in range(2)]
        hb = [C.sb("ohb%d" % i, [128, KD, 512], F32) for i in range(2)]
        B = BankRR(C, 8)
        hview = hT_d.rearrange("(k p) t -> p k t", p=128)
        yview = yT_d.rearrange("(k p) t -> p k t", p=128)
        for bi, (t0, nt, kind) in enumerate(blocks):
            i2 = bi % 2
            m = mods[("mix", kind)]
            y_, h_ = yb[i2], hb[i2]
            S.add("sp", mk("dma_start", out=y_[:, :, 0:nt], in_=yview[:, :, t0:t0 + nt]),
                  reads=["yT_all"], writes=[("oyb", i2)], dma=True)
            S.add("sp", mk("dma_start", out=h_[:, :, 0:nt], in_=hview[:, :, t0:t0 + nt]),
                  reads=hT_keys(t0, nt), writes=[("ohb", i2)], dma=True)
            for n in range(KD):
                b, bt, bk = B.next()
                for k in range(KD):
                    S.add("pe", mk("matmul", bt[:, b, 0:nt], wo[:, k, n * 128:(n + 1) * 128], y_[:, k, 0:nt],
                                                                                     start=(k == 0), stop=(k == KD - 1)),
                          reads=[("oyb", i2), ("wo", k)], writes=[bk])
                S.add("dve", mk("scalar_tensor_tensor",
                    out=h_[:, n, 0:nt], in0=bt[:, b, 0:nt], scalar=m["gh"][:, n:n + 1], in1=h_[:, n, 0:nt], op0=ALU.mult, op1=ALU.add),
                      reads=[bk, ("ohb", i2), m["key"]], writes=[("ohb", i2)])
            S.add("sp", mk("dma_start", out=hview[:, :, t0:t0 + nt], in_=h_[:, :, 0:nt]),
                  reads=[("ohb", i2)], writes=hT_keys(t0, nt), dma=True)


def mark_yT_done(C):
    S = C.S
    keys = [k for k in list(S.last_writer.keys()) if isinstance(k, tuple) and k[0] == "yT"]
    S.add("sp", mk("nop", ), reads=keys, writes=["yT_all"])


PAIRS = [[0, 1], [2, 3], [4, 5], [6, 7]]
XCH = (("lx", 384, 1024, F32), ("kT", 576, 1024, BF16), ("vA", 1024, 768, BF16),
       ("nkT", 256, 1024, BF16), ("nvA", 1024, 512, BF16))
XCHC = dict(lx=(384, NCX), kT=(576, NCX), vA=(NCX, 768), nkT=(256, NCX), nvA=(NCX, 512))


def build_fused(nlayers=DEPTH, dbg=False):
    C = Ctx()
    S = C.S
    nl = nlayers
    hT_in = C.dram("hT_in", [D, NT], F32, "ExternalInput")
    scin = C.dram("scin", [128, KD, 2], F32, "ExternalInput")
    ropeC = C.dram("ropeC", [32, NT], F32, "ExternalInput")
    ropeS = C.dram("ropeS", [32, NT], F32, "ExternalInput")
    halfmask = C.dram("halfmask", [128, 2], F32, "ExternalInput")
    natab = C.dram("natab", [nl, 128, 4, NTAB, 64], F32, "ExternalInput")
    w_mod = C.dram("w_mod", [nl, D, 9 * D], F32, "ExternalInput")
    b_mod = C.dram("b_mod", [nl, 128, 72], F32, "ExternalInput")
    vecs_d = C.dram("vecs", [nl, 128, NVEC], F32, "ExternalInput")
    f1_in = C.dram("f1_in", [nl, D, 2 * DFF], F32, "ExternalInput")
    f1_out = C.dram("f1_out", [nl, DFF, D], F32, "ExternalInput")
    f2_in = C.dram("f2_in", [nl, D, 2 * DFF], F32, "ExternalInput")
    f2_out = C.dram("f2_out", [nl, DFF, D], F32, "ExternalInput")
    winx = C.dram("winx", [nl, D, WINX], F32, "ExternalInput")
    wuq = C.dram("wuq", [nl, 256, 1152], F32, "ExternalInput")
    wukv = C.dram("wukv", [nl, 128, 768], F32, "ExternalInput")
    lruw = C.dram("lruw", [nl, 128, 1536], F32, "ExternalInput")
    wo = C.dram("wo", [nl, D, D], F32, "ExternalInput")
    hT = C.dram("hT", [D, NT], F32, "ExternalOutput")
    yT = C.dram("yT", [D, NT], BF16, "ExternalOutput" if dbg else "Internal")
    O = dict(qT=C.dram("qT", [6, 96, NT], BF16, "Internal"), nqT=C.dram("nqT", [256, NT], BF16, "Internal"),
             lgel=C.dram("lgel", [384, NT], F32, "Internal"))
    for nm, r, c, dt in XCH:
        O["L_" + nm] = C.dram("L_" + nm, [4, r, c], dt, "Internal")
        O["L_" + nm + "c"] = C.dram("L_" + nm + "c", list(XCHC[nm]), dt, "Internal")
        O["G_" + nm] = C.dram("G_" + nm, [4, 2, r, c], dt, "Internal")
    setup_consts(C)
    vt = C.sb("vecs", [128, nl, NVEC], F32)
    S.add("sp", mk("dma_start", out=vt[:], in_=vecs_d.rearrange("l p v -> p l v")), writes=["vecs"], dma=True)
    for t0 in range(0, NT, TB):
        S.add("sp", mk("dma_start", out=hT[:, t0:t0 + TB], in_=hT_in[:, t0:t0 + TB]), writes=[("hT", t0)], dma=True)
    S.mark("mods")
    mods = [mod_phase(C, w_mod[l], b_mod[l], vt[:, l, :], scin, "L%d" % l) for l in range(nl)]

    def exch(qq):
        for nm, r, c, dt in XCH:
            S.add("pool", mk("collective_compute", "AllGather", ALU.bypass, replica_groups=PAIRS,
                             ins=[O["L_" + nm][qq]], outs=[O["G_" + nm][qq].rearrange("r a b -> (r a) b")]),
                  reads=[("L", nm, qq)], writes=[("G", nm, qq)], cc=True)

    for l in range(nl):
        last = (l == DEPTH - 1)
        vl = vt[:, l, :]
        S.mark("ffn1_%d" % l)
        ffn_phase(C, hT, f1_in[l], f1_out[l], ffn_blocks(), mods[l], "ffn1")
        S.mark("m1_%d" % l)
        m1_phase(C, hT, dict(winx=winx[l], wuq=wuq[l], wukv=wukv[l]), vl, mods[l], ropeC, ropeS, O, exch)
        S.mark("lru_%d" % l)
        lru_phase(C, O, vl, lruw[l], halfmask, yT)
        S.mark("mla_%d" % l)
        mla_phase(C, O, yT, do_ctx=not last)
        S.mark("na_%d" % l)
        na_phase(C, O, natab[l], yT, do_ctx=not last)
        mark_yT_done(C)
        S.mark("outp_%d" % l)
        outproj_phase(C, hT, yT, wo[l], mods[l], do_ctx=not last)
        S.mark("ffn2_%d" % l)
        ffn_phase(C, hT, f2_in[l], f2_out[l], ffn_blocks(do_ctx=not last), mods[l], "ffn2")
    S.mark("end")
    C.marks = S.marks
    nc = C.finish()
    nc._marks = S.marks if hasattr(nc, "__dict__") else None
    return nc


def rope_swap_index():
    i = np.arange(32)
    axis, half, f = i // 16, (i // 8) % 2, i % 8
    return axis * 16 + (1 - half) * 8 + f


def rope_tables(s):
    i = np.arange(NL)
    row = (i // 32).astype(np.float32)
    col = (32 * s + i % 32).astype(np.float32)
    inv = (np.float32(10000.0) ** (-np.arange(0, 16, 2, dtype=np.float32) / np.float32(16))).astype(np.float32)
    Cc = np.ones((32, NT), np.float32)
    Ss = np.zeros((32, NT), np.float32)
    for d in range(32):
        axis, half, f = d // 16, (d // 8) % 2, d % 8
        pos = row if axis == 0 else col
        ang = (pos * inv[f]).astype(np.float32)
        Cc[d, :NL] = np.cos(ang)
        Ss[d, :NL] = (-np.sin(ang)) if half == 0 else np.sin(ang)
    return Cc, Ss


def fm(v, k=None):
    v = np.asarray(v, np.float32)
    return np.ascontiguousarray(v.reshape(-1, 128).T)


def build_na_tables(rpb_l, s):
    rpb_l = np.asarray(rpb_l, np.float32)
    c = 32 * s + np.arange(32)
    kc = np.arange(64)
    w0 = np.clip(c - 8, 0, 48)
    col_in = (kc[:, None] >= w0[None, :]) & (kc[:, None] < w0[None, :] + 16)
    col_off = np.clip(kc[:, None] - c[None, :] + 15, 0, 30)
    G = rpb_l[:, :, col_off]
    G = np.where(col_in[None, None], G, np.float32(NEG)).astype(np.float32)
    T = np.full((128, 4, NTAB, 64), NEG, np.float32)
    for (k0mr, off0, off1), ti in NA_CFG.items():
        for hf in range(2):
            for e in range(2):
                for eq in range(2):
                    rel = k0mr + e
                    off = (off0, off1)[eq]
                    if not (off <= rel < off + 8):
                        continue
                    di = rel - eq + 7
                    p0 = hf * 64 + e * 32
                    T[p0:p0 + 32, :, ti, eq * 32:(eq + 1) * 32] = np.transpose(G[:, di, hf * 32:(hf + 1) * 32, :], (1, 0, 2))
    return T


def layer_shared(inp, l):
    sw = rope_swap_index()
    w_in = np.asarray(inp["w_in"][l], np.float32)
    winx = np.concatenate([w_in[:, 0:1152], w_in[:, 1088:1184], w_in[:, 1088:1152], w_in[:, 1152 + sw],
                           w_in[:, 1184:1952]], axis=1)
    assert winx.shape[1] == WINX
    wuq0 = np.asarray(inp["mla_w_uq"][l], np.float32).reshape(256, 6, 96)
    wuq_sw = np.concatenate([wuq0[:, :, 0:64], wuq0[:, :, 64 + sw]], axis=2)
    wuq = np.stack([wuq0, wuq_sw], axis=2).reshape(256, 1152)
    wukv0 = np.asarray(inp["mla_w_ukv"][l], np.float32).reshape(128, 6, 128)
    wukv = np.concatenate([wukv0[:, :, 0:64].reshape(128, 384), wukv0[:, :, 64:128].reshape(128, 384)], axis=1)
    vecs = np.zeros((128, NVEC), np.float32)
    vecs[:, 0:8] = fm(inp["norm_ffn1"][l])
    vecs[:, 8:16] = fm(inp["norm_mix"][l])
    vecs[:, 16:24] = fm(inp["norm_ffn2"][l])
    vecs[:, 24:26] = fm(inp["mla_q_norm"][l])
    vecs[:, 26] = np.asarray(inp["mla_kv_norm"][l], np.float32)
    gq = np.asarray(inp["mla_q_gain"][l], np.float32)
    gk = np.asarray(inp["mla_k_gain"][l], np.float32)
    vecs[0:96, 27] = gq
    vecs[64:96, 28] = gq[64 + sw]
    vecs[0:96, 29] = gk
    vecs[64:96, 30] = gk[64 + sw]
    vecs[:, 31] = np.tile(np.asarray(inp["na_q_gain"][l], np.float32), 2)
    vecs[:, 32] = np.tile(np.asarray(inp["na_k_gain"][l], np.float32), 2)
    vecs[:, 33:36] = fm(inp["lru_conv_b"][l])
    cw = np.asarray(inp["lru_conv_w"][l], np.float32)
    for c in range(3):
        for j in range(4):
            vecs[:, 36 + 4 * c + j] = cw[j, c * 128:(c + 1) * 128]
    for d in range(2):
        vecs[:, 48 + d * 3:51 + d * 3] = fm(inp["lru_b_a"][l][d])
        vecs[:, 54 + d * 3:57 + d * 3] = fm(inp["lru_b_x"][l][d])
        vecs[:, 60 + d * 3:63 + d * 3] = fm(inp["lru_lambda"][l][d])
    lruw = np.zeros((128, 2, 2, 3, 128), np.float32)
    for g, nm in enumerate(("lru_w_a", "lru_w_x")):
        w = np.asarray(inp[nm][l], np.float32)
        for d in range(2):
            for c in range(3):
                for i in range(2):
                    lruw[64 * i:64 * i + 64, g, d, c, 64 * i:64 * i + 64] = w[d, 2 * c + i]
    b_mod = fm(inp["b_mod"][l])
    return dict(winx=np.ascontiguousarray(winx), wuq=np.ascontiguousarray(wuq), wukv=np.ascontiguousarray(wukv),
                vecs=vecs, lruw=np.ascontiguousarray(lruw.reshape(128, 1536)), b_mod=b_mod,
                w_mod=np.asarray(inp["w_mod"][l], np.float32))


_PROGS = {}


def host_inputs(inp, nlayers=DEPTH, ncore=8):
    x = np.asarray(inp["x"], np.float32)
    ctx = np.asarray(inp["ctx"], np.float32)
    c = np.asarray(inp["c"], np.float32)
    c_ctx = np.asarray(inp["c_ctx"], np.float32)
    Ls = [layer_shared(inp, l) for l in range(nlayers)]
    shared = dict(
        w_mod=np.ascontiguousarray(np.asarray(inp["w_mod"], np.float32)[:nlayers]),
        b_mod=np.stack([L["b_mod"] for L in Ls]), vecs=np.stack([L["vecs"] for L in Ls]),
        f1_in=np.ascontiguousarray(np.asarray(inp["ffn1_w_in"], np.float32)[:nlayers]),
        f1_out=np.ascontiguousarray(np.asarray(inp["ffn1_w_out"], np.float32)[:nlayers]),
        f2_in=np.ascontiguousarray(np.asarray(inp["ffn2_w_in"], np.float32)[:nlayers]),
        f2_out=np.ascontiguousarray(np.asarray(inp["ffn2_w_out"], np.float32)[:nlayers]),
        winx=np.stack([L["winx"] for L in Ls]), wuq=np.stack([L["wuq"] for L in Ls]),
        wukv=np.stack([L["wukv"] for L in Ls]), lruw=np.stack([L["lruw"] for L in Ls]),
        wo=np.ascontiguousarray(np.asarray(inp["w_out"], np.float32)[:nlayers]))
    natabs = [np.stack([build_na_tables(inp["na_rpb"][l], s) for l in range(nlayers)]) for s in range(2)]
    ropes = [rope_tables(s) for s in range(2)]
    maps = []
    for core in range(ncore):
        b, s = core // 2, core % 2
        xl = x[b].reshape(128, 64, D)[:, 32 * s:32 * s + 32, :].reshape(NL, D)
        hm = np.zeros((128, 2), np.float32)
        hm[:, s] = 1.0
        m = dict(shared)
        m.update(hT_in=np.ascontiguousarray(np.concatenate([xl, ctx[b]], axis=0).T),
                 scin=np.ascontiguousarray(np.stack([fm(c[b]), fm(c_ctx)], axis=2)),
                 ropeC=ropes[s][0], ropeS=ropes[s][1], halfmask=hm, natab=natabs[s])
        maps.append(m)
    return maps


def kernel(**inp):
    ncore = 8
    if "fused" not in _PROGS:
        _PROGS["fused"] = build_fused()
    maps = host_inputs(inp)
    res = run_bass_kernel_spmd(_PROGS["fused"], maps, core_ids=list(range(ncore))).results
    out = np.zeros((4, 128, 64, D), np.float32)
    for core in range(ncore):
        b, s = core // 2, core % 2
        out[b, :, 32 * s:32 * s + 32, :] = res[core]["hT"][:, :NL].T.reshape(128, 32, D)
    return out.reshape(4, 8192, D)
```
